# Optimizing a Trainium2 kernel written in Bass

```python
import jax, jax.numpy as jnp
from jax import lax
import numpy as np

D_MODEL = 1024
BATCH = 2
SEQ = 8192
DEPTH = 1
DEC_BATCH = 128
DEC_SEQ = 4
PAST_LEN = 16384
PAGE_SIZE = 128

D_LRU = D_MODEL // 2
LRU_BLOCKS = 8
LRU_BLOCK = D_LRU // LRU_BLOCKS
LRU_CONV_W = 4
LRU_C = 8.0
N_Q_HEADS = 8
N_KV_HEADS = 2
GROUP = N_Q_HEADS // N_KV_HEADS
HEAD_DIM = 64
D_ATTN = N_Q_HEADS * HEAD_DIM
D_KV = N_KV_HEADS * HEAD_DIM
WINDOW = 128
BLOCK_Q = WINDOW
W_BUF = min(WINDOW, PAST_LEN)
D_MIX = D_LRU + D_ATTN
D_IN = 2 * D_LRU + D_ATTN + 2 * D_KV
SPLITS = (D_LRU, 2 * D_LRU, 2 * D_LRU + D_ATTN, 2 * D_LRU + D_ATTN + D_KV)
MEM_LEN = 256
N_MEM_HEADS = 4
MEM_HEAD_DIM = 128
D_MEM = N_MEM_HEADS * MEM_HEAD_DIM
D_FF = 3 * D_MODEL
FFN_CONV_W = 3
EPS = 1e-6
NEG_INF = -1e30

kernel_name = 'hymba_rglru_swa_sink_convffn_step'


def rmsnorm(x, g):
    xf = x.astype(jnp.float32)
    r = xf * lax.rsqrt(jnp.mean(xf * xf, axis=-1, keepdims=True) + EPS)
    return (r * g.astype(jnp.float32)).astype(x.dtype)


def alibi_slopes():
    m = 2.0 ** (-8.0 * np.arange(1, N_Q_HEADS + 1) / N_Q_HEADS)
    return jnp.asarray(m, jnp.float32).reshape(N_KV_HEADS, GROUP)


def causal_dwconv(x, prev, w, b):
    K = w.shape[0]
    T = x.shape[1]
    xp = jnp.concatenate([prev.astype(x.dtype), x], axis=1)
    y = b + sum(w[k] * xp[:, k:k + T] for k in range(K))
    return y, xp[:, T:]


def _lru_combine(e1, e2):
    a1, b1 = e1
    a2, b2 = e2
    return a1 * a2, a2 * b1 + b2


def rg_lru(x, h0, w_a, b_a, w_x, b_x, lam):
    B, T, _ = x.shape
    xf = x.astype(jnp.float32)
    xb = xf.reshape(B, T, LRU_BLOCKS, LRU_BLOCK)
    r = jax.nn.sigmoid(jnp.einsum('btnc,ncd->btnd', xb, w_a.astype(jnp.float32)).reshape(B, T, D_LRU) + b_a)
    i = jax.nn.sigmoid(jnp.einsum('btnc,ncd->btnd', xb, w_x.astype(jnp.float32)).reshape(B, T, D_LRU) + b_x)
    log_a = -LRU_C * r * jax.nn.softplus(-lam.astype(jnp.float32))
    a = jnp.exp(log_a)
    b = jnp.sqrt(-jnp.expm1(2.0 * log_a)) * (i * xf)
    b = b.at[:, 0].add(a[:, 0] * h0.astype(jnp.float32))
    _, h = lax.associative_scan(_lru_combine, (a, b), axis=1)
    return h.astype(x.dtype), h[:, -1].astype(x.dtype)


def sink_attention(q, k, v, dist, valid, slopes, sinks):
    s = jnp.einsum('...qhgd,...khd->...hgqk', q, k).astype(jnp.float32) * (HEAD_DIM ** -0.5)
    s = s - slopes[:, :, None, None] * dist.astype(jnp.float32)
    s = jnp.where(valid, s, NEG_INF)
    sink = jnp.broadcast_to(sinks.astype(jnp.float32).reshape(N_KV_HEADS, GROUP)[:, :, None, None], s.shape[:-1] + (1,))
    p = jax.nn.softmax(jnp.concatenate([s, sink], axis=-1), axis=-1)[..., :-1]
    return jnp.einsum('...hgqk,...khd->...qhgd', p.astype(v.dtype), v)


def swa_prompt(q, k, v, slopes, sinks):
    B, T = q.shape[:2]
    nb = T // BLOCK_Q
    qb = q.reshape(B, nb, BLOCK_Q, N_KV_HEADS, GROUP, HEAD_DIM)
    kb = k.reshape(B, nb, BLOCK_Q, N_KV_HEADS, HEAD_DIM)
    vb = v.reshape(B, nb, BLOCK_Q, N_KV_HEADS, HEAD_DIM)
    kk = jnp.concatenate([jnp.concatenate([jnp.zeros_like(kb[:, :1]), kb[:, :-1]], axis=1), kb], axis=2)
    vv = jnp.concatenate([jnp.concatenate([jnp.zeros_like(vb[:, :1]), vb[:, :-1]], axis=1), vb], axis=2)
    qi = jnp.arange(BLOCK_Q)[:, None]
    kj = jnp.arange(2 * BLOCK_Q)[None, :]
    dist = qi + BLOCK_Q - kj
    kpos = jnp.arange(nb)[:, None] * BLOCK_Q - BLOCK_Q + kj
    valid = (dist >= 0) & (dist < WINDOW) & (kpos[:, None, :] >= 0)
    o = sink_attention(qb, kk, vv, dist, valid[:, None, None], slopes, sinks)
    return o.reshape(B, T, D_ATTN), k[:, -W_BUF:], v[:, -W_BUF:]


def swa_sample(q, k, v, cache_k, cache_v, slopes, sinks):
    B, T = q.shape[:2]
    kk = jnp.concatenate([cache_k.astype(k.dtype), k], axis=1)
    vv = jnp.concatenate([cache_v.astype(v.dtype), v], axis=1)
    qi = jnp.arange(T)[:, None]
    kj = jnp.arange(W_BUF + T)[None, :]
    dist = qi + W_BUF - kj
    valid = (dist >= 0) & (dist < WINDOW)
    o = sink_attention(q.reshape(B, T, N_KV_HEADS, GROUP, HEAD_DIM), kk, vv, dist, valid, slopes, sinks)
    return o.reshape(B, T, D_ATTN), kk[:, -W_BUF:], vv[:, -W_BUF:]


def layer(x, mem_k, mem_v, conv_prev, h0, ffn_prev, swa_buf, p, slopes):
    B, T, _ = x.shape
    n1 = rmsnorm(x, p['g_mix'])
    xr, gate, q, k, v = jnp.split(n1 @ p['w_in'], SPLITS, axis=-1)
    xc, conv_state = causal_dwconv(xr, conv_prev, p['w_lru_conv'], p['b_lru_conv'])
    h, h_last = rg_lru(xc, h0, p['w_lru_a'], p['b_lru_a'], p['w_lru_x'], p['b_lru_x'], p['lru_lambda'])
    y_lru = h * jax.nn.gelu(gate)
    k = k.reshape(B, T, N_KV_HEADS, HEAD_DIM)
    v = v.reshape(B, T, N_KV_HEADS, HEAD_DIM)
    if swa_buf is None:
        y_att, buf_k, buf_v = swa_prompt(q, k, v, slopes, p['attn_sinks'])
    else:
        y_att, buf_k, buf_v = swa_sample(q, k, v, swa_buf[0], swa_buf[1], slopes, p['attn_sinks'])
    x = x + jnp.concatenate([y_lru, y_att], axis=-1) @ p['w_out']
    qc = (rmsnorm(x, p['g_cross']) @ p['w_mem_q']).reshape(B, T, N_MEM_HEADS, MEM_HEAD_DIM)
    s = jnp.einsum('bthd,bshd->bhts', qc, mem_k.astype(qc.dtype)).astype(jnp.float32) * (MEM_HEAD_DIM ** -0.5)
    pr = jax.nn.softmax(s, axis=-1)
    oc = jnp.einsum('bhts,bshd->bthd', pr.astype(x.dtype), mem_v.astype(x.dtype)).reshape(B, T, D_MEM)
    x = x + oc @ p['w_mem_o']
    n3 = rmsnorm(x, p['g_ffn'])
    g, ffn_state = causal_dwconv(n3 @ p['w_ffn_gate'], ffn_prev, p['w_ffn_conv'], p['b_ffn_conv'])
    x = x + (jax.nn.gelu(g) * (n3 @ p['w_ffn_up'])) @ p['w_ffn_down']
    return x, (buf_k, buf_v, conv_state, h_last, ffn_state)


def setup_inputs(seed: int = 0) -> dict:
    key = jax.random.key(seed)
    ks = iter(jax.random.split(key, 40))

    def nrm(shape, scale=1.0):
        return scale * jax.random.normal(next(ks), shape, jnp.float32)

    def gain(shape):
        return 1.0 + nrm(shape, 0.02)

    u = jax.random.uniform(next(ks), (DEPTH, D_LRU), jnp.float32, minval=0.9, maxval=0.999)
    sa = u ** (1.0 / LRU_C)
    lam = jnp.log(sa) - jnp.log1p(-sa)
    return {
        'x_prompt': nrm((BATCH, SEQ, D_MODEL)),
        'x_sample': nrm((DEC_BATCH, DEC_SEQ, D_MODEL)),
        'cache_swa_k': nrm((DEPTH, DEC_BATCH, W_BUF, N_KV_HEADS, HEAD_DIM)),
        'cache_swa_v': nrm((DEPTH, DEC_BATCH, W_BUF, N_KV_HEADS, HEAD_DIM)),
        'cache_mem_k': nrm((DEPTH, DEC_BATCH, MEM_LEN, N_MEM_HEADS, MEM_HEAD_DIM)),
        'cache_mem_v': nrm((DEPTH, DEC_BATCH, MEM_LEN, N_MEM_HEADS, MEM_HEAD_DIM)),
        'state_lru_conv': nrm((DEPTH, DEC_BATCH, LRU_CONV_W - 1, D_LRU)),
        'state_lru_h': nrm((DEPTH, DEC_BATCH, D_LRU)),
        'state_ffn_conv': nrm((DEPTH, DEC_BATCH, FFN_CONV_W - 1, D_FF)),
        'mem_prompt': nrm((BATCH, MEM_LEN, D_MODEL)),
        'g_mix': gain((DEPTH, D_MODEL)),
        'w_in': nrm((DEPTH, D_MODEL, D_IN), D_MODEL ** -0.5),
        'w_lru_conv': nrm((DEPTH, LRU_CONV_W, D_LRU), LRU_CONV_W ** -0.5),
        'b_lru_conv': nrm((DEPTH, D_LRU), 0.02),
        'w_lru_a': nrm((DEPTH, LRU_BLOCKS, LRU_BLOCK, LRU_BLOCK), LRU_BLOCK ** -0.5),
        'b_lru_a': nrm((DEPTH, D_LRU), 0.02),
        'w_lru_x': nrm((DEPTH, LRU_BLOCKS, LRU_BLOCK, LRU_BLOCK), LRU_BLOCK ** -0.5),
        'b_lru_x': nrm((DEPTH, D_LRU), 0.02),
        'lru_lambda': lam,
        'attn_sinks': nrm((DEPTH, N_Q_HEADS)),
        'w_out': nrm((DEPTH, D_MIX, D_MODEL), D_MIX ** -0.5),
        'g_cross': gain((DEPTH, D_MODEL)),
        'g_mem': gain((DEPTH, D_MODEL)),
        'w_mem_q': nrm((DEPTH, D_MODEL, D_MEM), D_MODEL ** -0.5),
        'w_mem_k': nrm((DEPTH, D_MODEL, D_MEM), D_MODEL ** -0.5),
        'w_mem_v': nrm((DEPTH, D_MODEL, D_MEM), D_MODEL ** -0.5),
        'w_mem_o': nrm((DEPTH, D_MEM, D_MODEL), D_MEM ** -0.5),
        'g_ffn': gain((DEPTH, D_MODEL)),
        'w_ffn_gate': nrm((DEPTH, D_MODEL, D_FF), D_MODEL ** -0.5),
        'w_ffn_up': nrm((DEPTH, D_MODEL, D_FF), D_MODEL ** -0.5),
        'w_ffn_conv': nrm((DEPTH, FFN_CONV_W, D_FF), FFN_CONV_W ** -0.5),
        'b_ffn_conv': nrm((DEPTH, D_FF), 0.02),
        'w_ffn_down': nrm((DEPTH, D_FF, D_MODEL), D_FF ** -0.5),
        'g_final': gain((D_MODEL,)),
    }


def reference(x_prompt, x_sample, cache_swa_k, cache_swa_v, cache_mem_k, cache_mem_v, state_lru_conv, state_lru_h, state_ffn_conv, mem_prompt, g_mix, w_in, w_lru_conv, b_lru_conv, w_lru_a, b_lru_a, w_lru_x, b_lru_x, lru_lambda, attn_sinks, w_out, g_cross, g_mem, w_mem_q, w_mem_k, w_mem_v, w_mem_o, g_ffn, w_ffn_gate, w_ffn_up, w_ffn_conv, b_ffn_conv, w_ffn_down, g_final):
    slopes = alibi_slopes()
    xp, xs = x_prompt, x_sample
    B = x_prompt.shape[0]
    p_states, s_states = [], []
    for l in range(DEPTH):
        p = dict(g_mix=g_mix[l], w_in=w_in[l], w_lru_conv=w_lru_conv[l], b_lru_conv=b_lru_conv[l],
                 w_lru_a=w_lru_a[l], b_lru_a=b_lru_a[l], w_lru_x=w_lru_x[l], b_lru_x=b_lru_x[l],
                 lru_lambda=lru_lambda[l], attn_sinks=attn_sinks[l], w_out=w_out[l], g_cross=g_cross[l],
                 w_mem_q=w_mem_q[l], w_mem_o=w_mem_o[l], g_ffn=g_ffn[l], w_ffn_gate=w_ffn_gate[l],
                 w_ffn_up=w_ffn_up[l], w_ffn_conv=w_ffn_conv[l], b_ffn_conv=b_ffn_conv[l], w_ffn_down=w_ffn_down[l])
        mem_n = rmsnorm(mem_prompt, g_mem[l])
        mk = (mem_n @ w_mem_k[l]).reshape(B, MEM_LEN, N_MEM_HEADS, MEM_HEAD_DIM)
        mv = (mem_n @ w_mem_v[l]).reshape(B, MEM_LEN, N_MEM_HEADS, MEM_HEAD_DIM)
        xp, (pk, pv, pconv, ph, pffn) = layer(
            xp, mk, mv,
            jnp.zeros((B, LRU_CONV_W - 1, D_LRU), xp.dtype),
            jnp.zeros((B, D_LRU), xp.dtype),
            jnp.zeros((B, FFN_CONV_W - 1, D_FF), xp.dtype),
            None, p, slopes)
        p_states.append((pk, pv, mk, mv, pconv, ph, pffn))
        xs, (sk, sv, sconv, sh, sffn) = layer(
            xs, cache_mem_k[l], cache_mem_v[l], state_lru_conv[l], state_lru_h[l], state_ffn_conv[l],
            (cache_swa_k[l], cache_swa_v[l]), p, slopes)
        s_states.append((sk, sv, sconv, sh, sffn))
    p_swa_k, p_swa_v, p_mem_k, p_mem_v, p_lru_conv, p_lru_h, p_ffn_conv = [jnp.stack(a, axis=0) for a in zip(*p_states)]
    s_swa_k, s_swa_v, s_lru_conv, s_lru_h, s_ffn_conv = [jnp.stack(a, axis=0) for a in zip(*s_states)]
    y_prompt = rmsnorm(xp, g_final)
    y_sample = rmsnorm(xs, g_final)
    return (y_prompt, y_sample, p_swa_k, p_swa_v, p_mem_k, p_mem_v, p_lru_conv, p_lru_h, p_ffn_conv, s_swa_k, s_swa_v, s_lru_conv, s_lru_h, s_ffn_conv)
```

```python
import contextlib
import numpy as np
import concourse.bass as bass
import concourse.mybir as mybir
from concourse.bass_utils import run_bass_kernel_spmd

F32 = mybir.dt.float32
BF16 = mybir.dt.bfloat16
ACT = mybir.ActivationFunctionType
ALU = mybir.AluOpType
AX = mybir.AxisListType

NCORES = 8
D = 1024
PRE_T = 48
MAIN_T = 16
NEG = -240000.0
SC_MEM = 128.0 ** -0.5

V_GMIX, V_GCROSS, V_GFFN, V_GMEM = 0, 8, 16, 24
V_CW = 32
V_BCONV = 48
V_BA = 52
V_BX = 56
V_LAM = 60
V_FW = 64
V_FB = 136
V_SINK = 160
V_FLAG = 168
V_EPS = 172
V_ONE = 173
NV = 176


class Sched:
    def __init__(self, nc, es, dma_pool=None):
        self.nc = nc
        self.ops = []
        self.last_w = {}
        self.readers = {}
        self.dma_pool = dma_pool or {"sync": 8, "gpsimd": 6, "scalar": 4}
        self.sems = {}
        for e in ("scalar", "vector", "gpsimd", "tensor"):
            self.sems[e] = es.enter_context(nc.semaphore("c_" + e))
        self.dsems = {}
        for q, n in self.dma_pool.items():
            self.dsems[q] = [es.enter_context(nc.semaphore("d_%s%d" % (q, i))) for i in range(n)]
        self.dcount = {q: 0 for q in self.dma_pool}

    def add(self, eng, fn, r=(), w=(), dma=False):
        i = len(self.ops)
        deps = {}
        for k in r:
            a = self.last_w.get(k)
            if a is not None:
                deps[a] = "raw"
        for k in w:
            a = self.last_w.get(k)
            if a is not None:
                deps[a] = "raw"
            for a in self.readers.get(k, ()):
                deps.setdefault(a, "war")
        op = dict(eng=eng, fn=fn, deps=deps, dma=dma, signal=False)
        if dma:
            j = self.dcount[eng]
            self.dcount[eng] += 1
            n = self.dma_pool[eng]
            op["dsem"] = (eng, j % n)
            op["dval"] = 16 * (j // n + 1)
        self.ops.append(op)
        for k in w:
            self.last_w[k] = i
            self.readers[k] = []
        for k in r:
            lst = self.readers.setdefault(k, [])
            if not dma:
                lst[:] = [a for a in lst if self.ops[a]["dma"] or self.ops[a]["eng"] != eng]
            lst.append(i)
        return i

    def act(self, fn, r=(), w=()):
        return self.add("scalar", fn, r, w)

    def dve(self, fn, r=(), w=()):
        return self.add("vector", fn, r, w)

    def pool(self, fn, r=(), w=()):
        return self.add("gpsimd", fn, r, w)

    def pe(self, fn, r=(), w=()):
        return self.add("tensor", fn, r, w)

    def dma(self, q, out, in_, r=(), w=(), **kw):
        return self.add(q, lambda e: e.dma_start(out=out, in_=in_, **kw), r, w, dma=True)

    def emit(self):
        ops = self.ops
        for b in ops:
            for a, kind in b["deps"].items():
                A = ops[a]
                if A["dma"]:
                    continue
                if A["eng"] == b["eng"] and not b["dma"]:
                    if A["eng"] == "tensor":
                        continue
                A["signal"] = True
        cnt = {e: 0 for e in self.sems}
        for o in ops:
            if not o["dma"] and o["signal"]:
                cnt[o["eng"]] += 1
                o["sval"] = cnt[o["eng"]]
        seen = {}
        last_on_dsem = {}
        for o in ops:
            e = o["eng"]
            sn = seen.setdefault(e, {})
            need = {}
            for a, kind in o["deps"].items():
                A = ops[a]
                if A["dma"]:
                    key, val = ("d",) + A["dsem"], A["dval"]
                else:
                    if A["eng"] == e and not o["dma"]:
                        if e == "tensor":
                            continue
                    key, val = ("c", A["eng"]), A["sval"]
                if need.get(key, 0) < val:
                    need[key] = val
            if o["dma"]:
                key = ("d",) + o["dsem"]
                prev = o["dval"] - 16
                if prev > 0 and need.get(key, 0) < prev:
                    need[key] = prev
            waits = []
            for key, val in need.items():
                if sn.get(key, 0) < val:
                    sn[key] = val
                    waits.append((key, val))
            o["waits"] = waits
        final = {}
        for o in ops:
            if o["dma"]:
                final[("d",) + o["dsem"]] = o["dval"]
        engs = ["sync", "scalar", "vector", "gpsimd", "tensor"]
        per = {e: [o for o in ops if o["eng"] == e] for e in engs}

        def semof(key):
            if key[0] == "c":
                return self.sems[key[1]]
            return self.dsems[key[1]][key[2]]

        def run(e, lst, tail):
            for o in lst:
                for key, val in o["waits"]:
                    e.wait_ge(semof(key), val)
                ins = o["fn"](e)
                if o["dma"]:
                    ins.then_inc(semof(("d",) + o["dsem"]), 16)
                elif o["signal"]:
                    ins.then_inc(self.sems[o["eng"]], 1)
            if tail:
                for key, val in final.items():
                    e.wait_ge(semof(key), val)

        with self.nc.Block() as block:
            block.sync(lambda e: run(e, per["sync"], True))
            block.scalar(lambda e: run(e, per["scalar"], False))
            block.vector(lambda e: run(e, per["vector"], False))
            block.gpsimd(lambda e: run(e, per["gpsimd"], False))
            block.tensor(lambda e: run(e, per["tensor"], False))


def build_nc(do_sample=True, n_pre=PRE_T, n_main=MAIN_T, max_ops=None):
    nc = bass.Bass("TRN2", target_bir_lowering=False)
    NT = n_pre + n_main + 1

    def din(name, shape):
        return nc.dram_tensor(name, list(shape), F32, kind="ExternalInput").ap()

    def dout(name, shape):
        return nc.dram_tensor(name, list(shape), F32, kind="ExternalOutput").ap()

    xs = din("xs", [NT * 128, D])
    memx = din("memx", [256, D])
    vec_d = din("vec", [128, NV])
    gfin_d = din("gfin", [128, D])
    ident_d = din("ident", [128, 128])
    bias_d = din("bias", [128, 8 * 256])
    bias0_d = din("bias0", [128, 8 * 128])
    w_in_d = din("w_in", [D, 1792])
    w_out_d = din("w_out", [D, D])
    w_q_d = din("w_q", [D, 512])
    w_k_d = din("w_k", [D, 512])
    w_v_d = din("w_v", [D, 512])
    w_o_d = din("w_o", [512, D])
    wa_d = din("wa_bd", [128, 512])
    wx_d = din("wx_bd", [128, 512])
    w_g_d = din("w_g", [D, 3072])
    w_u_d = din("w_u", [D, 3072])
    w_d_d = din("w_d", [3072, D])

    ident32_d = din("ident32", [128, 128])
    bown_d = din("bias_own", [128, 8 * 256])
    cswk_d = din("c_swa_k", [16, 128, 128])
    cswv_d = din("c_swa_v", [16, 128, 128])
    cmk_d = din("c_mem_k", [16, 256, 512])
    cmv_d = din("c_mem_v", [16, 256, 512])
    slc_d = din("st_lconv", [48, 512])
    slh_d = din("st_lh", [16, 512])
    sfc_d = din("st_fconv", [32, 3072])
    osk_d = dout("o_sk", [16, 128, 128])
    osv_d = dout("o_sv", [16, 128, 128])
    oslc_d = dout("o_slc", [48, 512])
    oslh_d = dout("o_slh", [16, 512])
    osfc_d = dout("o_sfc", [32, 3072])
    y_d = dout("y", [(n_main + 1) * 128, D])
    okv_d = dout("o_kv", [128, 256])
    omk_d = dout("o_mk", [256, 512])
    omv_d = dout("o_mv", [256, 512])
    olc_d = dout("o_lconv", [3, 512])
    olh_d = dout("o_lh", [512])
    ofc_d = dout("o_fconv", [2, 3072])

    es = contextlib.ExitStack()
    with es:
        def sb(name, shape, dt=F32):
            return es.enter_context(nc.sbuf_tensor("s_" + name, list(shape), dt))

        S = Sched(nc, es)
        psall = es.enter_context(nc.psum_tensor("psall", [128, 8 * 512], F32))
        ps_ctr = [0]

        def psbank(n=1):
            b0 = ps_ctr[0]
            if b0 + n > 8:
                b0 = 0
            ps_ctr[0] = (b0 + n) % 8
            return b0, ["ps%d" % (b0 + i) for i in range(n)]

        def psf(b0, n=1):
            return psall[:, b0 * 512:(b0 + n) * 512]

        psall_bf = psall.bitcast(BF16)

        def psb(b0, n=1):
            return psall_bf[:, b0 * 1024:(b0 + n) * 1024]

        vec = sb("vec", [128, NV])
        gfin = sb("gfin", [128, D])
        ident = sb("ident", [128, 128], BF16)
        biasT = sb("biasT", [128, 8, 256], BF16)
        bias0 = sb("bias0", [128, 8, 128], BF16)
        w_in = sb("w_in", [128, 8, 1792], BF16)
        w_out = sb("w_out", [128, 8, 1024], BF16)
        w_q = sb("w_q", [128, 8, 512], BF16)
        w_o = sb("w_o", [128, 4, 1024], BF16)
        wa = sb("wa", [128, 4, 128], BF16)
        wx = sb("wx", [128, 4, 128], BF16)
        NRING = 4
        ring = [sb("ring%d" % i, [128, 4096], BF16) for i in range(NRING)]
        ring_ctr = [0]
        mkT = sb("mkT", [128, 4, 256], BF16)
        mvb = sb("mvb", [128, 2, 512], BF16)
        cl = sb("cl", [128, 4])
        sink8 = sb("sink8", [128, 8])
        hc = sb("hc", [128, 4])
        NXB = 5
        xt = [sb("xt%d" % i, [128, D]) for i in range(NXB)]
        nbf = sb("nbf", [128, D], BF16)
        junk = nbf
        nT = [sb("nT%d" % i, [128, 8, 128], BF16) for i in range(2)]
        n3T = sb("n3T", [128, 8, 512], BF16)
        st = sb("stat", [128, 64])
        xrp = [sb("xrp%d" % i, [128, 4, 131]) for i in range(2)]
        xc = sb("xc", [128, 4, 128])
        xcb = sb("xcb", [128, 4, 128], BF16)
        rg = sb("rg", [128, 4, 128])
        ig = sb("ig", [128, 4, 128])
        av = sb("av", [128, 4, 128])
        bv = sb("bv", [128, 4, 128])
        hv = sb("hv", [128, 4, 128])
        yT = sb("yT", [128, 8, 128], BF16)
        QT = sb("QT", [128, 4, 128], BF16)
        KT = [sb("KT%d" % i, [128, 128], BF16) for i in range(2)]
        Vp = [sb("Vp%d" % i, [128, 2, 128], BF16) for i in range(2)]
        kvf = sb("kvf", [128, 256])
        Pm = sb("Pm", [128, 8, 256], BF16)
        PT = sb("PT", [128, 16, 128], BF16)
        QC = sb("QC", [128, 4, 128], BF16)
        OC = sb("OC", [128, 4, 128], BF16)
        Gs = [sb("Gs%d" % i, [128, 514]) for i in range(2)]
        t1 = [sb("t1_%d" % i, [128, 512]) for i in range(2)]
        mtmp = t1[0]
        gg = rg
        hm = sb("hm", [128, 12, 512], BF16)
        Gh = sb("Gh", [128, 24, 2])

        ident32 = sb("ident32", [128, 128])
        XP = sb("XP", [128, 4, 16, 7])
        XC2 = sb("XC2", [128, 4, 48])
        HS = sb("HS", [128, 4, 16])
        HS2 = sb("HS2", [128, 4, 16])
        stL = sb("stL", [128, 512])
        stH = sb("stH", [128, 512])

        def vcol(c, n=1):
            return vec[:, c:c + n]

        S.dma("sync", vec[:], vec_d, w=["vec"])
        S.dma("sync", gfin[:], gfin_d, w=["gfin"])
        S.dma("gpsimd", ident[:], ident_d, w=["ident"])
        S.dma("gpsimd", biasT[:].rearrange("p a b -> p (a b)"), bias_d, w=["biasT"])
        S.dma("gpsimd", bias0[:].rearrange("p a b -> p (a b)"), bias0_d, w=["bias0"])
        S.dma("gpsimd", wa[:].rearrange("p a b -> p (a b)"), wa_d, w=["wa"])
        S.dma("gpsimd", wx[:].rearrange("p a b -> p (a b)"), wx_d, w=["wx"])
        S.dma("gpsimd", w_in[:], w_in_d.rearrange("(k p) n -> p k n", p=128), w=["w_in"])
        wk_s = ring[0][:].rearrange("p (k n) -> p k n", k=8)
        wv_s = ring[1][:].rearrange("p (k n) -> p k n", k=8)
        S.dma("gpsimd", wk_s, w_k_d.rearrange("(k p) n -> p k n", p=128), w=["ring0"])
        S.dma("gpsimd", wv_s, w_v_d.rearrange("(k p) n -> p k n", p=128), w=["ring1"])
        S.dma("gpsimd", w_out[:], w_out_d.rearrange("(k p) n -> p k n", p=128), w=["w_out"])
        S.dma("gpsimd", w_q[:], w_q_d.rearrange("(k p) n -> p k n", p=128), w=["w_q"])
        S.dma("gpsimd", w_o[:], w_o_d.rearrange("(k p) n -> p k n", p=128), w=["w_o"])

        S.pool(lambda e: e.memset(hc[:], 0.0), w=["hc"])
        S.pool(lambda e: e.memset(xrp[1][:], 0.0), w=["xrp1"])
        S.pool(lambda e: e.memset(xrp[0][:], 0.0), w=["xrp0"])
        for i in range(2):
            S.pool(lambda e, i=i: e.memset(Vp[i][:], 0.0), w=["Vp%d" % i])
            S.pool(lambda e, i=i: e.memset(KT[i][:], 0.0), w=["KT%d" % i])
        S.pool(lambda e: e.memset(Gh[:], 0.0), w=["Gh"])

        S.act(lambda e: e.activation(out=st[:, 0:4], in_=vcol(V_LAM, 4), func=ACT.Exp, scale=-1.0),
              r=["vec"], w=["st_a"])
        S.act(lambda e: e.activation(out=st[:, 4:8], in_=st[:, 0:4], func=ACT.Ln, bias=vcol(V_ONE), scale=1.0),
              r=["st_a", "vec"], w=["st_b"])
        S.dve(lambda e: e.tensor_scalar(out=cl[:], in0=st[:, 4:8], scalar1=-8.0, scalar2=None, op0=ALU.mult),
              r=["st_b"], w=["cl"])
        S.dve(lambda e: e.tensor_scalar(out=sink8[:], in0=vcol(V_SINK, 8), scalar1=8.0, scalar2=None, op0=ALU.mult),
              r=["vec"], w=["sink8"])

        def rmsnorm_T(xap, xkey, gcol, dst, dstkey, tagn):
            S.act(lambda e: e.activation(out=junk[:], in_=xap, func=ACT.Square, accum_out=st[:, 8:9]),
                  r=[xkey], w=["nbf", "st_ss"])
            S.act(lambda e: e.activation(out=st[:, 9:10], in_=st[:, 8:9], func=ACT.Sqrt,
                                         bias=vcol(V_EPS), scale=1.0 / D),
                  r=["st_ss", "vec"], w=["st_sd"])
            S.dve(lambda e: e.reciprocal(out=st[:, 10:11], in_=st[:, 9:10]), r=["st_sd"], w=["st_rs"])
            S.dve(lambda e: e.tensor_scalar(out=nbf[:], in0=xap, scalar1=st[:, 10:11], scalar2=None, op0=ALU.mult),
                  r=[xkey, "st_rs"], w=["nbf"])
            b0, pk = psbank(1)
            pv = psb(b0).rearrange("p (k n) -> p k n", k=8)
            for k in range(8):
                S.pe(lambda e, k=k: e.transpose(out=pv[:, k, :], in_=nbf[:, k * 128:(k + 1) * 128], identity=ident[:]),
                     r=["nbf", "ident"], w=pk)
            gb = vec[:, gcol:gcol + 8].unsqueeze(2).to_broadcast([128, 8, 128])
            S.dve(lambda e: e.tensor_tensor(out=dst, in0=pv, in1=gb, op=ALU.mult),
                  r=pk + ["vec"], w=[dstkey])

        def fm_proj(wsb, wkey, col0, nchunks, src, srckey, n=128, width=128):
            b0, pk = psbank(1)
            pv = psf(b0).rearrange("p (c n) -> p c n", c=512 // n)
            for c in range(nchunks):
                for k in range(8):
                    S.pe(lambda e, c=c, k=k: e.matmul(pv[:, c, :], lhsT=wsb[:, k, col0 + c * width:col0 + (c + 1) * width],
                                                       rhs=src[:, k, :], start=(k == 0), stop=(k == 7)),
                         r=[wkey, srckey], w=pk)
            return pv, pk

        def lru_stage(ti, nTt, nTkey, smp=False):
            cur, prv = xrp[ti % 2], xrp[(ti + 1) % 2]
            ck, pk_ = "xrp%d" % (ti % 2), "xrp%d" % ((ti + 1) % 2)
            if not smp:
                S.pool(lambda e: e.tensor_copy(out=cur[:, :, 0:3], in_=prv[:, :, 128:131]), r=[pk_], w=[ck])
            pxr, kxr = fm_proj(w_in, "w_in", 0, 4, nTt, nTkey)
            if not smp:
                S.act(lambda e: e.activation(out=cur[:, :, 3:131], in_=pxr, func=ACT.Copy), r=kxr, w=[ck])
            else:
                for c in range(4):
                    S.act(lambda e, c=c: e.activation(out=XP[:, c, :, 3:7],
                                                      in_=pxr[:, c, 0:64].rearrange("p (s t) -> p s t", t=4),
                                                      func=ACT.Copy), r=kxr, w=["XP"])
            for c in range(4):
                if not smp:
                    o_ap = xc[:, c, :]
                    in_tap = lambda tap, c=c: cur[:, c, tap:tap + 128]
                    srck = ck
                else:
                    o_ap = xc[:, c, 0:64].rearrange("p (s t) -> p s t", t=4)
                    in_tap = lambda tap, c=c: XP[:, c, :, tap:tap + 4]
                    srck = "XP"
                S.dve(lambda e, c=c, o_ap=o_ap, in_tap=in_tap: e.tensor_scalar(
                    out=o_ap, in0=in_tap(3), scalar1=vcol(V_CW + 12 + c), scalar2=vcol(V_BCONV + c),
                    op0=ALU.mult, op1=ALU.add), r=[srck, "vec"], w=["xc%d" % c])
                for tap in range(3):
                    S.dve(lambda e, c=c, tap=tap, o_ap=o_ap, in_tap=in_tap: e.scalar_tensor_tensor(
                        out=o_ap, in0=in_tap(tap), scalar=vcol(V_CW + tap * 4 + c),
                        in1=o_ap, op0=ALU.mult, op1=ALU.add),
                        r=[srck, "vec", "xc%d" % c], w=["xc%d" % c])
            xck = ["xc%d" % c for c in range(4)]
            S.act(lambda e: e.activation(out=xcb[:], in_=xc[:], func=ACT.Copy), r=xck, w=["xcb"])
            b0, pk = psbank(2)
            pr = psf(b0).rearrange("p (c n) -> p c n", c=4)
            pi = psf(b0 + 1).rearrange("p (c n) -> p c n", c=4)
            for c in range(4):
                S.pe(lambda e, c=c: e.matmul(pr[:, c, :], lhsT=wa[:, c, :], rhs=xcb[:, c, :], start=True, stop=True),
                     r=["wa", "xcb"], w=[pk[0]])
            for c in range(4):
                S.pe(lambda e, c=c: e.matmul(pi[:, c, :], lhsT=wx[:, c, :], rhs=xcb[:, c, :], start=True, stop=True),
                     r=["wx", "xcb"], w=[pk[1]])
            for c in range(4):
                S.act(lambda e, c=c: e.activation(out=rg[:, c, :], in_=pr[:, c, :], func=ACT.Sigmoid,
                                                  bias=vcol(V_BA + c), scale=1.0),
                      r=[pk[0], "vec"], w=["rg"])
            for c in range(4):
                S.act(lambda e, c=c: e.activation(out=ig[:, c, :], in_=pi[:, c, :], func=ACT.Sigmoid,
                                                  bias=vcol(V_BX + c), scale=1.0),
                      r=[pk[1], "vec"], w=["ig"])
            for c in range(4):
                S.act(lambda e, c=c: e.activation(out=av[:, c, :], in_=rg[:, c, :], func=ACT.Exp, scale=cl[:, c:c + 1]),
                      r=["rg", "cl"], w=["av"])
            S.pool(lambda e: e.tensor_tensor(out=bv[:], in0=av[:], in1=av[:], op=ALU.mult), r=["av"], w=["bv"])
            S.pool(lambda e: e.tensor_scalar(out=bv[:], in0=bv[:], scalar1=-1.0, scalar2=1.0, op0=ALU.mult, op1=ALU.add),
                   r=["bv"], w=["bv"])
            S.act(lambda e: e.activation(out=bv[:], in_=bv[:], func=ACT.Sqrt), r=["bv"], w=["bv"])
            S.pool(lambda e: e.tensor_tensor(out=ig[:], in0=ig[:], in1=xc[:], op=ALU.mult), r=["ig"] + xck, w=["ig"])
            S.dve(lambda e: e.tensor_tensor(out=bv[:], in0=bv[:], in1=ig[:], op=ALU.mult), r=["bv", "ig"], w=["bv"])
            if smp:
                v4 = lambda t_, tt: t_[:, :, 0:64].rearrange("p c (s t) -> p c s t", t=4)[:, :, :, tt]
                for tt in range(4):
                    hprev = HS[:] if tt == 0 else v4(hv, tt - 1)
                    S.dve(lambda e, tt=tt, hprev=hprev: e.tensor_tensor(out=v4(hv, tt), in0=v4(av, tt), in1=hprev,
                                                                        op=ALU.mult), r=["av", "hv", "HS"], w=["hv"])
                    S.dve(lambda e, tt=tt: e.tensor_tensor(out=v4(hv, tt), in0=v4(hv, tt), in1=v4(bv, tt),
                                                           op=ALU.add), r=["bv", "hv"], w=["hv"])
                S.dve(lambda e: e.tensor_copy(out=HS2[:], in_=v4(hv, 3)), r=["hv"], w=["HS2"])
                return
            for c in range(4):
                S.dve(lambda e, c=c: e.tensor_tensor_scan(out=hv[:, c, :], data0=av[:, c, :], data1=bv[:, c, :],
                                                           initial=hc[:, c:c + 1], op0=ALU.mult, op1=ALU.add),
                      r=["av", "bv", "hc"], w=["hv"])
            S.dve(lambda e: e.tensor_copy(out=hc[:], in_=hv[:, :, 127]), r=["hv"], w=["hc"])

        def attn_core(nh, Sps, Skeys, scale, sinkcol, Pbuf, Pkey, nkb, rows=128):
            W = nkb * 128
            S.dve(lambda e: e.reduce_max(out=st[:rows, 16:16 + nh], in_=Sps[:rows], axis=AX.X), r=Skeys, w=["st_mx"])
            if sinkcol is not None:
                S.dve(lambda e: e.tensor_tensor(out=st[:rows, 16:16 + nh], in0=st[:rows, 16:16 + nh],
                                                in1=sink8[:rows, :], op=ALU.max),
                      r=["st_mx", "sink8"], w=["st_mx"])
            S.dve(lambda e: e.tensor_scalar(out=st[:rows, 24:24 + nh], in0=st[:rows, 16:16 + nh], scalar1=-scale,
                                            scalar2=None, op0=ALU.mult),
                  r=["st_mx"], w=["st_nm"])
            for h in range(nh):
                S.act(lambda e, h=h: e.activation(out=Pbuf[:rows, h, 0:W], in_=Sps[:rows, h, :], func=ACT.Exp,
                                                  bias=st[:rows, 24 + h:25 + h], scale=scale,
                                                  accum_out=st[:rows, 32 + h:33 + h]),
                      r=Skeys + ["st_nm"], w=[Pkey, "st_rs%d" % h])
            rsk = ["st_rs%d" % h for h in range(nh)]
            if sinkcol is not None:
                S.dve(lambda e: e.tensor_tensor(out=st[:rows, 40:48], in0=st[:rows, 24:32],
                                                in1=vec[:rows, sinkcol:sinkcol + 8], op=ALU.add),
                      r=["st_nm", "vec"], w=["st_es"])
                S.act(lambda e: e.activation(out=st[:rows, 40:48], in_=st[:rows, 40:48], func=ACT.Exp),
                      r=["st_es"], w=["st_es"])
                S.dve(lambda e: e.tensor_tensor(out=st[:rows, 32:40], in0=st[:rows, 32:40], in1=st[:rows, 40:48],
                                                op=ALU.add),
                      r=["st_es"] + rsk, w=rsk)
            S.dve(lambda e: e.reciprocal(out=st[:rows, 48:48 + nh], in_=st[:rows, 32:32 + nh]), r=rsk, w=["st_ri"])
            for h in range(nh):
                S.dve(lambda e, h=h: e.tensor_scalar(out=Pbuf[:rows, h, 0:W], in0=Pbuf[:rows, h, 0:W],
                                                     scalar1=st[:rows, 48 + h:49 + h], scalar2=None, op0=ALU.mult),
                      r=[Pkey, "st_ri"], w=[Pkey])
            nt = nh * nkb
            nb = (nt * 128 + 1023) // 1024
            b0, pk = psbank(nb)
            pv = psb(b0, nb).rearrange("p (t n) -> p t n", n=128)
            for h in range(nh):
                for kb in range(nkb):
                    t = h * nkb + kb
                    S.pe(lambda e, h=h, kb=kb, t=t: e.transpose(out=pv[:, t, 0:rows],
                                                                in_=Pbuf[:rows, h, kb * 128:(kb + 1) * 128],
                                                                identity=ident[:rows, :rows]),
                         r=[Pkey, "ident"], w=[pk[(t * 128) // 1024]])
            half = nt // 2
            if nb == 1:
                S.act(lambda e: e.activation(out=PT[:, 0:nt, 0:rows], in_=pv[:, 0:nt, 0:rows], func=ACT.Copy),
                      r=pk, w=["PTa", "PTb"])
            else:
                S.act(lambda e: e.activation(out=PT[:, 0:half, 0:rows], in_=pv[:, 0:half, 0:rows], func=ACT.Copy),
                      r=pk, w=["PTa"])
                S.dve(lambda e: e.tensor_copy(out=PT[:, half:nt, 0:rows], in_=pv[:, half:nt, 0:rows]),
                      r=pk, w=["PTb"])

        def swa_stage(ti, nTt, nTkey, first_block, bias_first, rows=128, qcols=None):
            cb, pb = ti % 2, (ti + 1) % 2
            pq, kq = fm_proj(w_in, "w_in", 1024, 4, nTt, nTkey)
            S.act(lambda e: e.activation(out=QT[:], in_=pq, func=ACT.Copy), r=kq, w=["QT"])
            b0, pk = psbank(1)
            pkk = psf(b0)[:, 0:128]
            pkv = psf(b0)[:, 128:384]
            for k in range(8):
                S.pe(lambda e, k=k: e.matmul(pkk, lhsT=w_in[:, k, 1536:1664], rhs=nTt[:, k, :],
                                             start=(k == 0), stop=(k == 7)), r=["w_in", nTkey], w=pk)
            for k in range(8):
                S.pe(lambda e, k=k: e.matmul(pkv, lhsT=nTt[:, k, :], rhs=w_in[:, k, 1536:1792],
                                             start=(k == 0), stop=(k == 7)), r=["w_in", nTkey], w=pk)
            S.act(lambda e: e.activation(out=KT[cb][:], in_=pkk, func=ACT.Copy), r=pk, w=["KT%d" % cb])
            S.dve(lambda e: e.tensor_copy(out=Vp[cb][:, 0, 0:64], in_=pkv[:, 128:192]), r=pk, w=["Vp%d" % cb])
            S.dve(lambda e: e.tensor_copy(out=Vp[cb][:, 1, 64:128], in_=pkv[:, 192:256]), r=pk, w=["Vp%d" % cb])
            S.act(lambda e: e.activation(out=kvf[:], in_=pkv, func=ACT.Copy), r=pk, w=["kvf"])
            return cb, pb

        def swa_attend(cb, pb, bias_first, smp=None):
            b0, sk = psbank(4)
            Sps = psf(b0, 4).rearrange("p (h n) -> p h n", h=8)
            if smp is None:
                rows, q0 = 128, 0
                xkeys = []
            else:
                rows, q0 = 4, 4 * smp["s"]
                xkeys = smp["keys"]
            idn = ident[0:rows, 0:rows]
            for j in range(4):
                for b in range(2):
                    s_ = 2 * j + b
                    key = [sk[s_ // 2]]
                    lo, hi = 64 * b, 64 * b + 64
                    if smp is None:
                        kprev = KT[pb][lo:hi, :]
                        bprev = bias0[:, s_, :] if bias_first else biasT[:, s_, 0:128]
                        bown = biasT[:, s_, 128:256]
                        kpk = "KT%d" % pb
                    else:
                        kprev = smp["KTc"][lo:hi, smp["s"], :]
                        bprev = biasT[0:4, s_, 0:128]
                        o0 = 128 - 4 * smp["s"]
                        bown = smp["Bown"][0:4, s_, o0:o0 + 128]
                        kpk = "KT%d" % cb
                    S.pe(lambda e, j=j, lo=lo, hi=hi, s_=s_, kprev=kprev: e.matmul(
                        Sps[0:rows, s_, 0:128], lhsT=QT[lo:hi, j, q0:q0 + rows], rhs=kprev, start=True, stop=False),
                        r=["QT", kpk] + xkeys, w=key)
                    S.pe(lambda e, s_=s_, bprev=bprev: e.matmul(Sps[0:rows, s_, 0:128], lhsT=idn, rhs=bprev,
                                                                start=False, stop=True),
                         r=["ident", "bias0", "biasT"], w=key)
                    S.pe(lambda e, j=j, lo=lo, hi=hi, s_=s_: e.matmul(
                        Sps[0:rows, s_, 128:256], lhsT=QT[lo:hi, j, q0:q0 + rows], rhs=KT[cb][lo:hi, :],
                        start=True, stop=False), r=["QT", "KT%d" % cb], w=key)
                    S.pe(lambda e, s_=s_, bown=bown: e.matmul(Sps[0:rows, s_, 128:256], lhsT=idn, rhs=bown,
                                                              start=False, stop=True),
                         r=["ident", "biasT"] + xkeys, w=key)
            attn_core(8, Sps, sk, 0.125, V_SINK, Pm, "Pm", 2, rows=rows)
            b0, ok = psbank(1)
            pO = psf(b0).rearrange("p (c n) -> p c n", c=4)
            for j in range(4):
                n = 0
                for b in range(2):
                    s_ = 2 * j + b
                    for kb in (0, 1):
                        if kb == 1:
                            vap, vk = Vp[cb][:, b, :], ["Vp%d" % cb]
                        elif smp is None:
                            vap, vk = Vp[pb][:, b, :], ["Vp%d" % pb]
                        else:
                            vap, vk = smp["Vc"][b][:, smp["s"], :], xkeys
                        S.pe(lambda e, j=j, s_=s_, kb=kb, vap=vap, n=n: e.matmul(
                            pO[:, j, 0:rows], lhsT=vap, rhs=PT[:, s_ * 2 + kb, 0:rows],
                            start=(n == 0), stop=(n == 3)),
                            r=vk + ["PTa", "PTb"], w=ok)
                        n += 1
            S.act(lambda e: e.activation(out=yT[:, 4:8, q0:q0 + rows], in_=pO[:, :, 0:rows], func=ACT.Copy),
                  r=ok, w=["yT_att"])

        def mixer_out(xti, xkey, nTt, nTkey):
            pg, kg = fm_proj(w_in, "w_in", 512, 4, nTt, nTkey)
            S.act(lambda e: e.activation(out=gg[:], in_=pg, func=ACT.Gelu_apprx_tanh), r=kg, w=["rg"])
            S.dve(lambda e: e.tensor_tensor(out=yT[:, 0:4, :], in0=hv[:], in1=gg[:], op=ALU.mult),
                  r=["hv", "rg"], w=["yT_lru"])
            b0, pk = psbank(2)
            po = psf(b0, 2)
            for hh in range(2):
                for k in range(8):
                    S.pe(lambda e, hh=hh, k=k: e.matmul(po[:, hh * 512:(hh + 1) * 512], lhsT=yT[:, k, :],
                                                        rhs=w_out[:, k, hh * 512:(hh + 1) * 512],
                                                        start=(k == 0), stop=(k == 7)),
                         r=["yT_lru", "yT_att", "w_out"], w=[pk[hh]])
            S.dve(lambda e: e.tensor_tensor(out=xti[:], in0=xti[:], in1=po, op=ALU.add), r=[xkey] + pk, w=[xkey])

        def cross_attend(smp=None):
            if smp is None:
                rows, q0, kT, vv, xkeys = 128, 0, mkT, mvb, ["mkT", "mvb"]
            else:
                rows, q0, kT, vv, xkeys = 4, 4 * smp["s"], smp["mkT"], smp["mv"], smp["keys"]
            b0, sk = psbank(2)
            Sps = psf(b0, 2).rearrange("p (h n) -> p h n", h=4)
            for h in range(4):
                S.pe(lambda e, h=h: e.matmul(Sps[0:rows, h, :], lhsT=QC[:, h, q0:q0 + rows], rhs=kT[:, h, :],
                                             start=True, stop=True),
                     r=["QC"] + xkeys, w=[sk[h // 2]])
            attn_core(4, Sps, sk, SC_MEM, None, Pm, "Pm", 2, rows=rows)
            b0, ok = psbank(1)
            pO = psf(b0).rearrange("p (c n) -> p c n", c=4)
            for h in range(4):
                for kb in range(2):
                    S.pe(lambda e, h=h, kb=kb: e.matmul(pO[:, h, 0:rows], lhsT=vv[:, kb, h * 128:(h + 1) * 128],
                                                        rhs=PT[:, h * 2 + kb, 0:rows], start=(kb == 0), stop=(kb == 1)),
                         r=xkeys + ["PTa", "PTb"], w=ok)
            S.act(lambda e: e.activation(out=OC[:, :, q0:q0 + rows], in_=pO[:, :, 0:rows], func=ACT.Copy),
                  r=ok, w=["OC"])

        def cross_stage(xti, xkey, nTt, nTkey, smp_iter=None):
            rmsnorm_T(xti[:], xkey, V_GCROSS, nTt[:], nTkey, "n2")
            pq, kq = fm_proj(w_q, "w_q", 0, 4, nTt, nTkey)
            S.act(lambda e: e.activation(out=QC[:], in_=pq, func=ACT.Copy), r=kq, w=["QC"])
            if smp_iter is None:
                cross_attend(None)
            else:
                smp_iter()
            b0, pk = psbank(2)
            po = psf(b0, 2)
            for hh in range(2):
                for k in range(4):
                    S.pe(lambda e, hh=hh, k=k: e.matmul(po[:, hh * 512:(hh + 1) * 512], lhsT=OC[:, k, :],
                                                        rhs=w_o[:, k, hh * 512:(hh + 1) * 512],
                                                        start=(k == 0), stop=(k == 3)),
                         r=["OC", "w_o"], w=[pk[hh]])
            S.dve(lambda e: e.tensor_tensor(out=xti[:], in0=xti[:], in1=po, op=ALU.add), r=[xkey] + pk, w=[xkey])

        v_k8 = lambda t: t[:].rearrange("p (k n) -> p k n", k=8)
        wq = []
        wissued = [0]

        def wview(i):
            return ring[i % NRING][:].rearrange("p (k n) -> p k n", k=wq[i][1])

        def wq_add_macro(gate_only=False):
            base = len(wq)
            if gate_only:
                for g in range(6):
                    wq.append((w_g_d[:, g * 512:(g + 1) * 512].rearrange("(k p) n -> p k n", p=128), 8))
                return base
            for fh in range(2):
                for gi in range(3):
                    g = 3 * fh + gi
                    wq.append((w_g_d[:, g * 512:(g + 1) * 512].rearrange("(k p) n -> p k n", p=128), 8))
                    wq.append((w_u_d[:, g * 512:(g + 1) * 512].rearrange("(k p) n -> p k n", p=128), 8))
                for gi in range(3):
                    g = 3 * fh + gi
                    wq.append((w_d_d[g * 512:(g + 1) * 512, :].rearrange("(k p) n -> p k n", p=128), 4))
            return base

        def wq_issue(upto):
            while wissued[0] < len(wq) and wissued[0] <= upto:
                j = wissued[0]
                S.dma("gpsimd", wview(j), wq[j][0], w=["ring%d" % (j % NRING)])
                wissued[0] += 1

        def wq_get(i):
            wq_issue(i)
            return wview(i), "ring%d" % (i % NRING)

        def wq_done(i):
            wq_issue(i + NRING)

        def load_x(ti, slot):
            S.dma("sync", xt[slot][:], xs[ti * 128:(ti + 1) * 128, :], w=["xt%d" % slot])

        for mt in range(2):
            S.dma("sync", xt[mt][:], memx[mt * 128:(mt + 1) * 128, :], w=["xt%d" % mt])
        for mt in range(2):
            rmsnorm_T(xt[mt][:], "xt%d" % mt, V_GMEM, nT[mt][:], "nT%d" % mt, "nm")
            for (wsl, wkey, od, is_k) in ((wk_s, "ring0", omk_d, True), (wv_s, "ring1", omv_d, False)):
                b0, pk = psbank(1)
                pm = psf(b0)
                for k in range(8):
                    S.pe(lambda e, k=k, wsl=wsl, pm=pm, mt=mt: e.matmul(pm, lhsT=nT[mt][:, k, :], rhs=wsl[:, k, :],
                                                                 start=(k == 0), stop=(k == 7)),
                         r=["nT%d" % mt, wkey], w=pk)
                S.act(lambda e, pm=pm: e.activation(out=mtmp[:], in_=pm, func=ACT.Copy), r=pk, w=["t1_0"])
                if not is_k:
                    S.dve(lambda e, pm=pm, mt=mt: e.tensor_copy(out=mvb[:, mt, :], in_=pm), r=pk, w=["mvb"])
                S.dma("sync", od[mt * 128:(mt + 1) * 128, :], mtmp[:], r=["t1_0"])
            pkT, kk = fm_proj(wk_s, "ring0", 0, 4, nT[mt], "nT%d" % mt)
            S.act(lambda e, pkT=pkT, mt=mt: e.activation(out=mkT[:, :, mt * 128:(mt + 1) * 128], in_=pkT, func=ACT.Copy),
                  r=kk, w=["mkT"])

        def ffn_macro(tiles, rows_out, N, halo=Gh, halokey="Gh", smp=None):
            wb = wq_add_macro()
            for fh in range(2):
                for gi in range(3):
                    g = 3 * fh + gi
                    wg3, wgk = wq_get(wb + fh * 9 + 2 * gi)
                    wu3, wuk = wq_get(wb + fh * 9 + 2 * gi + 1)
                    for c in range(4):
                        fc = g * 4 + c
                        hmi = gi * 4 + c
                        i2 = fc % 2
                        b0, pk = psbank(2)
                        pG, pU = psf(b0)[:, 0:N], psf(b0 + 1)[:, 0:N]
                        for k in range(8):
                            S.pe(lambda e, k=k, c=c, wg3=wg3, pG=pG: e.matmul(
                                pG, lhsT=wg3[:, k, c * 128:(c + 1) * 128], rhs=n3T[:, k, 0:N],
                                start=(k == 0), stop=(k == 7)), r=[wgk, "n3T"], w=[pk[0]])
                        for k in range(8):
                            S.pe(lambda e, k=k, c=c, wu3=wu3, pU=pU: e.matmul(
                                pU, lhsT=wu3[:, k, c * 128:(c + 1) * 128], rhs=n3T[:, k, 0:N],
                                start=(k == 0), stop=(k == 7)), r=[wuk, "n3T"], w=[pk[1]])
                        Gsb, gk = Gs[i2], "Gs%d" % i2
                        tk = "t1_%d" % i2
                        if smp is None:
                            S.dve(lambda e, fc=fc, Gsb=Gsb: e.tensor_copy(out=Gsb[:, 0:2], in_=halo[:, fc, :]),
                                  r=[halokey + "%d" % fc, halokey], w=[gk])
                            S.act(lambda e, Gsb=Gsb, pG=pG: e.activation(out=Gsb[:, 2:2 + N], in_=pG, func=ACT.Copy),
                                  r=[pk[0]], w=[gk])
                            S.dve(lambda e, fc=fc, Gsb=Gsb: e.tensor_copy(out=halo[:, fc, :], in_=Gsb[:, N:N + 2]),
                                  r=[gk], w=[halokey + "%d" % fc])
                            t1v = t1[i2][:, 0:N]
                            g_tap = lambda tap, Gsb=Gsb: Gsb[:, tap:tap + N]
                            pGv = pG
                        else:
                            FS, FSn, fkeys = smp["FS"], smp["FSn"], smp["keys"]
                            G3 = Gsb[:, 0:96].rearrange("p (s t) -> p s t", t=6)
                            S.dve(lambda e, fc=fc, G3=G3, FS=FS: e.tensor_copy(out=G3[:, :, 0:2], in_=FS[:, fc, :, :]),
                                  r=[fkeys[0]], w=[gk])
                            S.act(lambda e, G3=G3, pG=pG: e.activation(
                                out=G3[:, :, 2:6], in_=pG[:, 0:64].rearrange("p (s t) -> p s t", t=4), func=ACT.Copy),
                                r=[pk[0]], w=[gk])
                            S.dve(lambda e, fc=fc, G3=G3, FSn=FSn: e.tensor_copy(out=FSn[:, fc, :, :], in_=G3[:, :, 4:6]),
                                  r=[gk], w=[fkeys[1]])
                            t1v = t1[i2][:, 0:64].rearrange("p (s t) -> p s t", t=4)
                            g_tap = lambda tap, G3=G3: G3[:, :, tap:tap + 4]
                            pGv = pG[:, 0:64].rearrange("p (s t) -> p s t", t=4)
                        S.act(lambda e, fc=fc, pGv=pGv, t1v=t1v: e.activation(
                            out=t1v, in_=pGv, func=ACT.Identity, bias=vcol(V_FB + fc),
                            scale=vcol(V_FW + 48 + fc)), r=[pk[0], "vec"], w=[tk])
                        for tap in (1, 0):
                            S.dve(lambda e, fc=fc, tap=tap, g_tap=g_tap, t1v=t1v: e.scalar_tensor_tensor(
                                out=t1v, in0=g_tap(tap), scalar=vcol(V_FW + tap * 24 + fc),
                                in1=t1v, op0=ALU.mult, op1=ALU.add),
                                r=[gk, "vec", tk], w=[tk])
                        S.act(lambda e, i2=i2: e.activation(out=t1[i2][:, 0:N], in_=t1[i2][:, 0:N],
                                                            func=ACT.Gelu_apprx_tanh), r=[tk], w=[tk])
                        S.dve(lambda e, hmi=hmi, i2=i2, pU=pU: e.tensor_tensor(out=hm[:, hmi, 0:N], in0=t1[i2][:, 0:N],
                                                                               in1=pU, op=ALU.mult),
                              r=[tk, pk[1]], w=["hm%d" % hmi])
                    wq_done(wb + fh * 9 + 2 * gi + 1)
                wds = [wq_get(wb + fh * 9 + 6 + q) for q in range(3)]
                for j, tj in enumerate(tiles):
                    sl = xslot(tj)
                    for hh in range(2):
                        b0, pk = psbank(1)
                        pD = psf(b0)
                        for q in range(3):
                            for c in range(4):
                                hmi = q * 4 + c
                                S.pe(lambda e, q=q, c=c, hmi=hmi, j=j, hh=hh, pD=pD, wds=wds: e.matmul(
                                    pD, lhsT=hm[:, hmi, j * 128:(j + 1) * 128],
                                    rhs=wds[q][0][:, c, hh * 512:(hh + 1) * 512],
                                    start=(hmi == 0), stop=(hmi == 11)),
                                    r=[wds[q][1], "hm%d" % hmi], w=pk)
                        S.dve(lambda e, sl=sl, hh=hh, pD=pD: e.tensor_tensor(
                            out=xt[sl][:, hh * 512:(hh + 1) * 512], in0=xt[sl][:, hh * 512:(hh + 1) * 512],
                            in1=pD, op=ALU.add), r=["xt%d" % sl] + pk, w=["xt%d" % sl])
                wq_done(wb + fh * 9 + 8)
            for j, tj in enumerate(tiles):
                sl = xslot(tj)
                xk2 = "xt%d" % sl
                S.act(lambda e, sl=sl: e.activation(out=junk[:], in_=xt[sl][:], func=ACT.Square, accum_out=st[:, 8:9]),
                      r=[xk2], w=["nbf", "st_ss"])
                S.act(lambda e: e.activation(out=st[:, 9:10], in_=st[:, 8:9], func=ACT.Sqrt, bias=vcol(V_EPS),
                                             scale=1.0 / D), r=["st_ss", "vec"], w=["st_sd"])
                S.dve(lambda e: e.reciprocal(out=st[:, 10:11], in_=st[:, 9:10]), r=["st_sd"], w=["st_rs"])
                S.dve(lambda e, sl=sl: e.scalar_tensor_tensor(out=xt[sl][:], in0=xt[sl][:], scalar=st[:, 10:11],
                                                              in1=gfin[:], op0=ALU.mult, op1=ALU.mult),
                      r=[xk2, "st_rs", "gfin"], w=[xk2])
                S.dma("sync", y_d[rows_out[j]:rows_out[j] + 128, :], xt[sl][:], r=[xk2])

        total = n_pre + n_main
        xslot = lambda ti: ti % NXB
        load_x(0, 0)
        for ti in range(total):
            slot = xslot(ti)
            xk = "xt%d" % slot
            if ti + 1 < total:
                load_x(ti + 1, xslot(ti + 1))
            nTt = nT[ti % 2]
            nTk = "nT%d" % (ti % 2)
            rmsnorm_T(xt[slot][:], xk, V_GMIX, nTt[:], nTk, "n1")
            lru_stage(ti, nTt, nTk)
            full = ti >= n_pre - 1
            if ti >= n_pre - 2:
                cb, pb = swa_stage(ti, nTt, nTk, False, False)
            if ti < n_pre and (ti + 1) % 16 == 0:
                fcol = V_FLAG + (ti + 1) // 16 - 1
                S.dve(lambda e, fcol=fcol: e.tensor_scalar(out=hc[:], in0=hc[:], scalar1=vcol(fcol), scalar2=None,
                                                           op0=ALU.mult), r=["hc", "vec"], w=["hc"])
            if not full:
                continue
            swa_attend(cb, pb, ti == n_pre)
            mixer_out(xt[slot], xk, nTt, nTk)
            cross_stage(xt[slot], xk, nTt, nTk)
            mi = (ti - n_pre) % 4 if ti >= n_pre else 0
            if ti == n_pre - 1:
                rmsnorm_T(xt[slot][:], xk, V_GFFN, nTt[:], nTk, "n3")
                b0, pk = psbank(1)
                ph = psf(b0)[:, 0:48].rearrange("p (c n) -> p c n", n=2)
                wb = wq_add_macro(gate_only=True)
                for g in range(6):
                    wg3, wgk = wq_get(wb + g)
                    for c in range(4):
                        for k in range(8):
                            S.pe(lambda e, g=g, c=c, k=k, wg3=wg3, ph=ph, nTt=nTt: e.matmul(
                                ph[:, g * 4 + c, :], lhsT=wg3[:, k, c * 128:(c + 1) * 128], rhs=nTt[:, k, 126:128],
                                start=(k == 0), stop=(k == 7)), r=[wgk, nTk], w=pk)
                    wq_done(wb + g)
                S.dve(lambda e, ph=ph: e.tensor_scalar(out=Gh[:], in0=ph, scalar1=vcol(V_FLAG + 3), scalar2=None, op0=ALU.mult),
                      r=pk + ["vec"], w=["Gh"])
                continue
            rmsnorm_T(xt[slot][:], xk, V_GFFN, n3T[:, :, mi * 128:(mi + 1) * 128], "n3T", "n3")
            if ti == total - 1:
                S.dma("sync", okv_d, kvf[:], r=["kvf"])
                for t in range(3):
                    S.dma("sync", olc_d[t, :].rearrange("(c p) -> p c", p=128), xrp[ti % 2][:, :, 128 + t],
                          r=["xrp%d" % (ti % 2)], allow_slow_non_contiguous=True)
                S.dma("sync", olh_d.rearrange("(c p) -> p c", p=128), hc[:], r=["hc"],
                      allow_slow_non_contiguous=True)
            if mi != 3:
                continue
            ffn_macro([ti - 3, ti - 2, ti - 1, ti], [(tj - n_pre) * 128 for tj in (ti - 3, ti - 2, ti - 1, ti)], 512)
            if ti == total - 1:
                for t in range(2):
                    S.dma("sync", ofc_d[t, :].rearrange("(c p) -> p c", p=128), Gh[:, :, t],
                          r=["Gh"] + ["Gh%d" % fc for fc in range(24)], allow_slow_non_contiguous=True)

        if do_sample:
            ti = total
            slot = xslot(ti)
            xk = "xt%d" % slot
            free = [i for i in range(NXB) if i != slot]
            fk = ["xt%d" % i for i in free]
            Vc = [xt[free[0]].bitcast(BF16)[:, 0:2048].rearrange("p (s d) -> p s d", d=128),
                  xt[free[1]].bitcast(BF16)[:, 0:2048].rearrange("p (s d) -> p s d", d=128)]
            FS = xt[free[2]][:, 0:768].rearrange("p (c s k) -> p c s k", c=24, k=2)
            FSn = xt[free[3]][:, 0:768].rearrange("p (c s k) -> p c s k", c=24, k=2)
            n3flat = n3T[:].rearrange("p a b -> p (a b)")
            KTc = n3flat[:, 0:2048].rearrange("p (s d) -> p s d", d=128)
            Bown = n3flat[:, 2048:4096].rearrange("p (h n) -> p h n", n=256)
            hmflat = hm[:].rearrange("p a b -> p (a b)")
            S.dma("sync", xt[slot][:], xs[ti * 128:(ti + 1) * 128, :], w=[xk])
            S.dma("sync", ident32[:], ident32_d, w=["ident32"])
            S.dma("sync", stL[0:48, :], slc_d, w=["stL"])
            S.dma("sync", stH[0:16, :], slh_d, w=["stH"])
            S.dma("sync", osk_d[:, 0:124, :], cswk_d[:, 4:128, :])
            S.dma("sync", osv_d[:, 0:124, :], cswv_d[:, 4:128, :])
            S.pool(lambda e: e.memset(xt[free[0]][:], 0.0), w=[fk[0]])
            S.pool(lambda e: e.memset(xt[free[1]][:], 0.0), w=[fk[1]])
            Kc = hmflat[:, 0:2048].rearrange("p (s d) -> p s d", d=128)
            hk03 = ["hm0", "hm1", "hm2", "hm3"]
            S.dma("gpsimd", Kc, cswk_d.rearrange("s k d -> k s d"), w=hk03)
            S.dma("gpsimd", Vc[0][:, :, 0:64], cswv_d.rearrange("s k d -> k s d")[:, :, 0:64], w=[fk[0]])
            S.dma("gpsimd", Vc[1][:, :, 64:128], cswv_d.rearrange("s k d -> k s d")[:, :, 64:128], w=[fk[1]])
            S.dma("gpsimd", Bown, bown_d.rearrange("p (h n) -> p h n", n=256), w=["n3T"])
            for half in range(2):
                b0, pk = psbank(1)
                pv = psb(b0).rearrange("p (t n) -> p t n", n=128)
                for i in range(8):
                    S.pe(lambda e, i=i, half=half, pv=pv: e.transpose(out=pv[:, i, :], in_=Kc[:, half * 8 + i, :],
                                                                      identity=ident[:]),
                         r=hk03 + ["ident"], w=pk)
                S.act(lambda e, half=half, pv=pv: e.activation(out=KTc[:, half * 8:(half + 1) * 8, :], in_=pv,
                                                               func=ACT.Copy), r=pk, w=["n3T"])
            b0, pk = psbank(1)
            p32 = psf(b0)
            for c in range(4):
                S.pe(lambda e, c=c: e.transpose(out=p32[:, c * 48:(c + 1) * 48], in_=stL[0:48, c * 128:(c + 1) * 128],
                                                identity=ident32[0:48, 0:48]), r=["stL", "ident32"], w=pk)
            S.dve(lambda e: e.tensor_copy(out=XP[:, :, :, 0:3],
                                          in_=p32[:, 0:192].rearrange("p (c s k) -> p c s k", c=4, k=3)),
                  r=pk, w=["XP"])
            b0, pk = psbank(1)
            p32b = psf(b0)
            for c in range(4):
                S.pe(lambda e, c=c: e.transpose(out=p32b[:, c * 16:(c + 1) * 16], in_=stH[0:16, c * 128:(c + 1) * 128],
                                                identity=ident32[0:16, 0:16]), r=["stH", "ident32"], w=pk)
            S.dve(lambda e: e.tensor_copy(out=HS[:], in_=p32b[:, 0:64].rearrange("p (c s) -> p c s", c=4)),
                  r=pk, w=["HS"])
            for g in range(6):
                S.dma("sync", stL[0:32, :], sfc_d[:, g * 512:(g + 1) * 512], w=["stL"])
                b0, pk = psbank(1)
                pf = psf(b0)
                for c in range(4):
                    S.pe(lambda e, c=c, pf=pf: e.transpose(out=pf[:, c * 32:(c + 1) * 32],
                                                           in_=stL[0:32, c * 128:(c + 1) * 128],
                                                           identity=ident32[0:32, 0:32]), r=["stL", "ident32"], w=pk)
                S.dve(lambda e, g=g, pf=pf: e.tensor_copy(
                    out=FS[:, g * 4:(g + 1) * 4, :, :],
                    in_=pf[:, 0:128].rearrange("p (c s k) -> p c s k", c=4, k=2)), r=pk, w=[fk[2]])
            nTt, nTk = nT[ti % 2], "nT%d" % (ti % 2)
            rmsnorm_T(xt[slot][:], xk, V_GMIX, nTt[:], nTk, "n1")
            lru_stage(ti, nTt, nTk, smp=True)
            cb, pb = swa_stage(ti, nTt, nTk, False, False)
            S.dve(lambda e: e.tensor_copy(out=XC2[:].rearrange("p c (s k) -> p c s k", k=3), in_=XP[:, :, :, 4:7]),
                  r=["XP"], w=["XC2"])
            b0, pk = psbank(1)
            po32 = psf(b0)
            for c in range(4):
                S.pe(lambda e, c=c: e.transpose(out=po32[0:48, c * 128:(c + 1) * 128], in_=XC2[:, c, :],
                                                identity=ident32[:]), r=["XC2", "ident32"], w=pk)
            S.act(lambda e: e.activation(out=stL[0:48, :], in_=po32[0:48, :], func=ACT.Copy), r=pk, w=["stL"])
            S.dma("sync", oslc_d, stL[0:48, :], r=["stL"])
            b0, pk = psbank(1)
            po32b = psf(b0)
            for c in range(4):
                S.pe(lambda e, c=c: e.transpose(out=po32b[0:16, c * 128:(c + 1) * 128], in_=HS2[:, c, :],
                                                identity=ident32[:]), r=["HS2", "ident32"], w=pk)
            S.act(lambda e: e.activation(out=stH[0:16, :], in_=po32b[0:16, :], func=ACT.Copy), r=pk, w=["stH"])
            S.dma("sync", oslh_d, stH[0:16, :], r=["stH"])
            for t4 in range(4):
                S.dma("sync", osk_d[:, 124 + t4, :], kvf[t4:64:4, 0:128], r=["kvf"])
                S.dma("sync", osv_d[:, 124 + t4, :], kvf[t4:64:4, 128:256], r=["kvf"])
            for sq in range(16):
                swa_attend(cb, pb, False, smp=dict(s=sq, KTc=KTc, Vc=Vc, Bown=Bown, keys=["n3T", fk[0], fk[1]]))
            mixer_out(xt[slot], xk, nTt, nTk)

            def cross_iter():
                for sq in range(16):
                    i2 = sq % 2
                    mkc = hmflat[:, i2 * 1024:(i2 + 1) * 1024].rearrange("p (a b) -> p a b", a=2)
                    mvc = hmflat[:, 2048 + i2 * 1024:2048 + (i2 + 1) * 1024].rearrange("p (a b) -> p a b", a=2)
                    mkTs = hmflat[:, 4096 + i2 * 1024:4096 + (i2 + 1) * 1024].rearrange("p (a b) -> p a b", a=4)
                    kk = ["hm%d" % (2 * i2), "hm%d" % (2 * i2 + 1)]
                    kv = ["hm%d" % (4 + 2 * i2), "hm%d" % (5 + 2 * i2)]
                    kt = ["hm%d" % (8 + 2 * i2), "hm%d" % (9 + 2 * i2)]
                    S.dma("gpsimd", mkc, cmk_d[sq].rearrange("(a p) n -> p a n", p=128), w=kk)
                    S.dma("gpsimd", mvc, cmv_d[sq].rearrange("(a p) n -> p a n", p=128), w=kv)
                    b0, pk = psbank(1)
                    pv = psb(b0).rearrange("p (t n) -> p t n", n=128)
                    for h in range(4):
                        for kb in range(2):
                            S.pe(lambda e, h=h, kb=kb, pv=pv, mkc=mkc: e.transpose(
                                out=pv[:, h * 2 + kb, :], in_=mkc[:, kb, h * 128:(h + 1) * 128], identity=ident[:]),
                                r=kk + ["ident"], w=pk)
                    S.dve(lambda e, pv=pv, mkTs=mkTs: e.tensor_copy(out=mkTs, in_=pv.rearrange("p (h b) n -> p h (b n)", b=2)),
                          r=pk, w=kt)
                    cross_attend(dict(s=sq, mkT=mkTs, mv=mvc, keys=kk + kv + kt))
            cross_stage(xt[slot], xk, nTt, nTk, smp_iter=cross_iter)
            rmsnorm_T(xt[slot][:], xk, V_GFFN, n3T[:, :, 0:128], "n3T", "n3")
            ffn_macro([ti], [n_main * 128], 128, smp=dict(FS=FS, FSn=FSn, keys=[fk[2], fk[3]]))
            for g in range(6):
                b0, pk = psbank(1)
                pf = psf(b0)
                for c in range(4):
                    S.pe(lambda e, c=c, g=g, pf=pf: e.transpose(
                        out=pf[0:32, c * 128:(c + 1) * 128],
                        in_=FSn[:, g * 4 + c, :, :].rearrange("p s k -> p (s k)"), identity=ident32[:]),
                        r=[fk[3], "ident32"], w=pk)
                S.act(lambda e, pf=pf: e.activation(out=stL[0:32, :], in_=pf[0:32, :], func=ACT.Copy), r=pk, w=["stL"])
                S.dma("sync", osfc_d[:, g * 512:(g + 1) * 512], stL[0:32, :], r=["stL"])

        if max_ops is not None:
            S.ops = S.ops[:max_ops]
        print('nops', len(S.ops))
        S.emit()
    return nc


def _slot_heads():
    return [(s // 2) + 4 * (s % 2) for s in range(8)]


def _bias_tables():
    slopes = 2.0 ** (-np.arange(1, 9, dtype=np.float64))
    qi = np.arange(128)[:, None]
    kj = np.arange(256)[None, :]
    dist = qi + 128 - kj
    valid = (dist >= 0) & (dist < 128)
    tab = np.empty((128, 8, 256), np.float32)
    for s, h in enumerate(_slot_heads()):
        tab[:, s, :] = np.where(valid, -8.0 * slopes[h] * dist, NEG)
    return tab


_NC_CACHE = {}


def kernel(**inp):
    f32 = np.float32
    xp = np.asarray(inp["x_prompt"], f32)
    xsmp = np.asarray(inp["x_sample"], f32)
    heads = _slot_heads()
    w_in = np.asarray(inp["w_in"][0], f32)
    qcols = np.concatenate([np.arange(1024 + h * 64, 1024 + (h + 1) * 64) for h in heads])
    w_in_p = np.ascontiguousarray(np.concatenate([w_in[:, :1024], w_in[:, qcols], w_in[:, 1536:]], axis=1))
    w_out = np.asarray(inp["w_out"][0], f32)
    orow = np.concatenate([np.arange(512 + h * 64, 512 + (h + 1) * 64) for h in heads])
    w_out_p = np.ascontiguousarray(np.concatenate([w_out[:512], w_out[orow]], axis=0))

    def bd(w):
        o = np.zeros((128, 4, 128), f32)
        for c in range(4):
            o[0:64, c, 0:64] = w[2 * c]
            o[64:128, c, 64:128] = w[2 * c + 1]
        return o.reshape(128, 512)

    def fm(v, nchunk):
        return np.asarray(v, f32).reshape(nchunk, 128).T

    vec = np.zeros((128, NV), f32)
    vec[:, V_GMIX:V_GMIX + 8] = fm(inp["g_mix"][0], 8)
    vec[:, V_GCROSS:V_GCROSS + 8] = fm(inp["g_cross"][0], 8)
    vec[:, V_GFFN:V_GFFN + 8] = fm(inp["g_ffn"][0], 8)
    vec[:, V_GMEM:V_GMEM + 8] = fm(inp["g_mem"][0], 8)
    for tap in range(4):
        vec[:, V_CW + tap * 4:V_CW + tap * 4 + 4] = fm(inp["w_lru_conv"][0, tap], 4)
    vec[:, V_BCONV:V_BCONV + 4] = fm(inp["b_lru_conv"][0], 4)
    vec[:, V_BA:V_BA + 4] = fm(inp["b_lru_a"][0], 4)
    vec[:, V_BX:V_BX + 4] = fm(inp["b_lru_x"][0], 4)
    vec[:, V_LAM:V_LAM + 4] = fm(inp["lru_lambda"][0], 4)
    for tap in range(3):
        vec[:, V_FW + tap * 24:V_FW + tap * 24 + 24] = fm(inp["w_ffn_conv"][0, tap], 24)
    vec[:, V_FB:V_FB + 24] = fm(inp["b_ffn_conv"][0], 24)
    vec[:, V_SINK:V_SINK + 8] = np.asarray(inp["attn_sinks"][0], f32)[heads][None, :]
    vec[:, V_EPS] = 1e-6
    vec[:, V_ONE] = 1.0

    slopes = 2.0 ** (-np.arange(1, 9, dtype=np.float64))
    bown = np.full((128, 8, 256), NEG, f32)
    for s_i, h in enumerate(heads):
        for t in range(4):
            for j in range(t + 1):
                bown[t, s_i, 128 + j] = -8.0 * slopes[h] * (t - j)
    gfin = np.ascontiguousarray(np.broadcast_to(np.asarray(inp["g_final"], f32)[None, :], (128, D)))
    ident = np.eye(128, dtype=f32)
    bias = _bias_tables()
    common = dict(
        gfin=gfin, ident=ident, ident32=ident, bias_own=bown.reshape(128, -1),
        bias=bias.reshape(128, -1), w_in=w_in_p, w_out=w_out_p,
        w_q=np.ascontiguousarray(inp["w_mem_q"][0], f32), w_k=np.ascontiguousarray(inp["w_mem_k"][0], f32),
        w_v=np.ascontiguousarray(inp["w_mem_v"][0], f32), w_o=np.ascontiguousarray(inp["w_mem_o"][0], f32),
        wa_bd=bd(np.asarray(inp["w_lru_a"][0], f32)), wx_bd=bd(np.asarray(inp["w_lru_x"][0], f32)),
        w_g=np.ascontiguousarray(inp["w_ffn_gate"][0], f32), w_u=np.ascontiguousarray(inp["w_ffn_up"][0], f32),
        w_d=np.ascontiguousarray(inp["w_ffn_down"][0], f32),
    )
    in_maps = []
    for c in range(NCORES):
        b, q = c // 4, c % 4
        pre = np.zeros((PRE_T * 128, D), f32)
        if q > 0:
            pre[(3 - q) * 2048:] = xp[b, :q * 2048]
        main = xp[b, q * 2048:(q + 1) * 2048]
        smp = np.zeros((128, D), f32)
        smp[:64] = xsmp[c * 16:(c + 1) * 16].reshape(64, D)
        v = vec.copy()
        for k in range(3):
            v[:, V_FLAG + k] = 1.0 if k >= 3 - q else 0.0
        v[:, V_FLAG + 3] = 1.0 if q > 0 else 0.0
        b0 = bias[:, :, 0:128].copy()
        if q == 0:
            b0[:] = NEG
        m = dict(common)
        m.update(xs=np.ascontiguousarray(np.concatenate([pre, main, smp], axis=0)),
                 memx=np.ascontiguousarray(inp["mem_prompt"][b], f32), vec=v,
                 c_swa_k=np.ascontiguousarray(inp["cache_swa_k"][0, c * 16:(c + 1) * 16], f32).reshape(16, 128, 128),
                 c_swa_v=np.ascontiguousarray(inp["cache_swa_v"][0, c * 16:(c + 1) * 16], f32).reshape(16, 128, 128),
                 c_mem_k=np.ascontiguousarray(inp["cache_mem_k"][0, c * 16:(c + 1) * 16], f32).reshape(16, 256, 512),
                 c_mem_v=np.ascontiguousarray(inp["cache_mem_v"][0, c * 16:(c + 1) * 16], f32).reshape(16, 256, 512),
                 st_lconv=np.ascontiguousarray(inp["state_lru_conv"][0, c * 16:(c + 1) * 16], f32).reshape(48, 512),
                 st_lh=np.ascontiguousarray(inp["state_lru_h"][0, c * 16:(c + 1) * 16], f32),
                 st_fconv=np.ascontiguousarray(inp["state_ffn_conv"][0, c * 16:(c + 1) * 16], f32).reshape(32, 3072),
                 bias0=np.ascontiguousarray(b0.reshape(128, -1)))
        in_maps.append(m)

    if "nc" not in _NC_CACHE:
        _NC_CACHE["nc"] = build_nc()
    nc = _NC_CACHE["nc"]
    res = run_bass_kernel_spmd(nc, in_maps, core_ids=list(range(NCORES)))
    R = res.results

    y_prompt = np.zeros((2, 8192, D), f32)
    y_sample = np.zeros((128, 4, D), f32)
    for c in range(NCORES):
        b, q = c // 4, c % 4
        y_prompt[b, q * 2048:(q + 1) * 2048] = R[c]["y"][:2048]
        y_sample[c * 16:(c + 1) * 16] = R[c]["y"][2048:2048 + 64].reshape(16, 4, D)
    p_swa_k = np.stack([R[3]["o_kv"][:, :128], R[7]["o_kv"][:, :128]]).reshape(1, 2, 128, 2, 64)
    p_swa_v = np.stack([R[3]["o_kv"][:, 128:], R[7]["o_kv"][:, 128:]]).reshape(1, 2, 128, 2, 64)
    p_mem_k = np.stack([R[0]["o_mk"], R[4]["o_mk"]]).reshape(1, 2, 256, 4, 128)
    p_mem_v = np.stack([R[0]["o_mv"], R[4]["o_mv"]]).reshape(1, 2, 256, 4, 128)
    p_lru_conv = np.stack([R[3]["o_lconv"], R[7]["o_lconv"]]).reshape(1, 2, 3, 512)
    p_lru_h = np.stack([R[3]["o_lh"], R[7]["o_lh"]]).reshape(1, 2, 512)
    p_ffn_conv = np.stack([R[3]["o_fconv"], R[7]["o_fconv"]]).reshape(1, 2, 2, 3072)
    cat = lambda k: np.concatenate([R[c][k] for c in range(NCORES)], axis=0)
    s_swa_k = cat("o_sk").reshape(1, 128, 128, 2, 64)
    s_swa_v = cat("o_sv").reshape(1, 128, 128, 2, 64)
    s_lru_conv = cat("o_slc").reshape(1, 128, 3, 512)
    s_lru_h = cat("o_slh").reshape(1, 128, 512)
    s_ffn_conv = cat("o_sfc").reshape(1, 128, 2, 3072)
    return (y_prompt, y_sample, p_swa_k, p_swa_v, p_mem_k, p_mem_v, p_lru_conv, p_lru_h, p_ffn_conv,
            s_swa_k, s_swa_v, s_lru_conv, s_lru_h, s_ffn_conv)
```

```python
import contextlib
import numpy as np
import concourse.bass as bass
import concourse.mybir as mybir
from concourse.bass_utils import run_bass_kernel_spmd

F32 = mybir.dt.float32
BF16 = mybir.dt.bfloat16
ACT = mybir.ActivationFunctionType
ALU = mybir.AluOpType
AX = mybir.AxisListType

NCORES = 8
D = 1024
PRE_T = 48
MAIN_T = 16
NEG = -240000.0
SC_MEM = 128.0 ** -0.5

V_GMIX, V_GCROSS, V_GFFN, V_GMEM = 0, 8, 16, 24
V_CW = 32
V_BCONV = 48
V_BA = 52
V_BX = 56
V_LAM = 60
V_FW = 64
V_FB = 136
V_SINK = 160
V_FLAG = 168
V_EPS = 172
V_ONE = 173
NV = 176


class Sched:
    def __init__(self, nc, es, dma_pool=None):
        self.nc = nc
        self.ops = []
        self.last_w = {}
        self.readers = {}
        self.dma_pool = dma_pool or {"sync": 8, "gpsimd": 6, "scalar": 4}
        self.sems = {}
        for e in ("scalar", "vector", "gpsimd", "tensor"):
            self.sems[e] = es.enter_context(nc.semaphore("c_" + e))
        self.dsems = {}
        for q, n in self.dma_pool.items():
            self.dsems[q] = [es.enter_context(nc.semaphore("d_%s%d" % (q, i))) for i in range(n)]
        self.dcount = {q: 0 for q in self.dma_pool}

    def add(self, eng, fn, r=(), w=(), dma=False):
        i = len(self.ops)
        deps = {}
        for k in r:
            a = self.last_w.get(k)
            if a is not None:
                deps[a] = "raw"
        for k in w:
            a = self.last_w.get(k)
            if a is not None:
                deps[a] = "raw"
            for a in self.readers.get(k, ()):
                deps.setdefault(a, "war")
        op = dict(eng=eng, fn=fn, deps=deps, dma=dma, signal=False)
        if dma:
            j = self.dcount[eng]
            self.dcount[eng] += 1
            n = self.dma_pool[eng]
            op["dsem"] = (eng, j % n)
            op["dval"] = 16 * (j // n + 1)
        self.ops.append(op)
        for k in w:
            self.last_w[k] = i
            self.readers[k] = []
        for k in r:
            lst = self.readers.setdefault(k, [])
            if not dma:
                lst[:] = [a for a in lst if self.ops[a]["dma"] or self.ops[a]["eng"] != eng]
            lst.append(i)
        return i

    def act(self, fn, r=(), w=()):
        return self.add("scalar", fn, r, w)

    def dve(self, fn, r=(), w=()):
        return self.add("vector", fn, r, w)

    def pool(self, fn, r=(), w=()):
        return self.add("gpsimd", fn, r, w)

    def pe(self, fn, r=(), w=()):
        return self.add("tensor", fn, r, w)

    def dma(self, q, out, in_, r=(), w=(), **kw):
        return self.add(q, lambda e: e.dma_start(out=out, in_=in_, **kw), r, w, dma=True)

    def emit(self):
        ops = self.ops
        for b in ops:
            for a, kind in b["deps"].items():
                A = ops[a]
                if A["dma"]:
                    continue
                if A["eng"] == b["eng"] and not b["dma"]:
                    if A["eng"] == "tensor":
                        continue
                A["signal"] = True
        cnt = {e: 0 for e in self.sems}
        for o in ops:
            if not o["dma"] and o["signal"]:
                cnt[o["eng"]] += 1
                o["sval"] = cnt[o["eng"]]
        seen = {}
        last_on_dsem = {}
        for o in ops:
            e = o["eng"]
            sn = seen.setdefault(e, {})
            need = {}
            for a, kind in o["deps"].items():
                A = ops[a]
                if A["dma"]:
                    key, val = ("d",) + A["dsem"], A["dval"]
                else:
                    if A["eng"] == e and not o["dma"]:
                        if e == "tensor":
                            continue
                    key, val = ("c", A["eng"]), A["sval"]
                if need.get(key, 0) < val:
                    need[key] = val
            if o["dma"]:
                key = ("d",) + o["dsem"]
                prev = o["dval"] - 16
                if prev > 0 and need.get(key, 0) < prev:
                    need[key] = prev
            waits = []
            for key, val in need.items():
                if sn.get(key, 0) < val:
                    sn[key] = val
                    waits.append((key, val))
            o["waits"] = waits
        final = {}
        for o in ops:
            if o["dma"]:
                final[("d",) + o["dsem"]] = o["dval"]
        engs = ["sync", "scalar", "vector", "gpsimd", "tensor"]
        per = {e: [o for o in ops if o["eng"] == e] for e in engs}

        def semof(key):
            if key[0] == "c":
                return self.sems[key[1]]
            return self.dsems[key[1]][key[2]]

        def run(e, lst, tail):
            for o in lst:
                for key, val in o["waits"]:
                    e.wait_ge(semof(key), val)
                ins = o["fn"](e)
                if o["dma"]:
                    ins.then_inc(semof(("d",) + o["dsem"]), 16)
                elif o["signal"]:
                    ins.then_inc(self.sems[o["eng"]], 1)
            if tail:
                for key, val in final.items():
                    e.wait_ge(semof(key), val)

        with self.nc.Block() as block:
            block.sync(lambda e: run(e, per["sync"], True))
            block.scalar(lambda e: run(e, per["scalar"], False))
            block.vector(lambda e: run(e, per["vector"], False))
            block.gpsimd(lambda e: run(e, per["gpsimd"], False))
            block.tensor(lambda e: run(e, per["tensor"], False))


def build_nc(do_sample=True, n_pre=PRE_T, n_main=MAIN_T, max_ops=None):
    nc = bass.Bass("TRN2", target_bir_lowering=False)
    NT = n_pre + n_main + 1

    def din(name, shape):
        return nc.dram_tensor(name, list(shape), F32, kind="ExternalInput").ap()

    def dout(name, shape):
        return nc.dram_tensor(name, list(shape), F32, kind="ExternalOutput").ap()

    xs = din("xs", [NT * 128, D])
    memx = din("memx", [256, D])
    vec_d = din("vec", [128, NV])
    gfin_d = din("gfin", [128, D])
    ident_d = din("ident", [128, 128])
    bias_d = din("bias", [128, 8 * 256])
    bias0_d = din("bias0", [128, 8 * 128])
    w_in_d = din("w_in", [D, 1792])
    w_out_d = din("w_out", [D, D])
    w_q_d = din("w_q", [D, 512])
    w_k_d = din("w_k", [D, 512])
    w_v_d = din("w_v", [D, 512])
    w_o_d = din("w_o", [512, D])
    wa_d = din("wa_bd", [128, 512])
    wx_d = din("wx_bd", [128, 512])
    w_g_d = din("w_g", [D, 3072])
    w_u_d = din("w_u", [D, 3072])
    w_d_d = din("w_d", [3072, D])

    ident32_d = din("ident32", [128, 128])
    bown_d = din("bias_own", [128, 8 * 256])
    cswk_d = din("c_swa_k", [16, 128, 128])
    cswv_d = din("c_swa_v", [16, 128, 128])
    cmk_d = din("c_mem_k", [16, 256, 512])
    cmv_d = din("c_mem_v", [16, 256, 512])
    slc_d = din("st_lconv", [48, 512])
    slh_d = din("st_lh", [16, 512])
    sfc_d = din("st_fconv", [32, 3072])
    osk_d = dout("o_sk", [16, 128, 128])
    osv_d = dout("o_sv", [16, 128, 128])
    oslc_d = dout("o_slc", [48, 512])
    oslh_d = dout("o_slh", [16, 512])
    osfc_d = dout("o_sfc", [32, 3072])
    y_d = dout("y", [(n_main + 1) * 128, D])
    okv_d = dout("o_kv", [128, 256])
    omk_d = dout("o_mk", [256, 512])
    omv_d = dout("o_mv", [256, 512])
    olc_d = dout("o_lconv", [3, 512])
    olh_d = dout("o_lh", [512])
    ofc_d = dout("o_fconv", [2, 3072])

    es = contextlib.ExitStack()
    with es:
        def sb(name, shape, dt=F32):
            return es.enter_context(nc.sbuf_tensor("s_" + name, list(shape), dt))

        S = Sched(nc, es)
        psall = es.enter_context(nc.psum_tensor("psall", [128, 8 * 512], F32))
        ps_ctr = [0]

        def psbank(n=1):
            b0 = ps_ctr[0]
            if b0 + n > 8:
                b0 = 0
            ps_ctr[0] = (b0 + n) % 8
            return b0, ["ps%d" % (b0 + i) for i in range(n)]

        def psf(b0, n=1):
            return psall[:, b0 * 512:(b0 + n) * 512]

        psall_bf = psall.bitcast(BF16)

        def psb(b0, n=1):
            return psall_bf[:, b0 * 1024:(b0 + n) * 1024]

        vec = sb("vec", [128, NV])
        gfin = sb("gfin", [128, D])
        ident = sb("ident", [128, 128], BF16)
        biasT = sb("biasT", [128, 8, 256], BF16)
        bias0 = sb("bias0", [128, 8, 128], BF16)
        w_in = sb("w_in", [128, 8, 1792], BF16)
        w_out = sb("w_out", [128, 8, 1024], BF16)
        w_q = sb("w_q", [128, 8, 512], BF16)
        w_o = sb("w_o", [128, 4, 1024], BF16)
        wa = sb("wa", [128, 4, 128], BF16)
        wx = sb("wx", [128, 4, 128], BF16)
        NRING = 4
        ring = [sb("ring%d" % i, [128, 4096], BF16) for i in range(NRING)]
        ring_ctr = [0]
        mkT = sb("mkT", [128, 4, 256], BF16)
        mvb = sb("mvb", [128, 2, 512], BF16)
        cl = sb("cl", [128, 4])
        sink8 = sb("sink8", [128, 8])
        hc = sb("hc", [128, 4])
        NXB = 5
        xt = [sb("xt%d" % i, [128, D]) for i in range(NXB)]
        nbf = sb("nbf", [128, D], BF16)
        junk = nbf
        nT = [sb("nT%d" % i, [128, 8, 128], BF16) for i in range(2)]
        n3T = sb("n3T", [128, 8, 512], BF16)
        st = sb("stat", [128, 64])
        xrp = [sb("xrp%d" % i, [128, 4, 131]) for i in range(2)]
        xc = sb("xc", [128, 4, 128])
        xcb = sb("xcb", [128, 4, 128], BF16)
        rg = sb("rg", [128, 4, 128])
        ig = sb("ig", [128, 4, 128])
        av = sb("av", [128, 4, 128])
        bv = sb("bv", [128, 4, 128])
        hv = sb("hv", [128, 4, 128])
        yT = sb("yT", [128, 8, 128], BF16)
        QT = sb("QT", [128, 4, 128], BF16)
        KT = [sb("KT%d" % i, [128, 128], BF16) for i in range(2)]
        Vp = [sb("Vp%d" % i, [128, 2, 128], BF16) for i in range(2)]
        kvf = sb("kvf", [128, 256])
        Pm = sb("Pm", [128, 8, 256], BF16)
        PT = sb("PT", [128, 16, 128], BF16)
        QC = sb("QC", [128, 4, 128], BF16)
        OC = sb("OC", [128, 4, 128], BF16)
        Gs = [sb("Gs%d" % i, [128, 514]) for i in range(2)]
        t1 = [sb("t1_%d" % i, [128, 512]) for i in range(2)]
        mtmp = t1[0]
        gg = rg
        hm = sb("hm", [128, 12, 512], BF16)
        Gh = sb("Gh", [128, 24, 2])

        ident32 = sb("ident32", [128, 128])
        XP = sb("XP", [128, 4, 16, 7])
        XC2 = sb("XC2", [128, 4, 48])
        HS = sb("HS", [128, 4, 16])
        HS2 = sb("HS2", [128, 4, 16])
        stL = sb("stL", [128, 512])
        stH = sb("stH", [128, 512])

        def vcol(c, n=1):
            return vec[:, c:c + n]

        S.dma("sync", vec[:], vec_d, w=["vec"])
        S.dma("sync", gfin[:], gfin_d, w=["gfin"])
        S.dma("gpsimd", ident[:], ident_d, w=["ident"])
        S.dma("gpsimd", biasT[:].rearrange("p a b -> p (a b)"), bias_d, w=["biasT"])
        S.dma("gpsimd", bias0[:].rearrange("p a b -> p (a b)"), bias0_d, w=["bias0"])
        S.dma("gpsimd", wa[:].rearrange("p a b -> p (a b)"), wa_d, w=["wa"])
        S.dma("gpsimd", wx[:].rearrange("p a b -> p (a b)"), wx_d, w=["wx"])
        S.dma("gpsimd", w_in[:], w_in_d.rearrange("(k p) n -> p k n", p=128), w=["w_in"])
        wk_s = ring[0][:].rearrange("p (k n) -> p k n", k=8)
        wv_s = ring[1][:].rearrange("p (k n) -> p k n", k=8)
        S.dma("gpsimd", wk_s, w_k_d.rearrange("(k p) n -> p k n", p=128), w=["ring0"])
        S.dma("gpsimd", wv_s, w_v_d.rearrange("(k p) n -> p k n", p=128), w=["ring1"])
        S.dma("gpsimd", w_out[:], w_out_d.rearrange("(k p) n -> p k n", p=128), w=["w_out"])
        S.dma("gpsimd", w_q[:], w_q_d.rearrange("(k p) n -> p k n", p=128), w=["w_q"])
        S.dma("gpsimd", w_o[:], w_o_d.rearrange("(k p) n -> p k n", p=128), w=["w_o"])

        S.pool(lambda e: e.memset(hc[:], 0.0), w=["hc"])
        S.pool(lambda e: e.memset(xrp[1][:], 0.0), w=["xrp1"])
        S.pool(lambda e: e.memset(xrp[0][:], 0.0), w=["xrp0"])
        for i in range(2):
            S.pool(lambda e, i=i: e.memset(Vp[i][:], 0.0), w=["Vp%d" % i])
            S.pool(lambda e, i=i: e.memset(KT[i][:], 0.0), w=["KT%d" % i])
        S.pool(lambda e: e.memset(Gh[:], 0.0), w=["Gh"])

        S.act(lambda e: e.activation(out=st[:, 0:4], in_=vcol(V_LAM, 4), func=ACT.Exp, scale=-1.0),
              r=["vec"], w=["st_a"])
        S.act(lambda e: e.activation(out=st[:, 4:8], in_=st[:, 0:4], func=ACT.Ln, bias=vcol(V_ONE), scale=1.0),
              r=["st_a", "vec"], w=["st_b"])
        S.dve(lambda e: e.tensor_scalar(out=cl[:], in0=st[:, 4:8], scalar1=-8.0, scalar2=None, op0=ALU.mult),
              r=["st_b"], w=["cl"])
        S.dve(lambda e: e.tensor_scalar(out=sink8[:], in0=vcol(V_SINK, 8), scalar1=8.0, scalar2=None, op0=ALU.mult),
              r=["vec"], w=["sink8"])

        def rmsnorm_T(xap, xkey, gcol, dst, dstkey, tagn):
            S.act(lambda e: e.activation(out=junk[:], in_=xap, func=ACT.Square, accum_out=st[:, 8:9]),
                  r=[xkey], w=["nbf", "st_ss"])
            S.act(lambda e: e.activation(out=st[:, 9:10], in_=st[:, 8:9], func=ACT.Sqrt,
                                         bias=vcol(V_EPS), scale=1.0 / D),
                  r=["st_ss", "vec"], w=["st_sd"])
            S.dve(lambda e: e.reciprocal(out=st[:, 10:11], in_=st[:, 9:10]), r=["st_sd"], w=["st_rs"])
            S.dve(lambda e: e.tensor_scalar(out=nbf[:], in0=xap, scalar1=st[:, 10:11], scalar2=None, op0=ALU.mult),
                  r=[xkey, "st_rs"], w=["nbf"])
            b0, pk = psbank(1)
            pv = psb(b0).rearrange("p (k n) -> p k n", k=8)
            for k in range(8):
                S.pe(lambda e, k=k: e.transpose(out=pv[:, k, :], in_=nbf[:, k * 128:(k + 1) * 128], identity=ident[:]),
                     r=["nbf", "ident"], w=pk)
            gb = vec[:, gcol:gcol + 8].unsqueeze(2).to_broadcast([128, 8, 128])
            S.dve(lambda e: e.tensor_tensor(out=dst, in0=pv, in1=gb, op=ALU.mult),
                  r=pk + ["vec"], w=[dstkey])

        def fm_proj(wsb, wkey, col0, nchunks, src, srckey, n=128, width=128):
            b0, pk = psbank(1)
            pv = psf(b0).rearrange("p (c n) -> p c n", c=512 // n)
            for c in range(nchunks):
                for k in range(8):
                    S.pe(lambda e, c=c, k=k: e.matmul(pv[:, c, :], lhsT=wsb[:, k, col0 + c * width:col0 + (c + 1) * width],
                                                       rhs=src[:, k, :], start=(k == 0), stop=(k == 7)),
                         r=[wkey, srckey], w=pk)
            return pv, pk

        def lru_stage(ti, nTt, nTkey, smp=False):
            cur, prv = xrp[ti % 2], xrp[(ti + 1) % 2]
            ck, pk_ = "xrp%d" % (ti % 2), "xrp%d" % ((ti + 1) % 2)
            if not smp:
                S.pool(lambda e: e.tensor_copy(out=cur[:, :, 0:3], in_=prv[:, :, 128:131]), r=[pk_], w=[ck])
            pxr, kxr = fm_proj(w_in, "w_in", 0, 4, nTt, nTkey)
            if not smp:
                S.act(lambda e: e.activation(out=cur[:, :, 3:131], in_=pxr, func=ACT.Copy), r=kxr, w=[ck])
            else:
                for c in range(4):
                    S.act(lambda e, c=c: e.activation(out=XP[:, c, :, 3:7],
                                                      in_=pxr[:, c, 0:64].rearrange("p (s t) -> p s t", t=4),
                                                      func=ACT.Copy), r=kxr, w=["XP"])
            for c in range(4):
                if not smp:
                    o_ap = xc[:, c, :]
                    in_tap = lambda tap, c=c: cur[:, c, tap:tap + 128]
                    srck = ck
                else:
                    o_ap = xc[:, c, 0:64].rearrange("p (s t) -> p s t", t=4)
                    in_tap = lambda tap, c=c: XP[:, c, :, tap:tap + 4]
                    srck = "XP"
                S.dve(lambda e, c=c, o_ap=o_ap, in_tap=in_tap: e.tensor_scalar(
                    out=o_ap, in0=in_tap(3), scalar1=vcol(V_CW + 12 + c), scalar2=vcol(V_BCONV + c),
                    op0=ALU.mult, op1=ALU.add), r=[srck, "vec"], w=["xc%d" % c])
                for tap in range(3):
                    S.dve(lambda e, c=c, tap=tap, o_ap=o_ap, in_tap=in_tap: e.scalar_tensor_tensor(
                        out=o_ap, in0=in_tap(tap), scalar=vcol(V_CW + tap * 4 + c),
                        in1=o_ap, op0=ALU.mult, op1=ALU.add),
                        r=[srck, "vec", "xc%d" % c], w=["xc%d" % c])
            xck = ["xc%d" % c for c in range(4)]
            S.act(lambda e: e.activation(out=xcb[:], in_=xc[:], func=ACT.Copy), r=xck, w=["xcb"])
            b0, pk = psbank(2)
            pr = psf(b0).rearrange("p (c n) -> p c n", c=4)
            pi = psf(b0 + 1).rearrange("p (c n) -> p c n", c=4)
            for c in range(4):
                S.pe(lambda e, c=c: e.matmul(pr[:, c, :], lhsT=wa[:, c, :], rhs=xcb[:, c, :], start=True, stop=True),
                     r=["wa", "xcb"], w=[pk[0]])
            for c in range(4):
                S.pe(lambda e, c=c: e.matmul(pi[:, c, :], lhsT=wx[:, c, :], rhs=xcb[:, c, :], start=True, stop=True),
                     r=["wx", "xcb"], w=[pk[1]])
            for c in range(4):
                S.act(lambda e, c=c: e.activation(out=rg[:, c, :], in_=pr[:, c, :], func=ACT.Sigmoid,
                                                  bias=vcol(V_BA + c), scale=1.0),
                      r=[pk[0], "vec"], w=["rg"])
            for c in range(4):
                S.act(lambda e, c=c: e.activation(out=ig[:, c, :], in_=pi[:, c, :], func=ACT.Sigmoid,
                                                  bias=vcol(V_BX + c), scale=1.0),
                      r=[pk[1], "vec"], w=["ig"])
            for c in range(4):
                S.act(lambda e, c=c: e.activation(out=av[:, c, :], in_=rg[:, c, :], func=ACT.Exp, scale=cl[:, c:c + 1]),
                      r=["rg", "cl"], w=["av"])
            S.pool(lambda e: e.tensor_tensor(out=bv[:], in0=av[:], in1=av[:], op=ALU.mult), r=["av"], w=["bv"])
            S.pool(lambda e: e.tensor_scalar(out=bv[:], in0=bv[:], scalar1=-1.0, scalar2=1.0, op0=ALU.mult, op1=ALU.add),
                   r=["bv"], w=["bv"])
            S.act(lambda e: e.activation(out=bv[:], in_=bv[:], func=ACT.Sqrt), r=["bv"], w=["bv"])
            S.pool(lambda e: e.tensor_tensor(out=ig[:], in0=ig[:], in1=xc[:], op=ALU.mult), r=["ig"] + xck, w=["ig"])
            S.dve(lambda e: e.tensor_tensor(out=bv[:], in0=bv[:], in1=ig[:], op=ALU.mult), r=["bv", "ig"], w=["bv"])
            if smp:
                v4 = lambda t_, tt: t_[:, :, 0:64].rearrange("p c (s t) -> p c s t", t=4)[:, :, :, tt]
                for tt in range(4):
                    hprev = HS[:] if tt == 0 else v4(hv, tt - 1)
                    S.dve(lambda e, tt=tt, hprev=hprev: e.tensor_tensor(out=v4(hv, tt), in0=v4(av, tt), in1=hprev,
                                                                        op=ALU.mult), r=["av", "hv", "HS"], w=["hv"])
                    S.dve(lambda e, tt=tt: e.tensor_tensor(out=v4(hv, tt), in0=v4(hv, tt), in1=v4(bv, tt),
                                                           op=ALU.add), r=["bv", "hv"], w=["hv"])
                S.dve(lambda e: e.tensor_copy(out=HS2[:], in_=v4(hv, 3)), r=["hv"], w=["HS2"])
                return
            for c in range(4):
                S.dve(lambda e, c=c: e.tensor_tensor_scan(out=hv[:, c, :], data0=av[:, c, :], data1=bv[:, c, :],
                                                           initial=hc[:, c:c + 1], op0=ALU.mult, op1=ALU.add),
                      r=["av", "bv", "hc"], w=["hv"])
            S.dve(lambda e: e.tensor_copy(out=hc[:], in_=hv[:, :, 127]), r=["hv"], w=["hc"])

        def attn_core(nh, Sps, Skeys, scale, sinkcol, Pbuf, Pkey, nkb, rows=128):
            W = nkb * 128
            S.dve(lambda e: e.reduce_max(out=st[:rows, 16:16 + nh], in_=Sps[:rows], axis=AX.X), r=Skeys, w=["st_mx"])
            if sinkcol is not None:
                S.dve(lambda e: e.tensor_tensor(out=st[:rows, 16:16 + nh], in0=st[:rows, 16:16 + nh],
                                                in1=sink8[:rows, :], op=ALU.max),
                      r=["st_mx", "sink8"], w=["st_mx"])
            S.dve(lambda e: e.tensor_scalar(out=st[:rows, 24:24 + nh], in0=st[:rows, 16:16 + nh], scalar1=-scale,
                                            scalar2=None, op0=ALU.mult),
                  r=["st_mx"], w=["st_nm"])
            for h in range(nh):
                S.act(lambda e, h=h: e.activation(out=Pbuf[:rows, h, 0:W], in_=Sps[:rows, h, :], func=ACT.Exp,
                                                  bias=st[:rows, 24 + h:25 + h], scale=scale,
                                                  accum_out=st[:rows, 32 + h:33 + h]),
                      r=Skeys + ["st_nm"], w=[Pkey, "st_rs%d" % h])
            rsk = ["st_rs%d" % h for h in range(nh)]
            if sinkcol is not None:
                S.dve(lambda e: e.tensor_tensor(out=st[:rows, 40:48], in0=st[:rows, 24:32],
                                                in1=vec[:rows, sinkcol:sinkcol + 8], op=ALU.add),
                      r=["st_nm", "vec"], w=["st_es"])
                S.act(lambda e: e.activation(out=st[:rows, 40:48], in_=st[:rows, 40:48], func=ACT.Exp),
                      r=["st_es"], w=["st_es"])
                S.dve(lambda e: e.tensor_tensor(out=st[:rows, 32:40], in0=st[:rows, 32:40], in1=st[:rows, 40:48],
                                                op=ALU.add),
                      r=["st_es"] + rsk, w=rsk)
            S.dve(lambda e: e.reciprocal(out=st[:rows, 48:48 + nh], in_=st[:rows, 32:32 + nh]), r=rsk, w=["st_ri"])
            for h in range(nh):
                S.dve(lambda e, h=h: e.tensor_scalar(out=Pbuf[:rows, h, 0:W], in0=Pbuf[:rows, h, 0:W],
                                                     scalar1=st[:rows, 48 + h:49 + h], scalar2=None, op0=ALU.mult),
                      r=[Pkey, "st_ri"], w=[Pkey])
            nt = nh * nkb
            nb = (nt * 128 + 1023) // 1024
            b0, pk = psbank(nb)
            pv = psb(b0, nb).rearrange("p (t n) -> p t n", n=128)
            for h in range(nh):
                for kb in range(nkb):
                    t = h * nkb + kb
                    S.pe(lambda e, h=h, kb=kb, t=t: e.transpose(out=pv[:, t, 0:rows],
                                                                in_=Pbuf[:rows, h, kb * 128:(kb + 1) * 128],
                                                                identity=ident[:rows, :rows]),
                         r=[Pkey, "ident"], w=[pk[(t * 128) // 1024]])
            half = nt // 2
            if nb == 1:
                S.act(lambda e: e.activation(out=PT[:, 0:nt, 0:rows], in_=pv[:, 0:nt, 0:rows], func=ACT.Copy),
                      r=pk, w=["PTa", "PTb"])
            else:
                S.act(lambda e: e.activation(out=PT[:, 0:half, 0:rows], in_=pv[:, 0:half, 0:rows], func=ACT.Copy),
                      r=pk, w=["PTa"])
                S.dve(lambda e: e.tensor_copy(out=PT[:, half:nt, 0:rows], in_=pv[:, half:nt, 0:rows]),
                      r=pk, w=["PTb"])

        def swa_stage(ti, nTt, nTkey, first_block, bias_first, rows=128, qcols=None):
            cb, pb = ti % 2, (ti + 1) % 2
            pq, kq = fm_proj(w_in, "w_in", 1024, 4, nTt, nTkey)
            S.act(lambda e: e.activation(out=QT[:], in_=pq, func=ACT.Copy), r=kq, w=["QT"])
            b0, pk = psbank(1)
            pkk = psf(b0)[:, 0:128]
            pkv = psf(b0)[:, 128:384]
            for k in range(8):
                S.pe(lambda e, k=k: e.matmul(pkk, lhsT=w_in[:, k, 1536:1664], rhs=nTt[:, k, :],
                                             start=(k == 0), stop=(k == 7)), r=["w_in", nTkey], w=pk)
            for k in range(8):
                S.pe(lambda e, k=k: e.matmul(pkv, lhsT=nTt[:, k, :], rhs=w_in[:, k, 1536:1792],
                                             start=(k == 0), stop=(k == 7)), r=["w_in", nTkey], w=pk)
            S.act(lambda e: e.activation(out=KT[cb][:], in_=pkk, func=ACT.Copy), r=pk, w=["KT%d" % cb])
            S.dve(lambda e: e.tensor_copy(out=Vp[cb][:, 0, 0:64], in_=pkv[:, 128:192]), r=pk, w=["Vp%d" % cb])
            S.dve(lambda e: e.tensor_copy(out=Vp[cb][:, 1, 64:128], in_=pkv[:, 192:256]), r=pk, w=["Vp%d" % cb])
            S.act(lambda e: e.activation(out=kvf[:], in_=pkv, func=ACT.Copy), r=pk, w=["kvf"])
            return cb, pb

        def swa_attend(cb, pb, bias_first, smp=None):
            b0, sk = psbank(4)
            Sps = psf(b0, 4).rearrange("p (h n) -> p h n", h=8)
            if smp is None:
                rows, q0 = 128, 0
                xkeys = []
            else:
                rows, q0 = 4, 4 * smp["s"]
                xkeys = smp["keys"]
            idn = ident[0:rows, 0:rows]
            for j in range(4):
                for b in range(2):
                    s_ = 2 * j + b
                    key = [sk[s_ // 2]]
                    lo, hi = 64 * b, 64 * b + 64
                    if smp is None:
                        kprev = KT[pb][lo:hi, :]
                        bprev = bias0[:, s_, :] if bias_first else biasT[:, s_, 0:128]
                        bown = biasT[:, s_, 128:256]
                        kpk = "KT%d" % pb
                    else:
                        kprev = smp["KTc"][lo:hi, smp["s"], :]
                        bprev = biasT[0:4, s_, 0:128]
                        o0 = 128 - 4 * smp["s"]
                        bown = smp["Bown"][0:4, s_, o0:o0 + 128]
                        kpk = "KT%d" % cb
                    S.pe(lambda e, j=j, lo=lo, hi=hi, s_=s_, kprev=kprev: e.matmul(
                        Sps[0:rows, s_, 0:128], lhsT=QT[lo:hi, j, q0:q0 + rows], rhs=kprev, start=True, stop=False),
                        r=["QT", kpk] + xkeys, w=key)
                    S.pe(lambda e, s_=s_, bprev=bprev: e.matmul(Sps[0:rows, s_, 0:128], lhsT=idn, rhs=bprev,
                                                                start=False, stop=True),
                         r=["ident", "bias0", "biasT"], w=key)
                    S.pe(lambda e, j=j, lo=lo, hi=hi, s_=s_: e.matmul(
                        Sps[0:rows, s_, 128:256], lhsT=QT[lo:hi, j, q0:q0 + rows], rhs=KT[cb][lo:hi, :],
                        start=True, stop=False), r=["QT", "KT%d" % cb], w=key)
                    S.pe(lambda e, s_=s_, bown=bown: e.matmul(Sps[0:rows, s_, 128:256], lhsT=idn, rhs=bown,
                                                              start=False, stop=True),
                         r=["ident", "biasT"] + xkeys, w=key)
            attn_core(8, Sps, sk, 0.125, V_SINK, Pm, "Pm", 2, rows=rows)
            b0, ok = psbank(1)
            pO = psf(b0).rearrange("p (c n) -> p c n", c=4)
            for j in range(4):
                n = 0
                for b in range(2):
                    s_ = 2 * j + b
                    for kb in (0, 1):
                        if kb == 1:
                            vap, vk = Vp[cb][:, b, :], ["Vp%d" % cb]
                        elif smp is None:
                            vap, vk = Vp[pb][:, b, :], ["Vp%d" % pb]
                        else:
                            vap, vk = smp["Vc"][b][:, smp["s"], :], xkeys
                        S.pe(lambda e, j=j, s_=s_, kb=kb, vap=vap, n=n: e.matmul(
                            pO[:, j, 0:rows], lhsT=vap, rhs=PT[:, s_ * 2 + kb, 0:rows],
                            start=(n == 0), stop=(n == 3)),
                            r=vk + ["PTa", "PTb"], w=ok)
                        n += 1
            S.act(lambda e: e.activation(out=yT[:, 4:8, q0:q0 + rows], in_=pO[:, :, 0:rows], func=ACT.Copy),
                  r=ok, w=["yT_att"])

        def mixer_out(xti, xkey, nTt, nTkey):
            pg, kg = fm_proj(w_in, "w_in", 512, 4, nTt, nTkey)
            S.act(lambda e: e.activation(out=gg[:], in_=pg, func=ACT.Gelu_apprx_tanh), r=kg, w=["rg"])
            S.dve(lambda e: e.tensor_tensor(out=yT[:, 0:4, :], in0=hv[:], in1=gg[:], op=ALU.mult),
                  r=["hv", "rg"], w=["yT_lru"])
            b0, pk = psbank(2)
            po = psf(b0, 2)
            for hh in range(2):
                for k in range(8):
                    S.pe(lambda e, hh=hh, k=k: e.matmul(po[:, hh * 512:(hh + 1) * 512], lhsT=yT[:, k, :],
                                                        rhs=w_out[:, k, hh * 512:(hh + 1) * 512],
                                                        start=(k == 0), stop=(k == 7)),
                         r=["yT_lru", "yT_att", "w_out"], w=[pk[hh]])
            S.dve(lambda e: e.tensor_tensor(out=xti[:], in0=xti[:], in1=po, op=ALU.add), r=[xkey] + pk, w=[xkey])

        def cross_attend(smp=None):
            if smp is None:
                rows, q0, kT, vv, xkeys = 128, 0, mkT, mvb, ["mkT", "mvb"]
            else:
                rows, q0, kT, vv, xkeys = 4, 4 * smp["s"], smp["mkT"], smp["mv"], smp["keys"]
            b0, sk = psbank(2)
            Sps = psf(b0, 2).rearrange("p (h n) -> p h n", h=4)
            for h in range(4):
                S.pe(lambda e, h=h: e.matmul(Sps[0:rows, h, :], lhsT=QC[:, h, q0:q0 + rows], rhs=kT[:, h, :],
                                             start=True, stop=True),
                     r=["QC"] + xkeys, w=[sk[h // 2]])
            attn_core(4, Sps, sk, SC_MEM, None, Pm, "Pm", 2, rows=rows)
            b0, ok = psbank(1)
            pO = psf(b0).rearrange("p (c n) -> p c n", c=4)
            for h in range(4):
                for kb in range(2):
                    S.pe(lambda e, h=h, kb=kb: e.matmul(pO[:, h, 0:rows], lhsT=vv[:, kb, h * 128:(h + 1) * 128],
                                                        rhs=PT[:, h * 2 + kb, 0:rows], start=(kb == 0), stop=(kb == 1)),
                         r=xkeys + ["PTa", "PTb"], w=ok)
            S.act(lambda e: e.activation(out=OC[:, :, q0:q0 + rows], in_=pO[:, :, 0:rows], func=ACT.Copy),
                  r=ok, w=["OC"])

        def cross_stage(xti, xkey, nTt, nTkey, smp_iter=None):
            rmsnorm_T(xti[:], xkey, V_GCROSS, nTt[:], nTkey, "n2")
            pq, kq = fm_proj(w_q, "w_q", 0, 4, nTt, nTkey)
            S.act(lambda e: e.activation(out=QC[:], in_=pq, func=ACT.Copy), r=kq, w=["QC"])
            if smp_iter is None:
                cross_attend(None)
            else:
                smp_iter()
            b0, pk = psbank(2)
            po = psf(b0, 2)
            for hh in range(2):
                for k in range(4):
                    S.pe(lambda e, hh=hh, k=k: e.matmul(po[:, hh * 512:(hh + 1) * 512], lhsT=OC[:, k, :],
                                                        rhs=w_o[:, k, hh * 512:(hh + 1) * 512],
                                                        start=(k == 0), stop=(k == 3)),
                         r=["OC", "w_o"], w=[pk[hh]])
            S.dve(lambda e: e.tensor_tensor(out=xti[:], in0=xti[:], in1=po, op=ALU.add), r=[xkey] + pk, w=[xkey])

        v_k8 = lambda t: t[:].rearrange("p (k n) -> p k n", k=8)
        wq = []
        wissued = [0]

        def wview(i):
            return ring[i % NRING][:].rearrange("p (k n) -> p k n", k=wq[i][1])

        def wq_add_macro(gate_only=False):
            base = len(wq)
            if gate_only:
                for g in range(6):
                    wq.append((w_g_d[:, g * 512:(g + 1) * 512].rearrange("(k p) n -> p k n", p=128), 8))
                return base
            for fh in range(2):
                for gi in range(3):
                    g = 3 * fh + gi
                    wq.append((w_g_d[:, g * 512:(g + 1) * 512].rearrange("(k p) n -> p k n", p=128), 8))
                    wq.append((w_u_d[:, g * 512:(g + 1) * 512].rearrange("(k p) n -> p k n", p=128), 8))
                for gi in range(3):
                    g = 3 * fh + gi
                    wq.append((w_d_d[g * 512:(g + 1) * 512, :].rearrange("(k p) n -> p k n", p=128), 4))
            return base

        def wq_issue(upto):
            while wissued[0] < len(wq) and wissued[0] <= upto:
                j = wissued[0]
                S.dma("gpsimd", wview(j), wq[j][0], w=["ring%d" % (j % NRING)])
                wissued[0] += 1

        def wq_get(i):
            wq_issue(i)
            return wview(i), "ring%d" % (i % NRING)

        def wq_done(i):
            wq_issue(i + NRING)

        def load_x(ti, slot):
            S.dma("sync", xt[slot][:], xs[ti * 128:(ti + 1) * 128, :], w=["xt%d" % slot])

        for mt in range(2):
            S.dma("sync", xt[mt][:], memx[mt * 128:(mt + 1) * 128, :], w=["xt%d" % mt])
        for mt in range(2):
            rmsnorm_T(xt[mt][:], "xt%d" % mt, V_GMEM, nT[mt][:], "nT%d" % mt, "nm")
            for (wsl, wkey, od, is_k) in ((wk_s, "ring0", omk_d, True), (wv_s, "ring1", omv_d, False)):
                b0, pk = psbank(1)
                pm = psf(b0)
                for k in range(8):
                    S.pe(lambda e, k=k, wsl=wsl, pm=pm, mt=mt: e.matmul(pm, lhsT=nT[mt][:, k, :], rhs=wsl[:, k, :],
                                                                 start=(k == 0), stop=(k == 7)),
                         r=["nT%d" % mt, wkey], w=pk)
                S.act(lambda e, pm=pm: e.activation(out=mtmp[:], in_=pm, func=ACT.Copy), r=pk, w=["t1_0"])
                if not is_k:
                    S.dve(lambda e, pm=pm, mt=mt: e.tensor_copy(out=mvb[:, mt, :], in_=pm), r=pk, w=["mvb"])
                S.dma("sync", od[mt * 128:(mt + 1) * 128, :], mtmp[:], r=["t1_0"])
            pkT, kk = fm_proj(wk_s, "ring0", 0, 4, nT[mt], "nT%d" % mt)
            S.act(lambda e, pkT=pkT, mt=mt: e.activation(out=mkT[:, :, mt * 128:(mt + 1) * 128], in_=pkT, func=ACT.Copy),
                  r=kk, w=["mkT"])

        def ffn_macro(tiles, rows_out, N, halo=Gh, halokey="Gh", smp=None):
            wb = wq_add_macro()
            for fh in range(2):
                for gi in range(3):
                    g = 3 * fh + gi
                    wg3, wgk = wq_get(wb + fh * 9 + 2 * gi)
                    wu3, wuk = wq_get(wb + fh * 9 + 2 * gi + 1)
                    for c in range(4):
                        fc = g * 4 + c
                        hmi = gi * 4 + c
                        i2 = fc % 2
                        b0, pk = psbank(2)
                        pG, pU = psf(b0)[:, 0:N], psf(b0 + 1)[:, 0:N]
                        for k in range(8):
                            S.pe(lambda e, k=k, c=c, wg3=wg3, pG=pG: e.matmul(
                                pG, lhsT=wg3[:, k, c * 128:(c + 1) * 128], rhs=n3T[:, k, 0:N],
                                start=(k == 0), stop=(k == 7)), r=[wgk, "n3T"], w=[pk[0]])
                        for k in range(8):
                            S.pe(lambda e, k=k, c=c, wu3=wu3, pU=pU: e.matmul(
                                pU, lhsT=wu3[:, k, c * 128:(c + 1) * 128], rhs=n3T[:, k, 0:N],
                                start=(k == 0), stop=(k == 7)), r=[wuk, "n3T"], w=[pk[1]])
                        Gsb, gk = Gs[i2], "Gs%d" % i2
                        tk = "t1_%d" % i2
                        if smp is None:
                            S.dve(lambda e, fc=fc, Gsb=Gsb: e.tensor_copy(out=Gsb[:, 0:2], in_=halo[:, fc, :]),
                                  r=[halokey + "%d" % fc, halokey], w=[gk])
                            S.act(lambda e, Gsb=Gsb, pG=pG: e.activation(out=Gsb[:, 2:2 + N], in_=pG, func=ACT.Copy),
                                  r=[pk[0]], w=[gk])
                            S.dve(lambda e, fc=fc, Gsb=Gsb: e.tensor_copy(out=halo[:, fc, :], in_=Gsb[:, N:N + 2]),
                                  r=[gk], w=[halokey + "%d" % fc])
                            t1v = t1[i2][:, 0:N]
                            g_tap = lambda tap, Gsb=Gsb: Gsb[:, tap:tap + N]
                            pGv = pG
                        else:
                            FS, FSn, fkeys = smp["FS"], smp["FSn"], smp["keys"]
                            G3 = Gsb[:, 0:96].rearrange("p (s t) -> p s t", t=6)
                            S.dve(lambda e, fc=fc, G3=G3, FS=FS: e.tensor_copy(out=G3[:, :, 0:2], in_=FS[:, fc, :, :]),
                                  r=[fkeys[0]], w=[gk])
                            S.act(lambda e, G3=G3, pG=pG: e.activation(
                                out=G3[:, :, 2:6], in_=pG[:, 0:64].rearrange("p (s t) -> p s t", t=4), func=ACT.Copy),
                                r=[pk[0]], w=[gk])
                            S.dve(lambda e, fc=fc, G3=G3, FSn=FSn: e.tensor_copy(out=FSn[:, fc, :, :], in_=G3[:, :, 4:6]),
                                  r=[gk], w=[fkeys[1]])
                            t1v = t1[i2][:, 0:64].rearrange("p (s t) -> p s t", t=4)
                            g_tap = lambda tap, G3=G3: G3[:, :, tap:tap + 4]
                            pGv = pG[:, 0:64].rearrange("p (s t) -> p s t", t=4)
                        S.act(lambda e, fc=fc, pGv=pGv, t1v=t1v: e.activation(
                            out=t1v, in_=pGv, func=ACT.Identity, bias=vcol(V_FB + fc),
                            scale=vcol(V_FW + 48 + fc)), r=[pk[0], "vec"], w=[tk])
                        for tap in (1, 0):
                            S.dve(lambda e, fc=fc, tap=tap, g_tap=g_tap, t1v=t1v: e.scalar_tensor_tensor(
                                out=t1v, in0=g_tap(tap), scalar=vcol(V_FW + tap * 24 + fc),
                                in1=t1v, op0=ALU.mult, op1=ALU.add),
                                r=[gk, "vec", tk], w=[tk])
                        S.act(lambda e, i2=i2: e.activation(out=t1[i2][:, 0:N], in_=t1[i2][:, 0:N],
                                                            func=ACT.Gelu_apprx_tanh), r=[tk], w=[tk])
                        S.dve(lambda e, hmi=hmi, i2=i2, pU=pU: e.tensor_tensor(out=hm[:, hmi, 0:N], in0=t1[i2][:, 0:N],
                                                                               in1=pU, op=ALU.mult),
                              r=[tk, pk[1]], w=["hm%d" % hmi])
                    wq_done(wb + fh * 9 + 2 * gi + 1)
                wds = [wq_get(wb + fh * 9 + 6 + q) for q in range(3)]
                for j, tj in enumerate(tiles):
                    sl = xslot(tj)
                    for hh in range(2):
                        b0, pk = psbank(1)
                        pD = psf(b0)
                        for q in range(3):
                            for c in range(4):
                                hmi = q * 4 + c
                                S.pe(lambda e, q=q, c=c, hmi=hmi, j=j, hh=hh, pD=pD, wds=wds: e.matmul(
                                    pD, lhsT=hm[:, hmi, j * 128:(j + 1) * 128],
                                    rhs=wds[q][0][:, c, hh * 512:(hh + 1) * 512],
                                    start=(hmi == 0), stop=(hmi == 11)),
                                    r=[wds[q][1], "hm%d" % hmi], w=pk)
                        S.dve(lambda e, sl=sl, hh=hh, pD=pD: e.tensor_tensor(
                            out=xt[sl][:, hh * 512:(hh + 1) * 512], in0=xt[sl][:, hh * 512:(hh + 1) * 512],
                            in1=pD, op=ALU.add), r=["xt%d" % sl] + pk, w=["xt%d" % sl])
                wq_done(wb + fh * 9 + 8)
            for j, tj in enumerate(tiles):
                sl = xslot(tj)
                xk2 = "xt%d" % sl
                S.act(lambda e, sl=sl: e.activation(out=junk[:], in_=xt[sl][:], func=ACT.Square, accum_out=st[:, 8:9]),
                      r=[xk2], w=["nbf", "st_ss"])
                S.act(lambda e: e.activation(out=st[:, 9:10], in_=st[:, 8:9], func=ACT.Sqrt, bias=vcol(V_EPS),
                                             scale=1.0 / D), r=["st_ss", "vec"], w=["st_sd"])
                S.dve(lambda e: e.reciprocal(out=st[:, 10:11], in_=st[:, 9:10]), r=["st_sd"], w=["st_rs"])
                S.dve(lambda e, sl=sl: e.scalar_tensor_tensor(out=xt[sl][:], in0=xt[sl][:], scalar=st[:, 10:11],
                                                              in1=gfin[:], op0=ALU.mult, op1=ALU.mult),
                      r=[xk2, "st_rs", "gfin"], w=[xk2])
                S.dma("sync", y_d[rows_out[j]:rows_out[j] + 128, :], xt[sl][:], r=[xk2])

        total = n_pre + n_main
        xslot = lambda ti: ti % NXB
        load_x(0, 0)
        loaded, normed = {0}, set()

        def ensure_load(tj):
            if tj < total and tj not in loaded:
                loaded.add(tj)
                load_x(tj, xslot(tj))

        def do_norm(tj):
            if tj not in normed:
                normed.add(tj)
                rmsnorm_T(xt[xslot(tj)][:], "xt%d" % xslot(tj), V_GMIX, nT[tj % 2][:], "nT%d" % (tj % 2), "n1")

        for ti in range(total):
            slot = xslot(ti)
            xk = "xt%d" % slot
            ensure_load(ti + 1)
            nTt = nT[ti % 2]
            nTk = "nT%d" % (ti % 2)
            do_norm(ti)
            if ti + 1 < n_pre - 2:
                do_norm(ti + 1)
            lru_stage(ti, nTt, nTk)
            full = ti >= n_pre - 1
            if ti >= n_pre - 2:
                cb, pb = swa_stage(ti, nTt, nTk, False, False)
            if ti < n_pre and (ti + 1) % 16 == 0:
                fcol = V_FLAG + (ti + 1) // 16 - 1
                S.dve(lambda e, fcol=fcol: e.tensor_scalar(out=hc[:], in0=hc[:], scalar1=vcol(fcol), scalar2=None,
                                                           op0=ALU.mult), r=["hc", "vec"], w=["hc"])
            if not full:
                continue
            swa_attend(cb, pb, ti == n_pre)
            mixer_out(xt[slot], xk, nTt, nTk)
            cross_stage(xt[slot], xk, nTt, nTk)
            mi = (ti - n_pre) % 4 if ti >= n_pre else 0
            if ti == n_pre - 1:
                rmsnorm_T(xt[slot][:], xk, V_GFFN, nTt[:], nTk, "n3")
                b0, pk = psbank(1)
                ph = psf(b0)[:, 0:48].rearrange("p (c n) -> p c n", n=2)
                wb = wq_add_macro(gate_only=True)
                for g in range(6):
                    wg3, wgk = wq_get(wb + g)
                    for c in range(4):
                        for k in range(8):
                            S.pe(lambda e, g=g, c=c, k=k, wg3=wg3, ph=ph, nTt=nTt: e.matmul(
                                ph[:, g * 4 + c, :], lhsT=wg3[:, k, c * 128:(c + 1) * 128], rhs=nTt[:, k, 126:128],
                                start=(k == 0), stop=(k == 7)), r=[wgk, nTk], w=pk)
                    wq_done(wb + g)
                S.dve(lambda e, ph=ph: e.tensor_scalar(out=Gh[:], in0=ph, scalar1=vcol(V_FLAG + 3), scalar2=None, op0=ALU.mult),
                      r=pk + ["vec"], w=["Gh"])
                continue
            rmsnorm_T(xt[slot][:], xk, V_GFFN, n3T[:, :, mi * 128:(mi + 1) * 128], "n3T", "n3")
            if ti == total - 1:
                S.dma("sync", okv_d, kvf[:], r=["kvf"])
                for t in range(3):
                    S.dma("sync", olc_d[t, :].rearrange("(c p) -> p c", p=128), xrp[ti % 2][:, :, 128 + t],
                          r=["xrp%d" % (ti % 2)], allow_slow_non_contiguous=True)
                S.dma("sync", olh_d.rearrange("(c p) -> p c", p=128), hc[:], r=["hc"],
                      allow_slow_non_contiguous=True)
            if mi != 3:
                continue
            ffn_macro([ti - 3, ti - 2, ti - 1, ti], [(tj - n_pre) * 128 for tj in (ti - 3, ti - 2, ti - 1, ti)], 512)
            if ti == total - 1:
                for t in range(2):
                    S.dma("sync", ofc_d[t, :].rearrange("(c p) -> p c", p=128), Gh[:, :, t],
                          r=["Gh"] + ["Gh%d" % fc for fc in range(24)], allow_slow_non_contiguous=True)

        if do_sample:
            ti = total
            slot = xslot(ti)
            xk = "xt%d" % slot
            free = [i for i in range(NXB) if i != slot]
            fk = ["xt%d" % i for i in free]
            Vc = [xt[free[0]].bitcast(BF16)[:, 0:2048].rearrange("p (s d) -> p s d", d=128),
                  xt[free[1]].bitcast(BF16)[:, 0:2048].rearrange("p (s d) -> p s d", d=128)]
            FS = xt[free[2]][:, 0:768].rearrange("p (c s k) -> p c s k", c=24, k=2)
            FSn = xt[free[3]][:, 0:768].rearrange("p (c s k) -> p c s k", c=24, k=2)
            n3flat = n3T[:].rearrange("p a b -> p (a b)")
            KTc = n3flat[:, 0:2048].rearrange("p (s d) -> p s d", d=128)
            Bown = n3flat[:, 2048:4096].rearrange("p (h n) -> p h n", n=256)
            hmflat = hm[:].rearrange("p a b -> p (a b)")
            S.dma("sync", xt[slot][:], xs[ti * 128:(ti + 1) * 128, :], w=[xk])
            S.dma("sync", ident32[:], ident32_d, w=["ident32"])
            S.dma("sync", stL[0:48, :], slc_d, w=["stL"])
            S.dma("sync", stH[0:16, :], slh_d, w=["stH"])
            S.dma("sync", osk_d[:, 0:124, :], cswk_d[:, 4:128, :])
            S.dma("sync", osv_d[:, 0:124, :], cswv_d[:, 4:128, :])
            S.pool(lambda e: e.memset(xt[free[0]][:], 0.0), w=[fk[0]])
            S.pool(lambda e: e.memset(xt[free[1]][:], 0.0), w=[fk[1]])
            Kc = hmflat[:, 0:2048].rearrange("p (s d) -> p s d", d=128)
            hk03 = ["hm0", "hm1", "hm2", "hm3"]
            S.dma("gpsimd", Kc, cswk_d.rearrange("s k d -> k s d"), w=hk03)
            S.dma("gpsimd", Vc[0][:, :, 0:64], cswv_d.rearrange("s k d -> k s d")[:, :, 0:64], w=[fk[0]])
            S.dma("gpsimd", Vc[1][:, :, 64:128], cswv_d.rearrange("s k d -> k s d")[:, :, 64:128], w=[fk[1]])
            S.dma("gpsimd", Bown, bown_d.rearrange("p (h n) -> p h n", n=256), w=["n3T"])
            for half in range(2):
                b0, pk = psbank(1)
                pv = psb(b0).rearrange("p (t n) -> p t n", n=128)
                for i in range(8):
                    S.pe(lambda e, i=i, half=half, pv=pv: e.transpose(out=pv[:, i, :], in_=Kc[:, half * 8 + i, :],
                                                                      identity=ident[:]),
                         r=hk03 + ["ident"], w=pk)
                S.act(lambda e, half=half, pv=pv: e.activation(out=KTc[:, half * 8:(half + 1) * 8, :], in_=pv,
                                                               func=ACT.Copy), r=pk, w=["n3T"])
            b0, pk = psbank(1)
            p32 = psf(b0)
            for c in range(4):
                S.pe(lambda e, c=c: e.transpose(out=p32[:, c * 48:(c + 1) * 48], in_=stL[0:48, c * 128:(c + 1) * 128],
                                                identity=ident32[0:48, 0:48]), r=["stL", "ident32"], w=pk)
            S.dve(lambda e: e.tensor_copy(out=XP[:, :, :, 0:3],
                                          in_=p32[:, 0:192].rearrange("p (c s k) -> p c s k", c=4, k=3)),
                  r=pk, w=["XP"])
            b0, pk = psbank(1)
            p32b = psf(b0)
            for c in range(4):
                S.pe(lambda e, c=c: e.transpose(out=p32b[:, c * 16:(c + 1) * 16], in_=stH[0:16, c * 128:(c + 1) * 128],
                                                identity=ident32[0:16, 0:16]), r=["stH", "ident32"], w=pk)
            S.dve(lambda e: e.tensor_copy(out=HS[:], in_=p32b[:, 0:64].rearrange("p (c s) -> p c s", c=4)),
                  r=pk, w=["HS"])
            for g in range(6):
                S.dma("sync", stL[0:32, :], sfc_d[:, g * 512:(g + 1) * 512], w=["stL"])
                b0, pk = psbank(1)
                pf = psf(b0)
                for c in range(4):
                    S.pe(lambda e, c=c, pf=pf: e.transpose(out=pf[:, c * 32:(c + 1) * 32],
                                                           in_=stL[0:32, c * 128:(c + 1) * 128],
                                                           identity=ident32[0:32, 0:32]), r=["stL", "ident32"], w=pk)
                S.dve(lambda e, g=g, pf=pf: e.tensor_copy(
                    out=FS[:, g * 4:(g + 1) * 4, :, :],
                    in_=pf[:, 0:128].rearrange("p (c s k) -> p c s k", c=4, k=2)), r=pk, w=[fk[2]])
            nTt, nTk = nT[ti % 2], "nT%d" % (ti % 2)
            rmsnorm_T(xt[slot][:], xk, V_GMIX, nTt[:], nTk, "n1")
            lru_stage(ti, nTt, nTk, smp=True)
            cb, pb = swa_stage(ti, nTt, nTk, False, False)
            S.dve(lambda e: e.tensor_copy(out=XC2[:].rearrange("p c (s k) -> p c s k", k=3), in_=XP[:, :, :, 4:7]),
                  r=["XP"], w=["XC2"])
            b0, pk = psbank(1)
            po32 = psf(b0)
            for c in range(4):
                S.pe(lambda e, c=c: e.transpose(out=po32[0:48, c * 128:(c + 1) * 128], in_=XC2[:, c, :],
                                                identity=ident32[:]), r=["XC2", "ident32"], w=pk)
            S.act(lambda e: e.activation(out=stL[0:48, :], in_=po32[0:48, :], func=ACT.Copy), r=pk, w=["stL"])
            S.dma("sync", oslc_d, stL[0:48, :], r=["stL"])
            b0, pk = psbank(1)
            po32b = psf(b0)
            for c in range(4):
                S.pe(lambda e, c=c: e.transpose(out=po32b[0:16, c * 128:(c + 1) * 128], in_=HS2[:, c, :],
                                                identity=ident32[:]), r=["HS2", "ident32"], w=pk)
            S.act(lambda e: e.activation(out=stH[0:16, :], in_=po32b[0:16, :], func=ACT.Copy), r=pk, w=["stH"])
            S.dma("sync", oslh_d, stH[0:16, :], r=["stH"])
            for t4 in range(4):
                S.dma("sync", osk_d[:, 124 + t4, :], kvf[t4:64:4, 0:128], r=["kvf"])
                S.dma("sync", osv_d[:, 124 + t4, :], kvf[t4:64:4, 128:256], r=["kvf"])
            for sq in range(16):
                swa_attend(cb, pb, False, smp=dict(s=sq, KTc=KTc, Vc=Vc, Bown=Bown, keys=["n3T", fk[0], fk[1]]))
            mixer_out(xt[slot], xk, nTt, nTk)

            def cross_iter():
                for sq in range(16):
                    i2 = sq % 2
                    mkc = hmflat[:, i2 * 1024:(i2 + 1) * 1024].rearrange("p (a b) -> p a b", a=2)
                    mvc = hmflat[:, 2048 + i2 * 1024:2048 + (i2 + 1) * 1024].rearrange("p (a b) -> p a b", a=2)
                    mkTs = hmflat[:, 4096 + i2 * 1024:4096 + (i2 + 1) * 1024].rearrange("p (a b) -> p a b", a=4)
                    kk = ["hm%d" % (2 * i2), "hm%d" % (2 * i2 + 1)]
                    kv = ["hm%d" % (4 + 2 * i2), "hm%d" % (5 + 2 * i2)]
                    kt = ["hm%d" % (8 + 2 * i2), "hm%d" % (9 + 2 * i2)]
                    S.dma("gpsimd", mkc, cmk_d[sq].rearrange("(a p) n -> p a n", p=128), w=kk)
                    S.dma("gpsimd", mvc, cmv_d[sq].rearrange("(a p) n -> p a n", p=128), w=kv)
                    b0, pk = psbank(1)
                    pv = psb(b0).rearrange("p (t n) -> p t n", n=128)
                    for h in range(4):
                        for kb in range(2):
                            S.pe(lambda e, h=h, kb=kb, pv=pv, mkc=mkc: e.transpose(
                                out=pv[:, h * 2 + kb, :], in_=mkc[:, kb, h * 128:(h + 1) * 128], identity=ident[:]),
                                r=kk + ["ident"], w=pk)
                    S.dve(lambda e, pv=pv, mkTs=mkTs: e.tensor_copy(out=mkTs, in_=pv.rearrange("p (h b) n -> p h (b n)", b=2)),
                          r=pk, w=kt)
                    cross_attend(dict(s=sq, mkT=mkTs, mv=mvc, keys=kk + kv + kt))
            cross_stage(xt[slot], xk, nTt, nTk, smp_iter=cross_iter)
            rmsnorm_T(xt[slot][:], xk, V_GFFN, n3T[:, :, 0:128], "n3T", "n3")
            ffn_macro([ti], [n_main * 128], 128, smp=dict(FS=FS, FSn=FSn, keys=[fk[2], fk[3]]))
            for g in range(6):
                b0, pk = psbank(1)
                pf = psf(b0)
                for c in range(4):
                    S.pe(lambda e, c=c, g=g, pf=pf: e.transpose(
                        out=pf[0:32, c * 128:(c + 1) * 128],
                        in_=FSn[:, g * 4 + c, :, :].rearrange("p s k -> p (s k)"), identity=ident32[:]),
                        r=[fk[3], "ident32"], w=pk)
                S.act(lambda e, pf=pf: e.activation(out=stL[0:32, :], in_=pf[0:32, :], func=ACT.Copy), r=pk, w=["stL"])
                S.dma("sync", osfc_d[:, g * 512:(g + 1) * 512], stL[0:32, :], r=["stL"])

        if max_ops is not None:
            S.ops = S.ops[:max_ops]
        print('nops', len(S.ops))
        S.emit()
    return nc


def _slot_heads():
    return [(s // 2) + 4 * (s % 2) for s in range(8)]


def _bias_tables():
    slopes = 2.0 ** (-np.arange(1, 9, dtype=np.float64))
    qi = np.arange(128)[:, None]
    kj = np.arange(256)[None, :]
    dist = qi + 128 - kj
    valid = (dist >= 0) & (dist < 128)
    tab = np.empty((128, 8, 256), np.float32)
    for s, h in enumerate(_slot_heads()):
        tab[:, s, :] = np.where(valid, -8.0 * slopes[h] * dist, NEG)
    return tab


_NC_CACHE = {}


def kernel(**inp):
    f32 = np.float32
    xp = np.asarray(inp["x_prompt"], f32)
    xsmp = np.asarray(inp["x_sample"], f32)
    heads = _slot_heads()
    w_in = np.asarray(inp["w_in"][0], f32)
    qcols = np.concatenate([np.arange(1024 + h * 64, 1024 + (h + 1) * 64) for h in heads])
    w_in_p = np.ascontiguousarray(np.concatenate([w_in[:, :1024], w_in[:, qcols], w_in[:, 1536:]], axis=1))
    w_out = np.asarray(inp["w_out"][0], f32)
    orow = np.concatenate([np.arange(512 + h * 64, 512 + (h + 1) * 64) for h in heads])
    w_out_p = np.ascontiguousarray(np.concatenate([w_out[:512], w_out[orow]], axis=0))

    def bd(w):
        o = np.zeros((128, 4, 128), f32)
        for c in range(4):
            o[0:64, c, 0:64] = w[2 * c]
            o[64:128, c, 64:128] = w[2 * c + 1]
        return o.reshape(128, 512)

    def fm(v, nchunk):
        return np.asarray(v, f32).reshape(nchunk, 128).T

    vec = np.zeros((128, NV), f32)
    vec[:, V_GMIX:V_GMIX + 8] = fm(inp["g_mix"][0], 8)
    vec[:, V_GCROSS:V_GCROSS + 8] = fm(inp["g_cross"][0], 8)
    vec[:, V_GFFN:V_GFFN + 8] = fm(inp["g_ffn"][0], 8)
    vec[:, V_GMEM:V_GMEM + 8] = fm(inp["g_mem"][0], 8)
    for tap in range(4):
        vec[:, V_CW + tap * 4:V_CW + tap * 4 + 4] = fm(inp["w_lru_conv"][0, tap], 4)
    vec[:, V_BCONV:V_BCONV + 4] = fm(inp["b_lru_conv"][0], 4)
    vec[:, V_BA:V_BA + 4] = fm(inp["b_lru_a"][0], 4)
    vec[:, V_BX:V_BX + 4] = fm(inp["b_lru_x"][0], 4)
    vec[:, V_LAM:V_LAM + 4] = fm(inp["lru_lambda"][0], 4)
    for tap in range(3):
        vec[:, V_FW + tap * 24:V_FW + tap * 24 + 24] = fm(inp["w_ffn_conv"][0, tap], 24)
    vec[:, V_FB:V_FB + 24] = fm(inp["b_ffn_conv"][0], 24)
    vec[:, V_SINK:V_SINK + 8] = np.asarray(inp["attn_sinks"][0], f32)[heads][None, :]
    vec[:, V_EPS] = 1e-6
    vec[:, V_ONE] = 1.0

    slopes = 2.0 ** (-np.arange(1, 9, dtype=np.float64))
    bown = np.full((128, 8, 256), NEG, f32)
    for s_i, h in enumerate(heads):
        for t in range(4):
            for j in range(t + 1):
                bown[t, s_i, 128 + j] = -8.0 * slopes[h] * (t - j)
    gfin = np.ascontiguousarray(np.broadcast_to(np.asarray(inp["g_final"], f32)[None, :], (128, D)))
    ident = np.eye(128, dtype=f32)
    bias = _bias_tables()
    common = dict(
        gfin=gfin, ident=ident, ident32=ident, bias_own=bown.reshape(128, -1),
        bias=bias.reshape(128, -1), w_in=w_in_p, w_out=w_out_p,
        w_q=np.ascontiguousarray(inp["w_mem_q"][0], f32), w_k=np.ascontiguousarray(inp["w_mem_k"][0], f32),
        w_v=np.ascontiguousarray(inp["w_mem_v"][0], f32), w_o=np.ascontiguousarray(inp["w_mem_o"][0], f32),
        wa_bd=bd(np.asarray(inp["w_lru_a"][0], f32)), wx_bd=bd(np.asarray(inp["w_lru_x"][0], f32)),
        w_g=np.ascontiguousarray(inp["w_ffn_gate"][0], f32), w_u=np.ascontiguousarray(inp["w_ffn_up"][0], f32),
        w_d=np.ascontiguousarray(inp["w_ffn_down"][0], f32),
    )
    in_maps = []
    for c in range(NCORES):
        b, q = c // 4, c % 4
        pre = np.zeros((PRE_T * 128, D), f32)
        if q > 0:
            pre[(3 - q) * 2048:] = xp[b, :q * 2048]
        main = xp[b, q * 2048:(q + 1) * 2048]
        smp = np.zeros((128, D), f32)
        smp[:64] = xsmp[c * 16:(c + 1) * 16].reshape(64, D)
        v = vec.copy()
        for k in range(3):
            v[:, V_FLAG + k] = 1.0 if k >= 3 - q else 0.0
        v[:, V_FLAG + 3] = 1.0 if q > 0 else 0.0
        b0 = bias[:, :, 0:128].copy()
        if q == 0:
            b0[:] = NEG
        m = dict(common)
        m.update(xs=np.ascontiguousarray(np.concatenate([pre, main, smp], axis=0)),
                 memx=np.ascontiguousarray(inp["mem_prompt"][b], f32), vec=v,
                 c_swa_k=np.ascontiguousarray(inp["cache_swa_k"][0, c * 16:(c + 1) * 16], f32).reshape(16, 128, 128),
                 c_swa_v=np.ascontiguousarray(inp["cache_swa_v"][0, c * 16:(c + 1) * 16], f32).reshape(16, 128, 128),
                 c_mem_k=np.ascontiguousarray(inp["cache_mem_k"][0, c * 16:(c + 1) * 16], f32).reshape(16, 256, 512),
                 c_mem_v=np.ascontiguousarray(inp["cache_mem_v"][0, c * 16:(c + 1) * 16], f32).reshape(16, 256, 512),
                 st_lconv=np.ascontiguousarray(inp["state_lru_conv"][0, c * 16:(c + 1) * 16], f32).reshape(48, 512),
                 st_lh=np.ascontiguousarray(inp["state_lru_h"][0, c * 16:(c + 1) * 16], f32),
                 st_fconv=np.ascontiguousarray(inp["state_ffn_conv"][0, c * 16:(c + 1) * 16], f32).reshape(32, 3072),
                 bias0=np.ascontiguousarray(b0.reshape(128, -1)))
        in_maps.append(m)

    if "nc" not in _NC_CACHE:
        _NC_CACHE["nc"] = build_nc()
    nc = _NC_CACHE["nc"]
    res = run_bass_kernel_spmd(nc, in_maps, core_ids=list(range(NCORES)))
    R = res.results

    y_prompt = np.zeros((2, 8192, D), f32)
    y_sample = np.zeros((128, 4, D), f32)
    for c in range(NCORES):
        b, q = c // 4, c % 4
        y_prompt[b, q * 2048:(q + 1) * 2048] = R[c]["y"][:2048]
        y_sample[c * 16:(c + 1) * 16] = R[c]["y"][2048:2048 + 64].reshape(16, 4, D)
    p_swa_k = np.stack([R[3]["o_kv"][:, :128], R[7]["o_kv"][:, :128]]).reshape(1, 2, 128, 2, 64)
    p_swa_v = np.stack([R[3]["o_kv"][:, 128:], R[7]["o_kv"][:, 128:]]).reshape(1, 2, 128, 2, 64)
    p_mem_k = np.stack([R[0]["o_mk"], R[4]["o_mk"]]).reshape(1, 2, 256, 4, 128)
    p_mem_v = np.stack([R[0]["o_mv"], R[4]["o_mv"]]).reshape(1, 2, 256, 4, 128)
    p_lru_conv = np.stack([R[3]["o_lconv"], R[7]["o_lconv"]]).reshape(1, 2, 3, 512)
    p_lru_h = np.stack([R[3]["o_lh"], R[7]["o_lh"]]).reshape(1, 2, 512)
    p_ffn_conv = np.stack([R[3]["o_fconv"], R[7]["o_fconv"]]).reshape(1, 2, 2, 3072)
    cat = lambda k: np.concatenate([R[c][k] for c in range(NCORES)], axis=0)
    s_swa_k = cat("o_sk").reshape(1, 128, 128, 2, 64)
    s_swa_v = cat("o_sv").reshape(1, 128, 128, 2, 64)
    s_lru_conv = cat("o_slc").reshape(1, 128, 3, 512)
    s_lru_h = cat("o_slh").reshape(1, 128, 512)
    s_ffn_conv = cat("o_sfc").reshape(1, 128, 2, 3072)
    return (y_prompt, y_sample, p_swa_k, p_swa_v, p_mem_k, p_mem_v, p_lru_conv, p_lru_h, p_ffn_conv,
            s_swa_k, s_swa_v, s_lru_conv, s_lru_h, s_ffn_conv)
```

```python
import contextlib
import numpy as np
import concourse.bass as bass
import concourse.mybir as mybir
from concourse.bass_utils import run_bass_kernel_spmd

F32 = mybir.dt.float32
BF16 = mybir.dt.bfloat16
ACT = mybir.ActivationFunctionType
ALU = mybir.AluOpType
AX = mybir.AxisListType

NCORES = 8
D = 1024
PRE_T = 48
MAIN_T = 16
NEG = -240000.0
SC_MEM = 128.0 ** -0.5

V_GMIX, V_GCROSS, V_GFFN, V_GMEM = 0, 8, 16, 24
V_CW = 32
V_BCONV = 48
V_BA = 52
V_BX = 56
V_LAM = 60
V_FW = 64
V_FB = 136
V_SINK = 160
V_FLAG = 168
V_EPS = 172
V_ONE = 173
NV = 176


class Sched:
    def __init__(self, nc, es, dma_pool=None):
        self.nc = nc
        self.ops = []
        self.last_w = {}
        self.readers = {}
        self.dma_pool = dma_pool or {"sync": 8, "gpsimd": 2, "scalar": 4}
        self.sems = {}
        for e in ("scalar", "vector", "gpsimd", "tensor"):
            self.sems[e] = es.enter_context(nc.semaphore("c_" + e))
        self.dsems = {}
        for q, n in self.dma_pool.items():
            self.dsems[q] = [es.enter_context(nc.semaphore("d_%s%d" % (q, i))) for i in range(n)]
        self.dcount = {q: 0 for q in self.dma_pool}

    def add(self, eng, fn, r=(), w=(), dma=False):
        i = len(self.ops)
        deps = {}
        for k in r:
            a = self.last_w.get(k)
            if a is not None:
                deps[a] = "raw"
        for k in w:
            a = self.last_w.get(k)
            if a is not None:
                deps[a] = "raw"
            for a in self.readers.get(k, ()):
                deps.setdefault(a, "war")
        op = dict(eng=eng, fn=fn, deps=deps, dma=dma, signal=False)
        if dma:
            j = self.dcount[eng]
            self.dcount[eng] += 1
            n = self.dma_pool[eng]
            op["dsem"] = (eng, j % n)
            op["dval"] = 16 * (j // n + 1)
        self.ops.append(op)
        for k in w:
            self.last_w[k] = i
            self.readers[k] = []
        for k in r:
            lst = self.readers.setdefault(k, [])
            if not dma:
                lst[:] = [a for a in lst if self.ops[a]["dma"] or self.ops[a]["eng"] != eng]
            lst.append(i)
        return i

    def act(self, fn, r=(), w=()):
        return self.add("scalar", fn, r, w)

    def dve(self, fn, r=(), w=()):
        return self.add("vector", fn, r, w)

    def pool(self, fn, r=(), w=()):
        return self.add("gpsimd", fn, r, w)

    def pe(self, fn, r=(), w=()):
        return self.add("tensor", fn, r, w)

    def dma(self, q, out, in_, r=(), w=(), **kw):
        return self.add(q, lambda e: e.dma_start(out=out, in_=in_, **kw), r, w, dma=True)

    def emit(self):
        ops = self.ops
        for b in ops:
            for a, kind in b["deps"].items():
                A = ops[a]
                if A["dma"]:
                    continue
                if A["eng"] == b["eng"] and not b["dma"]:
                    if A["eng"] == "tensor":
                        continue
                A["signal"] = True
        cnt = {e: 0 for e in self.sems}
        for o in ops:
            if not o["dma"] and o["signal"]:
                cnt[o["eng"]] += 1
                o["sval"] = cnt[o["eng"]]
        seen = {}
        last_on_dsem = {}
        for o in ops:
            e = o["eng"]
            sn = seen.setdefault(e, {})
            need = {}
            for a, kind in o["deps"].items():
                A = ops[a]
                if A["dma"]:
                    key, val = ("d",) + A["dsem"], A["dval"]
                else:
                    if A["eng"] == e and not o["dma"]:
                        if e == "tensor":
                            continue
                    key, val = ("c", A["eng"]), A["sval"]
                if need.get(key, 0) < val:
                    need[key] = val
            if o["dma"]:
                key = ("d",) + o["dsem"]
                prev = o["dval"] - 16
                if prev > 0 and need.get(key, 0) < prev:
                    need[key] = prev
            waits = []
            for key, val in need.items():
                if sn.get(key, 0) < val:
                    sn[key] = val
                    waits.append((key, val))
            o["waits"] = waits
        final = {}
        for o in ops:
            if o["dma"]:
                final[("d",) + o["dsem"]] = o["dval"]
        engs = ["sync", "scalar", "vector", "gpsimd", "tensor"]
        per = {e: [o for o in ops if o["eng"] == e] for e in engs}

        def semof(key):
            if key[0] == "c":
                return self.sems[key[1]]
            return self.dsems[key[1]][key[2]]

        def run(e, lst, tail):
            for o in lst:
                for key, val in o["waits"]:
                    e.wait_ge(semof(key), val)
                ins = o["fn"](e)
                if o["dma"]:
                    ins.then_inc(semof(("d",) + o["dsem"]), 16)
                elif o["signal"]:
                    ins.then_inc(self.sems[o["eng"]], 1)
            if tail:
                for key, val in final.items():
                    e.wait_ge(semof(key), val)

        with self.nc.Block() as block:
            block.sync(lambda e: run(e, per["sync"], True))
            block.scalar(lambda e: run(e, per["scalar"], False))
            block.vector(lambda e: run(e, per["vector"], False))
            block.gpsimd(lambda e: run(e, per["gpsimd"], False))
            block.tensor(lambda e: run(e, per["tensor"], False))


def build_nc(do_sample=True, n_pre=PRE_T, n_main=MAIN_T, max_ops=None):
    nc = bass.Bass("TRN2", target_bir_lowering=False)
    NT = n_pre + n_main + 1

    def din(name, shape):
        return nc.dram_tensor(name, list(shape), F32, kind="ExternalInput").ap()

    def dout(name, shape):
        return nc.dram_tensor(name, list(shape), F32, kind="ExternalOutput").ap()

    xs = din("xs", [NT * 128, D])
    memx = din("memx", [256, D])
    vec_d = din("vec", [128, NV])
    gfin_d = din("gfin", [128, D])
    ident_d = din("ident", [128, 128])
    bias_d = din("bias", [128, 8 * 256])
    bias0_d = din("bias0", [128, 8 * 128])
    w_in_d = din("w_in", [D, 1792])
    w_out_d = din("w_out", [D, D])
    w_q_d = din("w_q", [D, 512])
    w_k_d = din("w_k", [D, 512])
    w_v_d = din("w_v", [D, 512])
    w_o_d = din("w_o", [512, D])
    wa_d = din("wa_bd", [128, 512])
    wx_d = din("wx_bd", [128, 512])
    w_g_d = din("w_g", [D, 3072])
    w_u_d = din("w_u", [D, 3072])
    w_d_d = din("w_d", [3072, D])

    ident32_d = din("ident32", [128, 128])
    bown_d = din("bias_own", [128, 8 * 256])
    cswk_d = din("c_swa_k", [16, 128, 128])
    cswv_d = din("c_swa_v", [16, 128, 128])
    cmk_d = din("c_mem_k", [16, 256, 512])
    cmv_d = din("c_mem_v", [16, 256, 512])
    slc_d = din("st_lconv", [48, 512])
    slh_d = din("st_lh", [16, 512])
    sfc_d = din("st_fconv", [32, 3072])
    osk_d = dout("o_sk", [16, 128, 128])
    osv_d = dout("o_sv", [16, 128, 128])
    oslc_d = dout("o_slc", [48, 512])
    oslh_d = dout("o_slh", [16, 512])
    osfc_d = dout("o_sfc", [32, 3072])
    y_d = dout("y", [(n_main + 1) * 128, D])
    okv_d = dout("o_kv", [128, 256])
    omk_d = dout("o_mk", [256, 512])
    omv_d = dout("o_mv", [256, 512])
    olc_d = dout("o_lconv", [3, 512])
    olh_d = dout("o_lh", [512])
    ofc_d = dout("o_fconv", [2, 3072])

    es = contextlib.ExitStack()
    with es:
        def sb(name, shape, dt=F32):
            return es.enter_context(nc.sbuf_tensor("s_" + name, list(shape), dt))

        S = Sched(nc, es)
        psall = es.enter_context(nc.psum_tensor("psall", [128, 8 * 512], F32))
        ps_ctr = [0]

        def psbank(n=1):
            b0 = ps_ctr[0]
            if b0 + n > 8:
                b0 = 0
            ps_ctr[0] = (b0 + n) % 8
            return b0, ["ps%d" % (b0 + i) for i in range(n)]

        def psf(b0, n=1):
            return psall[:, b0 * 512:(b0 + n) * 512]

        psall_bf = psall.bitcast(BF16)

        def psb(b0, n=1):
            return psall_bf[:, b0 * 1024:(b0 + n) * 1024]

        vec = sb("vec", [128, NV])
        gfin = sb("gfin", [128, D])
        ident = sb("ident", [128, 128], BF16)
        biasT = sb("biasT", [128, 8, 256], BF16)
        bias0 = sb("bias0", [128, 8, 128], BF16)
        w_in = sb("w_in", [128, 8, 1792], BF16)
        w_out = sb("w_out", [128, 8, 1024], BF16)
        w_q = sb("w_q", [128, 8, 512], BF16)
        w_o = sb("w_o", [128, 4, 1024], BF16)
        wa = sb("wa", [128, 4, 128], BF16)
        wx = sb("wx", [128, 4, 128], BF16)
        NRING = 4
        ring = [sb("ring%d" % i, [128, 4096], BF16) for i in range(NRING)]
        ring_ctr = [0]
        mkT = sb("mkT", [128, 4, 256], BF16)
        mvb = sb("mvb", [128, 2, 512], BF16)
        cl = sb("cl", [128, 4])
        sink8 = sb("sink8", [128, 8])
        hc = sb("hc", [128, 4])
        NXB = 5
        xt = [sb("xt%d" % i, [128, D]) for i in range(NXB)]
        nbf = sb("nbf", [128, D], BF16)
        junk = nbf
        nT = [sb("nT%d" % i, [128, 8, 128], BF16) for i in range(2)]
        n3T = sb("n3T", [128, 8, 512], BF16)
        st = sb("stat", [128, 64])
        xrp = [sb("xrp%d" % i, [128, 4, 131]) for i in range(2)]
        xc = sb("xc", [128, 4, 128])
        xcb = sb("xcb", [128, 4, 128], BF16)
        rg = sb("rg", [128, 4, 128])
        ig = sb("ig", [128, 4, 128])
        av = sb("av", [128, 4, 128])
        bv = sb("bv", [128, 4, 128])
        hv = sb("hv", [128, 4, 128])
        yT = sb("yT", [128, 8, 128], BF16)
        QT = sb("QT", [128, 4, 128], BF16)
        KT = [sb("KT%d" % i, [128, 128], BF16) for i in range(2)]
        Vp = [sb("Vp%d" % i, [128, 2, 128], BF16) for i in range(2)]
        kvf = sb("kvf", [128, 256])
        Pm = sb("Pm", [128, 8, 256], BF16)
        PT = sb("PT", [128, 16, 128], BF16)
        QC = sb("QC", [128, 4, 128], BF16)
        OC = sb("OC", [128, 4, 128], BF16)
        Gs = [sb("Gs%d" % i, [128, 514]) for i in range(2)]
        t1 = [sb("t1_%d" % i, [128, 512]) for i in range(2)]
        mtmp = t1[0]
        gg = rg
        hm = sb("hm", [128, 12, 512], BF16)
        Gh = sb("Gh", [128, 24, 2])

        ident32 = sb("ident32", [128, 128])
        XP = sb("XP", [128, 4, 16, 7])
        XC2 = sb("XC2", [128, 4, 48])
        HS = sb("HS", [128, 4, 16])
        HS2 = sb("HS2", [128, 4, 16])
        stL = sb("stL", [128, 512])
        stH = sb("stH", [128, 512])

        HALO = sb("HALO", [128, 4, 3])
        fence = sb("fence", [128, 2])

        def vcol(c, n=1):
            return vec[:, c:c + n]

        S.dma("sync", vec[:], vec_d, w=["vec"])
        S.dma("sync", gfin[:], gfin_d, w=["gfin"])
        S.dma("sync", ident32[:], ident32_d, w=["ident32"])
        S.dma("gpsimd", ident[:], ident_d, w=["ident"])
        S.dma("gpsimd", biasT[:].rearrange("p a b -> p (a b)"), bias_d, w=["biasT"])
        S.dma("gpsimd", bias0[:].rearrange("p a b -> p (a b)"), bias0_d, w=["bias0"])
        S.dma("gpsimd", wa[:].rearrange("p a b -> p (a b)"), wa_d, w=["wa"])
        S.dma("gpsimd", wx[:].rearrange("p a b -> p (a b)"), wx_d, w=["wx"])
        S.dma("gpsimd", w_in[:], w_in_d.rearrange("(k p) n -> p k n", p=128), w=["w_in"])
        wk_s = ring[0][:].rearrange("p (k n) -> p k n", k=8)
        wv_s = ring[1][:].rearrange("p (k n) -> p k n", k=8)
        S.dma("gpsimd", wk_s, w_k_d.rearrange("(k p) n -> p k n", p=128), w=["ring0"])
        S.dma("gpsimd", wv_s, w_v_d.rearrange("(k p) n -> p k n", p=128), w=["ring1"])
        S.dma("gpsimd", w_q[:], w_q_d.rearrange("(k p) n -> p k n", p=128), w=["w_q"])
        S.dma("gpsimd", w_o[:], w_o_d.rearrange("(k p) n -> p k n", p=128), w=["w_o"])

        S.pool(lambda e: e.memset(hc[:], 0.0), w=["hc"])
        S.pool(lambda e: e.memset(xrp[1][:], 0.0), w=["xrp1"])
        S.pool(lambda e: e.memset(xrp[0][:], 0.0), w=["xrp0"])
        for i in range(2):
            S.pool(lambda e, i=i: e.memset(Vp[i][:], 0.0), w=["Vp%d" % i])
            S.pool(lambda e, i=i: e.memset(KT[i][:], 0.0), w=["KT%d" % i])
        S.pool(lambda e: e.memset(Gh[:], 0.0), w=["Gh"])

        S.act(lambda e: e.activation(out=st[:, 0:4], in_=vcol(V_LAM, 4), func=ACT.Exp, scale=-1.0),
              r=["vec"], w=["st_a"])
        S.act(lambda e: e.activation(out=st[:, 4:8], in_=st[:, 0:4], func=ACT.Ln, bias=vcol(V_ONE), scale=1.0),
              r=["st_a", "vec"], w=["st_b"])
        S.dve(lambda e: e.tensor_scalar(out=cl[:], in0=st[:, 4:8], scalar1=-8.0, scalar2=None, op0=ALU.mult),
              r=["st_b"], w=["cl"])
        S.dve(lambda e: e.tensor_scalar(out=sink8[:], in0=vcol(V_SINK, 8), scalar1=8.0, scalar2=None, op0=ALU.mult),
              r=["vec"], w=["sink8"])

        def rmsnorm_T(xap, xkey, gcol, dst, dstkey, tagn):
            S.act(lambda e: e.activation(out=junk[:], in_=xap, func=ACT.Square, accum_out=st[:, 8:9]),
                  r=[xkey], w=["nbf", "st_ss"])
            S.act(lambda e: e.activation(out=st[:, 9:10], in_=st[:, 8:9], func=ACT.Sqrt,
                                         bias=vcol(V_EPS), scale=1.0 / D),
                  r=["st_ss", "vec"], w=["st_sd"])
            S.dve(lambda e: e.reciprocal(out=st[:, 10:11], in_=st[:, 9:10]), r=["st_sd"], w=["st_rs"])
            S.dve(lambda e: e.tensor_scalar(out=nbf[:], in0=xap, scalar1=st[:, 10:11], scalar2=None, op0=ALU.mult),
                  r=[xkey, "st_rs"], w=["nbf"])
            b0, pk = psbank(1)
            pv = psb(b0).rearrange("p (k n) -> p k n", k=8)
            for k in range(8):
                S.pe(lambda e, k=k: e.transpose(out=pv[:, k, :], in_=nbf[:, k * 128:(k + 1) * 128], identity=ident[:]),
                     r=["nbf", "ident"], w=pk)
            gb = vec[:, gcol:gcol + 8].unsqueeze(2).to_broadcast([128, 8, 128])
            S.dve(lambda e: e.tensor_tensor(out=dst, in0=pv, in1=gb, op=ALU.mult),
                  r=pk + ["vec"], w=[dstkey])

        def fm_proj(wsb, wkey, col0, nchunks, src, srckey, n=128, width=128):
            b0, pk = psbank(1)
            pv = psf(b0).rearrange("p (c n) -> p c n", c=512 // n)
            for c in range(nchunks):
                for k in range(8):
                    S.pe(lambda e, c=c, k=k: e.matmul(pv[:, c, :], lhsT=wsb[:, k, col0 + c * width:col0 + (c + 1) * width],
                                                       rhs=src[:, k, :], start=(k == 0), stop=(k == 7)),
                         r=[wkey, srckey], w=pk)
            return pv, pk

        def lru_stage(ti, nTt, nTkey, smp=False):
            cur, prv = xrp[ti % 2], xrp[(ti + 1) % 2]
            ck, pk_ = "xrp%d" % (ti % 2), "xrp%d" % ((ti + 1) % 2)
            if not smp:
                S.pool(lambda e: e.tensor_copy(out=cur[:, :, 0:3], in_=prv[:, :, 128:131]), r=[pk_], w=[ck])
            pxr, kxr = fm_proj(w_in, "w_in", 0, 4, nTt, nTkey)
            if not smp:
                S.act(lambda e: e.activation(out=cur[:, :, 3:131], in_=pxr, func=ACT.Copy), r=kxr, w=[ck])
            else:
                for c in range(4):
                    S.act(lambda e, c=c: e.activation(out=XP[:, c, :, 3:7],
                                                      in_=pxr[:, c, 0:64].rearrange("p (s t) -> p s t", t=4),
                                                      func=ACT.Copy), r=kxr, w=["XP"])
            for c in range(4):
                if not smp:
                    o_ap = xc[:, c, :]
                    in_tap = lambda tap, c=c: cur[:, c, tap:tap + 128]
                    srck = ck
                else:
                    o_ap = xc[:, c, 0:64].rearrange("p (s t) -> p s t", t=4)
                    in_tap = lambda tap, c=c: XP[:, c, :, tap:tap + 4]
                    srck = "XP"
                S.dve(lambda e, c=c, o_ap=o_ap, in_tap=in_tap: e.tensor_scalar(
                    out=o_ap, in0=in_tap(3), scalar1=vcol(V_CW + 12 + c), scalar2=vcol(V_BCONV + c),
                    op0=ALU.mult, op1=ALU.add), r=[srck, "vec"], w=["xc%d" % c])
                for tap in range(3):
                    S.dve(lambda e, c=c, tap=tap, o_ap=o_ap, in_tap=in_tap: e.scalar_tensor_tensor(
                        out=o_ap, in0=in_tap(tap), scalar=vcol(V_CW + tap * 4 + c),
                        in1=o_ap, op0=ALU.mult, op1=ALU.add),
                        r=[srck, "vec", "xc%d" % c], w=["xc%d" % c])
            xck = ["xc%d" % c for c in range(4)]
            S.act(lambda e: e.activation(out=xcb[:], in_=xc[:], func=ACT.Copy), r=xck, w=["xcb"])
            b0, pk = psbank(2)
            pr = psf(b0).rearrange("p (c n) -> p c n", c=4)
            pi = psf(b0 + 1).rearrange("p (c n) -> p c n", c=4)
            for c in range(4):
                S.pe(lambda e, c=c: e.matmul(pr[:, c, :], lhsT=wa[:, c, :], rhs=xcb[:, c, :], start=True, stop=True),
                     r=["wa", "xcb"], w=[pk[0]])
            for c in range(4):
                S.pe(lambda e, c=c: e.matmul(pi[:, c, :], lhsT=wx[:, c, :], rhs=xcb[:, c, :], start=True, stop=True),
                     r=["wx", "xcb"], w=[pk[1]])
            for c in range(4):
                S.act(lambda e, c=c: e.activation(out=rg[:, c, :], in_=pr[:, c, :], func=ACT.Sigmoid,
                                                  bias=vcol(V_BA + c), scale=1.0),
                      r=[pk[0], "vec"], w=["rg"])
            for c in range(4):
                S.act(lambda e, c=c: e.activation(out=ig[:, c, :], in_=pi[:, c, :], func=ACT.Sigmoid,
                                                  bias=vcol(V_BX + c), scale=1.0),
                      r=[pk[1], "vec"], w=["ig"])
            for c in range(4):
                S.act(lambda e, c=c: e.activation(out=av[:, c, :], in_=rg[:, c, :], func=ACT.Exp, scale=cl[:, c:c + 1]),
                      r=["rg", "cl"], w=["av"])
            S.pool(lambda e: e.tensor_tensor(out=bv[:], in0=av[:], in1=av[:], op=ALU.mult), r=["av"], w=["bv"])
            S.pool(lambda e: e.tensor_scalar(out=bv[:], in0=bv[:], scalar1=-1.0, scalar2=1.0, op0=ALU.mult, op1=ALU.add),
                   r=["bv"], w=["bv"])
            S.act(lambda e: e.activation(out=bv[:], in_=bv[:], func=ACT.Sqrt), r=["bv"], w=["bv"])
            S.pool(lambda e: e.tensor_tensor(out=ig[:], in0=ig[:], in1=xc[:], op=ALU.mult), r=["ig"] + xck, w=["ig"])
            S.dve(lambda e: e.tensor_tensor(out=bv[:], in0=bv[:], in1=ig[:], op=ALU.mult), r=["bv", "ig"], w=["bv"])
            if smp:
                v4 = lambda t_, tt: t_[:, :, 0:64].rearrange("p c (s t) -> p c s t", t=4)[:, :, :, tt]
                for tt in range(4):
                    hprev = HS[:] if tt == 0 else v4(hv, tt - 1)
                    S.dve(lambda e, tt=tt, hprev=hprev: e.tensor_tensor(out=v4(hv, tt), in0=v4(av, tt), in1=hprev,
                                                                        op=ALU.mult), r=["av", "hv", "HS"], w=["hv"])
                    S.dve(lambda e, tt=tt: e.tensor_tensor(out=v4(hv, tt), in0=v4(hv, tt), in1=v4(bv, tt),
                                                           op=ALU.add), r=["bv", "hv"], w=["hv"])
                S.dve(lambda e: e.tensor_copy(out=HS2[:], in_=v4(hv, 3)), r=["hv"], w=["HS2"])
                return
            for c in range(4):
                S.dve(lambda e, c=c: e.tensor_tensor_scan(out=hv[:, c, :], data0=av[:, c, :], data1=bv[:, c, :],
                                                           initial=hc[:, c:c + 1], op0=ALU.mult, op1=ALU.add),
                      r=["av", "bv", "hc"], w=["hv"])
            S.dve(lambda e: e.tensor_copy(out=hc[:], in_=hv[:, :, 127]), r=["hv"], w=["hc"])

        def attn_core(nh, Sps, Skeys, scale, sinkcol, Pbuf, Pkey, nkb, rows=128):
            W = nkb * 128
            S.dve(lambda e: e.reduce_max(out=st[:rows, 16:16 + nh], in_=Sps[:rows], axis=AX.X), r=Skeys, w=["st_mx"])
            if sinkcol is not None:
                S.dve(lambda e: e.tensor_tensor(out=st[:rows, 16:16 + nh], in0=st[:rows, 16:16 + nh],
                                                in1=sink8[:rows, :], op=ALU.max),
                      r=["st_mx", "sink8"], w=["st_mx"])
            S.dve(lambda e: e.tensor_scalar(out=st[:rows, 24:24 + nh], in0=st[:rows, 16:16 + nh], scalar1=-scale,
                                            scalar2=None, op0=ALU.mult),
                  r=["st_mx"], w=["st_nm"])
            for h in range(nh):
                S.act(lambda e, h=h: e.activation(out=Pbuf[:rows, h, 0:W], in_=Sps[:rows, h, :], func=ACT.Exp,
                                                  bias=st[:rows, 24 + h:25 + h], scale=scale,
                                                  accum_out=st[:rows, 32 + h:33 + h]),
                      r=Skeys + ["st_nm"], w=[Pkey, "st_rs%d" % h])
            rsk = ["st_rs%d" % h for h in range(nh)]
            if sinkcol is not None:
                S.dve(lambda e: e.tensor_tensor(out=st[:rows, 40:48], in0=st[:rows, 24:32],
                                                in1=vec[:rows, sinkcol:sinkcol + 8], op=ALU.add),
                      r=["st_nm", "vec"], w=["st_es"])
                S.act(lambda e: e.activation(out=st[:rows, 40:48], in_=st[:rows, 40:48], func=ACT.Exp),
                      r=["st_es"], w=["st_es"])
                S.dve(lambda e: e.tensor_tensor(out=st[:rows, 32:40], in0=st[:rows, 32:40], in1=st[:rows, 40:48],
                                                op=ALU.add),
                      r=["st_es"] + rsk, w=rsk)
            S.dve(lambda e: e.reciprocal(out=st[:rows, 48:48 + nh], in_=st[:rows, 32:32 + nh]), r=rsk, w=["st_ri"])
            for h in range(nh):
                S.dve(lambda e, h=h: e.tensor_scalar(out=Pbuf[:rows, h, 0:W], in0=Pbuf[:rows, h, 0:W],
                                                     scalar1=st[:rows, 48 + h:49 + h], scalar2=None, op0=ALU.mult),
                      r=[Pkey, "st_ri"], w=[Pkey])
            nt = nh * nkb
            nb = (nt * 128 + 1023) // 1024
            b0, pk = psbank(nb)
            pv = psb(b0, nb).rearrange("p (t n) -> p t n", n=128)
            for h in range(nh):
                for kb in range(nkb):
                    t = h * nkb + kb
                    S.pe(lambda e, h=h, kb=kb, t=t: e.transpose(out=pv[:, t, 0:rows],
                                                                in_=Pbuf[:rows, h, kb * 128:(kb + 1) * 128],
                                                                identity=ident[:rows, :rows]),
                         r=[Pkey, "ident"], w=[pk[(t * 128) // 1024]])
            half = nt // 2
            if nb == 1:
                S.act(lambda e: e.activation(out=PT[:, 0:nt, 0:rows], in_=pv[:, 0:nt, 0:rows], func=ACT.Copy),
                      r=pk, w=["PTa", "PTb"])
            else:
                S.act(lambda e: e.activation(out=PT[:, 0:half, 0:rows], in_=pv[:, 0:half, 0:rows], func=ACT.Copy),
                      r=pk, w=["PTa"])
                S.dve(lambda e: e.tensor_copy(out=PT[:, half:nt, 0:rows], in_=pv[:, half:nt, 0:rows]),
                      r=pk, w=["PTb"])

        def swa_stage(ti, nTt, nTkey, first_block, bias_first, rows=128, qcols=None):
            cb, pb = ti % 2, (ti + 1) % 2
            pq, kq = fm_proj(w_in, "w_in", 1024, 4, nTt, nTkey)
            S.act(lambda e: e.activation(out=QT[:], in_=pq, func=ACT.Copy), r=kq, w=["QT"])
            b0, pk = psbank(1)
            pkk = psf(b0)[:, 0:128]
            pkv = psf(b0)[:, 128:384]
            for k in range(8):
                S.pe(lambda e, k=k: e.matmul(pkk, lhsT=w_in[:, k, 1536:1664], rhs=nTt[:, k, :],
                                             start=(k == 0), stop=(k == 7)), r=["w_in", nTkey], w=pk)
            for k in range(8):
                S.pe(lambda e, k=k: e.matmul(pkv, lhsT=nTt[:, k, :], rhs=w_in[:, k, 1536:1792],
                                             start=(k == 0), stop=(k == 7)), r=["w_in", nTkey], w=pk)
            S.act(lambda e: e.activation(out=KT[cb][:], in_=pkk, func=ACT.Copy), r=pk, w=["KT%d" % cb])
            S.act(lambda e: e.activation(out=Vp[cb][:, 0, 0:64], in_=pkv[:, 128:192], func=ACT.Copy), r=pk, w=["Vp%d" % cb])
            S.act(lambda e: e.activation(out=Vp[cb][:, 1, 64:128], in_=pkv[:, 192:256], func=ACT.Copy), r=pk, w=["Vp%d" % cb])
            S.act(lambda e: e.activation(out=kvf[:], in_=pkv, func=ACT.Copy), r=pk, w=["kvf"])
            return cb, pb

        def swa_attend(cb, pb, bias_first, smp=None):
            b0, sk = psbank(4)
            Sps = psf(b0, 4).rearrange("p (h n) -> p h n", h=8)
            if smp is None:
                rows, q0 = 128, 0
                xkeys = []
            else:
                rows, q0 = 4, 4 * smp["s"]
                xkeys = smp["keys"]
            idn = ident[0:rows, 0:rows]
            for j in range(4):
                for b in range(2):
                    s_ = 2 * j + b
                    key = [sk[s_ // 2]]
                    lo, hi = 64 * b, 64 * b + 64
                    if smp is None:
                        kprev = KT[pb][lo:hi, :]
                        bprev = bias0[:, s_, :] if bias_first else biasT[:, s_, 0:128]
                        bown = biasT[:, s_, 128:256]
                        kpk = "KT%d" % pb
                    else:
                        kprev = smp["KTc"][lo:hi, smp["s"], :]
                        bprev = biasT[0:4, s_, 0:128]
                        o0 = 128 - 4 * smp["s"]
                        bown = smp["Bown"][0:4, s_, o0:o0 + 128]
                        kpk = "KT%d" % cb
                    S.pe(lambda e, j=j, lo=lo, hi=hi, s_=s_, kprev=kprev: e.matmul(
                        Sps[0:rows, s_, 0:128], lhsT=QT[lo:hi, j, q0:q0 + rows], rhs=kprev, start=True, stop=False),
                        r=["QT", kpk] + xkeys, w=key)
                    S.pe(lambda e, s_=s_, bprev=bprev: e.matmul(Sps[0:rows, s_, 0:128], lhsT=idn, rhs=bprev,
                                                                start=False, stop=True),
                         r=["ident", "bias0", "biasT"], w=key)
                    S.pe(lambda e, j=j, lo=lo, hi=hi, s_=s_: e.matmul(
                        Sps[0:rows, s_, 128:256], lhsT=QT[lo:hi, j, q0:q0 + rows], rhs=KT[cb][lo:hi, :],
                        start=True, stop=False), r=["QT", "KT%d" % cb], w=key)
                    S.pe(lambda e, s_=s_, bown=bown: e.matmul(Sps[0:rows, s_, 128:256], lhsT=idn, rhs=bown,
                                                              start=False, stop=True),
                         r=["ident", "biasT"] + xkeys, w=key)
            attn_core(8, Sps, sk, 0.125, V_SINK, Pm, "Pm", 2, rows=rows)
            b0, ok = psbank(1)
            pO = psf(b0).rearrange("p (c n) -> p c n", c=4)
            for j in range(4):
                n = 0
                for b in range(2):
                    s_ = 2 * j + b
                    for kb in (0, 1):
                        if kb == 1:
                            vap, vk = Vp[cb][:, b, :], ["Vp%d" % cb]
                        elif smp is None:
                            vap, vk = Vp[pb][:, b, :], ["Vp%d" % pb]
                        else:
                            vap, vk = smp["Vc"][b][:, smp["s"], :], xkeys
                        S.pe(lambda e, j=j, s_=s_, kb=kb, vap=vap, n=n: e.matmul(
                            pO[:, j, 0:rows], lhsT=vap, rhs=PT[:, s_ * 2 + kb, 0:rows],
                            start=(n == 0), stop=(n == 3)),
                            r=vk + ["PTa", "PTb"], w=ok)
                        n += 1
            S.act(lambda e: e.activation(out=yT[:, 4:8, q0:q0 + rows], in_=pO[:, :, 0:rows], func=ACT.Copy),
                  r=ok, w=["yT_att"])

        def mixer_out(xti, xkey, nTt, nTkey):
            pg, kg = fm_proj(w_in, "w_in", 512, 4, nTt, nTkey)
            S.act(lambda e: e.activation(out=gg[:], in_=pg, func=ACT.Gelu_apprx_tanh), r=kg, w=["rg"])
            S.dve(lambda e: e.tensor_tensor(out=yT[:, 0:4, :], in0=hv[:], in1=gg[:], op=ALU.mult),
                  r=["hv", "rg"], w=["yT_lru"])
            b0, pk = psbank(2)
            po = psf(b0, 2)
            for hh in range(2):
                for k in range(8):
                    S.pe(lambda e, hh=hh, k=k: e.matmul(po[:, hh * 512:(hh + 1) * 512], lhsT=yT[:, k, :],
                                                        rhs=w_out[:, k, hh * 512:(hh + 1) * 512],
                                                        start=(k == 0), stop=(k == 7)),
                         r=["yT_lru", "yT_att", "w_out"], w=[pk[hh]])
            S.dve(lambda e: e.tensor_tensor(out=xti[:], in0=xti[:], in1=po, op=ALU.add), r=[xkey] + pk, w=[xkey])

        def cross_attend(smp=None):
            if smp is None:
                rows, q0, kT, vv, xkeys = 128, 0, mkT, mvb, ["mkT", "mvb"]
            else:
                rows, q0, kT, vv, xkeys = 4, 4 * smp["s"], smp["mkT"], smp["mv"], smp["keys"]
            b0, sk = psbank(2)
            Sps = psf(b0, 2).rearrange("p (h n) -> p h n", h=4)
            for h in range(4):
                S.pe(lambda e, h=h: e.matmul(Sps[0:rows, h, :], lhsT=QC[:, h, q0:q0 + rows], rhs=kT[:, h, :],
                                             start=True, stop=True),
                     r=["QC"] + xkeys, w=[sk[h // 2]])
            attn_core(4, Sps, sk, SC_MEM, None, Pm, "Pm", 2, rows=rows)
            b0, ok = psbank(1)
            pO = psf(b0).rearrange("p (c n) -> p c n", c=4)
            for h in range(4):
                for kb in range(2):
                    S.pe(lambda e, h=h, kb=kb: e.matmul(pO[:, h, 0:rows], lhsT=vv[:, kb, h * 128:(h + 1) * 128],
                                                        rhs=PT[:, h * 2 + kb, 0:rows], start=(kb == 0), stop=(kb == 1)),
                         r=xkeys + ["PTa", "PTb"], w=ok)
            S.act(lambda e: e.activation(out=OC[:, :, q0:q0 + rows], in_=pO[:, :, 0:rows], func=ACT.Copy),
                  r=ok, w=["OC"])

        def cross_stage(xti, xkey, nTt, nTkey, smp_iter=None):
            rmsnorm_T(xti[:], xkey, V_GCROSS, nTt[:], nTkey, "n2")
            pq, kq = fm_proj(w_q, "w_q", 0, 4, nTt, nTkey)
            S.act(lambda e: e.activation(out=QC[:], in_=pq, func=ACT.Copy), r=kq, w=["QC"])
            if smp_iter is None:
                cross_attend(None)
            else:
                smp_iter()
            b0, pk = psbank(2)
            po = psf(b0, 2)
            for hh in range(2):
                for k in range(4):
                    S.pe(lambda e, hh=hh, k=k: e.matmul(po[:, hh * 512:(hh + 1) * 512], lhsT=OC[:, k, :],
                                                        rhs=w_o[:, k, hh * 512:(hh + 1) * 512],
                                                        start=(k == 0), stop=(k == 3)),
                         r=["OC", "w_o"], w=[pk[hh]])
            S.dve(lambda e: e.tensor_tensor(out=xti[:], in0=xti[:], in1=po, op=ALU.add), r=[xkey] + pk, w=[xkey])

        v_k8 = lambda t: t[:].rearrange("p (k n) -> p k n", k=8)
        wq = []
        wissued = [0]

        def wview(i):
            return ring[i % NRING][:].rearrange("p (k n) -> p k n", k=wq[i][1])

        def wq_add_macro(gate_only=False):
            base = len(wq)
            if gate_only:
                for g in range(6):
                    wq.append((w_g_d[:, g * 512:(g + 1) * 512].rearrange("(k p) n -> p k n", p=128), 8))
                return base
            for fh in range(2):
                for gi in range(3):
                    g = 3 * fh + gi
                    wq.append((w_g_d[:, g * 512:(g + 1) * 512].rearrange("(k p) n -> p k n", p=128), 8))
                    wq.append((w_u_d[:, g * 512:(g + 1) * 512].rearrange("(k p) n -> p k n", p=128), 8))
                for gi in range(3):
                    g = 3 * fh + gi
                    wq.append((w_d_d[g * 512:(g + 1) * 512, :].rearrange("(k p) n -> p k n", p=128), 4))
            return base

        def wq_issue(upto):
            while wissued[0] < len(wq) and wissued[0] <= upto:
                j = wissued[0]
                S.dma("gpsimd", wview(j), wq[j][0], w=["ring%d" % (j % NRING)])
                wissued[0] += 1

        def wq_get(i):
            wq_issue(i)
            return wview(i), "ring%d" % (i % NRING)

        def wq_done(i):
            wq_issue(i + NRING)

        def load_x(ti, slot):
            S.dma("sync", xt[slot][:], xs[ti * 128:(ti + 1) * 128, :], w=["xt%d" % slot])

        for mt in range(2):
            S.dma("sync", xt[mt][:], memx[mt * 128:(mt + 1) * 128, :], w=["xt%d" % mt])
        for mt in range(2):
            rmsnorm_T(xt[mt][:], "xt%d" % mt, V_GMEM, nT[mt][:], "nT%d" % mt, "nm")
            for (wsl, wkey, od, is_k) in ((wk_s, "ring0", omk_d, True), (wv_s, "ring1", omv_d, False)):
                b0, pk = psbank(1)
                pm = psf(b0)
                for k in range(8):
                    S.pe(lambda e, k=k, wsl=wsl, pm=pm, mt=mt: e.matmul(pm, lhsT=nT[mt][:, k, :], rhs=wsl[:, k, :],
                                                                 start=(k == 0), stop=(k == 7)),
                         r=["nT%d" % mt, wkey], w=pk)
                S.act(lambda e, pm=pm: e.activation(out=mtmp[:], in_=pm, func=ACT.Copy), r=pk, w=["t1_0"])
                if not is_k:
                    S.dve(lambda e, mt=mt: e.tensor_copy(out=mvb[:, mt, :], in_=mtmp[:]), r=["t1_0"], w=["mvb"])
                S.dma("sync", od[mt * 128:(mt + 1) * 128, :], mtmp[:], r=["t1_0"])
            pkT, kk = fm_proj(wk_s, "ring0", 0, 4, nT[mt], "nT%d" % mt)
            S.act(lambda e, pkT=pkT, mt=mt: e.activation(out=mkT[:, :, mt * 128:(mt + 1) * 128], in_=pkT, func=ACT.Copy),
                  r=kk, w=["mkT"])

        def ffn_macro(tiles, rows_out, N, halo=Gh, halokey="Gh", smp=None):
            wb = wq_add_macro()
            for fh in range(2):
                for gi in range(3):
                    g = 3 * fh + gi
                    wg3, wgk = wq_get(wb + fh * 9 + 2 * gi)
                    wu3, wuk = wq_get(wb + fh * 9 + 2 * gi + 1)
                    for c in range(4):
                        fc = g * 4 + c
                        hmi = gi * 4 + c
                        i2 = fc % 2
                        b0, pk = psbank(2)
                        pG, pU = psf(b0)[:, 0:N], psf(b0 + 1)[:, 0:N]
                        for k in range(8):
                            S.pe(lambda e, k=k, c=c, wg3=wg3, pG=pG: e.matmul(
                                pG, lhsT=wg3[:, k, c * 128:(c + 1) * 128], rhs=n3T[:, k, 0:N],
                                start=(k == 0), stop=(k == 7)), r=[wgk, "n3T"], w=[pk[0]])
                        for k in range(8):
                            S.pe(lambda e, k=k, c=c, wu3=wu3, pU=pU: e.matmul(
                                pU, lhsT=wu3[:, k, c * 128:(c + 1) * 128], rhs=n3T[:, k, 0:N],
                                start=(k == 0), stop=(k == 7)), r=[wuk, "n3T"], w=[pk[1]])
                        Gsb, gk = Gs[i2], "Gs%d" % i2
                        tk = "t1_%d" % i2
                        if smp is None:
                            S.dve(lambda e, fc=fc, Gsb=Gsb: e.tensor_copy(out=Gsb[:, 0:2], in_=halo[:, fc, :]),
                                  r=[halokey + "%d" % fc, halokey], w=[gk])
                            S.act(lambda e, Gsb=Gsb, pG=pG: e.activation(out=Gsb[:, 2:2 + N], in_=pG, func=ACT.Copy),
                                  r=[pk[0]], w=[gk])
                            S.dve(lambda e, fc=fc, Gsb=Gsb: e.tensor_copy(out=halo[:, fc, :], in_=Gsb[:, N:N + 2]),
                                  r=[gk], w=[halokey + "%d" % fc])
                            t1v = t1[i2][:, 0:N]
                            g_tap = lambda tap, Gsb=Gsb: Gsb[:, tap:tap + N]
                            pGv = pG
                        else:
                            FS, FSn, fkeys = smp["FS"], smp["FSn"], smp["keys"]
                            G3 = Gsb[:, 0:96].rearrange("p (s t) -> p s t", t=6)
                            S.dve(lambda e, fc=fc, G3=G3, FS=FS: e.tensor_copy(out=G3[:, :, 0:2], in_=FS[:, fc, :, :]),
                                  r=[fkeys[0]], w=[gk])
                            S.act(lambda e, G3=G3, pG=pG: e.activation(
                                out=G3[:, :, 2:6], in_=pG[:, 0:64].rearrange("p (s t) -> p s t", t=4), func=ACT.Copy),
                                r=[pk[0]], w=[gk])
                            S.dve(lambda e, fc=fc, G3=G3, FSn=FSn: e.tensor_copy(out=FSn[:, fc, :, :], in_=G3[:, :, 4:6]),
                                  r=[gk], w=[fkeys[1]])
                            t1v = t1[i2][:, 0:64].rearrange("p (s t) -> p s t", t=4)
                            g_tap = lambda tap, G3=G3: G3[:, :, tap:tap + 4]
                            pGv = pG[:, 0:64].rearrange("p (s t) -> p s t", t=4)
                        S.act(lambda e, fc=fc, pGv=pGv, t1v=t1v: e.activation(
                            out=t1v, in_=pGv, func=ACT.Identity, bias=vcol(V_FB + fc),
                            scale=vcol(V_FW + 48 + fc)), r=[pk[0], "vec"], w=[tk])
                        for tap in (1, 0):
                            S.dve(lambda e, fc=fc, tap=tap, g_tap=g_tap, t1v=t1v: e.scalar_tensor_tensor(
                                out=t1v, in0=g_tap(tap), scalar=vcol(V_FW + tap * 24 + fc),
                                in1=t1v, op0=ALU.mult, op1=ALU.add),
                                r=[gk, "vec", tk], w=[tk])
                        S.act(lambda e, i2=i2: e.activation(out=t1[i2][:, 0:N], in_=t1[i2][:, 0:N],
                                                            func=ACT.Gelu_apprx_tanh), r=[tk], w=[tk])
                        S.dve(lambda e, hmi=hmi, i2=i2, pU=pU: e.tensor_tensor(out=hm[:, hmi, 0:N], in0=t1[i2][:, 0:N],
                                                                               in1=pU, op=ALU.mult),
                              r=[tk, pk[1]], w=["hm%d" % hmi])
                    wq_done(wb + fh * 9 + 2 * gi + 1)
                wds = [wq_get(wb + fh * 9 + 6 + q) for q in range(3)]
                for j, tj in enumerate(tiles):
                    sl = xslot(tj)
                    for hh in range(2):
                        b0, pk = psbank(1)
                        pD = psf(b0)
                        for q in range(3):
                            for c in range(4):
                                hmi = q * 4 + c
                                S.pe(lambda e, q=q, c=c, hmi=hmi, j=j, hh=hh, pD=pD, wds=wds: e.matmul(
                                    pD, lhsT=hm[:, hmi, j * 128:(j + 1) * 128],
                                    rhs=wds[q][0][:, c, hh * 512:(hh + 1) * 512],
                                    start=(hmi == 0), stop=(hmi == 11)),
                                    r=[wds[q][1], "hm%d" % hmi], w=pk)
                        S.dve(lambda e, sl=sl, hh=hh, pD=pD: e.tensor_tensor(
                            out=xt[sl][:, hh * 512:(hh + 1) * 512], in0=xt[sl][:, hh * 512:(hh + 1) * 512],
                            in1=pD, op=ALU.add), r=["xt%d" % sl] + pk, w=["xt%d" % sl])
                wq_done(wb + fh * 9 + 8)
            for j, tj in enumerate(tiles):
                sl = xslot(tj)
                xk2 = "xt%d" % sl
                S.act(lambda e, sl=sl: e.activation(out=junk[:], in_=xt[sl][:], func=ACT.Square, accum_out=st[:, 8:9]),
                      r=[xk2], w=["nbf", "st_ss"])
                S.act(lambda e: e.activation(out=st[:, 9:10], in_=st[:, 8:9], func=ACT.Sqrt, bias=vcol(V_EPS),
                                             scale=1.0 / D), r=["st_ss", "vec"], w=["st_sd"])
                S.dve(lambda e: e.reciprocal(out=st[:, 10:11], in_=st[:, 9:10]), r=["st_sd"], w=["st_rs"])
                S.dve(lambda e, sl=sl: e.scalar_tensor_tensor(out=xt[sl][:], in0=xt[sl][:], scalar=st[:, 10:11],
                                                              in1=gfin[:], op0=ALU.mult, op1=ALU.mult),
                      r=[xk2, "st_rs", "gfin"], w=[xk2])
                S.dma("sync", y_d[rows_out[j]:rows_out[j] + 128, :], xt[sl][:], r=[xk2])

        total = n_pre + n_main
        xslot = lambda ti: ti % NXB
        load_x(0, 0)
        import os
        n_fast = ((n_pre - 4) // 4) * 4 if (n_pre >= 8 and not os.environ.get("NOFAST")) else 0
        if n_fast:
            XCv = ring[0].bitcast(F32)[:, 0:2048].rearrange("p (c n) -> p c n", c=4)
            RGv = ring[1].bitcast(F32)[:, 0:2048].rearrange("p (c n) -> p c n", c=4)
            IGv = ring[2].bitcast(F32)[:, 0:2048].rearrange("p (c n) -> p c n", c=4)
            BVv = ring[3].bitcast(F32)[:, 0:2048].rearrange("p (c n) -> p c n", c=4)
            XBv = hm.bitcast(F32)[:].rearrange("p a b -> p (a b)")[:, 0:2060].rearrange("p (c n) -> p c n", c=4)
            XCBv = w_out[:].rearrange("p a b -> p (a b)")[:, 0:2048].rearrange("p (c n) -> p c n", c=4)
            basekeys = ["ring0", "ring1", "ring2", "ring3", "w_out"] + ["hm%d" % i for i in range(12)]
            ckeys = []
            for c in range(4):
                ckeys += ["fXC%d" % c, "fRG%d" % c, "fIG%d" % c, "fBV%d" % c, "fXB%d" % c, "fXCB%d" % c]
            S.pool(lambda e: e.memset(fence[:, 0:1], 0.0),
                   r=["w_in", "w_q", "w_o", "ident", "biasT", "bias0", "wa", "wx", "vec", "gfin", "mkT", "mvb"],
                   w=basekeys + ckeys)
            S.pool(lambda e: e.memset(HALO[:], 0.0), w=["HALO"])

            def fast_s1(c):
                b0, pk = psbank(1)
                pxr = psf(b0)
                for k in range(8):
                    S.pe(lambda e, c=c, k=k, pxr=pxr: e.matmul(pxr, lhsT=w_in[:, k, c * 128:(c + 1) * 128],
                                                               rhs=n3T[:, k, :], start=(k == 0), stop=(k == 7)),
                         r=["w_in", "n3T"], w=pk)
                S.dve(lambda e, c=c: e.tensor_copy(out=XBv[:, c, 0:3], in_=HALO[:, c, :]), r=["HALO"], w=["fXB%d" % c])
                S.act(lambda e, c=c, pxr=pxr: e.activation(out=XBv[:, c, 3:515], in_=pxr, func=ACT.Copy),
                      r=pk, w=["fXB%d" % c])
                S.dve(lambda e, c=c: e.tensor_copy(out=HALO[:, c, :], in_=XBv[:, c, 512:515]),
                      r=["fXB%d" % c], w=["HALO"])
                S.dve(lambda e, c=c: e.tensor_scalar(out=XCv[:, c, :], in0=XBv[:, c, 3:515],
                                                     scalar1=vcol(V_CW + 12 + c), scalar2=vcol(V_BCONV + c),
                                                     op0=ALU.mult, op1=ALU.add),
                      r=["fXB%d" % c, "vec"], w=["fXC%d" % c])
                for tap in range(3):
                    S.dve(lambda e, c=c, tap=tap: e.scalar_tensor_tensor(
                        out=XCv[:, c, :], in0=XBv[:, c, tap:tap + 512], scalar=vcol(V_CW + tap * 4 + c),
                        in1=XCv[:, c, :], op0=ALU.mult, op1=ALU.add),
                        r=["fXB%d" % c, "vec", "fXC%d" % c], w=["fXC%d" % c])
                S.act(lambda e, c=c: e.activation(out=XCBv[:, c, :], in_=XCv[:, c, :], func=ACT.Copy),
                      r=["fXC%d" % c], w=["fXCB%d" % c])

            def fast_s2(c):
                b0, pk = psbank(2)
                pr, pi = psf(b0), psf(b0 + 1)
                S.pe(lambda e, c=c, pr=pr: e.matmul(pr, lhsT=wa[:, c, :], rhs=XCBv[:, c, :], start=True, stop=True),
                     r=["wa", "fXCB%d" % c], w=[pk[0]])
                S.pe(lambda e, c=c, pi=pi: e.matmul(pi, lhsT=wx[:, c, :], rhs=XCBv[:, c, :], start=True, stop=True),
                     r=["wx", "fXCB%d" % c], w=[pk[1]])
                S.act(lambda e, c=c, pr=pr: e.activation(out=RGv[:, c, :], in_=pr, func=ACT.Sigmoid,
                                                         bias=vcol(V_BA + c), scale=1.0),
                      r=[pk[0], "vec"], w=["fRG%d" % c])
                S.act(lambda e, c=c, pi=pi: e.activation(out=IGv[:, c, :], in_=pi, func=ACT.Sigmoid,
                                                         bias=vcol(V_BX + c), scale=1.0),
                      r=[pk[1], "vec"], w=["fIG%d" % c])
                S.act(lambda e, c=c: e.activation(out=RGv[:, c, :], in_=RGv[:, c, :], func=ACT.Exp,
                                                  scale=cl[:, c:c + 1]), r=["fRG%d" % c, "cl"], w=["fRG%d" % c])
                S.act(lambda e, c=c: e.activation(out=BVv[:, c, :], in_=RGv[:, c, :], func=ACT.Square),
                      r=["fRG%d" % c], w=["fBV%d" % c])
                S.act(lambda e, c=c: e.activation(out=BVv[:, c, :], in_=BVv[:, c, :], func=ACT.Sqrt,
                                                  bias=vcol(V_ONE), scale=-1.0),
                      r=["fBV%d" % c, "vec"], w=["fBV%d" % c])
                S.dve(lambda e, c=c: e.tensor_tensor(out=IGv[:, c, :], in0=IGv[:, c, :], in1=XCv[:, c, :], op=ALU.mult),
                      r=["fIG%d" % c, "fXC%d" % c], w=["fIG%d" % c])
                S.dve(lambda e, c=c: e.tensor_tensor(out=BVv[:, c, :], in0=BVv[:, c, :], in1=IGv[:, c, :], op=ALU.mult),
                      r=["fBV%d" % c, "fIG%d" % c], w=["fBV%d" % c])
                S.dve(lambda e, c=c: e.tensor_tensor_scan(out=IGv[:, c, :], data0=RGv[:, c, :], data1=BVv[:, c, :],
                                                           initial=hc[:, c:c + 1], op0=ALU.mult, op1=ALU.add),
                      r=["fRG%d" % c, "fBV%d" % c, "hc"], w=["fIG%d" % c])
                S.dve(lambda e, c=c: e.tensor_copy(out=hc[:, c:c + 1], in_=IGv[:, c, 511:512]),
                      r=["fIG%d" % c], w=["hc"])

            for m0 in range(0, n_fast, 4):
                for j in range(4):
                    tj = m0 + j
                    if tj + 1 < total:
                        load_x(tj + 1, xslot(tj + 1))
                    rmsnorm_T(xt[xslot(tj)][:], "xt%d" % xslot(tj), V_GMIX, n3T[:, :, j * 128:(j + 1) * 128], "n3T", "n1")
                fast_s1(0)
                fast_s1(1)
                fast_s2(0)
                fast_s1(2)
                fast_s2(1)
                fast_s1(3)
                fast_s2(2)
                fast_s2(3)
                if (m0 + 4) % 16 == 0:
                    fcol = V_FLAG + (m0 + 4) // 16 - 1
                    S.dve(lambda e, fcol=fcol: e.tensor_scalar(out=hc[:], in0=hc[:], scalar1=vcol(fcol), scalar2=None,
                                                               op0=ALU.mult), r=["hc", "vec"], w=["hc"])
            S.pool(lambda e: e.tensor_copy(out=xrp[(n_fast + 1) % 2][:, :, 128:131], in_=HALO[:]),
                   r=["HALO"], w=["xrp%d" % ((n_fast + 1) % 2)])
            S.pool(lambda e: e.memset(fence[:, 1:2], 0.0), r=ckeys, w=basekeys)
        S.dma("gpsimd", w_out[:], w_out_d.rearrange("(k p) n -> p k n", p=128), w=["w_out"])
        for ti in range(n_fast, total):
            slot = xslot(ti)
            xk = "xt%d" % slot
            if ti + 1 < total:
                load_x(ti + 1, xslot(ti + 1))
            nTt = nT[ti % 2]
            nTk = "nT%d" % (ti % 2)
            rmsnorm_T(xt[slot][:], xk, V_GMIX, nTt[:], nTk, "n1")
            lru_stage(ti, nTt, nTk)
            full = ti >= n_pre - 1
            if ti >= n_pre - 2:
                cb, pb = swa_stage(ti, nTt, nTk, False, False)
            if ti < n_pre and (ti + 1) % 16 == 0:
                fcol = V_FLAG + (ti + 1) // 16 - 1
                S.dve(lambda e, fcol=fcol: e.tensor_scalar(out=hc[:], in0=hc[:], scalar1=vcol(fcol), scalar2=None,
                                                           op0=ALU.mult), r=["hc", "vec"], w=["hc"])
            if not full:
                continue
            swa_attend(cb, pb, ti == n_pre)
            mixer_out(xt[slot], xk, nTt, nTk)
            cross_stage(xt[slot], xk, nTt, nTk)
            mi = (ti - n_pre) % 4 if ti >= n_pre else 0
            if ti == n_pre - 1:
                rmsnorm_T(xt[slot][:], xk, V_GFFN, nTt[:], nTk, "n3")
                b0, pk = psbank(1)
                ph = psf(b0)[:, 0:48].rearrange("p (c n) -> p c n", n=2)
                wb = wq_add_macro(gate_only=True)
                for g in range(6):
                    wg3, wgk = wq_get(wb + g)
                    for c in range(4):
                        for k in range(8):
                            S.pe(lambda e, g=g, c=c, k=k, wg3=wg3, ph=ph, nTt=nTt: e.matmul(
                                ph[:, g * 4 + c, :], lhsT=wg3[:, k, c * 128:(c + 1) * 128], rhs=nTt[:, k, 126:128],
                                start=(k == 0), stop=(k == 7)), r=[wgk, nTk], w=pk)
                    wq_done(wb + g)
                S.dve(lambda e, ph=ph: e.tensor_scalar(out=Gh[:], in0=ph, scalar1=vcol(V_FLAG + 3), scalar2=None, op0=ALU.mult),
                      r=pk + ["vec"], w=["Gh"])
                continue
            rmsnorm_T(xt[slot][:], xk, V_GFFN, n3T[:, :, mi * 128:(mi + 1) * 128], "n3T", "n3")
            if ti == total - 1:
                S.dma("sync", okv_d, kvf[:], r=["kvf"])
                S.dve(lambda e, ti=ti: e.tensor_copy(out=XC2[:, :, 0:3], in_=xrp[ti % 2][:, :, 128:131]),
                      r=["xrp%d" % (ti % 2)], w=["XC2"])
                b0, pk = psbank(1)
                pq_ = psf(b0)
                for c in range(4):
                    S.pe(lambda e, c=c, pq_=pq_: e.transpose(out=pq_[0:3, c * 128:(c + 1) * 128], in_=XC2[:, c, 0:3],
                                                             identity=ident32[:]), r=["XC2", "ident32"], w=pk)
                S.act(lambda e, pq_=pq_: e.activation(out=stL[0:3, :], in_=pq_[0:3, :], func=ACT.Copy), r=pk, w=["stL"])
                S.dma("sync", olc_d, stL[0:3, :], r=["stL"])
                S.dve(lambda e: e.tensor_copy(out=HS2[:, :, 0:1], in_=hc[:].unsqueeze(2)), r=["hc"], w=["HS2"])
                b0, pk = psbank(1)
                ph_ = psf(b0)
                for c in range(4):
                    S.pe(lambda e, c=c, ph_=ph_: e.transpose(out=ph_[0:1, c * 128:(c + 1) * 128], in_=HS2[:, c, 0:1],
                                                             identity=ident32[:]), r=["HS2", "ident32"], w=pk)
                S.act(lambda e, ph_=ph_: e.activation(out=stH[0:1, :], in_=ph_[0:1, :], func=ACT.Copy), r=pk, w=["stH"])
                S.dma("sync", olh_d.rearrange("(a n) -> a n", a=1), stH[0:1, :], r=["stH"])
            if mi != 3:
                continue
            ffn_macro([ti - 3, ti - 2, ti - 1, ti], [(tj - n_pre) * 128 for tj in (ti - 3, ti - 2, ti - 1, ti)], 512)
            if ti == total - 1:
                for g in range(6):
                    b0, pk = psbank(1)
                    pg_ = psf(b0)
                    for c in range(4):
                        S.pe(lambda e, c=c, g=g, pg_=pg_: e.transpose(out=pg_[0:2, c * 128:(c + 1) * 128],
                                                                     in_=Gh[:, g * 4 + c, :], identity=ident32[:]),
                             r=["Gh", "ident32"] + ["Gh%d" % fc for fc in range(24)], w=pk)
                    S.act(lambda e, pg_=pg_: e.activation(out=stL[0:2, :], in_=pg_[0:2, :], func=ACT.Copy),
                          r=pk, w=["stL"])
                    S.dma("sync", ofc_d[:, g * 512:(g + 1) * 512], stL[0:2, :], r=["stL"])

        if do_sample:
            ti = total
            slot = xslot(ti)
            xk = "xt%d" % slot
            free = [i for i in range(NXB) if i != slot]
            fk = ["xt%d" % i for i in free]
            Vc = [xt[free[0]].bitcast(BF16)[:, 0:2048].rearrange("p (s d) -> p s d", d=128),
                  xt[free[1]].bitcast(BF16)[:, 0:2048].rearrange("p (s d) -> p s d", d=128)]
            FS = xt[free[2]][:, 0:768].rearrange("p (c s k) -> p c s k", c=24, k=2)
            FSn = xt[free[3]][:, 0:768].rearrange("p (c s k) -> p c s k", c=24, k=2)
            n3flat = n3T[:].rearrange("p a b -> p (a b)")
            KTc = n3flat[:, 0:2048].rearrange("p (s d) -> p s d", d=128)
            Bown = n3flat[:, 2048:4096].rearrange("p (h n) -> p h n", n=256)
            hmflat = hm[:].rearrange("p a b -> p (a b)")
            S.dma("sync", xt[slot][:], xs[ti * 128:(ti + 1) * 128, :], w=[xk])
            S.dma("sync", stL[0:48, :], slc_d, w=["stL"])
            S.dma("sync", stH[0:16, :], slh_d, w=["stH"])
            S.dma("sync", osk_d[:, 0:124, :], cswk_d[:, 4:128, :])
            S.dma("sync", osv_d[:, 0:124, :], cswv_d[:, 4:128, :])
            S.pool(lambda e: e.memset(xt[free[0]][:], 0.0), w=[fk[0]])
            S.pool(lambda e: e.memset(xt[free[1]][:], 0.0), w=[fk[1]])
            Kc = hmflat[:, 0:2048].rearrange("p (s d) -> p s d", d=128)
            hk03 = ["hm0", "hm1", "hm2", "hm3"]
            S.dma("gpsimd", Kc, cswk_d.rearrange("s k d -> k s d"), w=hk03)
            S.dma("gpsimd", Vc[0][:, :, 0:64], cswv_d.rearrange("s k d -> k s d")[:, :, 0:64], w=[fk[0]])
            S.dma("gpsimd", Vc[1][:, :, 64:128], cswv_d.rearrange("s k d -> k s d")[:, :, 64:128], w=[fk[1]])
            S.dma("gpsimd", Bown, bown_d.rearrange("p (h n) -> p h n", n=256), w=["n3T"])
            for half in range(2):
                b0, pk = psbank(1)
                pv = psb(b0).rearrange("p (t n) -> p t n", n=128)
                for i in range(8):
                    S.pe(lambda e, i=i, half=half, pv=pv: e.transpose(out=pv[:, i, :], in_=Kc[:, half * 8 + i, :],
                                                                      identity=ident[:]),
                         r=hk03 + ["ident"], w=pk)
                S.act(lambda e, half=half, pv=pv: e.activation(out=KTc[:, half * 8:(half + 1) * 8, :], in_=pv,
                                                               func=ACT.Copy), r=pk, w=["n3T"])
            b0, pk = psbank(1)
            p32 = psf(b0)
            for c in range(4):
                S.pe(lambda e, c=c: e.transpose(out=p32[:, c * 48:(c + 1) * 48], in_=stL[0:48, c * 128:(c + 1) * 128],
                                                identity=ident32[0:48, 0:48]), r=["stL", "ident32"], w=pk)
            S.dve(lambda e: e.tensor_copy(out=XP[:, :, :, 0:3],
                                          in_=p32[:, 0:192].rearrange("p (c s k) -> p c s k", c=4, k=3)),
                  r=pk, w=["XP"])
            b0, pk = psbank(1)
            p32b = psf(b0)
            for c in range(4):
                S.pe(lambda e, c=c: e.transpose(out=p32b[:, c * 16:(c + 1) * 16], in_=stH[0:16, c * 128:(c + 1) * 128],
                                                identity=ident32[0:16, 0:16]), r=["stH", "ident32"], w=pk)
            S.dve(lambda e: e.tensor_copy(out=HS[:], in_=p32b[:, 0:64].rearrange("p (c s) -> p c s", c=4)),
                  r=pk, w=["HS"])
            for g in range(6):
                S.dma("sync", stL[0:32, :], sfc_d[:, g * 512:(g + 1) * 512], w=["stL"])
                b0, pk = psbank(1)
                pf = psf(b0)
                for c in range(4):
                    S.pe(lambda e, c=c, pf=pf: e.transpose(out=pf[:, c * 32:(c + 1) * 32],
                                                           in_=stL[0:32, c * 128:(c + 1) * 128],
                                                           identity=ident32[0:32, 0:32]), r=["stL", "ident32"], w=pk)
                S.dve(lambda e, g=g, pf=pf: e.tensor_copy(
                    out=FS[:, g * 4:(g + 1) * 4, :, :],
                    in_=pf[:, 0:128].rearrange("p (c s k) -> p c s k", c=4, k=2)), r=pk, w=[fk[2]])
            nTt, nTk = nT[ti % 2], "nT%d" % (ti % 2)
            rmsnorm_T(xt[slot][:], xk, V_GMIX, nTt[:], nTk, "n1")
            lru_stage(ti, nTt, nTk, smp=True)
            cb, pb = swa_stage(ti, nTt, nTk, False, False)
            S.dve(lambda e: e.tensor_copy(out=XC2[:].rearrange("p c (s k) -> p c s k", k=3), in_=XP[:, :, :, 4:7]),
                  r=["XP"], w=["XC2"])
            b0, pk = psbank(1)
            po32 = psf(b0)
            for c in range(4):
                S.pe(lambda e, c=c: e.transpose(out=po32[0:48, c * 128:(c + 1) * 128], in_=XC2[:, c, :],
                                                identity=ident32[:]), r=["XC2", "ident32"], w=pk)
            S.act(lambda e: e.activation(out=stL[0:48, :], in_=po32[0:48, :], func=ACT.Copy), r=pk, w=["stL"])
            S.dma("sync", oslc_d, stL[0:48, :], r=["stL"])
            b0, pk = psbank(1)
            po32b = psf(b0)
            for c in range(4):
                S.pe(lambda e, c=c: e.transpose(out=po32b[0:16, c * 128:(c + 1) * 128], in_=HS2[:, c, :],
                                                identity=ident32[:]), r=["HS2", "ident32"], w=pk)
            S.act(lambda e: e.activation(out=stH[0:16, :], in_=po32b[0:16, :], func=ACT.Copy), r=pk, w=["stH"])
            S.dma("sync", oslh_d, stH[0:16, :], r=["stH"])
            for t4 in range(4):
                S.dma("sync", osk_d[:, 124 + t4, :], kvf[t4:64:4, 0:128], r=["kvf"])
                S.dma("sync", osv_d[:, 124 + t4, :], kvf[t4:64:4, 128:256], r=["kvf"])
            for sq in range(16):
                swa_attend(cb, pb, False, smp=dict(s=sq, KTc=KTc, Vc=Vc, Bown=Bown, keys=["n3T", fk[0], fk[1]]))
            mixer_out(xt[slot], xk, nTt, nTk)

            def cross_iter():
                for sq in range(16):
                    i2 = sq % 2
                    mkc = hmflat[:, i2 * 1024:(i2 + 1) * 1024].rearrange("p (a b) -> p a b", a=2)
                    mvc = hmflat[:, 2048 + i2 * 1024:2048 + (i2 + 1) * 1024].rearrange("p (a b) -> p a b", a=2)
                    mkTs = hmflat[:, 4096 + i2 * 1024:4096 + (i2 + 1) * 1024].rearrange("p (a b) -> p a b", a=4)
                    kk = ["hm%d" % (2 * i2), "hm%d" % (2 * i2 + 1)]
                    kv = ["hm%d" % (4 + 2 * i2), "hm%d" % (5 + 2 * i2)]
                    kt = ["hm%d" % (8 + 2 * i2), "hm%d" % (9 + 2 * i2)]
                    S.dma("gpsimd", mkc, cmk_d[sq].rearrange("(a p) n -> p a n", p=128), w=kk)
                    S.dma("gpsimd", mvc, cmv_d[sq].rearrange("(a p) n -> p a n", p=128), w=kv)
                    b0, pk = psbank(1)
                    pv = psb(b0).rearrange("p (t n) -> p t n", n=128)
                    for h in range(4):
                        for kb in range(2):
                            S.pe(lambda e, h=h, kb=kb, pv=pv, mkc=mkc: e.transpose(
                                out=pv[:, h * 2 + kb, :], in_=mkc[:, kb, h * 128:(h + 1) * 128], identity=ident[:]),
                                r=kk + ["ident"], w=pk)
                    S.dve(lambda e, pv=pv, mkTs=mkTs: e.tensor_copy(out=mkTs, in_=pv.rearrange("p (h b) n -> p h (b n)", b=2)),
                          r=pk, w=kt)
                    cross_attend(dict(s=sq, mkT=mkTs, mv=mvc, keys=kk + kv + kt))
            cross_stage(xt[slot], xk, nTt, nTk, smp_iter=cross_iter)
            rmsnorm_T(xt[slot][:], xk, V_GFFN, n3T[:, :, 0:128], "n3T", "n3")
            ffn_macro([ti], [n_main * 128], 128, smp=dict(FS=FS, FSn=FSn, keys=[fk[2], fk[3]]))
            for g in range(6):
                b0, pk = psbank(1)
                pf = psf(b0)
                for c in range(4):
                    S.pe(lambda e, c=c, g=g, pf=pf: e.transpose(
                        out=pf[0:32, c * 128:(c + 1) * 128],
                        in_=FSn[:, g * 4 + c, :, :].rearrange("p s k -> p (s k)"), identity=ident32[:]),
                        r=[fk[3], "ident32"], w=pk)
                S.act(lambda e, pf=pf: e.activation(out=stL[0:32, :], in_=pf[0:32, :], func=ACT.Copy), r=pk, w=["stL"])
                S.dma("sync", osfc_d[:, g * 512:(g + 1) * 512], stL[0:32, :], r=["stL"])

        if max_ops is not None:
            S.ops = S.ops[:max_ops]
        print('nops', len(S.ops))
        S.emit()
    return nc


def _slot_heads():
    return [(s // 2) + 4 * (s % 2) for s in range(8)]


def _bias_tables():
    slopes = 2.0 ** (-np.arange(1, 9, dtype=np.float64))
    qi = np.arange(128)[:, None]
    kj = np.arange(256)[None, :]
    dist = qi + 128 - kj
    valid = (dist >= 0) & (dist < 128)
    tab = np.empty((128, 8, 256), np.float32)
    for s, h in enumerate(_slot_heads()):
        tab[:, s, :] = np.where(valid, -8.0 * slopes[h] * dist, NEG)
    return tab


_NC_CACHE = {}


def kernel(**inp):
    f32 = np.float32
    xp = np.asarray(inp["x_prompt"], f32)
    xsmp = np.asarray(inp["x_sample"], f32)
    heads = _slot_heads()
    w_in = np.asarray(inp["w_in"][0], f32)
    qcols = np.concatenate([np.arange(1024 + h * 64, 1024 + (h + 1) * 64) for h in heads])
    w_in_p = np.ascontiguousarray(np.concatenate([w_in[:, :1024], w_in[:, qcols], w_in[:, 1536:]], axis=1))
    w_out = np.asarray(inp["w_out"][0], f32)
    orow = np.concatenate([np.arange(512 + h * 64, 512 + (h + 1) * 64) for h in heads])
    w_out_p = np.ascontiguousarray(np.concatenate([w_out[:512], w_out[orow]], axis=0))

    def bd(w):
        o = np.zeros((128, 4, 128), f32)
        for c in range(4):
            o[0:64, c, 0:64] = w[2 * c]
            o[64:128, c, 64:128] = w[2 * c + 1]
        return o.reshape(128, 512)

    def fm(v, nchunk):
        return np.asarray(v, f32).reshape(nchunk, 128).T

    vec = np.zeros((128, NV), f32)
    vec[:, V_GMIX:V_GMIX + 8] = fm(inp["g_mix"][0], 8)
    vec[:, V_GCROSS:V_GCROSS + 8] = fm(inp["g_cross"][0], 8)
    vec[:, V_GFFN:V_GFFN + 8] = fm(inp["g_ffn"][0], 8)
    vec[:, V_GMEM:V_GMEM + 8] = fm(inp["g_mem"][0], 8)
    for tap in range(4):
        vec[:, V_CW + tap * 4:V_CW + tap * 4 + 4] = fm(inp["w_lru_conv"][0, tap], 4)
    vec[:, V_BCONV:V_BCONV + 4] = fm(inp["b_lru_conv"][0], 4)
    vec[:, V_BA:V_BA + 4] = fm(inp["b_lru_a"][0], 4)
    vec[:, V_BX:V_BX + 4] = fm(inp["b_lru_x"][0], 4)
    vec[:, V_LAM:V_LAM + 4] = fm(inp["lru_lambda"][0], 4)
    for tap in range(3):
        vec[:, V_FW + tap * 24:V_FW + tap * 24 + 24] = fm(inp["w_ffn_conv"][0, tap], 24)
    vec[:, V_FB:V_FB + 24] = fm(inp["b_ffn_conv"][0], 24)
    vec[:, V_SINK:V_SINK + 8] = np.asarray(inp["attn_sinks"][0], f32)[heads][None, :]
    vec[:, V_EPS] = 1e-6
    vec[:, V_ONE] = 1.0

    slopes = 2.0 ** (-np.arange(1, 9, dtype=np.float64))
    bown = np.full((128, 8, 256), NEG, f32)
    for s_i, h in enumerate(heads):
        for t in range(4):
            for j in range(t + 1):
                bown[t, s_i, 128 + j] = -8.0 * slopes[h] * (t - j)
    gfin = np.ascontiguousarray(np.broadcast_to(np.asarray(inp["g_final"], f32)[None, :], (128, D)))
    ident = np.eye(128, dtype=f32)
    bias = _bias_tables()
    common = dict(
        gfin=gfin, ident=ident, ident32=ident, bias_own=bown.reshape(128, -1),
        bias=bias.reshape(128, -1), w_in=w_in_p, w_out=w_out_p,
        w_q=np.ascontiguousarray(inp["w_mem_q"][0], f32), w_k=np.ascontiguousarray(inp["w_mem_k"][0], f32),
        w_v=np.ascontiguousarray(inp["w_mem_v"][0], f32), w_o=np.ascontiguousarray(inp["w_mem_o"][0], f32),
        wa_bd=bd(np.asarray(inp["w_lru_a"][0], f32)), wx_bd=bd(np.asarray(inp["w_lru_x"][0], f32)),
        w_g=np.ascontiguousarray(inp["w_ffn_gate"][0], f32), w_u=np.ascontiguousarray(inp["w_ffn_up"][0], f32),
        w_d=np.ascontiguousarray(inp["w_ffn_down"][0], f32),
    )
    in_maps = []
    for c in range(NCORES):
        b, q = c // 4, c % 4
        pre = np.zeros((PRE_T * 128, D), f32)
        if q > 0:
            pre[(3 - q) * 2048:] = xp[b, :q * 2048]
        main = xp[b, q * 2048:(q + 1) * 2048]
        smp = np.zeros((128, D), f32)
        smp[:64] = xsmp[c * 16:(c + 1) * 16].reshape(64, D)
        v = vec.copy()
        for k in range(3):
            v[:, V_FLAG + k] = 1.0 if k >= 3 - q else 0.0
        v[:, V_FLAG + 3] = 1.0 if q > 0 else 0.0
        b0 = bias[:, :, 0:128].copy()
        if q == 0:
            b0[:] = NEG
        m = dict(common)
        m.update(xs=np.ascontiguousarray(np.concatenate([pre, main, smp], axis=0)),
                 memx=np.ascontiguousarray(inp["mem_prompt"][b], f32), vec=v,
                 c_swa_k=np.ascontiguousarray(inp["cache_swa_k"][0, c * 16:(c + 1) * 16], f32).reshape(16, 128, 128),
                 c_swa_v=np.ascontiguousarray(inp["cache_swa_v"][0, c * 16:(c + 1) * 16], f32).reshape(16, 128, 128),
                 c_mem_k=np.ascontiguousarray(inp["cache_mem_k"][0, c * 16:(c + 1) * 16], f32).reshape(16, 256, 512),
                 c_mem_v=np.ascontiguousarray(inp["cache_mem_v"][0, c * 16:(c + 1) * 16], f32).reshape(16, 256, 512),
                 st_lconv=np.ascontiguousarray(inp["state_lru_conv"][0, c * 16:(c + 1) * 16], f32).reshape(48, 512),
                 st_lh=np.ascontiguousarray(inp["state_lru_h"][0, c * 16:(c + 1) * 16], f32),
                 st_fconv=np.ascontiguousarray(inp["state_ffn_conv"][0, c * 16:(c + 1) * 16], f32).reshape(32, 3072),
                 bias0=np.ascontiguousarray(b0.reshape(128, -1)))
        in_maps.append(m)

    if "nc" not in _NC_CACHE:
        _NC_CACHE["nc"] = build_nc()
    nc = _NC_CACHE["nc"]
    res = run_bass_kernel_spmd(nc, in_maps, core_ids=list(range(NCORES)))
    R = res.results

    y_prompt = np.zeros((2, 8192, D), f32)
    y_sample = np.zeros((128, 4, D), f32)
    for c in range(NCORES):
        b, q = c // 4, c % 4
        y_prompt[b, q * 2048:(q + 1) * 2048] = R[c]["y"][:2048]
        y_sample[c * 16:(c + 1) * 16] = R[c]["y"][2048:2048 + 64].reshape(16, 4, D)
    p_swa_k = np.stack([R[3]["o_kv"][:, :128], R[7]["o_kv"][:, :128]]).reshape(1, 2, 128, 2, 64)
    p_swa_v = np.stack([R[3]["o_kv"][:, 128:], R[7]["o_kv"][:, 128:]]).reshape(1, 2, 128, 2, 64)
    p_mem_k = np.stack([R[0]["o_mk"], R[4]["o_mk"]]).reshape(1, 2, 256, 4, 128)
    p_mem_v = np.stack([R[0]["o_mv"], R[4]["o_mv"]]).reshape(1, 2, 256, 4, 128)
    p_lru_conv = np.stack([R[3]["o_lconv"], R[7]["o_lconv"]]).reshape(1, 2, 3, 512)
    p_lru_h = np.stack([R[3]["o_lh"], R[7]["o_lh"]]).reshape(1, 2, 512)
    p_ffn_conv = np.stack([R[3]["o_fconv"], R[7]["o_fconv"]]).reshape(1, 2, 2, 3072)
    cat = lambda k: np.concatenate([R[c][k] for c in range(NCORES)], axis=0)
    s_swa_k = cat("o_sk").reshape(1, 128, 128, 2, 64)
    s_swa_v = cat("o_sv").reshape(1, 128, 128, 2, 64)
    s_lru_conv = cat("o_slc").reshape(1, 128, 3, 512)
    s_lru_h = cat("o_slh").reshape(1, 128, 512)
    s_ffn_conv = cat("o_sfc").reshape(1, 128, 2, 3072)
    return (y_prompt, y_sample, p_swa_k, p_swa_v, p_mem_k, p_mem_v, p_lru_conv, p_lru_h, p_ffn_conv,
            s_swa_k, s_swa_v, s_lru_conv, s_lru_h, s_ffn_conv)
```

```python
import contextlib
import numpy as np
import concourse.bass as bass
import concourse.mybir as mybir
from concourse.bass_utils import run_bass_kernel_spmd

F32 = mybir.dt.float32
BF16 = mybir.dt.bfloat16
ACT = mybir.ActivationFunctionType
ALU = mybir.AluOpType
AX = mybir.AxisListType

NCORES = 8
D = 1024
PRE_T = 48
MAIN_T = 16
NEG = -240000.0
SC_MEM = 128.0 ** -0.5

V_GMIX, V_GCROSS, V_GFFN, V_GMEM = 0, 8, 16, 24
V_CW = 32
V_BCONV = 48
V_BA = 52
V_BX = 56
V_LAM = 60
V_FW = 64
V_FB = 136
V_SINK = 160
V_FLAG = 168
V_EPS = 172
V_ONE = 173
NV = 176


class Sched:
    def __init__(self, nc, es, dma_pool=None):
        self.nc = nc
        self.ops = []
        self.last_w = {}
        self.readers = {}
        self.dma_pool = dma_pool or {"sync": 8, "gpsimd": 6, "scalar": 4}
        self.sems = {}
        for e in ("scalar", "vector", "gpsimd", "tensor"):
            self.sems[e] = es.enter_context(nc.semaphore("c_" + e))
        self.dsems = {}
        for q, n in self.dma_pool.items():
            self.dsems[q] = [es.enter_context(nc.semaphore("d_%s%d" % (q, i))) for i in range(n)]
        self.dcount = {q: 0 for q in self.dma_pool}

    def add(self, eng, fn, r=(), w=(), dma=False):
        i = len(self.ops)
        deps = {}
        for k in r:
            a = self.last_w.get(k)
            if a is not None:
                deps[a] = "raw"
        for k in w:
            a = self.last_w.get(k)
            if a is not None:
                deps[a] = "raw"
            for a in self.readers.get(k, ()):
                deps.setdefault(a, "war")
        op = dict(eng=eng, fn=fn, deps=deps, dma=dma, signal=False)
        if dma:
            j = self.dcount[eng]
            self.dcount[eng] += 1
            n = self.dma_pool[eng]
            op["dsem"] = (eng, j % n)
            op["dval"] = 16 * (j // n + 1)
        self.ops.append(op)
        for k in w:
            self.last_w[k] = i
            self.readers[k] = []
        for k in r:
            lst = self.readers.setdefault(k, [])
            if not dma:
                lst[:] = [a for a in lst if self.ops[a]["dma"] or self.ops[a]["eng"] != eng]
            lst.append(i)
        return i

    def act(self, fn, r=(), w=()):
        return self.add("scalar", fn, r, w)

    def dve(self, fn, r=(), w=()):
        return self.add("vector", fn, r, w)

    def pool(self, fn, r=(), w=()):
        return self.add("gpsimd", fn, r, w)

    def pe(self, fn, r=(), w=()):
        return self.add("tensor", fn, r, w)

    def dma(self, q, out, in_, r=(), w=(), **kw):
        return self.add(q, lambda e: e.dma_start(out=out, in_=in_, **kw), r, w, dma=True)

    def emit(self):
        ops = self.ops
        for b in ops:
            for a, kind in b["deps"].items():
                A = ops[a]
                if A["dma"]:
                    continue
                if A["eng"] == b["eng"] and not b["dma"]:
                    if A["eng"] == "tensor":
                        continue
                A["signal"] = True
        cnt = {e: 0 for e in self.sems}
        for o in ops:
            if not o["dma"] and o["signal"]:
                cnt[o["eng"]] += 1
                o["sval"] = cnt[o["eng"]]
        seen = {}
        last_on_dsem = {}
        for o in ops:
            e = o["eng"]
            sn = seen.setdefault(e, {})
            need = {}
            for a, kind in o["deps"].items():
                A = ops[a]
                if A["dma"]:
                    key, val = ("d",) + A["dsem"], A["dval"]
                else:
                    if A["eng"] == e and not o["dma"]:
                        if e == "tensor":
                            continue
                    key, val = ("c", A["eng"]), A["sval"]
                if need.get(key, 0) < val:
                    need[key] = val
            if o["dma"]:
                key = ("d",) + o["dsem"]
                prev = o["dval"] - 16
                if prev > 0 and need.get(key, 0) < prev:
                    need[key] = prev
            waits = []
            for key, val in need.items():
                if sn.get(key, 0) < val:
                    sn[key] = val
                    waits.append((key, val))
            o["waits"] = waits
        final = {}
        for o in ops:
            if o["dma"]:
                final[("d",) + o["dsem"]] = o["dval"]
        engs = ["sync", "scalar", "vector", "gpsimd", "tensor"]
        per = {e: [o for o in ops if o["eng"] == e] for e in engs}

        def semof(key):
            if key[0] == "c":
                return self.sems[key[1]]
            return self.dsems[key[1]][key[2]]

        def run(e, lst, tail):
            for o in lst:
                for key, val in o["waits"]:
                    e.wait_ge(semof(key), val)
                ins = o["fn"](e)
                if o["dma"]:
                    ins.then_inc(semof(("d",) + o["dsem"]), 16)
                elif o["signal"]:
                    ins.then_inc(self.sems[o["eng"]], 1)
            if tail:
                for key, val in final.items():
                    e.wait_ge(semof(key), val)

        with self.nc.Block() as block:
            block.sync(lambda e: run(e, per["sync"], True))
            block.scalar(lambda e: run(e, per["scalar"], False))
            block.vector(lambda e: run(e, per["vector"], False))
            block.gpsimd(lambda e: run(e, per["gpsimd"], False))
            block.tensor(lambda e: run(e, per["tensor"], False))


def build_nc(do_sample=True, n_pre=PRE_T, n_main=MAIN_T, max_ops=None):
    nc = bass.Bass("TRN2", target_bir_lowering=False)
    NT = n_pre + n_main + 1

    def din(name, shape):
        return nc.dram_tensor(name, list(shape), F32, kind="ExternalInput").ap()

    def dout(name, shape):
        return nc.dram_tensor(name, list(shape), F32, kind="ExternalOutput").ap()

    xs = din("xs", [NT * 128, D])
    memx = din("memx", [256, D])
    vec_d = din("vec", [128, NV])
    gfin_d = din("gfin", [128, D])
    ident_d = din("ident", [128, 128])
    bias_d = din("bias", [128, 8 * 256])
    bias0_d = din("bias0", [128, 8 * 128])
    w_in_d = din("w_in", [D, 1792])
    w_out_d = din("w_out", [D, D])
    w_q_d = din("w_q", [D, 512])
    w_k_d = din("w_k", [D, 512])
    w_v_d = din("w_v", [D, 512])
    w_o_d = din("w_o", [512, D])
    wa_d = din("wa_bd", [128, 512])
    wx_d = din("wx_bd", [128, 512])
    w_g_d = din("w_g", [D, 3072])
    w_u_d = din("w_u", [D, 3072])
    w_d_d = din("w_d", [3072, D])

    ident32_d = din("ident32", [128, 128])
    bown_d = din("bias_own", [128, 8 * 256])
    cswk_d = din("c_swa_k", [16, 128, 128])
    cswv_d = din("c_swa_v", [16, 128, 128])
    cmk_d = din("c_mem_k", [16, 256, 512])
    cmv_d = din("c_mem_v", [16, 256, 512])
    slc_d = din("st_lconv", [48, 512])
    slh_d = din("st_lh", [16, 512])
    sfc_d = din("st_fconv", [32, 3072])
    osk_d = dout("o_sk", [16, 128, 128])
    osv_d = dout("o_sv", [16, 128, 128])
    oslc_d = dout("o_slc", [48, 512])
    oslh_d = dout("o_slh", [16, 512])
    osfc_d = dout("o_sfc", [32, 3072])
    y_d = dout("y", [(n_main + 1) * 128, D])
    okv_d = dout("o_kv", [128, 256])
    omk_d = dout("o_mk", [256, 512])
    omv_d = dout("o_mv", [256, 512])
    olc_d = dout("o_lconv", [3, 512])
    olh_d = dout("o_lh", [512])
    ofc_d = dout("o_fconv", [2, 3072])

    es = contextlib.ExitStack()
    with es:
        def sb(name, shape, dt=F32):
            return es.enter_context(nc.sbuf_tensor("s_" + name, list(shape), dt))

        S = Sched(nc, es)
        psall = es.enter_context(nc.psum_tensor("psall", [128, 8 * 512], F32))
        ps_ctr = [0]

        def psbank(n=1):
            b0 = ps_ctr[0]
            if b0 + n > 8:
                b0 = 0
            ps_ctr[0] = (b0 + n) % 8
            return b0, ["ps%d" % (b0 + i) for i in range(n)]

        def psf(b0, n=1):
            return psall[:, b0 * 512:(b0 + n) * 512]

        psall_bf = psall.bitcast(BF16)

        def psb(b0, n=1):
            return psall_bf[:, b0 * 1024:(b0 + n) * 1024]

        vec = sb("vec", [128, NV])
        gfin = sb("gfin", [128, D])
        ident = sb("ident", [128, 128], BF16)
        biasT = sb("biasT", [128, 8, 256], BF16)
        bias0 = sb("bias0", [128, 8, 128], BF16)
        w_in = sb("w_in", [128, 8, 1792], BF16)
        w_out = sb("w_out", [128, 8, 1024], BF16)
        w_q = sb("w_q", [128, 8, 512], BF16)
        w_o = sb("w_o", [128, 4, 1024], BF16)
        wa = sb("wa", [128, 4, 128], BF16)
        wx = sb("wx", [128, 4, 128], BF16)
        NRING = 4
        ring = [sb("ring%d" % i, [128, 4096], BF16) for i in range(NRING)]
        ring_ctr = [0]
        mkT = sb("mkT", [128, 4, 256], BF16)
        mvb = sb("mvb", [128, 2, 512], BF16)
        cl = sb("cl", [128, 4])
        sink8 = sb("sink8", [128, 8])
        hc = sb("hc", [128, 4])
        NXB = 5
        xt = [sb("xt%d" % i, [128, D]) for i in range(NXB)]
        nbf = sb("nbf", [128, D], BF16)
        junk = nbf
        nT = [sb("nT%d" % i, [128, 8, 128], BF16) for i in range(2)]
        n3T = sb("n3T", [128, 8, 512], BF16)
        st = sb("stat", [128, 64])
        xrp = [sb("xrp%d" % i, [128, 4, 131]) for i in range(2)]
        xc = sb("xc", [128, 4, 128])
        xcb = sb("xcb", [128, 4, 128], BF16)
        rg = sb("rg", [128, 4, 128])
        ig = sb("ig", [128, 4, 128])
        av = sb("av", [128, 4, 128])
        bv = sb("bv", [128, 4, 128])
        hv = sb("hv", [128, 4, 128])
        yT = sb("yT", [128, 8, 128], BF16)
        QT = sb("QT", [128, 4, 128], BF16)
        KT = [sb("KT%d" % i, [128, 128], BF16) for i in range(2)]
        Vp = [sb("Vp%d" % i, [128, 2, 128], BF16) for i in range(2)]
        kvf = sb("kvf", [128, 256])
        Pm = sb("Pm", [128, 8, 256], BF16)
        PT = sb("PT", [128, 16, 128], BF16)
        QC = sb("QC", [128, 4, 128], BF16)
        OC = sb("OC", [128, 4, 128], BF16)
        Gs = [sb("Gs%d" % i, [128, 514]) for i in range(2)]
        t1 = [sb("t1_%d" % i, [128, 512]) for i in range(2)]
        mtmp = t1[0]
        gg = rg
        hm = sb("hm", [128, 12, 512], BF16)
        Gh = sb("Gh", [128, 24, 2])

        ident32 = sb("ident32", [128, 128])
        XP = sb("XP", [128, 4, 16, 7])
        XC2 = sb("XC2", [128, 4, 48])
        HS = sb("HS", [128, 4, 16])
        HS2 = sb("HS2", [128, 4, 16])
        stL = sb("stL", [128, 512])
        stH = sb("stH", [128, 512])

        HALO = sb("HALO", [128, 4, 3])
        fence = sb("fence", [128, 2])

        def vcol(c, n=1):
            return vec[:, c:c + n]

        S.dma("sync", vec[:], vec_d, w=["vec"])
        S.dma("sync", gfin[:], gfin_d, w=["gfin"])
        S.dma("sync", ident32[:], ident32_d, w=["ident32"])
        S.dma("gpsimd", ident[:], ident_d, w=["ident"])
        S.dma("gpsimd", biasT[:].rearrange("p a b -> p (a b)"), bias_d, w=["biasT"])
        S.dma("gpsimd", bias0[:].rearrange("p a b -> p (a b)"), bias0_d, w=["bias0"])
        S.dma("gpsimd", wa[:].rearrange("p a b -> p (a b)"), wa_d, w=["wa"])
        S.dma("gpsimd", wx[:].rearrange("p a b -> p (a b)"), wx_d, w=["wx"])
        S.dma("gpsimd", w_in[:], w_in_d.rearrange("(k p) n -> p k n", p=128), w=["w_in"])
        wk_s = ring[0][:].rearrange("p (k n) -> p k n", k=8)
        wv_s = ring[1][:].rearrange("p (k n) -> p k n", k=8)
        S.dma("gpsimd", wk_s, w_k_d.rearrange("(k p) n -> p k n", p=128), w=["ring0"])
        S.dma("gpsimd", wv_s, w_v_d.rearrange("(k p) n -> p k n", p=128), w=["ring1"])
        S.dma("gpsimd", w_q[:], w_q_d.rearrange("(k p) n -> p k n", p=128), w=["w_q"])
        S.dma("gpsimd", w_o[:], w_o_d.rearrange("(k p) n -> p k n", p=128), w=["w_o"])

        S.pool(lambda e: e.memset(hc[:], 0.0), w=["hc"])
        S.pool(lambda e: e.memset(xrp[1][:], 0.0), w=["xrp1"])
        S.pool(lambda e: e.memset(xrp[0][:], 0.0), w=["xrp0"])
        for i in range(2):
            S.pool(lambda e, i=i: e.memset(Vp[i][:], 0.0), w=["Vp%d" % i])
            S.pool(lambda e, i=i: e.memset(KT[i][:], 0.0), w=["KT%d" % i])
        S.pool(lambda e: e.memset(Gh[:], 0.0), w=["Gh"])

        S.act(lambda e: e.activation(out=st[:, 0:4], in_=vcol(V_LAM, 4), func=ACT.Exp, scale=-1.0),
              r=["vec"], w=["st_a"])
        S.act(lambda e: e.activation(out=st[:, 4:8], in_=st[:, 0:4], func=ACT.Ln, bias=vcol(V_ONE), scale=1.0),
              r=["st_a", "vec"], w=["st_b"])
        S.dve(lambda e: e.tensor_scalar(out=cl[:], in0=st[:, 4:8], scalar1=-8.0, scalar2=None, op0=ALU.mult),
              r=["st_b"], w=["cl"])
        S.dve(lambda e: e.tensor_scalar(out=sink8[:], in0=vcol(V_SINK, 8), scalar1=8.0, scalar2=None, op0=ALU.mult),
              r=["vec"], w=["sink8"])

        def rmsnorm_T(xap, xkey, gcol, dst, dstkey, tagn):
            S.act(lambda e: e.activation(out=junk[:], in_=xap, func=ACT.Square, accum_out=st[:, 8:9]),
                  r=[xkey], w=["nbf", "st_ss"])
            S.act(lambda e: e.activation(out=st[:, 9:10], in_=st[:, 8:9], func=ACT.Sqrt,
                                         bias=vcol(V_EPS), scale=1.0 / D),
                  r=["st_ss", "vec"], w=["st_sd"])
            S.dve(lambda e: e.reciprocal(out=st[:, 10:11], in_=st[:, 9:10]), r=["st_sd"], w=["st_rs"])
            S.dve(lambda e: e.tensor_scalar(out=nbf[:], in0=xap, scalar1=st[:, 10:11], scalar2=None, op0=ALU.mult),
                  r=[xkey, "st_rs"], w=["nbf"])
            b0, pk = psbank(1)
            pv = psb(b0).rearrange("p (k n) -> p k n", k=8)
            for k in range(8):
                S.pe(lambda e, k=k: e.transpose(out=pv[:, k, :], in_=nbf[:, k * 128:(k + 1) * 128], identity=ident[:]),
                     r=["nbf", "ident"], w=pk)
            gb = vec[:, gcol:gcol + 8].unsqueeze(2).to_broadcast([128, 8, 128])
            S.dve(lambda e: e.tensor_tensor(out=dst, in0=pv, in1=gb, op=ALU.mult),
                  r=pk + ["vec"], w=[dstkey])

        def fm_proj(wsb, wkey, col0, nchunks, src, srckey, n=128, width=128):
            b0, pk = psbank(1)
            pv = psf(b0).rearrange("p (c n) -> p c n", c=512 // n)
            for c in range(nchunks):
                for k in range(8):
                    S.pe(lambda e, c=c, k=k: e.matmul(pv[:, c, :], lhsT=wsb[:, k, col0 + c * width:col0 + (c + 1) * width],
                                                       rhs=src[:, k, :], start=(k == 0), stop=(k == 7)),
                         r=[wkey, srckey], w=pk)
            return pv, pk

        def lru_stage(ti, nTt, nTkey, smp=False):
            cur, prv = xrp[ti % 2], xrp[(ti + 1) % 2]
            ck, pk_ = "xrp%d" % (ti % 2), "xrp%d" % ((ti + 1) % 2)
            if not smp:
                S.pool(lambda e: e.tensor_copy(out=cur[:, :, 0:3], in_=prv[:, :, 128:131]), r=[pk_], w=[ck])
            pxr, kxr = fm_proj(w_in, "w_in", 0, 4, nTt, nTkey)
            if not smp:
                S.act(lambda e: e.activation(out=cur[:, :, 3:131], in_=pxr, func=ACT.Copy), r=kxr, w=[ck])
            else:
                for c in range(4):
                    S.act(lambda e, c=c: e.activation(out=XP[:, c, :, 3:7],
                                                      in_=pxr[:, c, 0:64].rearrange("p (s t) -> p s t", t=4),
                                                      func=ACT.Copy), r=kxr, w=["XP"])
            convs = []
            for c in range(4):
                if not smp:
                    o_ap = xc[:, c, :]
                    in_tap = lambda tap, c=c: cur[:, c, tap:tap + 128]
                    srck = ck
                else:
                    o_ap = xc[:, c, 0:64].rearrange("p (s t) -> p s t", t=4)
                    in_tap = lambda tap, c=c: XP[:, c, :, tap:tap + 4]
                    srck = "XP"
                convs.append((o_ap, in_tap, srck))
            for c in range(4):
                o_ap, in_tap, srck = convs[c]
                S.dve(lambda e, c=c, o_ap=o_ap, in_tap=in_tap: e.tensor_scalar(
                    out=o_ap, in0=in_tap(3), scalar1=vcol(V_CW + 12 + c), scalar2=vcol(V_BCONV + c),
                    op0=ALU.mult, op1=ALU.add), r=[srck, "vec"], w=["xc%d" % c])
            for tap in range(3):
                for c in range(4):
                    o_ap, in_tap, srck = convs[c]
                    S.dve(lambda e, c=c, tap=tap, o_ap=o_ap, in_tap=in_tap: e.scalar_tensor_tensor(
                        out=o_ap, in0=in_tap(tap), scalar=vcol(V_CW + tap * 4 + c),
                        in1=o_ap, op0=ALU.mult, op1=ALU.add),
                        r=[srck, "vec", "xc%d" % c], w=["xc%d" % c])
            xck = ["xc%d" % c for c in range(4)]
            S.act(lambda e: e.activation(out=xcb[:], in_=xc[:], func=ACT.Copy), r=xck, w=["xcb"])
            b0, pk = psbank(2)
            pr = psf(b0).rearrange("p (c n) -> p c n", c=4)
            pi = psf(b0 + 1).rearrange("p (c n) -> p c n", c=4)
            for c in range(4):
                S.pe(lambda e, c=c: e.matmul(pr[:, c, :], lhsT=wa[:, c, :], rhs=xcb[:, c, :], start=True, stop=True),
                     r=["wa", "xcb"], w=[pk[0]])
            for c in range(4):
                S.pe(lambda e, c=c: e.matmul(pi[:, c, :], lhsT=wx[:, c, :], rhs=xcb[:, c, :], start=True, stop=True),
                     r=["wx", "xcb"], w=[pk[1]])
            for c in range(4):
                S.act(lambda e, c=c: e.activation(out=rg[:, c, :], in_=pr[:, c, :], func=ACT.Sigmoid,
                                                  bias=vcol(V_BA + c), scale=1.0),
                      r=[pk[0], "vec"], w=["rg"])
            for c in range(4):
                S.act(lambda e, c=c: e.activation(out=ig[:, c, :], in_=pi[:, c, :], func=ACT.Sigmoid,
                                                  bias=vcol(V_BX + c), scale=1.0),
                      r=[pk[1], "vec"], w=["ig"])
            for c in range(4):
                S.act(lambda e, c=c: e.activation(out=av[:, c, :], in_=rg[:, c, :], func=ACT.Exp, scale=cl[:, c:c + 1]),
                      r=["rg", "cl"], w=["av"])
            bvf, avf = bv[:].rearrange("p a b -> p (a b)"), av[:].rearrange("p a b -> p (a b)")
            S.dve(lambda e: e.scalar_tensor_tensor(out=bvf, in0=avf, scalar=-1.0, in1=avf, op0=ALU.mult, op1=ALU.mult),
                  r=["av"], w=["bv"])
            S.act(lambda e: e.activation(out=bv[:], in_=bv[:], func=ACT.Sqrt, bias=vcol(V_ONE), scale=1.0),
                  r=["bv", "vec"], w=["bv"])
            S.pool(lambda e: e.tensor_tensor(out=ig[:], in0=ig[:], in1=xc[:], op=ALU.mult), r=["ig"] + xck, w=["ig"])
            S.dve(lambda e: e.tensor_tensor(out=bv[:], in0=bv[:], in1=ig[:], op=ALU.mult), r=["bv", "ig"], w=["bv"])
            if smp:
                v4 = lambda t_, tt: t_[:, :, 0:64].rearrange("p c (s t) -> p c s t", t=4)[:, :, :, tt]
                for tt in range(4):
                    hprev = HS[:] if tt == 0 else v4(hv, tt - 1)
                    S.dve(lambda e, tt=tt, hprev=hprev: e.tensor_tensor(out=v4(hv, tt), in0=v4(av, tt), in1=hprev,
                                                                        op=ALU.mult), r=["av", "hv", "HS"], w=["hv"])
                    S.dve(lambda e, tt=tt: e.tensor_tensor(out=v4(hv, tt), in0=v4(hv, tt), in1=v4(bv, tt),
                                                           op=ALU.add), r=["bv", "hv"], w=["hv"])
                S.dve(lambda e: e.tensor_copy(out=HS2[:], in_=v4(hv, 3)), r=["hv"], w=["HS2"])
                return
            for c in range(4):
                S.dve(lambda e, c=c: e.tensor_tensor_scan(out=hv[:, c, :], data0=av[:, c, :], data1=bv[:, c, :],
                                                           initial=hc[:, c:c + 1], op0=ALU.mult, op1=ALU.add),
                      r=["av", "bv", "hc"], w=["hv"])
            S.dve(lambda e: e.tensor_copy(out=hc[:], in_=hv[:, :, 127]), r=["hv"], w=["hc"])

        def attn_core(nh, Sps, Skeys, scale, sinkcol, Pbuf, Pkey, nkb, rows=128):
            W = nkb * 128
            S.dve(lambda e: e.reduce_max(out=st[:rows, 16:16 + nh], in_=Sps[:rows], axis=AX.X), r=Skeys, w=["st_mx"])
            if sinkcol is not None:
                S.dve(lambda e: e.tensor_tensor(out=st[:rows, 16:16 + nh], in0=st[:rows, 16:16 + nh],
                                                in1=sink8[:rows, :], op=ALU.max),
                      r=["st_mx", "sink8"], w=["st_mx"])
            S.dve(lambda e: e.tensor_scalar(out=st[:rows, 24:24 + nh], in0=st[:rows, 16:16 + nh], scalar1=-scale,
                                            scalar2=None, op0=ALU.mult),
                  r=["st_mx"], w=["st_nm"])
            for h in range(nh):
                S.act(lambda e, h=h: e.activation(out=Pbuf[:rows, h, 0:W], in_=Sps[:rows, h, :], func=ACT.Exp,
                                                  bias=st[:rows, 24 + h:25 + h], scale=scale,
                                                  accum_out=st[:rows, 32 + h:33 + h]),
                      r=Skeys + ["st_nm"], w=[Pkey, "st_rs%d" % h])
            rsk = ["st_rs%d" % h for h in range(nh)]
            if sinkcol is not None:
                S.dve(lambda e: e.tensor_tensor(out=st[:rows, 40:48], in0=st[:rows, 24:32],
                                                in1=vec[:rows, sinkcol:sinkcol + 8], op=ALU.add),
                      r=["st_nm", "vec"], w=["st_es"])
                S.act(lambda e: e.activation(out=st[:rows, 40:48], in_=st[:rows, 40:48], func=ACT.Exp),
                      r=["st_es"], w=["st_es"])
                S.dve(lambda e: e.tensor_tensor(out=st[:rows, 32:40], in0=st[:rows, 32:40], in1=st[:rows, 40:48],
                                                op=ALU.add),
                      r=["st_es"] + rsk, w=rsk)
            S.dve(lambda e: e.reciprocal(out=st[:rows, 48:48 + nh], in_=st[:rows, 32:32 + nh]), r=rsk, w=["st_ri"])
            for h in range(nh):
                S.dve(lambda e, h=h: e.tensor_scalar(out=Pbuf[:rows, h, 0:W], in0=Pbuf[:rows, h, 0:W],
                                                     scalar1=st[:rows, 48 + h:49 + h], scalar2=None, op0=ALU.mult),
                      r=[Pkey, "st_ri"], w=[Pkey])
            nt = nh * nkb
            nb = (nt * 128 + 1023) // 1024
            b0, pk = psbank(nb)
            pv = psb(b0, nb).rearrange("p (t n) -> p t n", n=128)
            for h in range(nh):
                for kb in range(nkb):
                    t = h * nkb + kb
                    S.pe(lambda e, h=h, kb=kb, t=t: e.transpose(out=pv[:, t, 0:rows],
                                                                in_=Pbuf[:rows, h, kb * 128:(kb + 1) * 128],
                                                                identity=ident[:rows, :rows]),
                         r=[Pkey, "ident"], w=[pk[(t * 128) // 1024]])
            half = nt // 2
            if nb == 1:
                S.act(lambda e: e.activation(out=PT[:, 0:nt, 0:rows], in_=pv[:, 0:nt, 0:rows], func=ACT.Copy),
                      r=pk, w=["PTa", "PTb"])
            else:
                S.act(lambda e: e.activation(out=PT[:, 0:half, 0:rows], in_=pv[:, 0:half, 0:rows], func=ACT.Copy),
                      r=pk, w=["PTa"])
                S.dve(lambda e: e.tensor_copy(out=PT[:, half:nt, 0:rows], in_=pv[:, half:nt, 0:rows]),
                      r=pk, w=["PTb"])

        def swa_stage(ti, nTt, nTkey, first_block, bias_first, rows=128, qcols=None):
            cb, pb = ti % 2, (ti + 1) % 2
            pq, kq = fm_proj(w_in, "w_in", 1024, 4, nTt, nTkey)
            S.act(lambda e: e.activation(out=QT[:], in_=pq, func=ACT.Copy), r=kq, w=["QT"])
            b0, pk = psbank(1)
            pkk = psf(b0)[:, 0:128]
            pkv = psf(b0)[:, 128:384]
            for k in range(8):
                S.pe(lambda e, k=k: e.matmul(pkk, lhsT=w_in[:, k, 1536:1664], rhs=nTt[:, k, :],
                                             start=(k == 0), stop=(k == 7)), r=["w_in", nTkey], w=pk)
            for k in range(8):
                S.pe(lambda e, k=k: e.matmul(pkv, lhsT=nTt[:, k, :], rhs=w_in[:, k, 1536:1792],
                                             start=(k == 0), stop=(k == 7)), r=["w_in", nTkey], w=pk)
            S.act(lambda e: e.activation(out=KT[cb][:], in_=pkk, func=ACT.Copy), r=pk, w=["KT%d" % cb])
            S.act(lambda e: e.activation(out=Vp[cb][:, 0, 0:64], in_=pkv[:, 128:192], func=ACT.Copy), r=pk, w=["Vp%d" % cb])
            S.act(lambda e: e.activation(out=Vp[cb][:, 1, 64:128], in_=pkv[:, 192:256], func=ACT.Copy), r=pk, w=["Vp%d" % cb])
            S.act(lambda e: e.activation(out=kvf[:], in_=pkv, func=ACT.Copy), r=pk, w=["kvf"])
            return cb, pb

        def swa_attend(cb, pb, bias_first, smp=None):
            b0, sk = psbank(4)
            Sps = psf(b0, 4).rearrange("p (h n) -> p h n", h=8)
            if smp is None:
                rows, q0 = 128, 0
                xkeys = []
            else:
                rows, q0 = 4, 4 * smp["s"]
                xkeys = smp["keys"]
            idn = ident[0:rows, 0:rows]
            for j in range(4):
                for b in range(2):
                    s_ = 2 * j + b
                    key = [sk[s_ // 2]]
                    lo, hi = 64 * b, 64 * b + 64
                    if smp is None:
                        kprev = KT[pb][lo:hi, :]
                        bprev = bias0[:, s_, :] if bias_first else biasT[:, s_, 0:128]
                        bown = biasT[:, s_, 128:256]
                        kpk = "KT%d" % pb
                    else:
                        kprev = smp["KTc"][lo:hi, smp["s"], :]
                        bprev = biasT[0:4, s_, 0:128]
                        o0 = 128 - 4 * smp["s"]
                        bown = smp["Bown"][0:4, s_, o0:o0 + 128]
                        kpk = "KT%d" % cb
                    S.pe(lambda e, j=j, lo=lo, hi=hi, s_=s_, kprev=kprev: e.matmul(
                        Sps[0:rows, s_, 0:128], lhsT=QT[lo:hi, j, q0:q0 + rows], rhs=kprev, start=True, stop=False),
                        r=["QT", kpk] + xkeys, w=key)
                    S.pe(lambda e, s_=s_, bprev=bprev: e.matmul(Sps[0:rows, s_, 0:128], lhsT=idn, rhs=bprev,
                                                                start=False, stop=True),
                         r=["ident", "bias0", "biasT"], w=key)
                    S.pe(lambda e, j=j, lo=lo, hi=hi, s_=s_: e.matmul(
                        Sps[0:rows, s_, 128:256], lhsT=QT[lo:hi, j, q0:q0 + rows], rhs=KT[cb][lo:hi, :],
                        start=True, stop=False), r=["QT", "KT%d" % cb], w=key)
                    S.pe(lambda e, s_=s_, bown=bown: e.matmul(Sps[0:rows, s_, 128:256], lhsT=idn, rhs=bown,
                                                              start=False, stop=True),
                         r=["ident", "biasT"] + xkeys, w=key)
            attn_core(8, Sps, sk, 0.125, V_SINK, Pm, "Pm", 2, rows=rows)
            b0, ok = psbank(1)
            pO = psf(b0).rearrange("p (c n) -> p c n", c=4)
            for j in range(4):
                n = 0
                for b in range(2):
                    s_ = 2 * j + b
                    for kb in (0, 1):
                        if kb == 1:
                            vap, vk = Vp[cb][:, b, :], ["Vp%d" % cb]
                        elif smp is None:
                            vap, vk = Vp[pb][:, b, :], ["Vp%d" % pb]
                        else:
                            vap, vk = smp["Vc"][b][:, smp["s"], :], xkeys
                        S.pe(lambda e, j=j, s_=s_, kb=kb, vap=vap, n=n: e.matmul(
                            pO[:, j, 0:rows], lhsT=vap, rhs=PT[:, s_ * 2 + kb, 0:rows],
                            start=(n == 0), stop=(n == 3)),
                            r=vk + ["PTa", "PTb"], w=ok)
                        n += 1
            S.act(lambda e: e.activation(out=yT[:, 4:8, q0:q0 + rows], in_=pO[:, :, 0:rows], func=ACT.Copy),
                  r=ok, w=["yT_att"])

        def mixer_out(xti, xkey, nTt, nTkey):
            pg, kg = fm_proj(w_in, "w_in", 512, 4, nTt, nTkey)
            S.act(lambda e: e.activation(out=gg[:], in_=pg, func=ACT.Gelu_apprx_tanh), r=kg, w=["rg"])
            S.dve(lambda e: e.tensor_tensor(out=yT[:, 0:4, :], in0=hv[:], in1=gg[:], op=ALU.mult),
                  r=["hv", "rg"], w=["yT_lru"])
            b0, pk = psbank(2)
            po = psf(b0, 2)
            for hh in range(2):
                for k in range(8):
                    S.pe(lambda e, hh=hh, k=k: e.matmul(po[:, hh * 512:(hh + 1) * 512], lhsT=yT[:, k, :],
                                                        rhs=w_out[:, k, hh * 512:(hh + 1) * 512],
                                                        start=(k == 0), stop=(k == 7)),
                         r=["yT_lru", "yT_att", "w_out"], w=[pk[hh]])
            S.dve(lambda e: e.tensor_tensor(out=xti[:], in0=xti[:], in1=po, op=ALU.add), r=[xkey] + pk, w=[xkey])

        def cross_attend(smp=None):
            if smp is None:
                rows, q0, kT, vv, xkeys = 128, 0, mkT, mvb, ["mkT", "mvb"]
            else:
                rows, q0, kT, vv, xkeys = 4, 4 * smp["s"], smp["mkT"], smp["mv"], smp["keys"]
            b0, sk = psbank(2)
            Sps = psf(b0, 2).rearrange("p (h n) -> p h n", h=4)
            for h in range(4):
                S.pe(lambda e, h=h: e.matmul(Sps[0:rows, h, :], lhsT=QC[:, h, q0:q0 + rows], rhs=kT[:, h, :],
                                             start=True, stop=True),
                     r=["QC"] + xkeys, w=[sk[h // 2]])
            attn_core(4, Sps, sk, SC_MEM, None, Pm, "Pm", 2, rows=rows)
            b0, ok = psbank(1)
            pO = psf(b0).rearrange("p (c n) -> p c n", c=4)
            for h in range(4):
                for kb in range(2):
                    S.pe(lambda e, h=h, kb=kb: e.matmul(pO[:, h, 0:rows], lhsT=vv[:, kb, h * 128:(h + 1) * 128],
                                                        rhs=PT[:, h * 2 + kb, 0:rows], start=(kb == 0), stop=(kb == 1)),
                         r=xkeys + ["PTa", "PTb"], w=ok)
            S.act(lambda e: e.activation(out=OC[:, :, q0:q0 + rows], in_=pO[:, :, 0:rows], func=ACT.Copy),
                  r=ok, w=["OC"])

        def cross_stage(xti, xkey, nTt, nTkey, smp_iter=None):
            rmsnorm_T(xti[:], xkey, V_GCROSS, nTt[:], nTkey, "n2")
            pq, kq = fm_proj(w_q, "w_q", 0, 4, nTt, nTkey)
            S.act(lambda e: e.activation(out=QC[:], in_=pq, func=ACT.Copy), r=kq, w=["QC"])
            if smp_iter is None:
                cross_attend(None)
            else:
                smp_iter()
            b0, pk = psbank(2)
            po = psf(b0, 2)
            for hh in range(2):
                for k in range(4):
                    S.pe(lambda e, hh=hh, k=k: e.matmul(po[:, hh * 512:(hh + 1) * 512], lhsT=OC[:, k, :],
                                                        rhs=w_o[:, k, hh * 512:(hh + 1) * 512],
                                                        start=(k == 0), stop=(k == 3)),
                         r=["OC", "w_o"], w=[pk[hh]])
            S.dve(lambda e: e.tensor_tensor(out=xti[:], in0=xti[:], in1=po, op=ALU.add), r=[xkey] + pk, w=[xkey])

        v_k8 = lambda t: t[:].rearrange("p (k n) -> p k n", k=8)
        wq = []
        wissued = [0]

        def wview(i):
            return ring[i % NRING][:].rearrange("p (k n) -> p k n", k=wq[i][1])

        def wq_add_macro(gate_only=False):
            base = len(wq)
            if gate_only:
                for g in range(6):
                    wq.append((w_g_d[:, g * 512:(g + 1) * 512].rearrange("(k p) n -> p k n", p=128), 8))
                return base
            for fh in range(2):
                for gi in range(3):
                    g = 3 * fh + gi
                    wq.append((w_g_d[:, g * 512:(g + 1) * 512].rearrange("(k p) n -> p k n", p=128), 8))
                    wq.append((w_u_d[:, g * 512:(g + 1) * 512].rearrange("(k p) n -> p k n", p=128), 8))
                for gi in range(3):
                    g = 3 * fh + gi
                    wq.append((w_d_d[g * 512:(g + 1) * 512, :].rearrange("(k p) n -> p k n", p=128), 4))
            return base

        def wq_issue(upto):
            while wissued[0] < len(wq) and wissued[0] <= upto:
                j = wissued[0]
                S.dma("gpsimd", wview(j), wq[j][0], w=["ring%d" % (j % NRING)])
                wissued[0] += 1

        def wq_get(i):
            wq_issue(i)
            return wview(i), "ring%d" % (i % NRING)

        def wq_done(i):
            wq_issue(i + NRING)

        def load_x(ti, slot):
            S.dma("sync", xt[slot][:], xs[ti * 128:(ti + 1) * 128, :], w=["xt%d" % slot])

        for mt in range(2):
            S.dma("sync", xt[mt][:], memx[mt * 128:(mt + 1) * 128, :], w=["xt%d" % mt])
        for mt in range(2):
            rmsnorm_T(xt[mt][:], "xt%d" % mt, V_GMEM, nT[mt][:], "nT%d" % mt, "nm")
            for (wsl, wkey, od, is_k) in ((wk_s, "ring0", omk_d, True), (wv_s, "ring1", omv_d, False)):
                b0, pk = psbank(1)
                pm = psf(b0)
                for k in range(8):
                    S.pe(lambda e, k=k, wsl=wsl, pm=pm, mt=mt: e.matmul(pm, lhsT=nT[mt][:, k, :], rhs=wsl[:, k, :],
                                                                 start=(k == 0), stop=(k == 7)),
                         r=["nT%d" % mt, wkey], w=pk)
                S.act(lambda e, pm=pm: e.activation(out=mtmp[:], in_=pm, func=ACT.Copy), r=pk, w=["t1_0"])
                if not is_k:
                    S.dve(lambda e, mt=mt: e.tensor_copy(out=mvb[:, mt, :], in_=mtmp[:]), r=["t1_0"], w=["mvb"])
                S.dma("sync", od[mt * 128:(mt + 1) * 128, :], mtmp[:], r=["t1_0"])
            pkT, kk = fm_proj(wk_s, "ring0", 0, 4, nT[mt], "nT%d" % mt)
            S.act(lambda e, pkT=pkT, mt=mt: e.activation(out=mkT[:, :, mt * 128:(mt + 1) * 128], in_=pkT, func=ACT.Copy),
                  r=kk, w=["mkT"])

        def ffn_macro(tiles, rows_out, N, halo=Gh, halokey="Gh", smp=None):
            wb = wq_add_macro()
            for fh in range(2):
                for gi in range(3):
                    g = 3 * fh + gi
                    wg3, wgk = wq_get(wb + fh * 9 + 2 * gi)
                    wu3, wuk = wq_get(wb + fh * 9 + 2 * gi + 1)
                    for c in range(4):
                        fc = g * 4 + c
                        hmi = gi * 4 + c
                        i2 = fc % 2
                        b0, pk = psbank(2)
                        pG, pU = psf(b0)[:, 0:N], psf(b0 + 1)[:, 0:N]
                        for k in range(8):
                            S.pe(lambda e, k=k, c=c, wg3=wg3, pG=pG: e.matmul(
                                pG, lhsT=wg3[:, k, c * 128:(c + 1) * 128], rhs=n3T[:, k, 0:N],
                                start=(k == 0), stop=(k == 7)), r=[wgk, "n3T"], w=[pk[0]])
                        for k in range(8):
                            S.pe(lambda e, k=k, c=c, wu3=wu3, pU=pU: e.matmul(
                                pU, lhsT=wu3[:, k, c * 128:(c + 1) * 128], rhs=n3T[:, k, 0:N],
                                start=(k == 0), stop=(k == 7)), r=[wuk, "n3T"], w=[pk[1]])
                        Gsb, gk = Gs[i2], "Gs%d" % i2
                        tk = "t1_%d" % i2
                        if smp is None:
                            S.dve(lambda e, fc=fc, Gsb=Gsb: e.tensor_copy(out=Gsb[:, 0:2], in_=halo[:, fc, :]),
                                  r=[halokey + "%d" % fc, halokey], w=[gk])
                            S.act(lambda e, Gsb=Gsb, pG=pG: e.activation(out=Gsb[:, 2:2 + N], in_=pG, func=ACT.Copy),
                                  r=[pk[0]], w=[gk])
                            S.dve(lambda e, fc=fc, Gsb=Gsb: e.tensor_copy(out=halo[:, fc, :], in_=Gsb[:, N:N + 2]),
                                  r=[gk], w=[halokey + "%d" % fc])
                            t1v = t1[i2][:, 0:N]
                            g_tap = lambda tap, Gsb=Gsb: Gsb[:, tap:tap + N]
                            pGv = pG
                        else:
                            FS, FSn, fkeys = smp["FS"], smp["FSn"], smp["keys"]
                            G3 = Gsb[:, 0:96].rearrange("p (s t) -> p s t", t=6)
                            S.dve(lambda e, fc=fc, G3=G3, FS=FS: e.tensor_copy(out=G3[:, :, 0:2], in_=FS[:, fc, :, :]),
                                  r=[fkeys[0]], w=[gk])
                            S.act(lambda e, G3=G3, pG=pG: e.activation(
                                out=G3[:, :, 2:6], in_=pG[:, 0:64].rearrange("p (s t) -> p s t", t=4), func=ACT.Copy),
                                r=[pk[0]], w=[gk])
                            S.dve(lambda e, fc=fc, G3=G3, FSn=FSn: e.tensor_copy(out=FSn[:, fc, :, :], in_=G3[:, :, 4:6]),
                                  r=[gk], w=[fkeys[1]])
                            t1v = t1[i2][:, 0:64].rearrange("p (s t) -> p s t", t=4)
                            g_tap = lambda tap, G3=G3: G3[:, :, tap:tap + 4]
                            pGv = pG[:, 0:64].rearrange("p (s t) -> p s t", t=4)
                        S.act(lambda e, fc=fc, pGv=pGv, t1v=t1v: e.activation(
                            out=t1v, in_=pGv, func=ACT.Identity, bias=vcol(V_FB + fc),
                            scale=vcol(V_FW + 48 + fc)), r=[pk[0], "vec"], w=[tk])
                        for tap in (1, 0):
                            S.dve(lambda e, fc=fc, tap=tap, g_tap=g_tap, t1v=t1v: e.scalar_tensor_tensor(
                                out=t1v, in0=g_tap(tap), scalar=vcol(V_FW + tap * 24 + fc),
                                in1=t1v, op0=ALU.mult, op1=ALU.add),
                                r=[gk, "vec", tk], w=[tk])
                        S.act(lambda e, i2=i2: e.activation(out=t1[i2][:, 0:N], in_=t1[i2][:, 0:N],
                                                            func=ACT.Gelu_apprx_tanh), r=[tk], w=[tk])
                        S.dve(lambda e, hmi=hmi, i2=i2, pU=pU: e.tensor_tensor(out=hm[:, hmi, 0:N], in0=t1[i2][:, 0:N],
                                                                               in1=pU, op=ALU.mult),
                              r=[tk, pk[1]], w=["hm%d" % hmi])
                    wq_done(wb + fh * 9 + 2 * gi + 1)
                wds = [wq_get(wb + fh * 9 + 6 + q) for q in range(3)]
                for j, tj in enumerate(tiles):
                    sl = xslot(tj)
                    for hh in range(2):
                        b0, pk = psbank(1)
                        pD = psf(b0)
                        for q in range(3):
                            for c in range(4):
                                hmi = q * 4 + c
                                S.pe(lambda e, q=q, c=c, hmi=hmi, j=j, hh=hh, pD=pD, wds=wds: e.matmul(
                                    pD, lhsT=hm[:, hmi, j * 128:(j + 1) * 128],
                                    rhs=wds[q][0][:, c, hh * 512:(hh + 1) * 512],
                                    start=(hmi == 0), stop=(hmi == 11)),
                                    r=[wds[q][1], "hm%d" % hmi], w=pk)
                        S.dve(lambda e, sl=sl, hh=hh, pD=pD: e.tensor_tensor(
                            out=xt[sl][:, hh * 512:(hh + 1) * 512], in0=xt[sl][:, hh * 512:(hh + 1) * 512],
                            in1=pD, op=ALU.add), r=["xt%d" % sl] + pk, w=["xt%d" % sl])
                wq_done(wb + fh * 9 + 8)
            for j, tj in enumerate(tiles):
                sl = xslot(tj)
                xk2 = "xt%d" % sl
                S.act(lambda e, sl=sl: e.activation(out=junk[:], in_=xt[sl][:], func=ACT.Square, accum_out=st[:, 8:9]),
                      r=[xk2], w=["nbf", "st_ss"])
                S.act(lambda e: e.activation(out=st[:, 9:10], in_=st[:, 8:9], func=ACT.Sqrt, bias=vcol(V_EPS),
                                             scale=1.0 / D), r=["st_ss", "vec"], w=["st_sd"])
                S.dve(lambda e: e.reciprocal(out=st[:, 10:11], in_=st[:, 9:10]), r=["st_sd"], w=["st_rs"])
                S.dve(lambda e, sl=sl: e.scalar_tensor_tensor(out=xt[sl][:], in0=xt[sl][:], scalar=st[:, 10:11],
                                                              in1=gfin[:], op0=ALU.mult, op1=ALU.mult),
                      r=[xk2, "st_rs", "gfin"], w=[xk2])
                S.dma("sync", y_d[rows_out[j]:rows_out[j] + 128, :], xt[sl][:], r=[xk2])

        total = n_pre + n_main
        xslot = lambda ti: ti % NXB
        load_x(0, 0)
        import os
        n_fast = ((n_pre - 4) // 4) * 4 if (n_pre >= 8 and not os.environ.get("NOFAST")) else 0
        if n_fast:
            XCv = ring[0].bitcast(F32)[:, 0:2048].rearrange("p (c n) -> p c n", c=4)
            RGv = ring[1].bitcast(F32)[:, 0:2048].rearrange("p (c n) -> p c n", c=4)
            IGv = ring[2].bitcast(F32)[:, 0:2048].rearrange("p (c n) -> p c n", c=4)
            BVv = ring[3].bitcast(F32)[:, 0:2048].rearrange("p (c n) -> p c n", c=4)
            XBv = hm.bitcast(F32)[:].rearrange("p a b -> p (a b)")[:, 0:2060].rearrange("p (c n) -> p c n", c=4)
            XCBv = w_out[:].rearrange("p a b -> p (a b)")[:, 0:2048].rearrange("p (c n) -> p c n", c=4)
            basekeys = ["ring0", "ring1", "ring2", "ring3", "w_out"] + ["hm%d" % i for i in range(12)]
            ckeys = []
            for c in range(4):
                ckeys += ["fXC%d" % c, "fRG%d" % c, "fIG%d" % c, "fBV%d" % c, "fXB%d" % c, "fXCB%d" % c]
            S.pool(lambda e: e.memset(fence[:, 0:1], 0.0),
                   r=["w_in", "w_q", "w_o", "ident", "biasT", "bias0", "wa", "wx", "vec", "gfin", "mkT", "mvb"],
                   w=basekeys + ckeys)
            S.pool(lambda e: e.memset(HALO[:], 0.0), w=["HALO"])

            def fast_s1(c):
                b0, pk = psbank(1)
                pxr = psf(b0)
                for k in range(8):
                    S.pe(lambda e, c=c, k=k, pxr=pxr: e.matmul(pxr, lhsT=w_in[:, k, c * 128:(c + 1) * 128],
                                                               rhs=n3T[:, k, :], start=(k == 0), stop=(k == 7)),
                         r=["w_in", "n3T"], w=pk)
                S.dve(lambda e, c=c: e.tensor_copy(out=XBv[:, c, 0:3], in_=HALO[:, c, :]), r=["HALO"], w=["fXB%d" % c])
                S.act(lambda e, c=c, pxr=pxr: e.activation(out=XBv[:, c, 3:515], in_=pxr, func=ACT.Copy),
                      r=pk, w=["fXB%d" % c])
                S.dve(lambda e, c=c: e.tensor_copy(out=HALO[:, c, :], in_=XBv[:, c, 512:515]),
                      r=["fXB%d" % c], w=["HALO"])
                S.dve(lambda e, c=c: e.tensor_scalar(out=XCv[:, c, :], in0=XBv[:, c, 3:515],
                                                     scalar1=vcol(V_CW + 12 + c), scalar2=vcol(V_BCONV + c),
                                                     op0=ALU.mult, op1=ALU.add),
                      r=["fXB%d" % c, "vec"], w=["fXC%d" % c])
                for tap in range(3):
                    S.dve(lambda e, c=c, tap=tap: e.scalar_tensor_tensor(
                        out=XCv[:, c, :], in0=XBv[:, c, tap:tap + 512], scalar=vcol(V_CW + tap * 4 + c),
                        in1=XCv[:, c, :], op0=ALU.mult, op1=ALU.add),
                        r=["fXB%d" % c, "vec", "fXC%d" % c], w=["fXC%d" % c])
                S.act(lambda e, c=c: e.activation(out=XCBv[:, c, :], in_=XCv[:, c, :], func=ACT.Copy),
                      r=["fXC%d" % c], w=["fXCB%d" % c])

            def fast_s2(c):
                b0, pk = psbank(2)
                pr, pi = psf(b0), psf(b0 + 1)
                S.pe(lambda e, c=c, pr=pr: e.matmul(pr, lhsT=wa[:, c, :], rhs=XCBv[:, c, :], start=True, stop=True),
                     r=["wa", "fXCB%d" % c], w=[pk[0]])
                S.pe(lambda e, c=c, pi=pi: e.matmul(pi, lhsT=wx[:, c, :], rhs=XCBv[:, c, :], start=True, stop=True),
                     r=["wx", "fXCB%d" % c], w=[pk[1]])
                S.act(lambda e, c=c, pr=pr: e.activation(out=RGv[:, c, :], in_=pr, func=ACT.Sigmoid,
                                                         bias=vcol(V_BA + c), scale=1.0),
                      r=[pk[0], "vec"], w=["fRG%d" % c])
                S.act(lambda e, c=c, pi=pi: e.activation(out=IGv[:, c, :], in_=pi, func=ACT.Sigmoid,
                                                         bias=vcol(V_BX + c), scale=1.0),
                      r=[pk[1], "vec"], w=["fIG%d" % c])
                S.act(lambda e, c=c: e.activation(out=RGv[:, c, :], in_=RGv[:, c, :], func=ACT.Exp,
                                                  scale=cl[:, c:c + 1]), r=["fRG%d" % c, "cl"], w=["fRG%d" % c])
                S.act(lambda e, c=c: e.activation(out=BVv[:, c, :], in_=RGv[:, c, :], func=ACT.Square),
                      r=["fRG%d" % c], w=["fBV%d" % c])
                S.act(lambda e, c=c: e.activation(out=BVv[:, c, :], in_=BVv[:, c, :], func=ACT.Sqrt,
                                                  bias=vcol(V_ONE), scale=-1.0),
                      r=["fBV%d" % c, "vec"], w=["fBV%d" % c])
                S.dve(lambda e, c=c: e.tensor_tensor(out=IGv[:, c, :], in0=IGv[:, c, :], in1=XCv[:, c, :], op=ALU.mult),
                      r=["fIG%d" % c, "fXC%d" % c], w=["fIG%d" % c])
                S.dve(lambda e, c=c: e.tensor_tensor(out=BVv[:, c, :], in0=BVv[:, c, :], in1=IGv[:, c, :], op=ALU.mult),
                      r=["fBV%d" % c, "fIG%d" % c], w=["fBV%d" % c])
                S.dve(lambda e, c=c: e.tensor_tensor_scan(out=IGv[:, c, :], data0=RGv[:, c, :], data1=BVv[:, c, :],
                                                           initial=hc[:, c:c + 1], op0=ALU.mult, op1=ALU.add),
                      r=["fRG%d" % c, "fBV%d" % c, "hc"], w=["fIG%d" % c])
                S.dve(lambda e, c=c: e.tensor_copy(out=hc[:, c:c + 1], in_=IGv[:, c, 511:512]),
                      r=["fIG%d" % c], w=["hc"])

            for m0 in range(0, n_fast, 4):
                for j in range(4):
                    tj = m0 + j
                    if tj + 1 < total:
                        load_x(tj + 1, xslot(tj + 1))
                    rmsnorm_T(xt[xslot(tj)][:], "xt%d" % xslot(tj), V_GMIX, n3T[:, :, j * 128:(j + 1) * 128], "n3T", "n1")
                fast_s1(0)
                fast_s1(1)
                fast_s2(0)
                fast_s1(2)
                fast_s2(1)
                fast_s1(3)
                fast_s2(2)
                fast_s2(3)
                if (m0 + 4) % 16 == 0:
                    fcol = V_FLAG + (m0 + 4) // 16 - 1
                    S.dve(lambda e, fcol=fcol: e.tensor_scalar(out=hc[:], in0=hc[:], scalar1=vcol(fcol), scalar2=None,
                                                               op0=ALU.mult), r=["hc", "vec"], w=["hc"])
            S.pool(lambda e: e.tensor_copy(out=xrp[(n_fast + 1) % 2][:, :, 128:131], in_=HALO[:]),
                   r=["HALO"], w=["xrp%d" % ((n_fast + 1) % 2)])
            S.pool(lambda e: e.memset(fence[:, 1:2], 0.0), r=ckeys, w=basekeys)
        S.dma("gpsimd", w_out[:], w_out_d.rearrange("(k p) n -> p k n", p=128), w=["w_out"])
        for ti in range(n_fast, total):
            slot = xslot(ti)
            xk = "xt%d" % slot
            if ti + 1 < total:
                load_x(ti + 1, xslot(ti + 1))
            nTt = nT[ti % 2]
            nTk = "nT%d" % (ti % 2)
            rmsnorm_T(xt[slot][:], xk, V_GMIX, nTt[:], nTk, "n1")
            if ti >= n_pre - 2:
                cb, pb = swa_stage(ti, nTt, nTk, False, False)
            lru_stage(ti, nTt, nTk)
            full = ti >= n_pre - 1
            if ti < n_pre and (ti + 1) % 16 == 0:
                fcol = V_FLAG + (ti + 1) // 16 - 1
                S.dve(lambda e, fcol=fcol: e.tensor_scalar(out=hc[:], in0=hc[:], scalar1=vcol(fcol), scalar2=None,
                                                           op0=ALU.mult), r=["hc", "vec"], w=["hc"])
            if not full:
                continue
            swa_attend(cb, pb, ti == n_pre)
            mixer_out(xt[slot], xk, nTt, nTk)
            cross_stage(xt[slot], xk, nTt, nTk)
            mi = (ti - n_pre) % 4 if ti >= n_pre else 0
            if ti == n_pre - 1:
                rmsnorm_T(xt[slot][:], xk, V_GFFN, nTt[:], nTk, "n3")
                b0, pk = psbank(1)
                ph = psf(b0)[:, 0:48].rearrange("p (c n) -> p c n", n=2)
                wb = wq_add_macro(gate_only=True)
                for g in range(6):
                    wg3, wgk = wq_get(wb + g)
                    for c in range(4):
                        for k in range(8):
                            S.pe(lambda e, g=g, c=c, k=k, wg3=wg3, ph=ph, nTt=nTt: e.matmul(
                                ph[:, g * 4 + c, :], lhsT=wg3[:, k, c * 128:(c + 1) * 128], rhs=nTt[:, k, 126:128],
                                start=(k == 0), stop=(k == 7)), r=[wgk, nTk], w=pk)
                    wq_done(wb + g)
                S.dve(lambda e, ph=ph: e.tensor_scalar(out=Gh[:], in0=ph, scalar1=vcol(V_FLAG + 3), scalar2=None, op0=ALU.mult),
                      r=pk + ["vec"], w=["Gh"])
                continue
            rmsnorm_T(xt[slot][:], xk, V_GFFN, n3T[:, :, mi * 128:(mi + 1) * 128], "n3T", "n3")
            if ti == total - 1:
                S.dma("sync", okv_d, kvf[:], r=["kvf"])
                S.dve(lambda e, ti=ti: e.tensor_copy(out=XC2[:, :, 0:3], in_=xrp[ti % 2][:, :, 128:131]),
                      r=["xrp%d" % (ti % 2)], w=["XC2"])
                b0, pk = psbank(1)
                pq_ = psf(b0)
                for c in range(4):
                    S.pe(lambda e, c=c, pq_=pq_: e.transpose(out=pq_[0:3, c * 128:(c + 1) * 128], in_=XC2[:, c, 0:3],
                                                             identity=ident32[:]), r=["XC2", "ident32"], w=pk)
                S.act(lambda e, pq_=pq_: e.activation(out=stL[0:3, :], in_=pq_[0:3, :], func=ACT.Copy), r=pk, w=["stL"])
                S.dma("sync", olc_d, stL[0:3, :], r=["stL"])
                S.dve(lambda e: e.tensor_copy(out=HS2[:, :, 0:1], in_=hc[:].unsqueeze(2)), r=["hc"], w=["HS2"])
                b0, pk = psbank(1)
                ph_ = psf(b0)
                for c in range(4):
                    S.pe(lambda e, c=c, ph_=ph_: e.transpose(out=ph_[0:1, c * 128:(c + 1) * 128], in_=HS2[:, c, 0:1],
                                                             identity=ident32[:]), r=["HS2", "ident32"], w=pk)
                S.act(lambda e, ph_=ph_: e.activation(out=stH[0:1, :], in_=ph_[0:1, :], func=ACT.Copy), r=pk, w=["stH"])
                S.dma("sync", olh_d.rearrange("(a n) -> a n", a=1), stH[0:1, :], r=["stH"])
            if mi != 3:
                continue
            ffn_macro([ti - 3, ti - 2, ti - 1, ti], [(tj - n_pre) * 128 for tj in (ti - 3, ti - 2, ti - 1, ti)], 512)
            if ti == total - 1:
                for g in range(6):
                    b0, pk = psbank(1)
                    pg_ = psf(b0)
                    for c in range(4):
                        S.pe(lambda e, c=c, g=g, pg_=pg_: e.transpose(out=pg_[0:2, c * 128:(c + 1) * 128],
                                                                     in_=Gh[:, g * 4 + c, :], identity=ident32[:]),
                             r=["Gh", "ident32"] + ["Gh%d" % fc for fc in range(24)], w=pk)
                    S.act(lambda e, pg_=pg_: e.activation(out=stL[0:2, :], in_=pg_[0:2, :], func=ACT.Copy),
                          r=pk, w=["stL"])
                    S.dma("sync", ofc_d[:, g * 512:(g + 1) * 512], stL[0:2, :], r=["stL"])

        if do_sample:
            ti = total
            slot = xslot(ti)
            xk = "xt%d" % slot
            free = [i for i in range(NXB) if i != slot]
            fk = ["xt%d" % i for i in free]
            Vc = [xt[free[0]].bitcast(BF16)[:, 0:2048].rearrange("p (s d) -> p s d", d=128),
                  xt[free[1]].bitcast(BF16)[:, 0:2048].rearrange("p (s d) -> p s d", d=128)]
            FS = xt[free[2]][:, 0:768].rearrange("p (c s k) -> p c s k", c=24, k=2)
            FSn = xt[free[3]][:, 0:768].rearrange("p (c s k) -> p c s k", c=24, k=2)
            n3flat = n3T[:].rearrange("p a b -> p (a b)")
            KTc = n3flat[:, 0:2048].rearrange("p (s d) -> p s d", d=128)
            Bown = n3flat[:, 2048:4096].rearrange("p (h n) -> p h n", n=256)
            hmflat = hm[:].rearrange("p a b -> p (a b)")
            S.dma("sync", xt[slot][:], xs[ti * 128:(ti + 1) * 128, :], w=[xk])
            S.dma("sync", stL[0:48, :], slc_d, w=["stL"])
            S.dma("sync", stH[0:16, :], slh_d, w=["stH"])
            S.dma("sync", osk_d[:, 0:124, :], cswk_d[:, 4:128, :])
            S.dma("sync", osv_d[:, 0:124, :], cswv_d[:, 4:128, :])
            S.pool(lambda e: e.memset(xt[free[0]][:], 0.0), w=[fk[0]])
            S.pool(lambda e: e.memset(xt[free[1]][:], 0.0), w=[fk[1]])
            Kc = hmflat[:, 0:2048].rearrange("p (s d) -> p s d", d=128)
            hk03 = ["hm0", "hm1", "hm2", "hm3"]
            S.dma("gpsimd", Kc, cswk_d.rearrange("s k d -> k s d"), w=hk03)
            S.dma("gpsimd", Vc[0][:, :, 0:64], cswv_d.rearrange("s k d -> k s d")[:, :, 0:64], w=[fk[0]])
            S.dma("gpsimd", Vc[1][:, :, 64:128], cswv_d.rearrange("s k d -> k s d")[:, :, 64:128], w=[fk[1]])
            S.dma("gpsimd", Bown, bown_d.rearrange("p (h n) -> p h n", n=256), w=["n3T"])
            for half in range(2):
                b0, pk = psbank(1)
                pv = psb(b0).rearrange("p (t n) -> p t n", n=128)
                for i in range(8):
                    S.pe(lambda e, i=i, half=half, pv=pv: e.transpose(out=pv[:, i, :], in_=Kc[:, half * 8 + i, :],
                                                                      identity=ident[:]),
                         r=hk03 + ["ident"], w=pk)
                S.act(lambda e, half=half, pv=pv: e.activation(out=KTc[:, half * 8:(half + 1) * 8, :], in_=pv,
                                                               func=ACT.Copy), r=pk, w=["n3T"])
            b0, pk = psbank(1)
            p32 = psf(b0)
            for c in range(4):
                S.pe(lambda e, c=c: e.transpose(out=p32[:, c * 48:(c + 1) * 48], in_=stL[0:48, c * 128:(c + 1) * 128],
                                                identity=ident32[0:48, 0:48]), r=["stL", "ident32"], w=pk)
            S.dve(lambda e: e.tensor_copy(out=XP[:, :, :, 0:3],
                                          in_=p32[:, 0:192].rearrange("p (c s k) -> p c s k", c=4, k=3)),
                  r=pk, w=["XP"])
            b0, pk = psbank(1)
            p32b = psf(b0)
            for c in range(4):
                S.pe(lambda e, c=c: e.transpose(out=p32b[:, c * 16:(c + 1) * 16], in_=stH[0:16, c * 128:(c + 1) * 128],
                                                identity=ident32[0:16, 0:16]), r=["stH", "ident32"], w=pk)
            S.dve(lambda e: e.tensor_copy(out=HS[:], in_=p32b[:, 0:64].rearrange("p (c s) -> p c s", c=4)),
                  r=pk, w=["HS"])
            for g in range(6):
                S.dma("sync", stL[0:32, :], sfc_d[:, g * 512:(g + 1) * 512], w=["stL"])
                b0, pk = psbank(1)
                pf = psf(b0)
                for c in range(4):
                    S.pe(lambda e, c=c, pf=pf: e.transpose(out=pf[:, c * 32:(c + 1) * 32],
                                                           in_=stL[0:32, c * 128:(c + 1) * 128],
                                                           identity=ident32[0:32, 0:32]), r=["stL", "ident32"], w=pk)
                S.dve(lambda e, g=g, pf=pf: e.tensor_copy(
                    out=FS[:, g * 4:(g + 1) * 4, :, :],
                    in_=pf[:, 0:128].rearrange("p (c s k) -> p c s k", c=4, k=2)), r=pk, w=[fk[2]])
            nTt, nTk = nT[ti % 2], "nT%d" % (ti % 2)
            rmsnorm_T(xt[slot][:], xk, V_GMIX, nTt[:], nTk, "n1")
            lru_stage(ti, nTt, nTk, smp=True)
            cb, pb = swa_stage(ti, nTt, nTk, False, False)
            S.dve(lambda e: e.tensor_copy(out=XC2[:].rearrange("p c (s k) -> p c s k", k=3), in_=XP[:, :, :, 4:7]),
                  r=["XP"], w=["XC2"])
            b0, pk = psbank(1)
            po32 = psf(b0)
            for c in range(4):
                S.pe(lambda e, c=c: e.transpose(out=po32[0:48, c * 128:(c + 1) * 128], in_=XC2[:, c, :],
                                                identity=ident32[:]), r=["XC2", "ident32"], w=pk)
            S.act(lambda e: e.activation(out=stL[0:48, :], in_=po32[0:48, :], func=ACT.Copy), r=pk, w=["stL"])
            S.dma("sync", oslc_d, stL[0:48, :], r=["stL"])
            b0, pk = psbank(1)
            po32b = psf(b0)
            for c in range(4):
                S.pe(lambda e, c=c: e.transpose(out=po32b[0:16, c * 128:(c + 1) * 128], in_=HS2[:, c, :],
                                                identity=ident32[:]), r=["HS2", "ident32"], w=pk)
            S.act(lambda e: e.activation(out=stH[0:16, :], in_=po32b[0:16, :], func=ACT.Copy), r=pk, w=["stH"])
            S.dma("sync", oslh_d, stH[0:16, :], r=["stH"])
            for t4 in range(4):
                S.dma("sync", osk_d[:, 124 + t4, :], kvf[t4:64:4, 0:128], r=["kvf"])
                S.dma("sync", osv_d[:, 124 + t4, :], kvf[t4:64:4, 128:256], r=["kvf"])
            for sq in range(16):
                swa_attend(cb, pb, False, smp=dict(s=sq, KTc=KTc, Vc=Vc, Bown=Bown, keys=["n3T", fk[0], fk[1]]))
            mixer_out(xt[slot], xk, nTt, nTk)

            def cross_iter():
                for sq in range(16):
                    i2 = sq % 2
                    mkc = hmflat[:, i2 * 1024:(i2 + 1) * 1024].rearrange("p (a b) -> p a b", a=2)
                    mvc = hmflat[:, 2048 + i2 * 1024:2048 + (i2 + 1) * 1024].rearrange("p (a b) -> p a b", a=2)
                    mkTs = hmflat[:, 4096 + i2 * 1024:4096 + (i2 + 1) * 1024].rearrange("p (a b) -> p a b", a=4)
                    kk = ["hm%d" % (2 * i2), "hm%d" % (2 * i2 + 1)]
                    kv = ["hm%d" % (4 + 2 * i2), "hm%d" % (5 + 2 * i2)]
                    kt = ["hm%d" % (8 + 2 * i2), "hm%d" % (9 + 2 * i2)]
                    S.dma("gpsimd", mkc, cmk_d[sq].rearrange("(a p) n -> p a n", p=128), w=kk)
                    S.dma("gpsimd", mvc, cmv_d[sq].rearrange("(a p) n -> p a n", p=128), w=kv)
                    b0, pk = psbank(1)
                    pv = psb(b0).rearrange("p (t n) -> p t n", n=128)
                    for h in range(4):
                        for kb in range(2):
                            S.pe(lambda e, h=h, kb=kb, pv=pv, mkc=mkc: e.transpose(
                                out=pv[:, h * 2 + kb, :], in_=mkc[:, kb, h * 128:(h + 1) * 128], identity=ident[:]),
                                r=kk + ["ident"], w=pk)
                    S.dve(lambda e, pv=pv, mkTs=mkTs: e.tensor_copy(out=mkTs, in_=pv.rearrange("p (h b) n -> p h (b n)", b=2)),
                          r=pk, w=kt)
                    cross_attend(dict(s=sq, mkT=mkTs, mv=mvc, keys=kk + kv + kt))
            cross_stage(xt[slot], xk, nTt, nTk, smp_iter=cross_iter)
            rmsnorm_T(xt[slot][:], xk, V_GFFN, n3T[:, :, 0:128], "n3T", "n3")
            ffn_macro([ti], [n_main * 128], 128, smp=dict(FS=FS, FSn=FSn, keys=[fk[2], fk[3]]))
            for g in range(6):
                b0, pk = psbank(1)
                pf = psf(b0)
                for c in range(4):
                    S.pe(lambda e, c=c, g=g, pf=pf: e.transpose(
                        out=pf[0:32, c * 128:(c + 1) * 128],
                        in_=FSn[:, g * 4 + c, :, :].rearrange("p s k -> p (s k)"), identity=ident32[:]),
                        r=[fk[3], "ident32"], w=pk)
                S.act(lambda e, pf=pf: e.activation(out=stL[0:32, :], in_=pf[0:32, :], func=ACT.Copy), r=pk, w=["stL"])
                S.dma("sync", osfc_d[:, g * 512:(g + 1) * 512], stL[0:32, :], r=["stL"])

        if max_ops is not None:
            S.ops = S.ops[:max_ops]
        print('nops', len(S.ops))
        S.emit()
    return nc


def _slot_heads():
    return [(s // 2) + 4 * (s % 2) for s in range(8)]


def _bias_tables():
    slopes = 2.0 ** (-np.arange(1, 9, dtype=np.float64))
    qi = np.arange(128)[:, None]
    kj = np.arange(256)[None, :]
    dist = qi + 128 - kj
    valid = (dist >= 0) & (dist < 128)
    tab = np.empty((128, 8, 256), np.float32)
    for s, h in enumerate(_slot_heads()):
        tab[:, s, :] = np.where(valid, -8.0 * slopes[h] * dist, NEG)
    return tab


_NC_CACHE = {}


def kernel(**inp):
    f32 = np.float32
    xp = np.asarray(inp["x_prompt"], f32)
    xsmp = np.asarray(inp["x_sample"], f32)
    heads = _slot_heads()
    w_in = np.asarray(inp["w_in"][0], f32)
    qcols = np.concatenate([np.arange(1024 + h * 64, 1024 + (h + 1) * 64) for h in heads])
    w_in_p = np.ascontiguousarray(np.concatenate([w_in[:, :1024], w_in[:, qcols], w_in[:, 1536:]], axis=1))
    w_out = np.asarray(inp["w_out"][0], f32)
    orow = np.concatenate([np.arange(512 + h * 64, 512 + (h + 1) * 64) for h in heads])
    w_out_p = np.ascontiguousarray(np.concatenate([w_out[:512], w_out[orow]], axis=0))

    def bd(w):
        o = np.zeros((128, 4, 128), f32)
        for c in range(4):
            o[0:64, c, 0:64] = w[2 * c]
            o[64:128, c, 64:128] = w[2 * c + 1]
        return o.reshape(128, 512)

    def fm(v, nchunk):
        return np.asarray(v, f32).reshape(nchunk, 128).T

    vec = np.zeros((128, NV), f32)
    vec[:, V_GMIX:V_GMIX + 8] = fm(inp["g_mix"][0], 8)
    vec[:, V_GCROSS:V_GCROSS + 8] = fm(inp["g_cross"][0], 8)
    vec[:, V_GFFN:V_GFFN + 8] = fm(inp["g_ffn"][0], 8)
    vec[:, V_GMEM:V_GMEM + 8] = fm(inp["g_mem"][0], 8)
    for tap in range(4):
        vec[:, V_CW + tap * 4:V_CW + tap * 4 + 4] = fm(inp["w_lru_conv"][0, tap], 4)
    vec[:, V_BCONV:V_BCONV + 4] = fm(inp["b_lru_conv"][0], 4)
    vec[:, V_BA:V_BA + 4] = fm(inp["b_lru_a"][0], 4)
    vec[:, V_BX:V_BX + 4] = fm(inp["b_lru_x"][0], 4)
    vec[:, V_LAM:V_LAM + 4] = fm(inp["lru_lambda"][0], 4)
    for tap in range(3):
        vec[:, V_FW + tap * 24:V_FW + tap * 24 + 24] = fm(inp["w_ffn_conv"][0, tap], 24)
    vec[:, V_FB:V_FB + 24] = fm(inp["b_ffn_conv"][0], 24)
    vec[:, V_SINK:V_SINK + 8] = np.asarray(inp["attn_sinks"][0], f32)[heads][None, :]
    vec[:, V_EPS] = 1e-6
    vec[:, V_ONE] = 1.0

    slopes = 2.0 ** (-np.arange(1, 9, dtype=np.float64))
    bown = np.full((128, 8, 256), NEG, f32)
    for s_i, h in enumerate(heads):
        for t in range(4):
            for j in range(t + 1):
                bown[t, s_i, 128 + j] = -8.0 * slopes[h] * (t - j)
    gfin = np.ascontiguousarray(np.broadcast_to(np.asarray(inp["g_final"], f32)[None, :], (128, D)))
    ident = np.eye(128, dtype=f32)
    bias = _bias_tables()
    common = dict(
        gfin=gfin, ident=ident, ident32=ident, bias_own=bown.reshape(128, -1),
        bias=bias.reshape(128, -1), w_in=w_in_p, w_out=w_out_p,
        w_q=np.ascontiguousarray(inp["w_mem_q"][0], f32), w_k=np.ascontiguousarray(inp["w_mem_k"][0], f32),
        w_v=np.ascontiguousarray(inp["w_mem_v"][0], f32), w_o=np.ascontiguousarray(inp["w_mem_o"][0], f32),
        wa_bd=bd(np.asarray(inp["w_lru_a"][0], f32)), wx_bd=bd(np.asarray(inp["w_lru_x"][0], f32)),
        w_g=np.ascontiguousarray(inp["w_ffn_gate"][0], f32), w_u=np.ascontiguousarray(inp["w_ffn_up"][0], f32),
        w_d=np.ascontiguousarray(inp["w_ffn_down"][0], f32),
    )
    in_maps = []
    for c in range(NCORES):
        b, q = c // 4, c % 4
        pre = np.zeros((PRE_T * 128, D), f32)
        if q > 0:
            pre[(3 - q) * 2048:] = xp[b, :q * 2048]
        main = xp[b, q * 2048:(q + 1) * 2048]
        smp = np.zeros((128, D), f32)
        smp[:64] = xsmp[c * 16:(c + 1) * 16].reshape(64, D)
        v = vec.copy()
        for k in range(3):
            v[:, V_FLAG + k] = 1.0 if k >= 3 - q else 0.0
        v[:, V_FLAG + 3] = 1.0 if q > 0 else 0.0
        b0 = bias[:, :, 0:128].copy()
        if q == 0:
            b0[:] = NEG
        m = dict(common)
        m.update(xs=np.ascontiguousarray(np.concatenate([pre, main, smp], axis=0)),
                 memx=np.ascontiguousarray(inp["mem_prompt"][b], f32), vec=v,
                 c_swa_k=np.ascontiguousarray(inp["cache_swa_k"][0, c * 16:(c + 1) * 16], f32).reshape(16, 128, 128),
                 c_swa_v=np.ascontiguousarray(inp["cache_swa_v"][0, c * 16:(c + 1) * 16], f32).reshape(16, 128, 128),
                 c_mem_k=np.ascontiguousarray(inp["cache_mem_k"][0, c * 16:(c + 1) * 16], f32).reshape(16, 256, 512),
                 c_mem_v=np.ascontiguousarray(inp["cache_mem_v"][0, c * 16:(c + 1) * 16], f32).reshape(16, 256, 512),
                 st_lconv=np.ascontiguousarray(inp["state_lru_conv"][0, c * 16:(c + 1) * 16], f32).reshape(48, 512),
                 st_lh=np.ascontiguousarray(inp["state_lru_h"][0, c * 16:(c + 1) * 16], f32),
                 st_fconv=np.ascontiguousarray(inp["state_ffn_conv"][0, c * 16:(c + 1) * 16], f32).reshape(32, 3072),
                 bias0=np.ascontiguousarray(b0.reshape(128, -1)))
        in_maps.append(m)

    if "nc" not in _NC_CACHE:
        _NC_CACHE["nc"] = build_nc()
    nc = _NC_CACHE["nc"]
    res = run_bass_kernel_spmd(nc, in_maps, core_ids=list(range(NCORES)))
    R = res.results

    y_prompt = np.zeros((2, 8192, D), f32)
    y_sample = np.zeros((128, 4, D), f32)
    for c in range(NCORES):
        b, q = c // 4, c % 4
        y_prompt[b, q * 2048:(q + 1) * 2048] = R[c]["y"][:2048]
        y_sample[c * 16:(c + 1) * 16] = R[c]["y"][2048:2048 + 64].reshape(16, 4, D)
    p_swa_k = np.stack([R[3]["o_kv"][:, :128], R[7]["o_kv"][:, :128]]).reshape(1, 2, 128, 2, 64)
    p_swa_v = np.stack([R[3]["o_kv"][:, 128:], R[7]["o_kv"][:, 128:]]).reshape(1, 2, 128, 2, 64)
    p_mem_k = np.stack([R[0]["o_mk"], R[4]["o_mk"]]).reshape(1, 2, 256, 4, 128)
    p_mem_v = np.stack([R[0]["o_mv"], R[4]["o_mv"]]).reshape(1, 2, 256, 4, 128)
    p_lru_conv = np.stack([R[3]["o_lconv"], R[7]["o_lconv"]]).reshape(1, 2, 3, 512)
    p_lru_h = np.stack([R[3]["o_lh"], R[7]["o_lh"]]).reshape(1, 2, 512)
    p_ffn_conv = np.stack([R[3]["o_fconv"], R[7]["o_fconv"]]).reshape(1, 2, 2, 3072)
    cat = lambda k: np.concatenate([R[c][k] for c in range(NCORES)], axis=0)
    s_swa_k = cat("o_sk").reshape(1, 128, 128, 2, 64)
    s_swa_v = cat("o_sv").reshape(1, 128, 128, 2, 64)
    s_lru_conv = cat("o_slc").reshape(1, 128, 3, 512)
    s_lru_h = cat("o_slh").reshape(1, 128, 512)
    s_ffn_conv = cat("o_sfc").reshape(1, 128, 2, 3072)
    return (y_prompt, y_sample, p_swa_k, p_swa_v, p_mem_k, p_mem_v, p_lru_conv, p_lru_h, p_ffn_conv,
            s_swa_k, s_swa_v, s_lru_conv, s_lru_h, s_ffn_conv)
```

```python
import contextlib
import numpy as np
import concourse.bass as bass
import concourse.mybir as mybir
from concourse.bass_utils import run_bass_kernel_spmd

F32 = mybir.dt.float32
BF16 = mybir.dt.bfloat16
ACT = mybir.ActivationFunctionType
ALU = mybir.AluOpType
AX = mybir.AxisListType

NCORES = 8
D = 1024
PRE_T = 48
MAIN_T = 16
NEG = -240000.0
SC_MEM = 128.0 ** -0.5

V_GMIX, V_GCROSS, V_GFFN, V_GMEM = 0, 8, 16, 24
V_CW = 32
V_BCONV = 48
V_BA = 52
V_BX = 56
V_LAM = 60
V_FW = 64
V_FB = 136
V_SINK = 160
V_FLAG = 168
V_EPS = 172
V_ONE = 173
NV = 176


class Sched:
    def __init__(self, nc, es, dma_pool=None):
        self.nc = nc
        self.ops = []
        self.last_w = {}
        self.readers = {}
        self.dma_pool = dma_pool or {"sync": 8, "gpsimd": 6, "scalar": 4}
        self.sems = {}
        for e in ("scalar", "vector", "gpsimd", "tensor"):
            self.sems[e] = es.enter_context(nc.semaphore("c_" + e))
        self.dsems = {}
        for q, n in self.dma_pool.items():
            self.dsems[q] = [es.enter_context(nc.semaphore("d_%s%d" % (q, i))) for i in range(n)]
        self.dcount = {q: 0 for q in self.dma_pool}

    def add(self, eng, fn, r=(), w=(), dma=False):
        i = len(self.ops)
        deps = {}
        for k in r:
            a = self.last_w.get(k)
            if a is not None:
                deps[a] = "raw"
        for k in w:
            a = self.last_w.get(k)
            if a is not None:
                deps[a] = "raw"
            for a in self.readers.get(k, ()):
                deps.setdefault(a, "war")
        op = dict(eng=eng, fn=fn, deps=deps, dma=dma, signal=False)
        if dma:
            j = self.dcount[eng]
            self.dcount[eng] += 1
            n = self.dma_pool[eng]
            op["dsem"] = (eng, j % n)
            op["dval"] = 16 * (j // n + 1)
        self.ops.append(op)
        for k in w:
            self.last_w[k] = i
            self.readers[k] = []
        for k in r:
            lst = self.readers.setdefault(k, [])
            if not dma:
                lst[:] = [a for a in lst if self.ops[a]["dma"] or self.ops[a]["eng"] != eng]
            lst.append(i)
        return i

    def act(self, fn, r=(), w=()):
        return self.add("scalar", fn, r, w)

    def dve(self, fn, r=(), w=()):
        return self.add("vector", fn, r, w)

    def pool(self, fn, r=(), w=()):
        return self.add("gpsimd", fn, r, w)

    def pe(self, fn, r=(), w=()):
        return self.add("tensor", fn, r, w)

    def dma(self, q, out, in_, r=(), w=(), **kw):
        return self.add(q, lambda e: e.dma_start(out=out, in_=in_, **kw), r, w, dma=True)

    def emit(self):
        ops = self.ops
        for b in ops:
            for a, kind in b["deps"].items():
                A = ops[a]
                if A["dma"]:
                    continue
                if A["eng"] == b["eng"] and not b["dma"]:
                    if A["eng"] == "tensor":
                        continue
                A["signal"] = True
        cnt = {e: 0 for e in self.sems}
        for o in ops:
            if not o["dma"] and o["signal"]:
                cnt[o["eng"]] += 1
                o["sval"] = cnt[o["eng"]]
        seen = {}
        last_on_dsem = {}
        for o in ops:
            e = o["eng"]
            sn = seen.setdefault(e, {})
            need = {}
            for a, kind in o["deps"].items():
                A = ops[a]
                if A["dma"]:
                    key, val = ("d",) + A["dsem"], A["dval"]
                else:
                    if A["eng"] == e and not o["dma"]:
                        if e == "tensor":
                            continue
                    key, val = ("c", A["eng"]), A["sval"]
                if need.get(key, 0) < val:
                    need[key] = val
            if o["dma"]:
                key = ("d",) + o["dsem"]
                prev = o["dval"] - 16
                if prev > 0 and need.get(key, 0) < prev:
                    need[key] = prev
            waits = []
            for key, val in need.items():
                if sn.get(key, 0) < val:
                    sn[key] = val
                    waits.append((key, val))
            o["waits"] = waits
        final = {}
        for o in ops:
            if o["dma"]:
                final[("d",) + o["dsem"]] = o["dval"]
        engs = ["sync", "scalar", "vector", "gpsimd", "tensor"]
        per = {e: [o for o in ops if o["eng"] == e] for e in engs}

        def semof(key):
            if key[0] == "c":
                return self.sems[key[1]]
            return self.dsems[key[1]][key[2]]

        def run(e, lst, tail):
            for o in lst:
                for key, val in o["waits"]:
                    e.wait_ge(semof(key), val)
                ins = o["fn"](e)
                if o["dma"]:
                    ins.then_inc(semof(("d",) + o["dsem"]), 16)
                elif o["signal"]:
                    ins.then_inc(self.sems[o["eng"]], 1)
            if tail:
                for key, val in final.items():
                    e.wait_ge(semof(key), val)

        with self.nc.Block() as block:
            block.sync(lambda e: run(e, per["sync"], True))
            block.scalar(lambda e: run(e, per["scalar"], False))
            block.vector(lambda e: run(e, per["vector"], False))
            block.gpsimd(lambda e: run(e, per["gpsimd"], False))
            block.tensor(lambda e: run(e, per["tensor"], False))


def build_nc(do_sample=True, n_pre=PRE_T, n_main=MAIN_T, max_ops=None):
    nc = bass.Bass("TRN2", target_bir_lowering=False)
    NT = n_pre + n_main + 1

    def din(name, shape):
        return nc.dram_tensor(name, list(shape), F32, kind="ExternalInput").ap()

    def dout(name, shape):
        return nc.dram_tensor(name, list(shape), F32, kind="ExternalOutput").ap()

    xs = din("xs", [NT * 128, D])
    memx = din("memx", [256, D])
    vec_d = din("vec", [128, NV])
    gfin_d = din("gfin", [128, D])
    ident_d = din("ident", [128, 128])
    bias_d = din("bias", [128, 8 * 256])
    bias0_d = din("bias0", [128, 8 * 128])
    w_in_d = din("w_in", [D, 1792])
    w_out_d = din("w_out", [D, D])
    w_q_d = din("w_q", [D, 512])
    w_k_d = din("w_k", [D, 512])
    w_v_d = din("w_v", [D, 512])
    w_o_d = din("w_o", [512, D])
    wa_d = din("wa_bd", [128, 512])
    wx_d = din("wx_bd", [128, 512])
    w_g_d = din("w_g", [D, 3072])
    w_u_d = din("w_u", [D, 3072])
    w_d_d = din("w_d", [3072, D])

    ident32_d = din("ident32", [128, 128])
    bown_d = din("bias_own", [128, 8 * 256])
    cswk_d = din("c_swa_k", [16, 128, 128])
    cswv_d = din("c_swa_v", [16, 128, 128])
    cmk_d = din("c_mem_k", [16, 256, 512])
    cmv_d = din("c_mem_v", [16, 256, 512])
    slc_d = din("st_lconv", [48, 512])
    slh_d = din("st_lh", [16, 512])
    sfc_d = din("st_fconv", [32, 3072])
    osk_d = dout("o_sk", [16, 128, 128])
    osv_d = dout("o_sv", [16, 128, 128])
    oslc_d = dout("o_slc", [48, 512])
    oslh_d = dout("o_slh", [16, 512])
    osfc_d = dout("o_sfc", [32, 3072])
    y_d = dout("y", [(n_main + 1) * 128, D])
    okv_d = dout("o_kv", [128, 256])
    omk_d = dout("o_mk", [256, 512])
    omv_d = dout("o_mv", [256, 512])
    olc_d = dout("o_lconv", [3, 512])
    olh_d = dout("o_lh", [512])
    ofc_d = dout("o_fconv", [2, 3072])

    es = contextlib.ExitStack()
    with es:
        def sb(name, shape, dt=F32):
            return es.enter_context(nc.sbuf_tensor("s_" + name, list(shape), dt))

        S = Sched(nc, es)
        psall = es.enter_context(nc.psum_tensor("psall", [128, 8 * 512], F32))
        ps_ctr = [0]

        def psbank(n=1):
            b0 = ps_ctr[0]
            if b0 + n > 8:
                b0 = 0
            ps_ctr[0] = (b0 + n) % 8
            return b0, ["ps%d" % (b0 + i) for i in range(n)]

        def psf(b0, n=1):
            return psall[:, b0 * 512:(b0 + n) * 512]

        psall_bf = psall.bitcast(BF16)

        def psb(b0, n=1):
            return psall_bf[:, b0 * 1024:(b0 + n) * 1024]

        vec = sb("vec", [128, NV])
        gfin = sb("gfin", [128, D])
        ident = sb("ident", [128, 128], BF16)
        biasT = sb("biasT", [128, 8, 256], BF16)
        bias0 = sb("bias0", [128, 8, 128], BF16)
        w_in = sb("w_in", [128, 8, 1792], BF16)
        w_out = sb("w_out", [128, 8, 1024], BF16)
        w_q = sb("w_q", [128, 8, 512], BF16)
        w_o = sb("w_o", [128, 4, 1024], BF16)
        wa = sb("wa", [128, 4, 128], BF16)
        wx = sb("wx", [128, 4, 128], BF16)
        NRING = 4
        ring = [sb("ring%d" % i, [128, 4096], BF16) for i in range(NRING)]
        ring_ctr = [0]
        mkT = sb("mkT", [128, 4, 256], BF16)
        mvb = sb("mvb", [128, 2, 512], BF16)
        cl = sb("cl", [128, 4])
        sink8 = sb("sink8", [128, 8])
        hc = sb("hc", [128, 4])
        NXB = 5
        xt = [sb("xt%d" % i, [128, D]) for i in range(NXB)]
        nbf = sb("nbf", [128, D], BF16)
        junk = nbf
        nT = [sb("nT%d" % i, [128, 8, 128], BF16) for i in range(2)]
        n3T = sb("n3T", [128, 8, 512], BF16)
        st = sb("stat", [128, 64])
        xrp = [sb("xrp%d" % i, [128, 4, 131]) for i in range(2)]
        xc = sb("xc", [128, 4, 128])
        xcb = sb("xcb", [128, 4, 128], BF16)
        rg = sb("rg", [128, 4, 128])
        ig = sb("ig", [128, 4, 128])
        av = sb("av", [128, 4, 128])
        bv = sb("bv", [128, 4, 128])
        hv = sb("hv", [128, 4, 128])
        yT = sb("yT", [128, 8, 128], BF16)
        QT = sb("QT", [128, 4, 128], BF16)
        KT = [sb("KT%d" % i, [128, 128], BF16) for i in range(2)]
        Vp = [sb("Vp%d" % i, [128, 2, 128], BF16) for i in range(2)]
        kvf = sb("kvf", [128, 256])
        Pm = sb("Pm", [128, 8, 256], BF16)
        PT = sb("PT", [128, 16, 128], BF16)
        QC = sb("QC", [128, 4, 128], BF16)
        OC = sb("OC", [128, 4, 128], BF16)
        Gs = [sb("Gs%d" % i, [128, 514]) for i in range(2)]
        t1 = [sb("t1_%d" % i, [128, 512]) for i in range(2)]
        mtmp = t1[0]
        gg = rg
        hm = sb("hm", [128, 12, 512], BF16)
        Gh = sb("Gh", [128, 24, 2])

        ident32 = sb("ident32", [128, 128])
        XP = sb("XP", [128, 4, 16, 7])
        XC2 = sb("XC2", [128, 4, 48])
        HS = sb("HS", [128, 4, 16])
        HS2 = sb("HS2", [128, 4, 16])
        stL = sb("stL", [128, 512])
        stH = sb("stH", [128, 512])

        HALO = sb("HALO", [128, 4, 3])
        fence = sb("fence", [128, 2])

        def vcol(c, n=1):
            return vec[:, c:c + n]

        S.dma("sync", vec[:], vec_d, w=["vec"])
        S.dma("sync", gfin[:], gfin_d, w=["gfin"])
        S.dma("sync", ident32[:], ident32_d, w=["ident32"])
        S.dma("gpsimd", ident[:], ident_d, w=["ident"])
        S.dma("gpsimd", biasT[:].rearrange("p a b -> p (a b)"), bias_d, w=["biasT"])
        S.dma("gpsimd", bias0[:].rearrange("p a b -> p (a b)"), bias0_d, w=["bias0"])
        S.dma("gpsimd", wa[:].rearrange("p a b -> p (a b)"), wa_d, w=["wa"])
        S.dma("gpsimd", wx[:].rearrange("p a b -> p (a b)"), wx_d, w=["wx"])
        S.dma("gpsimd", w_in[:], w_in_d.rearrange("(k p) n -> p k n", p=128), w=["w_in"])
        wk_s = ring[0][:].rearrange("p (k n) -> p k n", k=8)
        wv_s = ring[1][:].rearrange("p (k n) -> p k n", k=8)
        S.dma("gpsimd", wk_s, w_k_d.rearrange("(k p) n -> p k n", p=128), w=["ring0"])
        S.dma("gpsimd", wv_s, w_v_d.rearrange("(k p) n -> p k n", p=128), w=["ring1"])
        S.dma("gpsimd", w_q[:], w_q_d.rearrange("(k p) n -> p k n", p=128), w=["w_q"])
        S.dma("gpsimd", w_o[:], w_o_d.rearrange("(k p) n -> p k n", p=128), w=["w_o"])

        S.pool(lambda e: e.memset(hc[:], 0.0), w=["hc"])
        S.pool(lambda e: e.memset(xrp[1][:], 0.0), w=["xrp1"])
        S.pool(lambda e: e.memset(xrp[0][:], 0.0), w=["xrp0"])
        for i in range(2):
            S.pool(lambda e, i=i: e.memset(Vp[i][:], 0.0), w=["Vp%d" % i])
            S.pool(lambda e, i=i: e.memset(KT[i][:], 0.0), w=["KT%d" % i])
        S.pool(lambda e: e.memset(Gh[:], 0.0), w=["Gh"])

        S.act(lambda e: e.activation(out=st[:, 0:4], in_=vcol(V_LAM, 4), func=ACT.Exp, scale=-1.0),
              r=["vec"], w=["st_a"])
        S.act(lambda e: e.activation(out=st[:, 4:8], in_=st[:, 0:4], func=ACT.Ln, bias=vcol(V_ONE), scale=1.0),
              r=["st_a", "vec"], w=["st_b"])
        S.dve(lambda e: e.tensor_scalar(out=cl[:], in0=st[:, 4:8], scalar1=-8.0, scalar2=None, op0=ALU.mult),
              r=["st_b"], w=["cl"])
        S.dve(lambda e: e.tensor_scalar(out=sink8[:], in0=vcol(V_SINK, 8), scalar1=8.0, scalar2=None, op0=ALU.mult),
              r=["vec"], w=["sink8"])

        def rmsnorm_T(xap, xkey, gcol, dst, dstkey, tagn):
            S.act(lambda e: e.activation(out=junk[:], in_=xap, func=ACT.Square, accum_out=st[:, 8:9]),
                  r=[xkey], w=["nbf", "st_ss"])
            S.act(lambda e: e.activation(out=st[:, 9:10], in_=st[:, 8:9], func=ACT.Sqrt,
                                         bias=vcol(V_EPS), scale=1.0 / D),
                  r=["st_ss", "vec"], w=["st_sd"])
            S.dve(lambda e: e.reciprocal(out=st[:, 10:11], in_=st[:, 9:10]), r=["st_sd"], w=["st_rs"])
            S.dve(lambda e: e.tensor_scalar(out=nbf[:], in0=xap, scalar1=st[:, 10:11], scalar2=None, op0=ALU.mult),
                  r=[xkey, "st_rs"], w=["nbf"])
            b0, pk = psbank(1)
            pv = psb(b0).rearrange("p (k n) -> p k n", k=8)
            for k in range(8):
                S.pe(lambda e, k=k: e.transpose(out=pv[:, k, :], in_=nbf[:, k * 128:(k + 1) * 128], identity=ident[:]),
                     r=["nbf", "ident"], w=pk)
            gb = vec[:, gcol:gcol + 8].unsqueeze(2).to_broadcast([128, 8, 128])
            S.dve(lambda e: e.tensor_tensor(out=dst, in0=pv, in1=gb, op=ALU.mult),
                  r=pk + ["vec"], w=[dstkey])

        def fm_proj(wsb, wkey, col0, nchunks, src, srckey, n=128, width=128):
            b0, pk = psbank(1)
            pv = psf(b0).rearrange("p (c n) -> p c n", c=512 // n)
            for c in range(nchunks):
                for k in range(8):
                    S.pe(lambda e, c=c, k=k: e.matmul(pv[:, c, :], lhsT=wsb[:, k, col0 + c * width:col0 + (c + 1) * width],
                                                       rhs=src[:, k, :], start=(k == 0), stop=(k == 7)),
                         r=[wkey, srckey], w=pk)
            return pv, pk

        def lru_stage(ti, nTt, nTkey, smp=False):
            cur, prv = xrp[ti % 2], xrp[(ti + 1) % 2]
            ck, pk_ = "xrp%d" % (ti % 2), "xrp%d" % ((ti + 1) % 2)
            if not smp:
                S.pool(lambda e: e.tensor_copy(out=cur[:, :, 0:3], in_=prv[:, :, 128:131]), r=[pk_], w=[ck])
            pxr, kxr = fm_proj(w_in, "w_in", 0, 4, nTt, nTkey)
            if not smp:
                S.act(lambda e: e.activation(out=cur[:, :, 3:131], in_=pxr, func=ACT.Copy), r=kxr, w=[ck])
            else:
                for c in range(4):
                    S.act(lambda e, c=c: e.activation(out=XP[:, c, :, 3:7],
                                                      in_=pxr[:, c, 0:64].rearrange("p (s t) -> p s t", t=4),
                                                      func=ACT.Copy), r=kxr, w=["XP"])
            convs = []
            for c in range(4):
                if not smp:
                    o_ap = xc[:, c, :]
                    in_tap = lambda tap, c=c: cur[:, c, tap:tap + 128]
                    srck = ck
                else:
                    o_ap = xc[:, c, 0:64].rearrange("p (s t) -> p s t", t=4)
                    in_tap = lambda tap, c=c: XP[:, c, :, tap:tap + 4]
                    srck = "XP"
                convs.append((o_ap, in_tap, srck))
            for c in range(4):
                o_ap, in_tap, srck = convs[c]
                S.dve(lambda e, c=c, o_ap=o_ap, in_tap=in_tap: e.tensor_scalar(
                    out=o_ap, in0=in_tap(3), scalar1=vcol(V_CW + 12 + c), scalar2=vcol(V_BCONV + c),
                    op0=ALU.mult, op1=ALU.add), r=[srck, "vec"], w=["xc%d" % c])
            for tap in range(3):
                for c in range(4):
                    o_ap, in_tap, srck = convs[c]
                    S.dve(lambda e, c=c, tap=tap, o_ap=o_ap, in_tap=in_tap: e.scalar_tensor_tensor(
                        out=o_ap, in0=in_tap(tap), scalar=vcol(V_CW + tap * 4 + c),
                        in1=o_ap, op0=ALU.mult, op1=ALU.add),
                        r=[srck, "vec", "xc%d" % c], w=["xc%d" % c])
            xck = ["xc%d" % c for c in range(4)]
            S.act(lambda e: e.activation(out=xcb[:], in_=xc[:], func=ACT.Copy), r=xck, w=["xcb"])
            b0, pk = psbank(2)
            pr = psf(b0).rearrange("p (c n) -> p c n", c=4)
            pi = psf(b0 + 1).rearrange("p (c n) -> p c n", c=4)
            for c in range(4):
                S.pe(lambda e, c=c: e.matmul(pr[:, c, :], lhsT=wa[:, c, :], rhs=xcb[:, c, :], start=True, stop=True),
                     r=["wa", "xcb"], w=[pk[0]])
            for c in range(4):
                S.pe(lambda e, c=c: e.matmul(pi[:, c, :], lhsT=wx[:, c, :], rhs=xcb[:, c, :], start=True, stop=True),
                     r=["wx", "xcb"], w=[pk[1]])
            for c in range(4):
                S.act(lambda e, c=c: e.activation(out=rg[:, c, :], in_=pr[:, c, :], func=ACT.Sigmoid,
                                                  bias=vcol(V_BA + c), scale=1.0),
                      r=[pk[0], "vec"], w=["rg"])
            for c in range(4):
                S.act(lambda e, c=c: e.activation(out=ig[:, c, :], in_=pi[:, c, :], func=ACT.Sigmoid,
                                                  bias=vcol(V_BX + c), scale=1.0),
                      r=[pk[1], "vec"], w=["ig"])
            for c in range(4):
                S.act(lambda e, c=c: e.activation(out=av[:, c, :], in_=rg[:, c, :], func=ACT.Exp, scale=cl[:, c:c + 1]),
                      r=["rg", "cl"], w=["av"])
            bvf, avf = bv[:].rearrange("p a b -> p (a b)"), av[:].rearrange("p a b -> p (a b)")
            S.dve(lambda e: e.scalar_tensor_tensor(out=bvf, in0=avf, scalar=-1.0, in1=avf, op0=ALU.mult, op1=ALU.mult),
                  r=["av"], w=["bv"])
            S.act(lambda e: e.activation(out=bv[:], in_=bv[:], func=ACT.Sqrt, bias=vcol(V_ONE), scale=1.0),
                  r=["bv", "vec"], w=["bv"])
            S.pool(lambda e: e.tensor_tensor(out=ig[:], in0=ig[:], in1=xc[:], op=ALU.mult), r=["ig"] + xck, w=["ig"])
            S.dve(lambda e: e.tensor_tensor(out=bv[:], in0=bv[:], in1=ig[:], op=ALU.mult), r=["bv", "ig"], w=["bv"])
            if smp:
                v4 = lambda t_, tt: t_[:, :, 0:64].rearrange("p c (s t) -> p c s t", t=4)[:, :, :, tt]
                for tt in range(4):
                    hprev = HS[:] if tt == 0 else v4(hv, tt - 1)
                    S.dve(lambda e, tt=tt, hprev=hprev: e.tensor_tensor(out=v4(hv, tt), in0=v4(av, tt), in1=hprev,
                                                                        op=ALU.mult), r=["av", "hv", "HS"], w=["hv"])
                    S.dve(lambda e, tt=tt: e.tensor_tensor(out=v4(hv, tt), in0=v4(hv, tt), in1=v4(bv, tt),
                                                           op=ALU.add), r=["bv", "hv"], w=["hv"])
                S.dve(lambda e: e.tensor_copy(out=HS2[:], in_=v4(hv, 3)), r=["hv"], w=["HS2"])
                return
            for c in range(4):
                S.dve(lambda e, c=c: e.tensor_tensor_scan(out=hv[:, c, :], data0=av[:, c, :], data1=bv[:, c, :],
                                                           initial=hc[:, c:c + 1], op0=ALU.mult, op1=ALU.add),
                      r=["av", "bv", "hc"], w=["hv"])
            S.dve(lambda e: e.tensor_copy(out=hc[:], in_=hv[:, :, 127]), r=["hv"], w=["hc"])

        def attn_core(nh, Sps, Skeys, scale, sinkcol, Pbuf, Pkey, nkb, rows=128):
            W = nkb * 128
            S.dve(lambda e: e.reduce_max(out=st[:rows, 16:16 + nh], in_=Sps[:rows], axis=AX.X), r=Skeys, w=["st_mx"])
            if sinkcol is not None:
                S.dve(lambda e: e.tensor_tensor(out=st[:rows, 16:16 + nh], in0=st[:rows, 16:16 + nh],
                                                in1=sink8[:rows, :], op=ALU.max),
                      r=["st_mx", "sink8"], w=["st_mx"])
            S.dve(lambda e: e.tensor_scalar(out=st[:rows, 24:24 + nh], in0=st[:rows, 16:16 + nh], scalar1=-scale,
                                            scalar2=None, op0=ALU.mult),
                  r=["st_mx"], w=["st_nm"])
            for h in range(nh):
                S.act(lambda e, h=h: e.activation(out=Pbuf[:rows, h, 0:W], in_=Sps[:rows, h, :], func=ACT.Exp,
                                                  bias=st[:rows, 24 + h:25 + h], scale=scale,
                                                  accum_out=st[:rows, 32 + h:33 + h]),
                      r=Skeys + ["st_nm"], w=[Pkey, "st_rs%d" % h])
            rsk = ["st_rs%d" % h for h in range(nh)]
            if sinkcol is not None:
                S.dve(lambda e: e.tensor_tensor(out=st[:rows, 40:48], in0=st[:rows, 24:32],
                                                in1=vec[:rows, sinkcol:sinkcol + 8], op=ALU.add),
                      r=["st_nm", "vec"], w=["st_es"])
                S.act(lambda e: e.activation(out=st[:rows, 40:48], in_=st[:rows, 40:48], func=ACT.Exp),
                      r=["st_es"], w=["st_es"])
                S.dve(lambda e: e.tensor_tensor(out=st[:rows, 32:40], in0=st[:rows, 32:40], in1=st[:rows, 40:48],
                                                op=ALU.add),
                      r=["st_es"] + rsk, w=rsk)
            S.dve(lambda e: e.reciprocal(out=st[:rows, 48:48 + nh], in_=st[:rows, 32:32 + nh]), r=rsk, w=["st_ri"])
            for h in range(nh):
                S.dve(lambda e, h=h: e.tensor_scalar(out=Pbuf[:rows, h, 0:W], in0=Pbuf[:rows, h, 0:W],
                                                     scalar1=st[:rows, 48 + h:49 + h], scalar2=None, op0=ALU.mult),
                      r=[Pkey, "st_ri"], w=[Pkey])
            nt = nh * nkb
            nb = (nt * 128 + 1023) // 1024
            b0, pk = psbank(nb)
            pv = psb(b0, nb).rearrange("p (t n) -> p t n", n=128)
            for h in range(nh):
                for kb in range(nkb):
                    t = h * nkb + kb
                    S.pe(lambda e, h=h, kb=kb, t=t: e.transpose(out=pv[:, t, 0:rows],
                                                                in_=Pbuf[:rows, h, kb * 128:(kb + 1) * 128],
                                                                identity=ident[:rows, :rows]),
                         r=[Pkey, "ident"], w=[pk[(t * 128) // 1024]])
            half = nt // 2
            if nb == 1:
                S.act(lambda e: e.activation(out=PT[:, 0:nt, 0:rows], in_=pv[:, 0:nt, 0:rows], func=ACT.Copy),
                      r=pk, w=["PTa", "PTb"])
            else:
                S.act(lambda e: e.activation(out=PT[:, 0:half, 0:rows], in_=pv[:, 0:half, 0:rows], func=ACT.Copy),
                      r=pk, w=["PTa"])
                S.dve(lambda e: e.tensor_copy(out=PT[:, half:nt, 0:rows], in_=pv[:, half:nt, 0:rows]),
                      r=pk, w=["PTb"])

        def swa_stage(ti, nTt, nTkey, first_block, bias_first, rows=128, qcols=None):
            cb, pb = ti % 2, (ti + 1) % 2
            pq, kq = fm_proj(w_in, "w_in", 1024, 4, nTt, nTkey)
            S.act(lambda e: e.activation(out=QT[:], in_=pq, func=ACT.Copy), r=kq, w=["QT"])
            b0, pk = psbank(1)
            pkk = psf(b0)[:, 0:128]
            pkv = psf(b0)[:, 128:384]
            for k in range(8):
                S.pe(lambda e, k=k: e.matmul(pkk, lhsT=w_in[:, k, 1536:1664], rhs=nTt[:, k, :],
                                             start=(k == 0), stop=(k == 7)), r=["w_in", nTkey], w=pk)
            for k in range(8):
                S.pe(lambda e, k=k: e.matmul(pkv, lhsT=nTt[:, k, :], rhs=w_in[:, k, 1536:1792],
                                             start=(k == 0), stop=(k == 7)), r=["w_in", nTkey], w=pk)
            S.act(lambda e: e.activation(out=KT[cb][:], in_=pkk, func=ACT.Copy), r=pk, w=["KT%d" % cb])
            S.act(lambda e: e.activation(out=Vp[cb][:, 0, 0:64], in_=pkv[:, 128:192], func=ACT.Copy), r=pk, w=["Vp%d" % cb])
            S.act(lambda e: e.activation(out=Vp[cb][:, 1, 64:128], in_=pkv[:, 192:256], func=ACT.Copy), r=pk, w=["Vp%d" % cb])
            S.act(lambda e: e.activation(out=kvf[:], in_=pkv, func=ACT.Copy), r=pk, w=["kvf"])
            return cb, pb

        def swa_attend(cb, pb, bias_first, smp=None):
            b0, sk = psbank(4)
            Sps = psf(b0, 4).rearrange("p (h n) -> p h n", h=8)
            if smp is None:
                rows, q0 = 128, 0
                xkeys = []
            else:
                rows, q0 = 4, 4 * smp["s"]
                xkeys = smp["keys"]
            idn = ident[0:rows, 0:rows]
            for j in range(4):
                for b in range(2):
                    s_ = 2 * j + b
                    key = [sk[s_ // 2]]
                    lo, hi = 64 * b, 64 * b + 64
                    if smp is None:
                        kprev = KT[pb][lo:hi, :]
                        bprev = bias0[:, s_, :] if bias_first else biasT[:, s_, 0:128]
                        bown = biasT[:, s_, 128:256]
                        kpk = "KT%d" % pb
                    else:
                        kprev = smp["KTc"][lo:hi, smp["s"], :]
                        bprev = biasT[0:4, s_, 0:128]
                        o0 = 128 - 4 * smp["s"]
                        bown = smp["Bown"][0:4, s_, o0:o0 + 128]
                        kpk = "KT%d" % cb
                    S.pe(lambda e, j=j, lo=lo, hi=hi, s_=s_, kprev=kprev: e.matmul(
                        Sps[0:rows, s_, 0:128], lhsT=QT[lo:hi, j, q0:q0 + rows], rhs=kprev, start=True, stop=False),
                        r=["QT", kpk] + xkeys, w=key)
                    S.pe(lambda e, s_=s_, bprev=bprev: e.matmul(Sps[0:rows, s_, 0:128], lhsT=idn, rhs=bprev,
                                                                start=False, stop=True),
                         r=["ident", "bias0", "biasT"], w=key)
                    S.pe(lambda e, j=j, lo=lo, hi=hi, s_=s_: e.matmul(
                        Sps[0:rows, s_, 128:256], lhsT=QT[lo:hi, j, q0:q0 + rows], rhs=KT[cb][lo:hi, :],
                        start=True, stop=False), r=["QT", "KT%d" % cb], w=key)
                    S.pe(lambda e, s_=s_, bown=bown: e.matmul(Sps[0:rows, s_, 128:256], lhsT=idn, rhs=bown,
                                                              start=False, stop=True),
                         r=["ident", "biasT"] + xkeys, w=key)
            attn_core(8, Sps, sk, 0.125, V_SINK, Pm, "Pm", 2, rows=rows)
            b0, ok = psbank(1)
            pO = psf(b0).rearrange("p (c n) -> p c n", c=4)
            for j in range(4):
                n = 0
                for b in range(2):
                    s_ = 2 * j + b
                    for kb in (0, 1):
                        if kb == 1:
                            vap, vk = Vp[cb][:, b, :], ["Vp%d" % cb]
                        elif smp is None:
                            vap, vk = Vp[pb][:, b, :], ["Vp%d" % pb]
                        else:
                            vap, vk = smp["Vc"][b][:, smp["s"], :], xkeys
                        S.pe(lambda e, j=j, s_=s_, kb=kb, vap=vap, n=n: e.matmul(
                            pO[:, j, 0:rows], lhsT=vap, rhs=PT[:, s_ * 2 + kb, 0:rows],
                            start=(n == 0), stop=(n == 3)),
                            r=vk + ["PTa", "PTb"], w=ok)
                        n += 1
            S.act(lambda e: e.activation(out=yT[:, 4:8, q0:q0 + rows], in_=pO[:, :, 0:rows], func=ACT.Copy),
                  r=ok, w=["yT_att"])

        def mixer_gate(nTt, nTkey):
            pg, kg = fm_proj(w_in, "w_in", 512, 4, nTt, nTkey)
            S.act(lambda e: e.activation(out=gg[:], in_=pg, func=ACT.Gelu_apprx_tanh), r=kg, w=["rg"])
            S.dve(lambda e: e.tensor_tensor(out=yT[:, 0:4, :], in0=hv[:], in1=gg[:], op=ALU.mult),
                  r=["hv", "rg"], w=["yT_lru"])

        def mixer_out(xti, xkey, nTt, nTkey, gate_done=False):
            if not gate_done:
                mixer_gate(nTt, nTkey)
            b0, pk = psbank(2)
            po = psf(b0, 2)
            for hh in range(2):
                for k in range(8):
                    S.pe(lambda e, hh=hh, k=k: e.matmul(po[:, hh * 512:(hh + 1) * 512], lhsT=yT[:, k, :],
                                                        rhs=w_out[:, k, hh * 512:(hh + 1) * 512],
                                                        start=(k == 0), stop=(k == 7)),
                         r=["yT_lru", "yT_att", "w_out"], w=[pk[hh]])
            S.dve(lambda e: e.tensor_tensor(out=xti[:], in0=xti[:], in1=po, op=ALU.add), r=[xkey] + pk, w=[xkey])

        def cross_attend(smp=None):
            if smp is None:
                rows, q0, kT, vv, xkeys = 128, 0, mkT, mvb, ["mkT", "mvb"]
            else:
                rows, q0, kT, vv, xkeys = 4, 4 * smp["s"], smp["mkT"], smp["mv"], smp["keys"]
            b0, sk = psbank(2)
            Sps = psf(b0, 2).rearrange("p (h n) -> p h n", h=4)
            for h in range(4):
                S.pe(lambda e, h=h: e.matmul(Sps[0:rows, h, :], lhsT=QC[:, h, q0:q0 + rows], rhs=kT[:, h, :],
                                             start=True, stop=True),
                     r=["QC"] + xkeys, w=[sk[h // 2]])
            attn_core(4, Sps, sk, SC_MEM, None, Pm, "Pm", 2, rows=rows)
            b0, ok = psbank(1)
            pO = psf(b0).rearrange("p (c n) -> p c n", c=4)
            for h in range(4):
                for kb in range(2):
                    S.pe(lambda e, h=h, kb=kb: e.matmul(pO[:, h, 0:rows], lhsT=vv[:, kb, h * 128:(h + 1) * 128],
                                                        rhs=PT[:, h * 2 + kb, 0:rows], start=(kb == 0), stop=(kb == 1)),
                         r=xkeys + ["PTa", "PTb"], w=ok)
            S.act(lambda e: e.activation(out=OC[:, :, q0:q0 + rows], in_=pO[:, :, 0:rows], func=ACT.Copy),
                  r=ok, w=["OC"])

        def cross_stage(xti, xkey, nTt, nTkey, smp_iter=None):
            rmsnorm_T(xti[:], xkey, V_GCROSS, nTt[:], nTkey, "n2")
            pq, kq = fm_proj(w_q, "w_q", 0, 4, nTt, nTkey)
            S.act(lambda e: e.activation(out=QC[:], in_=pq, func=ACT.Copy), r=kq, w=["QC"])
            if smp_iter is None:
                cross_attend(None)
            else:
                smp_iter()
            b0, pk = psbank(2)
            po = psf(b0, 2)
            for hh in range(2):
                for k in range(4):
                    S.pe(lambda e, hh=hh, k=k: e.matmul(po[:, hh * 512:(hh + 1) * 512], lhsT=OC[:, k, :],
                                                        rhs=w_o[:, k, hh * 512:(hh + 1) * 512],
                                                        start=(k == 0), stop=(k == 3)),
                         r=["OC", "w_o"], w=[pk[hh]])
            S.dve(lambda e: e.tensor_tensor(out=xti[:], in0=xti[:], in1=po, op=ALU.add), r=[xkey] + pk, w=[xkey])

        v_k8 = lambda t: t[:].rearrange("p (k n) -> p k n", k=8)
        wq = []
        wissued = [0]

        def wview(i):
            return ring[i % NRING][:].rearrange("p (k n) -> p k n", k=wq[i][1])

        def wq_add_macro(gate_only=False):
            base = len(wq)
            if gate_only:
                for g in range(6):
                    wq.append((w_g_d[:, g * 512:(g + 1) * 512].rearrange("(k p) n -> p k n", p=128), 8))
                return base
            for fh in range(2):
                for gi in range(3):
                    g = 3 * fh + gi
                    wq.append((w_g_d[:, g * 512:(g + 1) * 512].rearrange("(k p) n -> p k n", p=128), 8))
                    wq.append((w_u_d[:, g * 512:(g + 1) * 512].rearrange("(k p) n -> p k n", p=128), 8))
                for gi in range(3):
                    g = 3 * fh + gi
                    wq.append((w_d_d[g * 512:(g + 1) * 512, :].rearrange("(k p) n -> p k n", p=128), 4))
            return base

        def wq_issue(upto):
            while wissued[0] < len(wq) and wissued[0] <= upto:
                j = wissued[0]
                S.dma("gpsimd", wview(j), wq[j][0], w=["ring%d" % (j % NRING)])
                wissued[0] += 1

        def wq_get(i):
            wq_issue(i)
            return wview(i), "ring%d" % (i % NRING)

        def wq_done(i):
            wq_issue(i + NRING)

        def load_x(ti, slot):
            S.dma("sync", xt[slot][:], xs[ti * 128:(ti + 1) * 128, :], w=["xt%d" % slot])

        for mt in range(2):
            S.dma("sync", xt[mt][:], memx[mt * 128:(mt + 1) * 128, :], w=["xt%d" % mt])
        for mt in range(2):
            rmsnorm_T(xt[mt][:], "xt%d" % mt, V_GMEM, nT[mt][:], "nT%d" % mt, "nm")
            for (wsl, wkey, od, is_k) in ((wk_s, "ring0", omk_d, True), (wv_s, "ring1", omv_d, False)):
                b0, pk = psbank(1)
                pm = psf(b0)
                for k in range(8):
                    S.pe(lambda e, k=k, wsl=wsl, pm=pm, mt=mt: e.matmul(pm, lhsT=nT[mt][:, k, :], rhs=wsl[:, k, :],
                                                                 start=(k == 0), stop=(k == 7)),
                         r=["nT%d" % mt, wkey], w=pk)
                S.act(lambda e, pm=pm: e.activation(out=mtmp[:], in_=pm, func=ACT.Copy), r=pk, w=["t1_0"])
                if not is_k:
                    S.dve(lambda e, mt=mt: e.tensor_copy(out=mvb[:, mt, :], in_=mtmp[:]), r=["t1_0"], w=["mvb"])
                S.dma("sync", od[mt * 128:(mt + 1) * 128, :], mtmp[:], r=["t1_0"])
            pkT, kk = fm_proj(wk_s, "ring0", 0, 4, nT[mt], "nT%d" % mt)
            S.act(lambda e, pkT=pkT, mt=mt: e.activation(out=mkT[:, :, mt * 128:(mt + 1) * 128], in_=pkT, func=ACT.Copy),
                  r=kk, w=["mkT"])

        def ffn_macro(tiles, rows_out, N, halo=Gh, halokey="Gh", smp=None):
            wb = wq_add_macro()
            for fh in range(2):
                for gi in range(3):
                    g = 3 * fh + gi
                    wg3, wgk = wq_get(wb + fh * 9 + 2 * gi)
                    wu3, wuk = wq_get(wb + fh * 9 + 2 * gi + 1)
                    for c in range(4):
                        fc = g * 4 + c
                        hmi = gi * 4 + c
                        i2 = fc % 2
                        b0, pk = psbank(2)
                        pG, pU = psf(b0)[:, 0:N], psf(b0 + 1)[:, 0:N]
                        for k in range(8):
                            S.pe(lambda e, k=k, c=c, wg3=wg3, pG=pG: e.matmul(
                                pG, lhsT=wg3[:, k, c * 128:(c + 1) * 128], rhs=n3T[:, k, 0:N],
                                start=(k == 0), stop=(k == 7)), r=[wgk, "n3T"], w=[pk[0]])
                        for k in range(8):
                            S.pe(lambda e, k=k, c=c, wu3=wu3, pU=pU: e.matmul(
                                pU, lhsT=wu3[:, k, c * 128:(c + 1) * 128], rhs=n3T[:, k, 0:N],
                                start=(k == 0), stop=(k == 7)), r=[wuk, "n3T"], w=[pk[1]])
                        Gsb, gk = Gs[i2], "Gs%d" % i2
                        tk = "t1_%d" % i2
                        if smp is None:
                            S.dve(lambda e, fc=fc, Gsb=Gsb: e.tensor_copy(out=Gsb[:, 0:2], in_=halo[:, fc, :]),
                                  r=[halokey + "%d" % fc, halokey], w=[gk])
                            S.act(lambda e, Gsb=Gsb, pG=pG: e.activation(out=Gsb[:, 2:2 + N], in_=pG, func=ACT.Copy),
                                  r=[pk[0]], w=[gk])
                            S.dve(lambda e, fc=fc, Gsb=Gsb: e.tensor_copy(out=halo[:, fc, :], in_=Gsb[:, N:N + 2]),
                                  r=[gk], w=[halokey + "%d" % fc])
                            t1v = t1[i2][:, 0:N]
                            g_tap = lambda tap, Gsb=Gsb: Gsb[:, tap:tap + N]
                            pGv = pG
                        else:
                            FS, FSn, fkeys = smp["FS"], smp["FSn"], smp["keys"]
                            G3 = Gsb[:, 0:96].rearrange("p (s t) -> p s t", t=6)
                            S.dve(lambda e, fc=fc, G3=G3, FS=FS: e.tensor_copy(out=G3[:, :, 0:2], in_=FS[:, fc, :, :]),
                                  r=[fkeys[0]], w=[gk])
                            S.act(lambda e, G3=G3, pG=pG: e.activation(
                                out=G3[:, :, 2:6], in_=pG[:, 0:64].rearrange("p (s t) -> p s t", t=4), func=ACT.Copy),
                                r=[pk[0]], w=[gk])
                            S.dve(lambda e, fc=fc, G3=G3, FSn=FSn: e.tensor_copy(out=FSn[:, fc, :, :], in_=G3[:, :, 4:6]),
                                  r=[gk], w=[fkeys[1]])
                            t1v = t1[i2][:, 0:64].rearrange("p (s t) -> p s t", t=4)
                            g_tap = lambda tap, G3=G3: G3[:, :, tap:tap + 4]
                            pGv = pG[:, 0:64].rearrange("p (s t) -> p s t", t=4)
                        S.act(lambda e, fc=fc, pGv=pGv, t1v=t1v: e.activation(
                            out=t1v, in_=pGv, func=ACT.Identity, bias=vcol(V_FB + fc),
                            scale=vcol(V_FW + 48 + fc)), r=[pk[0], "vec"], w=[tk])
                        for tap in (1, 0):
                            S.dve(lambda e, fc=fc, tap=tap, g_tap=g_tap, t1v=t1v: e.scalar_tensor_tensor(
                                out=t1v, in0=g_tap(tap), scalar=vcol(V_FW + tap * 24 + fc),
                                in1=t1v, op0=ALU.mult, op1=ALU.add),
                                r=[gk, "vec", tk], w=[tk])
                        S.act(lambda e, i2=i2: e.activation(out=t1[i2][:, 0:N], in_=t1[i2][:, 0:N],
                                                            func=ACT.Gelu_apprx_tanh), r=[tk], w=[tk])
                        S.dve(lambda e, hmi=hmi, i2=i2, pU=pU: e.tensor_tensor(out=hm[:, hmi, 0:N], in0=t1[i2][:, 0:N],
                                                                               in1=pU, op=ALU.mult),
                              r=[tk, pk[1]], w=["hm%d" % hmi])
                    wq_done(wb + fh * 9 + 2 * gi + 1)
                wds = [wq_get(wb + fh * 9 + 6 + q) for q in range(3)]
                for j, tj in enumerate(tiles):
                    sl = xslot(tj)
                    for hh in range(2):
                        b0, pk = psbank(1)
                        pD = psf(b0)
                        for q in range(3):
                            for c in range(4):
                                hmi = q * 4 + c
                                S.pe(lambda e, q=q, c=c, hmi=hmi, j=j, hh=hh, pD=pD, wds=wds: e.matmul(
                                    pD, lhsT=hm[:, hmi, j * 128:(j + 1) * 128],
                                    rhs=wds[q][0][:, c, hh * 512:(hh + 1) * 512],
                                    start=(hmi == 0), stop=(hmi == 11)),
                                    r=[wds[q][1], "hm%d" % hmi], w=pk)
                        S.dve(lambda e, sl=sl, hh=hh, pD=pD: e.tensor_tensor(
                            out=xt[sl][:, hh * 512:(hh + 1) * 512], in0=xt[sl][:, hh * 512:(hh + 1) * 512],
                            in1=pD, op=ALU.add), r=["xt%d" % sl] + pk, w=["xt%d" % sl])
                wq_done(wb + fh * 9 + 8)
            for j, tj in enumerate(tiles):
                sl = xslot(tj)
                xk2 = "xt%d" % sl
                S.act(lambda e, sl=sl: e.activation(out=junk[:], in_=xt[sl][:], func=ACT.Square, accum_out=st[:, 8:9]),
                      r=[xk2], w=["nbf", "st_ss"])
                S.act(lambda e: e.activation(out=st[:, 9:10], in_=st[:, 8:9], func=ACT.Sqrt, bias=vcol(V_EPS),
                                             scale=1.0 / D), r=["st_ss", "vec"], w=["st_sd"])
                S.dve(lambda e: e.reciprocal(out=st[:, 10:11], in_=st[:, 9:10]), r=["st_sd"], w=["st_rs"])
                S.dve(lambda e, sl=sl: e.scalar_tensor_tensor(out=xt[sl][:], in0=xt[sl][:], scalar=st[:, 10:11],
                                                              in1=gfin[:], op0=ALU.mult, op1=ALU.mult),
                      r=[xk2, "st_rs", "gfin"], w=[xk2])
                S.dma("sync", y_d[rows_out[j]:rows_out[j] + 128, :], xt[sl][:], r=[xk2])

        total = n_pre + n_main
        xslot = lambda ti: ti % NXB
        load_x(0, 0)
        import os
        n_fast = ((n_pre - 4) // 4) * 4 if (n_pre >= 8 and not os.environ.get("NOFAST")) else 0
        if n_fast:
            XCv = ring[0].bitcast(F32)[:, 0:2048].rearrange("p (c n) -> p c n", c=4)
            RGv = ring[1].bitcast(F32)[:, 0:2048].rearrange("p (c n) -> p c n", c=4)
            IGv = ring[2].bitcast(F32)[:, 0:2048].rearrange("p (c n) -> p c n", c=4)
            BVv = ring[3].bitcast(F32)[:, 0:2048].rearrange("p (c n) -> p c n", c=4)
            XBv = hm.bitcast(F32)[:].rearrange("p a b -> p (a b)")[:, 0:2060].rearrange("p (c n) -> p c n", c=4)
            XCBv = w_out[:].rearrange("p a b -> p (a b)")[:, 0:2048].rearrange("p (c n) -> p c n", c=4)
            basekeys = ["ring0", "ring1", "ring2", "ring3", "w_out"] + ["hm%d" % i for i in range(12)]
            ckeys = []
            for c in range(4):
                ckeys += ["fXC%d" % c, "fRG%d" % c, "fIG%d" % c, "fBV%d" % c, "fXB%d" % c, "fXCB%d" % c]
            S.pool(lambda e: e.memset(fence[:, 0:1], 0.0),
                   r=["w_in", "w_q", "w_o", "ident", "biasT", "bias0", "wa", "wx", "vec", "gfin", "mkT", "mvb"],
                   w=basekeys + ckeys)
            S.pool(lambda e: e.memset(HALO[:], 0.0), w=["HALO"])

            def fast_s1(c):
                b0, pk = psbank(1)
                pxr = psf(b0)
                for k in range(8):
                    S.pe(lambda e, c=c, k=k, pxr=pxr: e.matmul(pxr, lhsT=w_in[:, k, c * 128:(c + 1) * 128],
                                                               rhs=n3T[:, k, :], start=(k == 0), stop=(k == 7)),
                         r=["w_in", "n3T"], w=pk)
                S.dve(lambda e, c=c: e.tensor_copy(out=XBv[:, c, 0:3], in_=HALO[:, c, :]), r=["HALO"], w=["fXB%d" % c])
                S.act(lambda e, c=c, pxr=pxr: e.activation(out=XBv[:, c, 3:515], in_=pxr, func=ACT.Copy),
                      r=pk, w=["fXB%d" % c])
                S.dve(lambda e, c=c: e.tensor_copy(out=HALO[:, c, :], in_=XBv[:, c, 512:515]),
                      r=["fXB%d" % c], w=["HALO"])
                S.dve(lambda e, c=c: e.tensor_scalar(out=XCv[:, c, :], in0=XBv[:, c, 3:515],
                                                     scalar1=vcol(V_CW + 12 + c), scalar2=vcol(V_BCONV + c),
                                                     op0=ALU.mult, op1=ALU.add),
                      r=["fXB%d" % c, "vec"], w=["fXC%d" % c])
                for tap in range(3):
                    S.dve(lambda e, c=c, tap=tap: e.scalar_tensor_tensor(
                        out=XCv[:, c, :], in0=XBv[:, c, tap:tap + 512], scalar=vcol(V_CW + tap * 4 + c),
                        in1=XCv[:, c, :], op0=ALU.mult, op1=ALU.add),
                        r=["fXB%d" % c, "vec", "fXC%d" % c], w=["fXC%d" % c])
                S.act(lambda e, c=c: e.activation(out=XCBv[:, c, :], in_=XCv[:, c, :], func=ACT.Copy),
                      r=["fXC%d" % c], w=["fXCB%d" % c])

            def fast_s2(c):
                b0, pk = psbank(2)
                pr, pi = psf(b0), psf(b0 + 1)
                S.pe(lambda e, c=c, pr=pr: e.matmul(pr, lhsT=wa[:, c, :], rhs=XCBv[:, c, :], start=True, stop=True),
                     r=["wa", "fXCB%d" % c], w=[pk[0]])
                S.pe(lambda e, c=c, pi=pi: e.matmul(pi, lhsT=wx[:, c, :], rhs=XCBv[:, c, :], start=True, stop=True),
                     r=["wx", "fXCB%d" % c], w=[pk[1]])
                S.act(lambda e, c=c, pr=pr: e.activation(out=RGv[:, c, :], in_=pr, func=ACT.Sigmoid,
                                                         bias=vcol(V_BA + c), scale=1.0),
                      r=[pk[0], "vec"], w=["fRG%d" % c])
                S.act(lambda e, c=c, pi=pi: e.activation(out=IGv[:, c, :], in_=pi, func=ACT.Sigmoid,
                                                         bias=vcol(V_BX + c), scale=1.0),
                      r=[pk[1], "vec"], w=["fIG%d" % c])
                S.act(lambda e, c=c: e.activation(out=RGv[:, c, :], in_=RGv[:, c, :], func=ACT.Exp,
                                                  scale=cl[:, c:c + 1]), r=["fRG%d" % c, "cl"], w=["fRG%d" % c])
                S.act(lambda e, c=c: e.activation(out=BVv[:, c, :], in_=RGv[:, c, :], func=ACT.Square),
                      r=["fRG%d" % c], w=["fBV%d" % c])
                S.act(lambda e, c=c: e.activation(out=BVv[:, c, :], in_=BVv[:, c, :], func=ACT.Sqrt,
                                                  bias=vcol(V_ONE), scale=-1.0),
                      r=["fBV%d" % c, "vec"], w=["fBV%d" % c])
                S.dve(lambda e, c=c: e.tensor_tensor(out=IGv[:, c, :], in0=IGv[:, c, :], in1=XCv[:, c, :], op=ALU.mult),
                      r=["fIG%d" % c, "fXC%d" % c], w=["fIG%d" % c])
                S.dve(lambda e, c=c: e.tensor_tensor(out=BVv[:, c, :], in0=BVv[:, c, :], in1=IGv[:, c, :], op=ALU.mult),
                      r=["fBV%d" % c, "fIG%d" % c], w=["fBV%d" % c])
                S.dve(lambda e, c=c: e.tensor_tensor_scan(out=IGv[:, c, :], data0=RGv[:, c, :], data1=BVv[:, c, :],
                                                           initial=hc[:, c:c + 1], op0=ALU.mult, op1=ALU.add),
                      r=["fRG%d" % c, "fBV%d" % c, "hc"], w=["fIG%d" % c])
                S.dve(lambda e, c=c: e.tensor_copy(out=hc[:, c:c + 1], in_=IGv[:, c, 511:512]),
                      r=["fIG%d" % c], w=["hc"])

            for m0 in range(0, n_fast, 4):
                for j in range(4):
                    tj = m0 + j
                    if tj + 1 < total:
                        load_x(tj + 1, xslot(tj + 1))
                    rmsnorm_T(xt[xslot(tj)][:], "xt%d" % xslot(tj), V_GMIX, n3T[:, :, j * 128:(j + 1) * 128], "n3T", "n1")
                fast_s1(0)
                fast_s1(1)
                fast_s2(0)
                fast_s1(2)
                fast_s2(1)
                fast_s1(3)
                fast_s2(2)
                fast_s2(3)
                if (m0 + 4) % 16 == 0:
                    fcol = V_FLAG + (m0 + 4) // 16 - 1
                    S.dve(lambda e, fcol=fcol: e.tensor_scalar(out=hc[:], in0=hc[:], scalar1=vcol(fcol), scalar2=None,
                                                               op0=ALU.mult), r=["hc", "vec"], w=["hc"])
            S.pool(lambda e: e.tensor_copy(out=xrp[(n_fast + 1) % 2][:, :, 128:131], in_=HALO[:]),
                   r=["HALO"], w=["xrp%d" % ((n_fast + 1) % 2)])
            S.pool(lambda e: e.memset(fence[:, 1:2], 0.0), r=ckeys, w=basekeys)
        S.dma("gpsimd", w_out[:], w_out_d.rearrange("(k p) n -> p k n", p=128), w=["w_out"])
        for ti in range(n_fast, total):
            slot = xslot(ti)
            xk = "xt%d" % slot
            if ti + 1 < total:
                load_x(ti + 1, xslot(ti + 1))
            nTt = nT[ti % 2]
            nTk = "nT%d" % (ti % 2)
            rmsnorm_T(xt[slot][:], xk, V_GMIX, nTt[:], nTk, "n1")
            if ti >= n_pre - 2:
                cb, pb = swa_stage(ti, nTt, nTk, False, False)
            lru_stage(ti, nTt, nTk)
            full = ti >= n_pre - 1
            if ti < n_pre and (ti + 1) % 16 == 0:
                fcol = V_FLAG + (ti + 1) // 16 - 1
                S.dve(lambda e, fcol=fcol: e.tensor_scalar(out=hc[:], in0=hc[:], scalar1=vcol(fcol), scalar2=None,
                                                           op0=ALU.mult), r=["hc", "vec"], w=["hc"])
            if not full:
                continue
            mixer_gate(nTt, nTk)
            swa_attend(cb, pb, ti == n_pre)
            mixer_out(xt[slot], xk, nTt, nTk, gate_done=True)
            cross_stage(xt[slot], xk, nTt, nTk)
            mi = (ti - n_pre) % 4 if ti >= n_pre else 0
            if ti == n_pre - 1:
                rmsnorm_T(xt[slot][:], xk, V_GFFN, nTt[:], nTk, "n3")
                b0, pk = psbank(1)
                ph = psf(b0)[:, 0:48].rearrange("p (c n) -> p c n", n=2)
                wb = wq_add_macro(gate_only=True)
                for g in range(6):
                    wg3, wgk = wq_get(wb + g)
                    for c in range(4):
                        for k in range(8):
                            S.pe(lambda e, g=g, c=c, k=k, wg3=wg3, ph=ph, nTt=nTt: e.matmul(
                                ph[:, g * 4 + c, :], lhsT=wg3[:, k, c * 128:(c + 1) * 128], rhs=nTt[:, k, 126:128],
                                start=(k == 0), stop=(k == 7)), r=[wgk, nTk], w=pk)
                    wq_done(wb + g)
                S.dve(lambda e, ph=ph: e.tensor_scalar(out=Gh[:], in0=ph, scalar1=vcol(V_FLAG + 3), scalar2=None, op0=ALU.mult),
                      r=pk + ["vec"], w=["Gh"])
                continue
            rmsnorm_T(xt[slot][:], xk, V_GFFN, n3T[:, :, mi * 128:(mi + 1) * 128], "n3T", "n3")
            if ti == total - 1:
                S.dma("sync", okv_d, kvf[:], r=["kvf"])
                S.dve(lambda e, ti=ti: e.tensor_copy(out=XC2[:, :, 0:3], in_=xrp[ti % 2][:, :, 128:131]),
                      r=["xrp%d" % (ti % 2)], w=["XC2"])
                b0, pk = psbank(1)
                pq_ = psf(b0)
                for c in range(4):
                    S.pe(lambda e, c=c, pq_=pq_: e.transpose(out=pq_[0:3, c * 128:(c + 1) * 128], in_=XC2[:, c, 0:3],
                                                             identity=ident32[:]), r=["XC2", "ident32"], w=pk)
                S.act(lambda e, pq_=pq_: e.activation(out=stL[0:3, :], in_=pq_[0:3, :], func=ACT.Copy), r=pk, w=["stL"])
                S.dma("sync", olc_d, stL[0:3, :], r=["stL"])
                S.dve(lambda e: e.tensor_copy(out=HS2[:, :, 0:1], in_=hc[:].unsqueeze(2)), r=["hc"], w=["HS2"])
                b0, pk = psbank(1)
                ph_ = psf(b0)
                for c in range(4):
                    S.pe(lambda e, c=c, ph_=ph_: e.transpose(out=ph_[0:1, c * 128:(c + 1) * 128], in_=HS2[:, c, 0:1],
                                                             identity=ident32[:]), r=["HS2", "ident32"], w=pk)
                S.act(lambda e, ph_=ph_: e.activation(out=stH[0:1, :], in_=ph_[0:1, :], func=ACT.Copy), r=pk, w=["stH"])
                S.dma("sync", olh_d.rearrange("(a n) -> a n", a=1), stH[0:1, :], r=["stH"])
            if mi != 3:
                continue
            ffn_macro([ti - 3, ti - 2, ti - 1, ti], [(tj - n_pre) * 128 for tj in (ti - 3, ti - 2, ti - 1, ti)], 512)
            if ti == total - 1:
                for g in range(6):
                    b0, pk = psbank(1)
                    pg_ = psf(b0)
                    for c in range(4):
                        S.pe(lambda e, c=c, g=g, pg_=pg_: e.transpose(out=pg_[0:2, c * 128:(c + 1) * 128],
                                                                     in_=Gh[:, g * 4 + c, :], identity=ident32[:]),
                             r=["Gh", "ident32"] + ["Gh%d" % fc for fc in range(24)], w=pk)
                    S.act(lambda e, pg_=pg_: e.activation(out=stL[0:2, :], in_=pg_[0:2, :], func=ACT.Copy),
                          r=pk, w=["stL"])
                    S.dma("sync", ofc_d[:, g * 512:(g + 1) * 512], stL[0:2, :], r=["stL"])

        if do_sample:
            ti = total
            slot = xslot(ti)
            xk = "xt%d" % slot
            free = [i for i in range(NXB) if i != slot]
            fk = ["xt%d" % i for i in free]
            Vc = [xt[free[0]].bitcast(BF16)[:, 0:2048].rearrange("p (s d) -> p s d", d=128),
                  xt[free[1]].bitcast(BF16)[:, 0:2048].rearrange("p (s d) -> p s d", d=128)]
            FS = xt[free[2]][:, 0:768].rearrange("p (c s k) -> p c s k", c=24, k=2)
            FSn = xt[free[3]][:, 0:768].rearrange("p (c s k) -> p c s k", c=24, k=2)
            n3flat = n3T[:].rearrange("p a b -> p (a b)")
            KTc = n3flat[:, 0:2048].rearrange("p (s d) -> p s d", d=128)
            Bown = n3flat[:, 2048:4096].rearrange("p (h n) -> p h n", n=256)
            hmflat = hm[:].rearrange("p a b -> p (a b)")
            S.dma("sync", xt[slot][:], xs[ti * 128:(ti + 1) * 128, :], w=[xk])
            S.dma("sync", stL[0:48, :], slc_d, w=["stL"])
            S.dma("sync", stH[0:16, :], slh_d, w=["stH"])
            S.dma("sync", osk_d[:, 0:124, :], cswk_d[:, 4:128, :])
            S.dma("sync", osv_d[:, 0:124, :], cswv_d[:, 4:128, :])
            S.pool(lambda e: e.memset(xt[free[0]][:], 0.0), w=[fk[0]])
            S.pool(lambda e: e.memset(xt[free[1]][:], 0.0), w=[fk[1]])
            Kc = hmflat[:, 0:2048].rearrange("p (s d) -> p s d", d=128)
            hk03 = ["hm0", "hm1", "hm2", "hm3"]
            S.dma("gpsimd", Kc, cswk_d.rearrange("s k d -> k s d"), w=hk03)
            S.dma("gpsimd", Vc[0][:, :, 0:64], cswv_d.rearrange("s k d -> k s d")[:, :, 0:64], w=[fk[0]])
            S.dma("gpsimd", Vc[1][:, :, 64:128], cswv_d.rearrange("s k d -> k s d")[:, :, 64:128], w=[fk[1]])
            S.dma("gpsimd", Bown, bown_d.rearrange("p (h n) -> p h n", n=256), w=["n3T"])
            for half in range(2):
                b0, pk = psbank(1)
                pv = psb(b0).rearrange("p (t n) -> p t n", n=128)
                for i in range(8):
                    S.pe(lambda e, i=i, half=half, pv=pv: e.transpose(out=pv[:, i, :], in_=Kc[:, half * 8 + i, :],
                                                                      identity=ident[:]),
                         r=hk03 + ["ident"], w=pk)
                S.act(lambda e, half=half, pv=pv: e.activation(out=KTc[:, half * 8:(half + 1) * 8, :], in_=pv,
                                                               func=ACT.Copy), r=pk, w=["n3T"])
            b0, pk = psbank(1)
            p32 = psf(b0)
            for c in range(4):
                S.pe(lambda e, c=c: e.transpose(out=p32[:, c * 48:(c + 1) * 48], in_=stL[0:48, c * 128:(c + 1) * 128],
                                                identity=ident32[0:48, 0:48]), r=["stL", "ident32"], w=pk)
            S.dve(lambda e: e.tensor_copy(out=XP[:, :, :, 0:3],
                                          in_=p32[:, 0:192].rearrange("p (c s k) -> p c s k", c=4, k=3)),
                  r=pk, w=["XP"])
            b0, pk = psbank(1)
            p32b = psf(b0)
            for c in range(4):
                S.pe(lambda e, c=c: e.transpose(out=p32b[:, c * 16:(c + 1) * 16], in_=stH[0:16, c * 128:(c + 1) * 128],
                                                identity=ident32[0:16, 0:16]), r=["stH", "ident32"], w=pk)
            S.dve(lambda e: e.tensor_copy(out=HS[:], in_=p32b[:, 0:64].rearrange("p (c s) -> p c s", c=4)),
                  r=pk, w=["HS"])
            for g in range(6):
                S.dma("sync", stL[0:32, :], sfc_d[:, g * 512:(g + 1) * 512], w=["stL"])
                b0, pk = psbank(1)
                pf = psf(b0)
                for c in range(4):
                    S.pe(lambda e, c=c, pf=pf: e.transpose(out=pf[:, c * 32:(c + 1) * 32],
                                                           in_=stL[0:32, c * 128:(c + 1) * 128],
                                                           identity=ident32[0:32, 0:32]), r=["stL", "ident32"], w=pk)
                S.dve(lambda e, g=g, pf=pf: e.tensor_copy(
                    out=FS[:, g * 4:(g + 1) * 4, :, :],
                    in_=pf[:, 0:128].rearrange("p (c s k) -> p c s k", c=4, k=2)), r=pk, w=[fk[2]])
            nTt, nTk = nT[ti % 2], "nT%d" % (ti % 2)
            rmsnorm_T(xt[slot][:], xk, V_GMIX, nTt[:], nTk, "n1")
            lru_stage(ti, nTt, nTk, smp=True)
            cb, pb = swa_stage(ti, nTt, nTk, False, False)
            S.dve(lambda e: e.tensor_copy(out=XC2[:].rearrange("p c (s k) -> p c s k", k=3), in_=XP[:, :, :, 4:7]),
                  r=["XP"], w=["XC2"])
            b0, pk = psbank(1)
            po32 = psf(b0)
            for c in range(4):
                S.pe(lambda e, c=c: e.transpose(out=po32[0:48, c * 128:(c + 1) * 128], in_=XC2[:, c, :],
                                                identity=ident32[:]), r=["XC2", "ident32"], w=pk)
            S.act(lambda e: e.activation(out=stL[0:48, :], in_=po32[0:48, :], func=ACT.Copy), r=pk, w=["stL"])
            S.dma("sync", oslc_d, stL[0:48, :], r=["stL"])
            b0, pk = psbank(1)
            po32b = psf(b0)
            for c in range(4):
                S.pe(lambda e, c=c: e.transpose(out=po32b[0:16, c * 128:(c + 1) * 128], in_=HS2[:, c, :],
                                                identity=ident32[:]), r=["HS2", "ident32"], w=pk)
            S.act(lambda e: e.activation(out=stH[0:16, :], in_=po32b[0:16, :], func=ACT.Copy), r=pk, w=["stH"])
            S.dma("sync", oslh_d, stH[0:16, :], r=["stH"])
            for t4 in range(4):
                S.dma("sync", osk_d[:, 124 + t4, :], kvf[t4:64:4, 0:128], r=["kvf"])
                S.dma("sync", osv_d[:, 124 + t4, :], kvf[t4:64:4, 128:256], r=["kvf"])
            for sq in range(16):
                swa_attend(cb, pb, False, smp=dict(s=sq, KTc=KTc, Vc=Vc, Bown=Bown, keys=["n3T", fk[0], fk[1]]))
            mixer_out(xt[slot], xk, nTt, nTk)

            def cross_iter():
                for sq in range(16):
                    i2 = sq % 2
                    mkc = hmflat[:, i2 * 1024:(i2 + 1) * 1024].rearrange("p (a b) -> p a b", a=2)
                    mvc = hmflat[:, 2048 + i2 * 1024:2048 + (i2 + 1) * 1024].rearrange("p (a b) -> p a b", a=2)
                    mkTs = hmflat[:, 4096 + i2 * 1024:4096 + (i2 + 1) * 1024].rearrange("p (a b) -> p a b", a=4)
                    kk = ["hm%d" % (2 * i2), "hm%d" % (2 * i2 + 1)]
                    kv = ["hm%d" % (4 + 2 * i2), "hm%d" % (5 + 2 * i2)]
                    kt = ["hm%d" % (8 + 2 * i2), "hm%d" % (9 + 2 * i2)]
                    S.dma("gpsimd", mkc, cmk_d[sq].rearrange("(a p) n -> p a n", p=128), w=kk)
                    S.dma("gpsimd", mvc, cmv_d[sq].rearrange("(a p) n -> p a n", p=128), w=kv)
                    b0, pk = psbank(1)
                    pv = psb(b0).rearrange("p (t n) -> p t n", n=128)
                    for h in range(4):
                        for kb in range(2):
                            S.pe(lambda e, h=h, kb=kb, pv=pv, mkc=mkc: e.transpose(
                                out=pv[:, h * 2 + kb, :], in_=mkc[:, kb, h * 128:(h + 1) * 128], identity=ident[:]),
                                r=kk + ["ident"], w=pk)
                    S.dve(lambda e, pv=pv, mkTs=mkTs: e.tensor_copy(out=mkTs, in_=pv.rearrange("p (h b) n -> p h (b n)", b=2)),
                          r=pk, w=kt)
                    cross_attend(dict(s=sq, mkT=mkTs, mv=mvc, keys=kk + kv + kt))
            cross_stage(xt[slot], xk, nTt, nTk, smp_iter=cross_iter)
            rmsnorm_T(xt[slot][:], xk, V_GFFN, n3T[:, :, 0:128], "n3T", "n3")
            ffn_macro([ti], [n_main * 128], 128, smp=dict(FS=FS, FSn=FSn, keys=[fk[2], fk[3]]))
            for g in range(6):
                b0, pk = psbank(1)
                pf = psf(b0)
                for c in range(4):
                    S.pe(lambda e, c=c, g=g, pf=pf: e.transpose(
                        out=pf[0:32, c * 128:(c + 1) * 128],
                        in_=FSn[:, g * 4 + c, :, :].rearrange("p s k -> p (s k)"), identity=ident32[:]),
                        r=[fk[3], "ident32"], w=pk)
                S.act(lambda e, pf=pf: e.activation(out=stL[0:32, :], in_=pf[0:32, :], func=ACT.Copy), r=pk, w=["stL"])
                S.dma("sync", osfc_d[:, g * 512:(g + 1) * 512], stL[0:32, :], r=["stL"])

        if max_ops is not None:
            S.ops = S.ops[:max_ops]
        print('nops', len(S.ops))
        S.emit()
    return nc


def _slot_heads():
    return [(s // 2) + 4 * (s % 2) for s in range(8)]


def _bias_tables():
    slopes = 2.0 ** (-np.arange(1, 9, dtype=np.float64))
    qi = np.arange(128)[:, None]
    kj = np.arange(256)[None, :]
    dist = qi + 128 - kj
    valid = (dist >= 0) & (dist < 128)
    tab = np.empty((128, 8, 256), np.float32)
    for s, h in enumerate(_slot_heads()):
        tab[:, s, :] = np.where(valid, -8.0 * slopes[h] * dist, NEG)
    return tab


_NC_CACHE = {}


def kernel(**inp):
    f32 = np.float32
    xp = np.asarray(inp["x_prompt"], f32)
    xsmp = np.asarray(inp["x_sample"], f32)
    heads = _slot_heads()
    w_in = np.asarray(inp["w_in"][0], f32)
    qcols = np.concatenate([np.arange(1024 + h * 64, 1024 + (h + 1) * 64) for h in heads])
    w_in_p = np.ascontiguousarray(np.concatenate([w_in[:, :1024], w_in[:, qcols], w_in[:, 1536:]], axis=1))
    w_out = np.asarray(inp["w_out"][0], f32)
    orow = np.concatenate([np.arange(512 + h * 64, 512 + (h + 1) * 64) for h in heads])
    w_out_p = np.ascontiguousarray(np.concatenate([w_out[:512], w_out[orow]], axis=0))

    def bd(w):
        o = np.zeros((128, 4, 128), f32)
        for c in range(4):
            o[0:64, c, 0:64] = w[2 * c]
            o[64:128, c, 64:128] = w[2 * c + 1]
        return o.reshape(128, 512)

    def fm(v, nchunk):
        return np.asarray(v, f32).reshape(nchunk, 128).T

    vec = np.zeros((128, NV), f32)
    vec[:, V_GMIX:V_GMIX + 8] = fm(inp["g_mix"][0], 8)
    vec[:, V_GCROSS:V_GCROSS + 8] = fm(inp["g_cross"][0], 8)
    vec[:, V_GFFN:V_GFFN + 8] = fm(inp["g_ffn"][0], 8)
    vec[:, V_GMEM:V_GMEM + 8] = fm(inp["g_mem"][0], 8)
    for tap in range(4):
        vec[:, V_CW + tap * 4:V_CW + tap * 4 + 4] = fm(inp["w_lru_conv"][0, tap], 4)
    vec[:, V_BCONV:V_BCONV + 4] = fm(inp["b_lru_conv"][0], 4)
    vec[:, V_BA:V_BA + 4] = fm(inp["b_lru_a"][0], 4)
    vec[:, V_BX:V_BX + 4] = fm(inp["b_lru_x"][0], 4)
    vec[:, V_LAM:V_LAM + 4] = fm(inp["lru_lambda"][0], 4)
    for tap in range(3):
        vec[:, V_FW + tap * 24:V_FW + tap * 24 + 24] = fm(inp["w_ffn_conv"][0, tap], 24)
    vec[:, V_FB:V_FB + 24] = fm(inp["b_ffn_conv"][0], 24)
    vec[:, V_SINK:V_SINK + 8] = np.asarray(inp["attn_sinks"][0], f32)[heads][None, :]
    vec[:, V_EPS] = 1e-6
    vec[:, V_ONE] = 1.0

    slopes = 2.0 ** (-np.arange(1, 9, dtype=np.float64))
    bown = np.full((128, 8, 256), NEG, f32)
    for s_i, h in enumerate(heads):
        for t in range(4):
            for j in range(t + 1):
                bown[t, s_i, 128 + j] = -8.0 * slopes[h] * (t - j)
    gfin = np.ascontiguousarray(np.broadcast_to(np.asarray(inp["g_final"], f32)[None, :], (128, D)))
    ident = np.eye(128, dtype=f32)
    bias = _bias_tables()
    common = dict(
        gfin=gfin, ident=ident, ident32=ident, bias_own=bown.reshape(128, -1),
        bias=bias.reshape(128, -1), w_in=w_in_p, w_out=w_out_p,
        w_q=np.ascontiguousarray(inp["w_mem_q"][0], f32), w_k=np.ascontiguousarray(inp["w_mem_k"][0], f32),
        w_v=np.ascontiguousarray(inp["w_mem_v"][0], f32), w_o=np.ascontiguousarray(inp["w_mem_o"][0], f32),
        wa_bd=bd(np.asarray(inp["w_lru_a"][0], f32)), wx_bd=bd(np.asarray(inp["w_lru_x"][0], f32)),
        w_g=np.ascontiguousarray(inp["w_ffn_gate"][0], f32), w_u=np.ascontiguousarray(inp["w_ffn_up"][0], f32),
        w_d=np.ascontiguousarray(inp["w_ffn_down"][0], f32),
    )
    in_maps = []
    for c in range(NCORES):
        b, q = c // 4, c % 4
        pre = np.zeros((PRE_T * 128, D), f32)
        if q > 0:
            pre[(3 - q) * 2048:] = xp[b, :q * 2048]
        main = xp[b, q * 2048:(q + 1) * 2048]
        smp = np.zeros((128, D), f32)
        smp[:64] = xsmp[c * 16:(c + 1) * 16].reshape(64, D)
        v = vec.copy()
        for k in range(3):
            v[:, V_FLAG + k] = 1.0 if k >= 3 - q else 0.0
        v[:, V_FLAG + 3] = 1.0 if q > 0 else 0.0
        b0 = bias[:, :, 0:128].copy()
        if q == 0:
            b0[:] = NEG
        m = dict(common)
        m.update(xs=np.ascontiguousarray(np.concatenate([pre, main, smp], axis=0)),
                 memx=np.ascontiguousarray(inp["mem_prompt"][b], f32), vec=v,
                 c_swa_k=np.ascontiguousarray(inp["cache_swa_k"][0, c * 16:(c + 1) * 16], f32).reshape(16, 128, 128),
                 c_swa_v=np.ascontiguousarray(inp["cache_swa_v"][0, c * 16:(c + 1) * 16], f32).reshape(16, 128, 128),
                 c_mem_k=np.ascontiguousarray(inp["cache_mem_k"][0, c * 16:(c + 1) * 16], f32).reshape(16, 256, 512),
                 c_mem_v=np.ascontiguousarray(inp["cache_mem_v"][0, c * 16:(c + 1) * 16], f32).reshape(16, 256, 512),
                 st_lconv=np.ascontiguousarray(inp["state_lru_conv"][0, c * 16:(c + 1) * 16], f32).reshape(48, 512),
                 st_lh=np.ascontiguousarray(inp["state_lru_h"][0, c * 16:(c + 1) * 16], f32),
                 st_fconv=np.ascontiguousarray(inp["state_ffn_conv"][0, c * 16:(c + 1) * 16], f32).reshape(32, 3072),
                 bias0=np.ascontiguousarray(b0.reshape(128, -1)))
        in_maps.append(m)

    if "nc" not in _NC_CACHE:
        _NC_CACHE["nc"] = build_nc()
    nc = _NC_CACHE["nc"]
    res = run_bass_kernel_spmd(nc, in_maps, core_ids=list(range(NCORES)))
    R = res.results

    y_prompt = np.zeros((2, 8192, D), f32)
    y_sample = np.zeros((128, 4, D), f32)
    for c in range(NCORES):
        b, q = c // 4, c % 4
        y_prompt[b, q * 2048:(q + 1) * 2048] = R[c]["y"][:2048]
        y_sample[c * 16:(c + 1) * 16] = R[c]["y"][2048:2048 + 64].reshape(16, 4, D)
    p_swa_k = np.stack([R[3]["o_kv"][:, :128], R[7]["o_kv"][:, :128]]).reshape(1, 2, 128, 2, 64)
    p_swa_v = np.stack([R[3]["o_kv"][:, 128:], R[7]["o_kv"][:, 128:]]).reshape(1, 2, 128, 2, 64)
    p_mem_k = np.stack([R[0]["o_mk"], R[4]["o_mk"]]).reshape(1, 2, 256, 4, 128)
    p_mem_v = np.stack([R[0]["o_mv"], R[4]["o_mv"]]).reshape(1, 2, 256, 4, 128)
    p_lru_conv = np.stack([R[3]["o_lconv"], R[7]["o_lconv"]]).reshape(1, 2, 3, 512)
    p_lru_h = np.stack([R[3]["o_lh"], R[7]["o_lh"]]).reshape(1, 2, 512)
    p_ffn_conv = np.stack([R[3]["o_fconv"], R[7]["o_fconv"]]).reshape(1, 2, 2, 3072)
    cat = lambda k: np.concatenate([R[c][k] for c in range(NCORES)], axis=0)
    s_swa_k = cat("o_sk").reshape(1, 128, 128, 2, 64)
    s_swa_v = cat("o_sv").reshape(1, 128, 128, 2, 64)
    s_lru_conv = cat("o_slc").reshape(1, 128, 3, 512)
    s_lru_h = cat("o_slh").reshape(1, 128, 512)
    s_ffn_conv = cat("o_sfc").reshape(1, 128, 2, 3072)
    return (y_prompt, y_sample, p_swa_k, p_swa_v, p_mem_k, p_mem_v, p_lru_conv, p_lru_h, p_ffn_conv,
            s_swa_k, s_swa_v, s_lru_conv, s_lru_h, s_ffn_conv)
```

```python
import contextlib
import numpy as np
import concourse.bass as bass
import concourse.mybir as mybir
from concourse.bass_utils import run_bass_kernel_spmd

F32 = mybir.dt.float32
BF16 = mybir.dt.bfloat16
ACT = mybir.ActivationFunctionType
ALU = mybir.AluOpType
AX = mybir.AxisListType

NCORES = 8
D = 1024
PRE_T = 48
MAIN_T = 16
NEG = -240000.0
SC_MEM = 128.0 ** -0.5

V_GMIX, V_GCROSS, V_GFFN, V_GMEM = 0, 8, 16, 24
V_CW = 32
V_BCONV = 48
V_BA = 52
V_BX = 56
V_LAM = 60
V_FW = 64
V_FB = 136
V_SINK = 160
V_FLAG = 168
V_EPS = 172
V_ONE = 173
NV = 176


class Sched:
    def __init__(self, nc, es, dma_pool=None):
        self.nc = nc
        self.ops = []
        self.last_w = {}
        self.readers = {}
        self.dma_pool = dma_pool or {"sync": 8, "gpsimd": 6, "scalar": 4}
        self.sems = {}
        for e in ("scalar", "vector", "gpsimd", "tensor"):
            self.sems[e] = es.enter_context(nc.semaphore("c_" + e))
        self.dsems = {}
        for q, n in self.dma_pool.items():
            self.dsems[q] = [es.enter_context(nc.semaphore("d_%s%d" % (q, i))) for i in range(n)]
        self.dcount = {q: 0 for q in self.dma_pool}

    def add(self, eng, fn, r=(), w=(), dma=False):
        i = len(self.ops)
        deps = {}
        for k in r:
            a = self.last_w.get(k)
            if a is not None:
                deps[a] = "raw"
        for k in w:
            a = self.last_w.get(k)
            if a is not None:
                deps[a] = "raw"
            for a in self.readers.get(k, ()):
                deps.setdefault(a, "war")
        op = dict(eng=eng, fn=fn, deps=deps, dma=dma, signal=False)
        if dma:
            j = self.dcount[eng]
            self.dcount[eng] += 1
            n = self.dma_pool[eng]
            op["dsem"] = (eng, j % n)
            op["dval"] = 16 * (j // n + 1)
        self.ops.append(op)
        for k in w:
            self.last_w[k] = i
            self.readers[k] = []
        for k in r:
            lst = self.readers.setdefault(k, [])
            if not dma:
                lst[:] = [a for a in lst if self.ops[a]["dma"] or self.ops[a]["eng"] != eng]
            lst.append(i)
        return i

    def act(self, fn, r=(), w=()):
        return self.add("scalar", fn, r, w)

    def dve(self, fn, r=(), w=()):
        return self.add("vector", fn, r, w)

    def pool(self, fn, r=(), w=()):
        return self.add("gpsimd", fn, r, w)

    def pe(self, fn, r=(), w=()):
        return self.add("tensor", fn, r, w)

    def dma(self, q, out, in_, r=(), w=(), **kw):
        return self.add(q, lambda e: e.dma_start(out=out, in_=in_, **kw), r, w, dma=True)

    def emit(self):
        ops = self.ops
        for b in ops:
            for a, kind in b["deps"].items():
                A = ops[a]
                if A["dma"]:
                    continue
                if A["eng"] == b["eng"] and not b["dma"]:
                    if A["eng"] == "tensor":
                        continue
                A["signal"] = True
        cnt = {e: 0 for e in self.sems}
        for o in ops:
            if not o["dma"] and o["signal"]:
                cnt[o["eng"]] += 1
                o["sval"] = cnt[o["eng"]]
        seen = {}
        last_on_dsem = {}
        for o in ops:
            e = o["eng"]
            sn = seen.setdefault(e, {})
            need = {}
            for a, kind in o["deps"].items():
                A = ops[a]
                if A["dma"]:
                    key, val = ("d",) + A["dsem"], A["dval"]
                else:
                    if A["eng"] == e and not o["dma"]:
                        if e == "tensor":
                            continue
                    key, val = ("c", A["eng"]), A["sval"]
                if need.get(key, 0) < val:
                    need[key] = val
            if o["dma"]:
                key = ("d",) + o["dsem"]
                prev = o["dval"] - 16
                if prev > 0 and need.get(key, 0) < prev:
                    need[key] = prev
            waits = []
            for key, val in need.items():
                if sn.get(key, 0) < val:
                    sn[key] = val
                    waits.append((key, val))
            o["waits"] = waits
        final = {}
        for o in ops:
            if o["dma"]:
                final[("d",) + o["dsem"]] = o["dval"]
        engs = ["sync", "scalar", "vector", "gpsimd", "tensor"]
        per = {e: [o for o in ops if o["eng"] == e] for e in engs}

        def semof(key):
            if key[0] == "c":
                return self.sems[key[1]]
            return self.dsems[key[1]][key[2]]

        def run(e, lst, tail):
            for o in lst:
                for key, val in o["waits"]:
                    e.wait_ge(semof(key), val)
                ins = o["fn"](e)
                if o["dma"]:
                    ins.then_inc(semof(("d",) + o["dsem"]), 16)
                elif o["signal"]:
                    ins.then_inc(self.sems[o["eng"]], 1)
            if tail:
                for key, val in final.items():
                    e.wait_ge(semof(key), val)

        with self.nc.Block() as block:
            block.sync(lambda e: run(e, per["sync"], True))
            block.scalar(lambda e: run(e, per["scalar"], False))
            block.vector(lambda e: run(e, per["vector"], False))
            block.gpsimd(lambda e: run(e, per["gpsimd"], False))
            block.tensor(lambda e: run(e, per["tensor"], False))


def build_nc(do_sample=True, n_pre=PRE_T, n_main=MAIN_T, max_ops=None):
    nc = bass.Bass("TRN2", target_bir_lowering=False)
    NT = n_pre + n_main + 1

    def din(name, shape):
        return nc.dram_tensor(name, list(shape), F32, kind="ExternalInput").ap()

    def dout(name, shape):
        return nc.dram_tensor(name, list(shape), F32, kind="ExternalOutput").ap()

    xs = din("xs", [NT * 128, D])
    memx = din("memx", [256, D])
    vec_d = din("vec", [128, NV])
    gfin_d = din("gfin", [128, D])
    ident_d = din("ident", [128, 128])
    bias_d = din("bias", [128, 8 * 256])
    bias0_d = din("bias0", [128, 8 * 128])
    w_in_d = din("w_in", [D, 1792])
    w_out_d = din("w_out", [D, D])
    w_q_d = din("w_q", [D, 512])
    w_k_d = din("w_k", [D, 512])
    w_v_d = din("w_v", [D, 512])
    w_o_d = din("w_o", [512, D])
    wa_d = din("wa_bd", [128, 512])
    wx_d = din("wx_bd", [128, 512])
    w_g_d = din("w_g", [D, 3072])
    w_u_d = din("w_u", [D, 3072])
    w_d_d = din("w_d", [3072, D])

    ident32_d = din("ident32", [128, 128])
    bown_d = din("bias_own", [128, 8 * 256])
    cswk_d = din("c_swa_k", [16, 128, 128])
    cswv_d = din("c_swa_v", [16, 128, 128])
    cmk_d = din("c_mem_k", [16, 256, 512])
    cmv_d = din("c_mem_v", [16, 256, 512])
    slc_d = din("st_lconv", [48, 512])
    slh_d = din("st_lh", [16, 512])
    sfc_d = din("st_fconv", [32, 3072])
    osk_d = dout("o_sk", [16, 128, 128])
    osv_d = dout("o_sv", [16, 128, 128])
    oslc_d = dout("o_slc", [48, 512])
    oslh_d = dout("o_slh", [16, 512])
    osfc_d = dout("o_sfc", [32, 3072])
    y_d = dout("y", [(n_main + 1) * 128, D])
    okv_d = dout("o_kv", [128, 256])
    omk_d = dout("o_mk", [256, 512])
    omv_d = dout("o_mv", [256, 512])
    olc_d = dout("o_lconv", [3, 512])
    olh_d = dout("o_lh", [512])
    ofc_d = dout("o_fconv", [2, 3072])

    es = contextlib.ExitStack()
    with es:
        def sb(name, shape, dt=F32):
            return es.enter_context(nc.sbuf_tensor("s_" + name, list(shape), dt))

        S = Sched(nc, es)
        psall = es.enter_context(nc.psum_tensor("psall", [128, 8 * 512], F32))
        ps_ctr = [0]

        def psbank(n=1):
            b0 = ps_ctr[0]
            if b0 + n > 8:
                b0 = 0
            ps_ctr[0] = (b0 + n) % 8
            return b0, ["ps%d" % (b0 + i) for i in range(n)]

        def psf(b0, n=1):
            return psall[:, b0 * 512:(b0 + n) * 512]

        psall_bf = psall.bitcast(BF16)

        def psb(b0, n=1):
            return psall_bf[:, b0 * 1024:(b0 + n) * 1024]

        vec = sb("vec", [128, NV])
        gfin = sb("gfin", [128, D])
        ident = sb("ident", [128, 128], BF16)
        biasT = sb("biasT", [128, 8, 256], BF16)
        bias0 = sb("bias0", [128, 8, 128], BF16)
        w_in = sb("w_in", [128, 8, 1792], BF16)
        w_out = sb("w_out", [128, 8, 1024], BF16)
        w_q = sb("w_q", [128, 8, 512], BF16)
        w_o = sb("w_o", [128, 4, 1024], BF16)
        wa = sb("wa", [128, 4, 128], BF16)
        wx = sb("wx", [128, 4, 128], BF16)
        NRING = 4
        ring = [sb("ring%d" % i, [128, 4096], BF16) for i in range(NRING)]
        ring_ctr = [0]
        mkT = sb("mkT", [128, 4, 256], BF16)
        mvb = sb("mvb", [128, 2, 512], BF16)
        cl = sb("cl", [128, 4])
        sink8 = sb("sink8", [128, 8])
        hc = sb("hc", [128, 4])
        NXB = 5
        xt = [sb("xt%d" % i, [128, D]) for i in range(NXB)]
        nbf = sb("nbf", [128, D], BF16)
        junk = nbf
        nT = [sb("nT%d" % i, [128, 8, 128], BF16) for i in range(2)]
        n3T = sb("n3T", [128, 8, 512], BF16)
        st = sb("stat", [128, 64])
        xrp = [sb("xrp%d" % i, [128, 4, 131]) for i in range(2)]
        xc = sb("xc", [128, 4, 128])
        xcb = sb("xcb", [128, 4, 128], BF16)
        rg = sb("rg", [128, 4, 128])
        ig = sb("ig", [128, 4, 128])
        av = sb("av", [128, 4, 128])
        bv = sb("bv", [128, 4, 128])
        hv = sb("hv", [128, 4, 128])
        yT = sb("yT", [128, 8, 128], BF16)
        QT = sb("QT", [128, 4, 128], BF16)
        KT = [sb("KT%d" % i, [128, 128], BF16) for i in range(2)]
        Vp = [sb("Vp%d" % i, [128, 2, 128], BF16) for i in range(2)]
        kvf = sb("kvf", [128, 256])
        Pm = sb("Pm", [128, 8, 256], BF16)
        PT = sb("PT", [128, 16, 128], BF16)
        QC = sb("QC", [128, 4, 128], BF16)
        OC = sb("OC", [128, 4, 128], BF16)
        Gs = [sb("Gs%d" % i, [128, 514]) for i in range(2)]
        t1 = [sb("t1_%d" % i, [128, 512]) for i in range(2)]
        mtmp = t1[0]
        gg = rg
        hm = sb("hm", [128, 12, 512], BF16)
        Gh = sb("Gh", [128, 24, 2])

        ident32 = sb("ident32", [128, 128])
        XP = sb("XP", [128, 4, 16, 7])
        XC2 = sb("XC2", [128, 4, 48])
        HS = sb("HS", [128, 4, 16])
        HS2 = sb("HS2", [128, 4, 16])
        stL = sb("stL", [128, 512])
        stH = sb("stH", [128, 512])

        HALO = sb("HALO", [128, 4, 3])
        fence = sb("fence", [128, 2])

        def vcol(c, n=1):
            return vec[:, c:c + n]

        S.dma("sync", vec[:], vec_d, w=["vec"])
        S.dma("sync", gfin[:], gfin_d, w=["gfin"])
        S.dma("sync", ident32[:], ident32_d, w=["ident32"])
        S.dma("gpsimd", ident[:], ident_d, w=["ident"])
        S.dma("gpsimd", biasT[:].rearrange("p a b -> p (a b)"), bias_d, w=["biasT"])
        S.dma("gpsimd", bias0[:].rearrange("p a b -> p (a b)"), bias0_d, w=["bias0"])
        S.dma("gpsimd", wa[:].rearrange("p a b -> p (a b)"), wa_d, w=["wa"])
        S.dma("gpsimd", wx[:].rearrange("p a b -> p (a b)"), wx_d, w=["wx"])
        S.dma("gpsimd", w_in[:], w_in_d.rearrange("(k p) n -> p k n", p=128), w=["w_in"])
        wk_s = ring[0][:].rearrange("p (k n) -> p k n", k=8)
        wv_s = ring[1][:].rearrange("p (k n) -> p k n", k=8)
        S.dma("gpsimd", wk_s, w_k_d.rearrange("(k p) n -> p k n", p=128), w=["ring0"])
        S.dma("gpsimd", wv_s, w_v_d.rearrange("(k p) n -> p k n", p=128), w=["ring1"])
        S.dma("gpsimd", w_q[:], w_q_d.rearrange("(k p) n -> p k n", p=128), w=["w_q"])
        S.dma("gpsimd", w_o[:], w_o_d.rearrange("(k p) n -> p k n", p=128), w=["w_o"])

        S.pool(lambda e: e.memset(hc[:], 0.0), w=["hc"])
        S.pool(lambda e: e.memset(xrp[1][:], 0.0), w=["xrp1"])
        S.pool(lambda e: e.memset(xrp[0][:], 0.0), w=["xrp0"])
        for i in range(2):
            S.pool(lambda e, i=i: e.memset(Vp[i][:], 0.0), w=["Vp%d" % i])
            S.pool(lambda e, i=i: e.memset(KT[i][:], 0.0), w=["KT%d" % i])
        S.pool(lambda e: e.memset(Gh[:], 0.0), w=["Gh"])

        S.act(lambda e: e.activation(out=st[:, 0:4], in_=vcol(V_LAM, 4), func=ACT.Exp, scale=-1.0),
              r=["vec"], w=["st_a"])
        S.act(lambda e: e.activation(out=st[:, 4:8], in_=st[:, 0:4], func=ACT.Ln, bias=vcol(V_ONE), scale=1.0),
              r=["st_a", "vec"], w=["st_b"])
        S.dve(lambda e: e.tensor_scalar(out=cl[:], in0=st[:, 4:8], scalar1=-8.0, scalar2=None, op0=ALU.mult),
              r=["st_b"], w=["cl"])
        S.dve(lambda e: e.tensor_scalar(out=sink8[:], in0=vcol(V_SINK, 8), scalar1=8.0, scalar2=None, op0=ALU.mult),
              r=["vec"], w=["sink8"])

        def rmsnorm_T(xap, xkey, gcol, dst, dstkey, tagn):
            S.act(lambda e: e.activation(out=junk[:], in_=xap, func=ACT.Square, accum_out=st[:, 8:9]),
                  r=[xkey], w=["nbf", "st_ss"])
            S.act(lambda e: e.activation(out=st[:, 9:10], in_=st[:, 8:9], func=ACT.Sqrt,
                                         bias=vcol(V_EPS), scale=1.0 / D),
                  r=["st_ss", "vec"], w=["st_sd"])
            S.dve(lambda e: e.reciprocal(out=st[:, 10:11], in_=st[:, 9:10]), r=["st_sd"], w=["st_rs"])
            S.dve(lambda e: e.tensor_scalar(out=nbf[:], in0=xap, scalar1=st[:, 10:11], scalar2=None, op0=ALU.mult),
                  r=[xkey, "st_rs"], w=["nbf"])
            b0, pk = psbank(1)
            pv = psb(b0).rearrange("p (k n) -> p k n", k=8)
            for k in range(8):
                S.pe(lambda e, k=k: e.transpose(out=pv[:, k, :], in_=nbf[:, k * 128:(k + 1) * 128], identity=ident[:]),
                     r=["nbf", "ident"], w=pk)
            gb = vec[:, gcol:gcol + 8].unsqueeze(2).to_broadcast([128, 8, 128])
            S.dve(lambda e: e.tensor_tensor(out=dst, in0=pv, in1=gb, op=ALU.mult),
                  r=pk + ["vec"], w=[dstkey])

        def fm_proj(wsb, wkey, col0, nchunks, src, srckey, n=128, width=128):
            b0, pk = psbank(1)
            pv = psf(b0).rearrange("p (c n) -> p c n", c=512 // n)
            for c in range(nchunks):
                for k in range(8):
                    S.pe(lambda e, c=c, k=k: e.matmul(pv[:, c, :], lhsT=wsb[:, k, col0 + c * width:col0 + (c + 1) * width],
                                                       rhs=src[:, k, :], start=(k == 0), stop=(k == 7)),
                         r=[wkey, srckey], w=pk)
            return pv, pk

        def lru_stage(ti, nTt, nTkey, smp=False):
            cur, prv = xrp[ti % 2], xrp[(ti + 1) % 2]
            ck, pk_ = "xrp%d" % (ti % 2), "xrp%d" % ((ti + 1) % 2)
            if not smp:
                S.pool(lambda e: e.tensor_copy(out=cur[:, :, 0:3], in_=prv[:, :, 128:131]), r=[pk_], w=[ck])
            pxr, kxr = fm_proj(w_in, "w_in", 0, 4, nTt, nTkey)
            if not smp:
                S.act(lambda e: e.activation(out=cur[:, :, 3:131], in_=pxr, func=ACT.Copy), r=kxr, w=[ck])
            else:
                for c in range(4):
                    S.act(lambda e, c=c: e.activation(out=XP[:, c, :, 3:7],
                                                      in_=pxr[:, c, 0:64].rearrange("p (s t) -> p s t", t=4),
                                                      func=ACT.Copy), r=kxr, w=["XP"])
            convs = []
            for c in range(4):
                if not smp:
                    o_ap = xc[:, c, :]
                    in_tap = lambda tap, c=c: cur[:, c, tap:tap + 128]
                    srck = ck
                else:
                    o_ap = xc[:, c, 0:64].rearrange("p (s t) -> p s t", t=4)
                    in_tap = lambda tap, c=c: XP[:, c, :, tap:tap + 4]
                    srck = "XP"
                convs.append((o_ap, in_tap, srck))
            for c in range(4):
                o_ap, in_tap, srck = convs[c]
                S.dve(lambda e, c=c, o_ap=o_ap, in_tap=in_tap: e.tensor_scalar(
                    out=o_ap, in0=in_tap(3), scalar1=vcol(V_CW + 12 + c), scalar2=vcol(V_BCONV + c),
                    op0=ALU.mult, op1=ALU.add), r=[srck, "vec"], w=["xc%d" % c])
            for tap in range(3):
                for c in range(4):
                    o_ap, in_tap, srck = convs[c]
                    S.dve(lambda e, c=c, tap=tap, o_ap=o_ap, in_tap=in_tap: e.scalar_tensor_tensor(
                        out=o_ap, in0=in_tap(tap), scalar=vcol(V_CW + tap * 4 + c),
                        in1=o_ap, op0=ALU.mult, op1=ALU.add),
                        r=[srck, "vec", "xc%d" % c], w=["xc%d" % c])
            xck = ["xc%d" % c for c in range(4)]
            S.act(lambda e: e.activation(out=xcb[:], in_=xc[:], func=ACT.Copy), r=xck, w=["xcb"])
            b0, pk = psbank(2)
            pr = psf(b0).rearrange("p (c n) -> p c n", c=4)
            pi = psf(b0 + 1).rearrange("p (c n) -> p c n", c=4)
            for c in range(4):
                S.pe(lambda e, c=c: e.matmul(pr[:, c, :], lhsT=wa[:, c, :], rhs=xcb[:, c, :], start=True, stop=True),
                     r=["wa", "xcb"], w=[pk[0]])
            for c in range(4):
                S.pe(lambda e, c=c: e.matmul(pi[:, c, :], lhsT=wx[:, c, :], rhs=xcb[:, c, :], start=True, stop=True),
                     r=["wx", "xcb"], w=[pk[1]])
            for c in range(4):
                S.act(lambda e, c=c: e.activation(out=rg[:, c, :], in_=pr[:, c, :], func=ACT.Sigmoid,
                                                  bias=vcol(V_BA + c), scale=1.0),
                      r=[pk[0], "vec"], w=["rg"])
            for c in range(4):
                S.act(lambda e, c=c: e.activation(out=ig[:, c, :], in_=pi[:, c, :], func=ACT.Sigmoid,
                                                  bias=vcol(V_BX + c), scale=1.0),
                      r=[pk[1], "vec"], w=["ig"])
            for c in range(4):
                S.act(lambda e, c=c: e.activation(out=av[:, c, :], in_=rg[:, c, :], func=ACT.Exp, scale=cl[:, c:c + 1]),
                      r=["rg", "cl"], w=["av"])
            bvf, avf = bv[:].rearrange("p a b -> p (a b)"), av[:].rearrange("p a b -> p (a b)")
            S.dve(lambda e: e.scalar_tensor_tensor(out=bvf, in0=avf, scalar=-1.0, in1=avf, op0=ALU.mult, op1=ALU.mult),
                  r=["av"], w=["bv"])
            S.act(lambda e: e.activation(out=bv[:], in_=bv[:], func=ACT.Sqrt, bias=vcol(V_ONE), scale=1.0),
                  r=["bv", "vec"], w=["bv"])
            S.pool(lambda e: e.tensor_tensor(out=ig[:], in0=ig[:], in1=xc[:], op=ALU.mult), r=["ig"] + xck, w=["ig"])
            S.dve(lambda e: e.tensor_tensor(out=bv[:], in0=bv[:], in1=ig[:], op=ALU.mult), r=["bv", "ig"], w=["bv"])
            if smp:
                v4 = lambda t_, tt: t_[:, :, 0:64].rearrange("p c (s t) -> p c s t", t=4)[:, :, :, tt]
                for tt in range(4):
                    hprev = HS[:] if tt == 0 else v4(hv, tt - 1)
                    S.dve(lambda e, tt=tt, hprev=hprev: e.tensor_tensor(out=v4(hv, tt), in0=v4(av, tt), in1=hprev,
                                                                        op=ALU.mult), r=["av", "hv", "HS"], w=["hv"])
                    S.dve(lambda e, tt=tt: e.tensor_tensor(out=v4(hv, tt), in0=v4(hv, tt), in1=v4(bv, tt),
                                                           op=ALU.add), r=["bv", "hv"], w=["hv"])
                S.dve(lambda e: e.tensor_copy(out=HS2[:], in_=v4(hv, 3)), r=["hv"], w=["HS2"])
                return
            for c in range(4):
                S.dve(lambda e, c=c: e.tensor_tensor_scan(out=hv[:, c, :], data0=av[:, c, :], data1=bv[:, c, :],
                                                           initial=hc[:, c:c + 1], op0=ALU.mult, op1=ALU.add),
                      r=["av", "bv", "hc"], w=["hv"])
            S.dve(lambda e: e.tensor_copy(out=hc[:], in_=hv[:, :, 127]), r=["hv"], w=["hc"])

        def attn_core(nh, Sps, Skeys, scale, sinkcol, Pbuf, Pkey, nkb, rows=128):
            W = nkb * 128
            S.dve(lambda e: e.reduce_max(out=st[:rows, 16:16 + nh], in_=Sps[:rows], axis=AX.X), r=Skeys, w=["st_mx"])
            if sinkcol is not None:
                S.dve(lambda e: e.tensor_tensor(out=st[:rows, 16:16 + nh], in0=st[:rows, 16:16 + nh],
                                                in1=sink8[:rows, :], op=ALU.max),
                      r=["st_mx", "sink8"], w=["st_mx"])
            S.dve(lambda e: e.tensor_scalar(out=st[:rows, 24:24 + nh], in0=st[:rows, 16:16 + nh], scalar1=-scale,
                                            scalar2=None, op0=ALU.mult),
                  r=["st_mx"], w=["st_nm"])
            for h in range(nh):
                S.act(lambda e, h=h: e.activation(out=Pbuf[:rows, h, 0:W], in_=Sps[:rows, h, :], func=ACT.Exp,
                                                  bias=st[:rows, 24 + h:25 + h], scale=scale,
                                                  accum_out=st[:rows, 32 + h:33 + h]),
                      r=Skeys + ["st_nm"], w=[Pkey, "st_rs%d" % h])
            rsk = ["st_rs%d" % h for h in range(nh)]
            if sinkcol is not None:
                S.dve(lambda e: e.tensor_tensor(out=st[:rows, 40:48], in0=st[:rows, 24:32],
                                                in1=vec[:rows, sinkcol:sinkcol + 8], op=ALU.add),
                      r=["st_nm", "vec"], w=["st_es"])
                S.act(lambda e: e.activation(out=st[:rows, 40:48], in_=st[:rows, 40:48], func=ACT.Exp),
                      r=["st_es"], w=["st_es"])
                S.dve(lambda e: e.tensor_tensor(out=st[:rows, 32:40], in0=st[:rows, 32:40], in1=st[:rows, 40:48],
                                                op=ALU.add),
                      r=["st_es"] + rsk, w=rsk)
            S.dve(lambda e: e.reciprocal(out=st[:rows, 48:48 + nh], in_=st[:rows, 32:32 + nh]), r=rsk, w=["st_ri"])
            for h in range(nh):
                S.dve(lambda e, h=h: e.tensor_scalar(out=Pbuf[:rows, h, 0:W], in0=Pbuf[:rows, h, 0:W],
                                                     scalar1=st[:rows, 48 + h:49 + h], scalar2=None, op0=ALU.mult),
                      r=[Pkey, "st_ri"], w=[Pkey])
            nt = nh * nkb
            nb = (nt * 128 + 1023) // 1024
            b0, pk = psbank(nb)
            pv = psb(b0, nb).rearrange("p (t n) -> p t n", n=128)
            for h in range(nh):
                for kb in range(nkb):
                    t = h * nkb + kb
                    S.pe(lambda e, h=h, kb=kb, t=t: e.transpose(out=pv[:, t, 0:rows],
                                                                in_=Pbuf[:rows, h, kb * 128:(kb + 1) * 128],
                                                                identity=ident[:rows, :rows]),
                         r=[Pkey, "ident"], w=[pk[(t * 128) // 1024]])
            half = nt // 2
            if nb == 1:
                S.act(lambda e: e.activation(out=PT[:, 0:nt, 0:rows], in_=pv[:, 0:nt, 0:rows], func=ACT.Copy),
                      r=pk, w=["PTa", "PTb"])
            else:
                S.act(lambda e: e.activation(out=PT[:, 0:half, 0:rows], in_=pv[:, 0:half, 0:rows], func=ACT.Copy),
                      r=pk, w=["PTa"])
                S.dve(lambda e: e.tensor_copy(out=PT[:, half:nt, 0:rows], in_=pv[:, half:nt, 0:rows]),
                      r=pk, w=["PTb"])

        def swa_stage(ti, nTt, nTkey, first_block, bias_first, rows=128, qcols=None):
            cb, pb = ti % 2, (ti + 1) % 2
            pq, kq = fm_proj(w_in, "w_in", 1024, 4, nTt, nTkey)
            S.act(lambda e: e.activation(out=QT[:], in_=pq, func=ACT.Copy), r=kq, w=["QT"])
            b0, pk = psbank(1)
            pkk = psf(b0)[:, 0:128]
            pkv = psf(b0)[:, 128:384]
            for k in range(8):
                S.pe(lambda e, k=k: e.matmul(pkk, lhsT=w_in[:, k, 1536:1664], rhs=nTt[:, k, :],
                                             start=(k == 0), stop=(k == 7)), r=["w_in", nTkey], w=pk)
            for k in range(8):
                S.pe(lambda e, k=k: e.matmul(pkv, lhsT=nTt[:, k, :], rhs=w_in[:, k, 1536:1792],
                                             start=(k == 0), stop=(k == 7)), r=["w_in", nTkey], w=pk)
            S.act(lambda e: e.activation(out=KT[cb][:], in_=pkk, func=ACT.Copy), r=pk, w=["KT%d" % cb])
            S.act(lambda e: e.activation(out=Vp[cb][:, 0, 0:64], in_=pkv[:, 128:192], func=ACT.Copy), r=pk, w=["Vp%d" % cb])
            S.act(lambda e: e.activation(out=Vp[cb][:, 1, 64:128], in_=pkv[:, 192:256], func=ACT.Copy), r=pk, w=["Vp%d" % cb])
            S.act(lambda e: e.activation(out=kvf[:], in_=pkv, func=ACT.Copy), r=pk, w=["kvf"])
            return cb, pb

        def swa_attend(cb, pb, bias_first, smp=None):
            b0, sk = psbank(4)
            Sps = psf(b0, 4).rearrange("p (h n) -> p h n", h=8)
            if smp is None:
                rows, q0 = 128, 0
                xkeys = []
            else:
                rows, q0 = 4, 4 * smp["s"]
                xkeys = smp["keys"]
            idn = ident[0:rows, 0:rows]
            for j in range(4):
                for b in range(2):
                    s_ = 2 * j + b
                    key = [sk[s_ // 2]]
                    lo, hi = 64 * b, 64 * b + 64
                    if smp is None:
                        kprev = KT[pb][lo:hi, :]
                        bprev = bias0[:, s_, :] if bias_first else biasT[:, s_, 0:128]
                        bown = biasT[:, s_, 128:256]
                        kpk = "KT%d" % pb
                    else:
                        kprev = smp["KTc"][lo:hi, smp["s"], :]
                        bprev = biasT[0:4, s_, 0:128]
                        o0 = 128 - 4 * smp["s"]
                        bown = smp["Bown"][0:4, s_, o0:o0 + 128]
                        kpk = "KT%d" % cb
                    S.pe(lambda e, j=j, lo=lo, hi=hi, s_=s_, kprev=kprev: e.matmul(
                        Sps[0:rows, s_, 0:128], lhsT=QT[lo:hi, j, q0:q0 + rows], rhs=kprev, start=True, stop=False),
                        r=["QT", kpk] + xkeys, w=key)
                    S.pe(lambda e, s_=s_, bprev=bprev: e.matmul(Sps[0:rows, s_, 0:128], lhsT=idn, rhs=bprev,
                                                                start=False, stop=True),
                         r=["ident", "bias0", "biasT"], w=key)
                    S.pe(lambda e, j=j, lo=lo, hi=hi, s_=s_: e.matmul(
                        Sps[0:rows, s_, 128:256], lhsT=QT[lo:hi, j, q0:q0 + rows], rhs=KT[cb][lo:hi, :],
                        start=True, stop=False), r=["QT", "KT%d" % cb], w=key)
                    S.pe(lambda e, s_=s_, bown=bown: e.matmul(Sps[0:rows, s_, 128:256], lhsT=idn, rhs=bown,
                                                              start=False, stop=True),
                         r=["ident", "biasT"] + xkeys, w=key)
            attn_core(8, Sps, sk, 0.125, V_SINK, Pm, "Pm", 2, rows=rows)
            b0, ok = psbank(1)
            pO = psf(b0).rearrange("p (c n) -> p c n", c=4)
            for j in range(4):
                n = 0
                for b in range(2):
                    s_ = 2 * j + b
                    for kb in (0, 1):
                        if kb == 1:
                            vap, vk = Vp[cb][:, b, :], ["Vp%d" % cb]
                        elif smp is None:
                            vap, vk = Vp[pb][:, b, :], ["Vp%d" % pb]
                        else:
                            vap, vk = smp["Vc"][b][:, smp["s"], :], xkeys
                        S.pe(lambda e, j=j, s_=s_, kb=kb, vap=vap, n=n: e.matmul(
                            pO[:, j, 0:rows], lhsT=vap, rhs=PT[:, s_ * 2 + kb, 0:rows],
                            start=(n == 0), stop=(n == 3)),
                            r=vk + ["PTa", "PTb"], w=ok)
                        n += 1
            S.act(lambda e: e.activation(out=yT[:, 4:8, q0:q0 + rows], in_=pO[:, :, 0:rows], func=ACT.Copy),
                  r=ok, w=["yT_att"])

        def mixer_gate(nTt, nTkey):
            pg, kg = fm_proj(w_in, "w_in", 512, 4, nTt, nTkey)
            S.act(lambda e: e.activation(out=gg[:], in_=pg, func=ACT.Gelu_apprx_tanh), r=kg, w=["rg"])
            S.dve(lambda e: e.tensor_tensor(out=yT[:, 0:4, :], in0=hv[:], in1=gg[:], op=ALU.mult),
                  r=["hv", "rg"], w=["yT_lru"])

        def mixer_out(xti, xkey, nTt, nTkey, gate_done=False):
            if not gate_done:
                mixer_gate(nTt, nTkey)
            b0, pk = psbank(2)
            po = psf(b0, 2)
            for hh in range(2):
                for k in range(8):
                    S.pe(lambda e, hh=hh, k=k: e.matmul(po[:, hh * 512:(hh + 1) * 512], lhsT=yT[:, k, :],
                                                        rhs=w_out[:, k, hh * 512:(hh + 1) * 512],
                                                        start=(k == 0), stop=(k == 7)),
                         r=["yT_lru", "yT_att", "w_out"], w=[pk[hh]])
            S.dve(lambda e: e.tensor_tensor(out=xti[:], in0=xti[:], in1=po, op=ALU.add), r=[xkey] + pk, w=[xkey])

        def cross_attend(smp=None):
            if smp is None:
                rows, q0, kT, vv, xkeys = 128, 0, mkT, mvb, ["mkT", "mvb"]
            else:
                rows, q0, kT, vv, xkeys = 4, 4 * smp["s"], smp["mkT"], smp["mv"], smp["keys"]
            b0, sk = psbank(2)
            Sps = psf(b0, 2).rearrange("p (h n) -> p h n", h=4)
            for h in range(4):
                S.pe(lambda e, h=h: e.matmul(Sps[0:rows, h, :], lhsT=QC[:, h, q0:q0 + rows], rhs=kT[:, h, :],
                                             start=True, stop=True),
                     r=["QC"] + xkeys, w=[sk[h // 2]])
            attn_core(4, Sps, sk, SC_MEM, None, Pm, "Pm", 2, rows=rows)
            b0, ok = psbank(1)
            pO = psf(b0).rearrange("p (c n) -> p c n", c=4)
            for h in range(4):
                for kb in range(2):
                    S.pe(lambda e, h=h, kb=kb: e.matmul(pO[:, h, 0:rows], lhsT=vv[:, kb, h * 128:(h + 1) * 128],
                                                        rhs=PT[:, h * 2 + kb, 0:rows], start=(kb == 0), stop=(kb == 1)),
                         r=xkeys + ["PTa", "PTb"], w=ok)
            S.act(lambda e: e.activation(out=OC[:, :, q0:q0 + rows], in_=pO[:, :, 0:rows], func=ACT.Copy),
                  r=ok, w=["OC"])

        def cross_stage(xti, xkey, nTt, nTkey, smp_iter=None):
            rmsnorm_T(xti[:], xkey, V_GCROSS, nTt[:], nTkey, "n2")
            pq, kq = fm_proj(w_q, "w_q", 0, 4, nTt, nTkey)
            S.act(lambda e: e.activation(out=QC[:], in_=pq, func=ACT.Copy), r=kq, w=["QC"])
            if smp_iter is None:
                cross_attend(None)
            else:
                smp_iter()
            b0, pk = psbank(2)
            po = psf(b0, 2)
            for hh in range(2):
                for k in range(4):
                    S.pe(lambda e, hh=hh, k=k: e.matmul(po[:, hh * 512:(hh + 1) * 512], lhsT=OC[:, k, :],
                                                        rhs=w_o[:, k, hh * 512:(hh + 1) * 512],
                                                        start=(k == 0), stop=(k == 3)),
                         r=["OC", "w_o"], w=[pk[hh]])
            S.dve(lambda e: e.tensor_tensor(out=xti[:], in0=xti[:], in1=po, op=ALU.add), r=[xkey] + pk, w=[xkey])

        v_k8 = lambda t: t[:].rearrange("p (k n) -> p k n", k=8)
        wq = []
        wissued = [0]

        def wview(i):
            return ring[i % NRING][:].rearrange("p (k n) -> p k n", k=wq[i][1])

        def wq_add_macro(gate_only=False):
            base = len(wq)
            if gate_only:
                for g in range(6):
                    wq.append((w_g_d[:, g * 512:(g + 1) * 512].rearrange("(k p) n -> p k n", p=128), 8))
                return base
            for fh in range(2):
                for gi in range(3):
                    g = 3 * fh + gi
                    wq.append((w_g_d[:, g * 512:(g + 1) * 512].rearrange("(k p) n -> p k n", p=128), 8))
                    wq.append((w_u_d[:, g * 512:(g + 1) * 512].rearrange("(k p) n -> p k n", p=128), 8))
                for gi in range(3):
                    g = 3 * fh + gi
                    wq.append((w_d_d[g * 512:(g + 1) * 512, :].rearrange("(k p) n -> p k n", p=128), 4))
            return base

        def wq_issue(upto):
            while wissued[0] < len(wq) and wissued[0] <= upto:
                j = wissued[0]
                S.dma("gpsimd", wview(j), wq[j][0], w=["ring%d" % (j % NRING)])
                wissued[0] += 1

        def wq_get(i):
            wq_issue(i)
            return wview(i), "ring%d" % (i % NRING)

        def wq_done(i):
            wq_issue(i + NRING)

        def load_x(ti, slot):
            S.dma("sync", xt[slot][:], xs[ti * 128:(ti + 1) * 128, :], w=["xt%d" % slot])

        for mt in range(2):
            S.dma("sync", xt[mt][:], memx[mt * 128:(mt + 1) * 128, :], w=["xt%d" % mt])
        for mt in range(2):
            rmsnorm_T(xt[mt][:], "xt%d" % mt, V_GMEM, nT[mt][:], "nT%d" % mt, "nm")
            for (wsl, wkey, od, is_k) in ((wk_s, "ring0", omk_d, True), (wv_s, "ring1", omv_d, False)):
                b0, pk = psbank(1)
                pm = psf(b0)
                for k in range(8):
                    S.pe(lambda e, k=k, wsl=wsl, pm=pm, mt=mt: e.matmul(pm, lhsT=nT[mt][:, k, :], rhs=wsl[:, k, :],
                                                                 start=(k == 0), stop=(k == 7)),
                         r=["nT%d" % mt, wkey], w=pk)
                S.act(lambda e, pm=pm: e.activation(out=mtmp[:], in_=pm, func=ACT.Copy), r=pk, w=["t1_0"])
                if not is_k:
                    S.dve(lambda e, mt=mt: e.tensor_copy(out=mvb[:, mt, :], in_=mtmp[:]), r=["t1_0"], w=["mvb"])
                S.dma("sync", od[mt * 128:(mt + 1) * 128, :], mtmp[:], r=["t1_0"])
            pkT, kk = fm_proj(wk_s, "ring0", 0, 4, nT[mt], "nT%d" % mt)
            S.act(lambda e, pkT=pkT, mt=mt: e.activation(out=mkT[:, :, mt * 128:(mt + 1) * 128], in_=pkT, func=ACT.Copy),
                  r=kk, w=["mkT"])

        def ffn_macro(tiles, rows_out, N, halo=Gh, halokey="Gh", smp=None):
            wb = wq_add_macro()
            pend = [None]
            for fh in range(2):
                for gi in range(3):
                    g = 3 * fh + gi
                    wg3, wgk = wq_get(wb + fh * 9 + 2 * gi)
                    wu3, wuk = wq_get(wb + fh * 9 + 2 * gi + 1)
                    for c in range(4):
                        fc = g * 4 + c
                        hmi = gi * 4 + c
                        i2 = fc % 2
                        b0, pk = psbank(2)
                        pG, pU = psf(b0)[:, 0:N], psf(b0 + 1)[:, 0:N]
                        for k in range(8):
                            S.pe(lambda e, k=k, c=c, wg3=wg3, pG=pG: e.matmul(
                                pG, lhsT=wg3[:, k, c * 128:(c + 1) * 128], rhs=n3T[:, k, 0:N],
                                start=(k == 0), stop=(k == 7)), r=[wgk, "n3T"], w=[pk[0]])
                        for k in range(8):
                            S.pe(lambda e, k=k, c=c, wu3=wu3, pU=pU: e.matmul(
                                pU, lhsT=wu3[:, k, c * 128:(c + 1) * 128], rhs=n3T[:, k, 0:N],
                                start=(k == 0), stop=(k == 7)), r=[wuk, "n3T"], w=[pk[1]])
                        Gsb, gk = Gs[i2], "Gs%d" % i2
                        tk = "t1_%d" % i2
                        if smp is None:
                            S.dve(lambda e, fc=fc, Gsb=Gsb: e.tensor_copy(out=Gsb[:, 0:2], in_=halo[:, fc, :]),
                                  r=[halokey + "%d" % fc, halokey], w=[gk])
                            S.act(lambda e, Gsb=Gsb, pG=pG: e.activation(out=Gsb[:, 2:2 + N], in_=pG, func=ACT.Copy),
                                  r=[pk[0]], w=[gk])
                            S.dve(lambda e, fc=fc, Gsb=Gsb: e.tensor_copy(out=halo[:, fc, :], in_=Gsb[:, N:N + 2]),
                                  r=[gk], w=[halokey + "%d" % fc])
                            t1v = t1[i2][:, 0:N]
                            g_tap = lambda tap, Gsb=Gsb: Gsb[:, tap:tap + N]
                            pGv = pG
                        else:
                            FS, FSn, fkeys = smp["FS"], smp["FSn"], smp["keys"]
                            G3 = Gsb[:, 0:96].rearrange("p (s t) -> p s t", t=6)
                            S.dve(lambda e, fc=fc, G3=G3, FS=FS: e.tensor_copy(out=G3[:, :, 0:2], in_=FS[:, fc, :, :]),
                                  r=[fkeys[0]], w=[gk])
                            S.act(lambda e, G3=G3, pG=pG: e.activation(
                                out=G3[:, :, 2:6], in_=pG[:, 0:64].rearrange("p (s t) -> p s t", t=4), func=ACT.Copy),
                                r=[pk[0]], w=[gk])
                            S.dve(lambda e, fc=fc, G3=G3, FSn=FSn: e.tensor_copy(out=FSn[:, fc, :, :], in_=G3[:, :, 4:6]),
                                  r=[gk], w=[fkeys[1]])
                            t1v = t1[i2][:, 0:64].rearrange("p (s t) -> p s t", t=4)
                            g_tap = lambda tap, G3=G3: G3[:, :, tap:tap + 4]
                            pGv = pG[:, 0:64].rearrange("p (s t) -> p s t", t=4)
                        S.act(lambda e, fc=fc, pGv=pGv, t1v=t1v: e.activation(
                            out=t1v, in_=pGv, func=ACT.Identity, bias=vcol(V_FB + fc),
                            scale=vcol(V_FW + 48 + fc)), r=[pk[0], "vec"], w=[tk])
                        for tap in (1, 0):
                            S.dve(lambda e, fc=fc, tap=tap, g_tap=g_tap, t1v=t1v: e.scalar_tensor_tensor(
                                out=t1v, in0=g_tap(tap), scalar=vcol(V_FW + tap * 24 + fc),
                                in1=t1v, op0=ALU.mult, op1=ALU.add),
                                r=[gk, "vec", tk], w=[tk])
                        S.act(lambda e, i2=i2: e.activation(out=t1[i2][:, 0:N], in_=t1[i2][:, 0:N],
                                                            func=ACT.Gelu_apprx_tanh), r=[tk], w=[tk])
                        mult = (lambda hmi=hmi, i2=i2, pU=pU, tk=tk, pk1=pk[1]: S.dve(
                            lambda e: e.tensor_tensor(out=hm[:, hmi, 0:N], in0=t1[i2][:, 0:N], in1=pU, op=ALU.mult),
                            r=[tk, pk1], w=["hm%d" % hmi]))
                        if pend[0] is not None:
                            pend[0]()
                        pend[0] = mult
                    wq_done(wb + fh * 9 + 2 * gi + 1)
                if pend[0] is not None:
                    pend[0]()
                    pend[0] = None
                wds = [wq_get(wb + fh * 9 + 6 + q) for q in range(3)]
                for j, tj in enumerate(tiles):
                    sl = xslot(tj)
                    for hh in range(2):
                        b0, pk = psbank(1)
                        pD = psf(b0)
                        for q in range(3):
                            for c in range(4):
                                hmi = q * 4 + c
                                S.pe(lambda e, q=q, c=c, hmi=hmi, j=j, hh=hh, pD=pD, wds=wds: e.matmul(
                                    pD, lhsT=hm[:, hmi, j * 128:(j + 1) * 128],
                                    rhs=wds[q][0][:, c, hh * 512:(hh + 1) * 512],
                                    start=(hmi == 0), stop=(hmi == 11)),
                                    r=[wds[q][1], "hm%d" % hmi], w=pk)
                        S.dve(lambda e, sl=sl, hh=hh, pD=pD: e.tensor_tensor(
                            out=xt[sl][:, hh * 512:(hh + 1) * 512], in0=xt[sl][:, hh * 512:(hh + 1) * 512],
                            in1=pD, op=ALU.add), r=["xt%d" % sl] + pk, w=["xt%d" % sl])
                wq_done(wb + fh * 9 + 8)
            for j, tj in enumerate(tiles):
                sl = xslot(tj)
                xk2 = "xt%d" % sl
                S.act(lambda e, sl=sl: e.activation(out=junk[:], in_=xt[sl][:], func=ACT.Square, accum_out=st[:, 8:9]),
                      r=[xk2], w=["nbf", "st_ss"])
                S.act(lambda e: e.activation(out=st[:, 9:10], in_=st[:, 8:9], func=ACT.Sqrt, bias=vcol(V_EPS),
                                             scale=1.0 / D), r=["st_ss", "vec"], w=["st_sd"])
                S.dve(lambda e: e.reciprocal(out=st[:, 10:11], in_=st[:, 9:10]), r=["st_sd"], w=["st_rs"])
                S.dve(lambda e, sl=sl: e.scalar_tensor_tensor(out=xt[sl][:], in0=xt[sl][:], scalar=st[:, 10:11],
                                                              in1=gfin[:], op0=ALU.mult, op1=ALU.mult),
                      r=[xk2, "st_rs", "gfin"], w=[xk2])
                S.dma("sync", y_d[rows_out[j]:rows_out[j] + 128, :], xt[sl][:], r=[xk2])

        total = n_pre + n_main
        xslot = lambda ti: ti % NXB
        load_x(0, 0)
        import os
        n_fast = ((n_pre - 4) // 4) * 4 if (n_pre >= 8 and not os.environ.get("NOFAST")) else 0
        if n_fast:
            XCv = ring[0].bitcast(F32)[:, 0:2048].rearrange("p (c n) -> p c n", c=4)
            RGv = ring[1].bitcast(F32)[:, 0:2048].rearrange("p (c n) -> p c n", c=4)
            IGv = ring[2].bitcast(F32)[:, 0:2048].rearrange("p (c n) -> p c n", c=4)
            BVv = ring[3].bitcast(F32)[:, 0:2048].rearrange("p (c n) -> p c n", c=4)
            XBv = hm.bitcast(F32)[:].rearrange("p a b -> p (a b)")[:, 0:2060].rearrange("p (c n) -> p c n", c=4)
            XCBv = w_out[:].rearrange("p a b -> p (a b)")[:, 0:2048].rearrange("p (c n) -> p c n", c=4)
            basekeys = ["ring0", "ring1", "ring2", "ring3", "w_out"] + ["hm%d" % i for i in range(12)]
            ckeys = []
            for c in range(4):
                ckeys += ["fXC%d" % c, "fRG%d" % c, "fIG%d" % c, "fBV%d" % c, "fXB%d" % c, "fXCB%d" % c]
            S.pool(lambda e: e.memset(fence[:, 0:1], 0.0),
                   r=["w_in", "w_q", "w_o", "ident", "biasT", "bias0", "wa", "wx", "vec", "gfin", "mkT", "mvb"],
                   w=basekeys + ckeys)
            S.pool(lambda e: e.memset(HALO[:], 0.0), w=["HALO"])

            def fast_s1(c):
                b0, pk = psbank(1)
                pxr = psf(b0)
                for k in range(8):
                    S.pe(lambda e, c=c, k=k, pxr=pxr: e.matmul(pxr, lhsT=w_in[:, k, c * 128:(c + 1) * 128],
                                                               rhs=n3T[:, k, :], start=(k == 0), stop=(k == 7)),
                         r=["w_in", "n3T"], w=pk)
                S.dve(lambda e, c=c: e.tensor_copy(out=XBv[:, c, 0:3], in_=HALO[:, c, :]), r=["HALO"], w=["fXB%d" % c])
                S.act(lambda e, c=c, pxr=pxr: e.activation(out=XBv[:, c, 3:515], in_=pxr, func=ACT.Copy),
                      r=pk, w=["fXB%d" % c])
                S.dve(lambda e, c=c: e.tensor_copy(out=HALO[:, c, :], in_=XBv[:, c, 512:515]),
                      r=["fXB%d" % c], w=["HALO"])
                S.dve(lambda e, c=c: e.tensor_scalar(out=XCv[:, c, :], in0=XBv[:, c, 3:515],
                                                     scalar1=vcol(V_CW + 12 + c), scalar2=vcol(V_BCONV + c),
                                                     op0=ALU.mult, op1=ALU.add),
                      r=["fXB%d" % c, "vec"], w=["fXC%d" % c])
                for tap in range(3):
                    S.dve(lambda e, c=c, tap=tap: e.scalar_tensor_tensor(
                        out=XCv[:, c, :], in0=XBv[:, c, tap:tap + 512], scalar=vcol(V_CW + tap * 4 + c),
                        in1=XCv[:, c, :], op0=ALU.mult, op1=ALU.add),
                        r=["fXB%d" % c, "vec", "fXC%d" % c], w=["fXC%d" % c])
                S.act(lambda e, c=c: e.activation(out=XCBv[:, c, :], in_=XCv[:, c, :], func=ACT.Copy),
                      r=["fXC%d" % c], w=["fXCB%d" % c])

            def fast_s2(c):
                b0, pk = psbank(2)
                pr, pi = psf(b0), psf(b0 + 1)
                S.pe(lambda e, c=c, pr=pr: e.matmul(pr, lhsT=wa[:, c, :], rhs=XCBv[:, c, :], start=True, stop=True),
                     r=["wa", "fXCB%d" % c], w=[pk[0]])
                S.pe(lambda e, c=c, pi=pi: e.matmul(pi, lhsT=wx[:, c, :], rhs=XCBv[:, c, :], start=True, stop=True),
                     r=["wx", "fXCB%d" % c], w=[pk[1]])
                S.act(lambda e, c=c, pr=pr: e.activation(out=RGv[:, c, :], in_=pr, func=ACT.Sigmoid,
                                                         bias=vcol(V_BA + c), scale=1.0),
                      r=[pk[0], "vec"], w=["fRG%d" % c])
                S.act(lambda e, c=c, pi=pi: e.activation(out=IGv[:, c, :], in_=pi, func=ACT.Sigmoid,
                                                         bias=vcol(V_BX + c), scale=1.0),
                      r=[pk[1], "vec"], w=["fIG%d" % c])
                S.act(lambda e, c=c: e.activation(out=RGv[:, c, :], in_=RGv[:, c, :], func=ACT.Exp,
                                                  scale=cl[:, c:c + 1]), r=["fRG%d" % c, "cl"], w=["fRG%d" % c])
                S.act(lambda e, c=c: e.activation(out=BVv[:, c, :], in_=RGv[:, c, :], func=ACT.Square),
                      r=["fRG%d" % c], w=["fBV%d" % c])
                S.act(lambda e, c=c: e.activation(out=BVv[:, c, :], in_=BVv[:, c, :], func=ACT.Sqrt,
                                                  bias=vcol(V_ONE), scale=-1.0),
                      r=["fBV%d" % c, "vec"], w=["fBV%d" % c])
                S.dve(lambda e, c=c: e.tensor_tensor(out=IGv[:, c, :], in0=IGv[:, c, :], in1=XCv[:, c, :], op=ALU.mult),
                      r=["fIG%d" % c, "fXC%d" % c], w=["fIG%d" % c])
                S.dve(lambda e, c=c: e.tensor_tensor(out=BVv[:, c, :], in0=BVv[:, c, :], in1=IGv[:, c, :], op=ALU.mult),
                      r=["fBV%d" % c, "fIG%d" % c], w=["fBV%d" % c])
                S.dve(lambda e, c=c: e.tensor_tensor_scan(out=IGv[:, c, :], data0=RGv[:, c, :], data1=BVv[:, c, :],
                                                           initial=hc[:, c:c + 1], op0=ALU.mult, op1=ALU.add),
                      r=["fRG%d" % c, "fBV%d" % c, "hc"], w=["fIG%d" % c])
                S.dve(lambda e, c=c: e.tensor_copy(out=hc[:, c:c + 1], in_=IGv[:, c, 511:512]),
                      r=["fIG%d" % c], w=["hc"])

            for m0 in range(0, n_fast, 4):
                for j in range(4):
                    tj = m0 + j
                    if tj + 1 < total:
                        load_x(tj + 1, xslot(tj + 1))
                    rmsnorm_T(xt[xslot(tj)][:], "xt%d" % xslot(tj), V_GMIX, n3T[:, :, j * 128:(j + 1) * 128], "n3T", "n1")
                fast_s1(0)
                fast_s1(1)
                fast_s2(0)
                fast_s1(2)
                fast_s2(1)
                fast_s1(3)
                fast_s2(2)
                fast_s2(3)
                if (m0 + 4) % 16 == 0:
                    fcol = V_FLAG + (m0 + 4) // 16 - 1
                    S.dve(lambda e, fcol=fcol: e.tensor_scalar(out=hc[:], in0=hc[:], scalar1=vcol(fcol), scalar2=None,
                                                               op0=ALU.mult), r=["hc", "vec"], w=["hc"])
            S.pool(lambda e: e.tensor_copy(out=xrp[(n_fast + 1) % 2][:, :, 128:131], in_=HALO[:]),
                   r=["HALO"], w=["xrp%d" % ((n_fast + 1) % 2)])
            S.pool(lambda e: e.memset(fence[:, 1:2], 0.0), r=ckeys, w=basekeys)
        S.dma("gpsimd", w_out[:], w_out_d.rearrange("(k p) n -> p k n", p=128), w=["w_out"])
        for ti in range(n_fast, total):
            slot = xslot(ti)
            xk = "xt%d" % slot
            if ti + 1 < total:
                load_x(ti + 1, xslot(ti + 1))
            nTt = nT[ti % 2]
            nTk = "nT%d" % (ti % 2)
            rmsnorm_T(xt[slot][:], xk, V_GMIX, nTt[:], nTk, "n1")
            if ti >= n_pre - 2:
                cb, pb = swa_stage(ti, nTt, nTk, False, False)
            lru_stage(ti, nTt, nTk)
            full = ti >= n_pre - 1
            if ti < n_pre and (ti + 1) % 16 == 0:
                fcol = V_FLAG + (ti + 1) // 16 - 1
                S.dve(lambda e, fcol=fcol: e.tensor_scalar(out=hc[:], in0=hc[:], scalar1=vcol(fcol), scalar2=None,
                                                           op0=ALU.mult), r=["hc", "vec"], w=["hc"])
            if not full:
                continue
            mixer_gate(nTt, nTk)
            swa_attend(cb, pb, ti == n_pre)
            mixer_out(xt[slot], xk, nTt, nTk, gate_done=True)
            cross_stage(xt[slot], xk, nTt, nTk)
            mi = (ti - n_pre) % 4 if ti >= n_pre else 0
            if ti == n_pre - 1:
                rmsnorm_T(xt[slot][:], xk, V_GFFN, nTt[:], nTk, "n3")
                b0, pk = psbank(1)
                ph = psf(b0)[:, 0:48].rearrange("p (c n) -> p c n", n=2)
                wb = wq_add_macro(gate_only=True)
                for g in range(6):
                    wg3, wgk = wq_get(wb + g)
                    for c in range(4):
                        for k in range(8):
                            S.pe(lambda e, g=g, c=c, k=k, wg3=wg3, ph=ph, nTt=nTt: e.matmul(
                                ph[:, g * 4 + c, :], lhsT=wg3[:, k, c * 128:(c + 1) * 128], rhs=nTt[:, k, 126:128],
                                start=(k == 0), stop=(k == 7)), r=[wgk, nTk], w=pk)
                    wq_done(wb + g)
                S.dve(lambda e, ph=ph: e.tensor_scalar(out=Gh[:], in0=ph, scalar1=vcol(V_FLAG + 3), scalar2=None, op0=ALU.mult),
                      r=pk + ["vec"], w=["Gh"])
                continue
            rmsnorm_T(xt[slot][:], xk, V_GFFN, n3T[:, :, mi * 128:(mi + 1) * 128], "n3T", "n3")
            if ti == total - 1:
                S.dma("sync", okv_d, kvf[:], r=["kvf"])
                S.dve(lambda e, ti=ti: e.tensor_copy(out=XC2[:, :, 0:3], in_=xrp[ti % 2][:, :, 128:131]),
                      r=["xrp%d" % (ti % 2)], w=["XC2"])
                b0, pk = psbank(1)
                pq_ = psf(b0)
                for c in range(4):
                    S.pe(lambda e, c=c, pq_=pq_: e.transpose(out=pq_[0:3, c * 128:(c + 1) * 128], in_=XC2[:, c, 0:3],
                                                             identity=ident32[:]), r=["XC2", "ident32"], w=pk)
                S.act(lambda e, pq_=pq_: e.activation(out=stL[0:3, :], in_=pq_[0:3, :], func=ACT.Copy), r=pk, w=["stL"])
                S.dma("sync", olc_d, stL[0:3, :], r=["stL"])
                S.dve(lambda e: e.tensor_copy(out=HS2[:, :, 0:1], in_=hc[:].unsqueeze(2)), r=["hc"], w=["HS2"])
                b0, pk = psbank(1)
                ph_ = psf(b0)
                for c in range(4):
                    S.pe(lambda e, c=c, ph_=ph_: e.transpose(out=ph_[0:1, c * 128:(c + 1) * 128], in_=HS2[:, c, 0:1],
                                                             identity=ident32[:]), r=["HS2", "ident32"], w=pk)
                S.act(lambda e, ph_=ph_: e.activation(out=stH[0:1, :], in_=ph_[0:1, :], func=ACT.Copy), r=pk, w=["stH"])
                S.dma("sync", olh_d.rearrange("(a n) -> a n", a=1), stH[0:1, :], r=["stH"])
            if mi != 3:
                continue
            ffn_macro([ti - 3, ti - 2, ti - 1, ti], [(tj - n_pre) * 128 for tj in (ti - 3, ti - 2, ti - 1, ti)], 512)
            if ti == total - 1:
                for g in range(6):
                    b0, pk = psbank(1)
                    pg_ = psf(b0)
                    for c in range(4):
                        S.pe(lambda e, c=c, g=g, pg_=pg_: e.transpose(out=pg_[0:2, c * 128:(c + 1) * 128],
                                                                     in_=Gh[:, g * 4 + c, :], identity=ident32[:]),
                             r=["Gh", "ident32"] + ["Gh%d" % fc for fc in range(24)], w=pk)
                    S.act(lambda e, pg_=pg_: e.activation(out=stL[0:2, :], in_=pg_[0:2, :], func=ACT.Copy),
                          r=pk, w=["stL"])
                    S.dma("sync", ofc_d[:, g * 512:(g + 1) * 512], stL[0:2, :], r=["stL"])

        if do_sample:
            ti = total
            slot = xslot(ti)
            xk = "xt%d" % slot
            free = [i for i in range(NXB) if i != slot]
            fk = ["xt%d" % i for i in free]
            Vc = [xt[free[0]].bitcast(BF16)[:, 0:2048].rearrange("p (s d) -> p s d", d=128),
                  xt[free[1]].bitcast(BF16)[:, 0:2048].rearrange("p (s d) -> p s d", d=128)]
            FS = xt[free[2]][:, 0:768].rearrange("p (c s k) -> p c s k", c=24, k=2)
            FSn = xt[free[3]][:, 0:768].rearrange("p (c s k) -> p c s k", c=24, k=2)
            n3flat = n3T[:].rearrange("p a b -> p (a b)")
            KTc = n3flat[:, 0:2048].rearrange("p (s d) -> p s d", d=128)
            Bown = n3flat[:, 2048:4096].rearrange("p (h n) -> p h n", n=256)
            hmflat = hm[:].rearrange("p a b -> p (a b)")
            S.dma("sync", xt[slot][:], xs[ti * 128:(ti + 1) * 128, :], w=[xk])
            S.dma("sync", stL[0:48, :], slc_d, w=["stL"])
            S.dma("sync", stH[0:16, :], slh_d, w=["stH"])
            S.dma("sync", osk_d[:, 0:124, :], cswk_d[:, 4:128, :])
            S.dma("sync", osv_d[:, 0:124, :], cswv_d[:, 4:128, :])
            S.pool(lambda e: e.memset(xt[free[0]][:], 0.0), w=[fk[0]])
            S.pool(lambda e: e.memset(xt[free[1]][:], 0.0), w=[fk[1]])
            Kc = hmflat[:, 0:2048].rearrange("p (s d) -> p s d", d=128)
            hk03 = ["hm0", "hm1", "hm2", "hm3"]
            S.dma("gpsimd", Kc, cswk_d.rearrange("s k d -> k s d"), w=hk03)
            S.dma("gpsimd", Vc[0][:, :, 0:64], cswv_d.rearrange("s k d -> k s d")[:, :, 0:64], w=[fk[0]])
            S.dma("gpsimd", Vc[1][:, :, 64:128], cswv_d.rearrange("s k d -> k s d")[:, :, 64:128], w=[fk[1]])
            S.dma("gpsimd", Bown, bown_d.rearrange("p (h n) -> p h n", n=256), w=["n3T"])
            for half in range(2):
                b0, pk = psbank(1)
                pv = psb(b0).rearrange("p (t n) -> p t n", n=128)
                for i in range(8):
                    S.pe(lambda e, i=i, half=half, pv=pv: e.transpose(out=pv[:, i, :], in_=Kc[:, half * 8 + i, :],
                                                                      identity=ident[:]),
                         r=hk03 + ["ident"], w=pk)
                S.act(lambda e, half=half, pv=pv: e.activation(out=KTc[:, half * 8:(half + 1) * 8, :], in_=pv,
                                                               func=ACT.Copy), r=pk, w=["n3T"])
            b0, pk = psbank(1)
            p32 = psf(b0)
            for c in range(4):
                S.pe(lambda e, c=c: e.transpose(out=p32[:, c * 48:(c + 1) * 48], in_=stL[0:48, c * 128:(c + 1) * 128],
                                                identity=ident32[0:48, 0:48]), r=["stL", "ident32"], w=pk)
            S.dve(lambda e: e.tensor_copy(out=XP[:, :, :, 0:3],
                                          in_=p32[:, 0:192].rearrange("p (c s k) -> p c s k", c=4, k=3)),
                  r=pk, w=["XP"])
            b0, pk = psbank(1)
            p32b = psf(b0)
            for c in range(4):
                S.pe(lambda e, c=c: e.transpose(out=p32b[:, c * 16:(c + 1) * 16], in_=stH[0:16, c * 128:(c + 1) * 128],
                                                identity=ident32[0:16, 0:16]), r=["stH", "ident32"], w=pk)
            S.dve(lambda e: e.tensor_copy(out=HS[:], in_=p32b[:, 0:64].rearrange("p (c s) -> p c s", c=4)),
                  r=pk, w=["HS"])
            for g in range(6):
                S.dma("sync", stL[0:32, :], sfc_d[:, g * 512:(g + 1) * 512], w=["stL"])
                b0, pk = psbank(1)
                pf = psf(b0)
                for c in range(4):
                    S.pe(lambda e, c=c, pf=pf: e.transpose(out=pf[:, c * 32:(c + 1) * 32],
                                                           in_=stL[0:32, c * 128:(c + 1) * 128],
                                                           identity=ident32[0:32, 0:32]), r=["stL", "ident32"], w=pk)
                S.dve(lambda e, g=g, pf=pf: e.tensor_copy(
                    out=FS[:, g * 4:(g + 1) * 4, :, :],
                    in_=pf[:, 0:128].rearrange("p (c s k) -> p c s k", c=4, k=2)), r=pk, w=[fk[2]])
            nTt, nTk = nT[ti % 2], "nT%d" % (ti % 2)
            rmsnorm_T(xt[slot][:], xk, V_GMIX, nTt[:], nTk, "n1")
            lru_stage(ti, nTt, nTk, smp=True)
            cb, pb = swa_stage(ti, nTt, nTk, False, False)
            S.dve(lambda e: e.tensor_copy(out=XC2[:].rearrange("p c (s k) -> p c s k", k=3), in_=XP[:, :, :, 4:7]),
                  r=["XP"], w=["XC2"])
            b0, pk = psbank(1)
            po32 = psf(b0)
            for c in range(4):
                S.pe(lambda e, c=c: e.transpose(out=po32[0:48, c * 128:(c + 1) * 128], in_=XC2[:, c, :],
                                                identity=ident32[:]), r=["XC2", "ident32"], w=pk)
            S.act(lambda e: e.activation(out=stL[0:48, :], in_=po32[0:48, :], func=ACT.Copy), r=pk, w=["stL"])
            S.dma("sync", oslc_d, stL[0:48, :], r=["stL"])
            b0, pk = psbank(1)
            po32b = psf(b0)
            for c in range(4):
                S.pe(lambda e, c=c: e.transpose(out=po32b[0:16, c * 128:(c + 1) * 128], in_=HS2[:, c, :],
                                                identity=ident32[:]), r=["HS2", "ident32"], w=pk)
            S.act(lambda e: e.activation(out=stH[0:16, :], in_=po32b[0:16, :], func=ACT.Copy), r=pk, w=["stH"])
            S.dma("sync", oslh_d, stH[0:16, :], r=["stH"])
            for t4 in range(4):
                S.dma("sync", osk_d[:, 124 + t4, :], kvf[t4:64:4, 0:128], r=["kvf"])
                S.dma("sync", osv_d[:, 124 + t4, :], kvf[t4:64:4, 128:256], r=["kvf"])
            for sq in range(16):
                swa_attend(cb, pb, False, smp=dict(s=sq, KTc=KTc, Vc=Vc, Bown=Bown, keys=["n3T", fk[0], fk[1]]))
            mixer_out(xt[slot], xk, nTt, nTk)

            def cross_iter():
                for sq in range(16):
                    i2 = sq % 2
                    mkc = hmflat[:, i2 * 1024:(i2 + 1) * 1024].rearrange("p (a b) -> p a b", a=2)
                    mvc = hmflat[:, 2048 + i2 * 1024:2048 + (i2 + 1) * 1024].rearrange("p (a b) -> p a b", a=2)
                    mkTs = hmflat[:, 4096 + i2 * 1024:4096 + (i2 + 1) * 1024].rearrange("p (a b) -> p a b", a=4)
                    kk = ["hm%d" % (2 * i2), "hm%d" % (2 * i2 + 1)]
                    kv = ["hm%d" % (4 + 2 * i2), "hm%d" % (5 + 2 * i2)]
                    kt = ["hm%d" % (8 + 2 * i2), "hm%d" % (9 + 2 * i2)]
                    S.dma("gpsimd", mkc, cmk_d[sq].rearrange("(a p) n -> p a n", p=128), w=kk)
                    S.dma("gpsimd", mvc, cmv_d[sq].rearrange("(a p) n -> p a n", p=128), w=kv)
                    b0, pk = psbank(1)
                    pv = psb(b0).rearrange("p (t n) -> p t n", n=128)
                    for h in range(4):
                        for kb in range(2):
                            S.pe(lambda e, h=h, kb=kb, pv=pv, mkc=mkc: e.transpose(
                                out=pv[:, h * 2 + kb, :], in_=mkc[:, kb, h * 128:(h + 1) * 128], identity=ident[:]),
                                r=kk + ["ident"], w=pk)
                    S.dve(lambda e, pv=pv, mkTs=mkTs: e.tensor_copy(out=mkTs, in_=pv.rearrange("p (h b) n -> p h (b n)", b=2)),
                          r=pk, w=kt)
                    cross_attend(dict(s=sq, mkT=mkTs, mv=mvc, keys=kk + kv + kt))
            cross_stage(xt[slot], xk, nTt, nTk, smp_iter=cross_iter)
            rmsnorm_T(xt[slot][:], xk, V_GFFN, n3T[:, :, 0:128], "n3T", "n3")
            ffn_macro([ti], [n_main * 128], 128, smp=dict(FS=FS, FSn=FSn, keys=[fk[2], fk[3]]))
            for g in range(6):
                b0, pk = psbank(1)
                pf = psf(b0)
                for c in range(4):
                    S.pe(lambda e, c=c, g=g, pf=pf: e.transpose(
                        out=pf[0:32, c * 128:(c + 1) * 128],
                        in_=FSn[:, g * 4 + c, :, :].rearrange("p s k -> p (s k)"), identity=ident32[:]),
                        r=[fk[3], "ident32"], w=pk)
                S.act(lambda e, pf=pf: e.activation(out=stL[0:32, :], in_=pf[0:32, :], func=ACT.Copy), r=pk, w=["stL"])
                S.dma("sync", osfc_d[:, g * 512:(g + 1) * 512], stL[0:32, :], r=["stL"])

        if max_ops is not None:
            S.ops = S.ops[:max_ops]
        print('nops', len(S.ops))
        S.emit()
    return nc


def _slot_heads():
    return [(s // 2) + 4 * (s % 2) for s in range(8)]


def _bias_tables():
    slopes = 2.0 ** (-np.arange(1, 9, dtype=np.float64))
    qi = np.arange(128)[:, None]
    kj = np.arange(256)[None, :]
    dist = qi + 128 - kj
    valid = (dist >= 0) & (dist < 128)
    tab = np.empty((128, 8, 256), np.float32)
    for s, h in enumerate(_slot_heads()):
        tab[:, s, :] = np.where(valid, -8.0 * slopes[h] * dist, NEG)
    return tab


_NC_CACHE = {}


def kernel(**inp):
    f32 = np.float32
    xp = np.asarray(inp["x_prompt"], f32)
    xsmp = np.asarray(inp["x_sample"], f32)
    heads = _slot_heads()
    w_in = np.asarray(inp["w_in"][0], f32)
    qcols = np.concatenate([np.arange(1024 + h * 64, 1024 + (h + 1) * 64) for h in heads])
    w_in_p = np.ascontiguousarray(np.concatenate([w_in[:, :1024], w_in[:, qcols], w_in[:, 1536:]], axis=1))
    w_out = np.asarray(inp["w_out"][0], f32)
    orow = np.concatenate([np.arange(512 + h * 64, 512 + (h + 1) * 64) for h in heads])
    w_out_p = np.ascontiguousarray(np.concatenate([w_out[:512], w_out[orow]], axis=0))

    def bd(w):
        o = np.zeros((128, 4, 128), f32)
        for c in range(4):
            o[0:64, c, 0:64] = w[2 * c]
            o[64:128, c, 64:128] = w[2 * c + 1]
        return o.reshape(128, 512)

    def fm(v, nchunk):
        return np.asarray(v, f32).reshape(nchunk, 128).T

    vec = np.zeros((128, NV), f32)
    vec[:, V_GMIX:V_GMIX + 8] = fm(inp["g_mix"][0], 8)
    vec[:, V_GCROSS:V_GCROSS + 8] = fm(inp["g_cross"][0], 8)
    vec[:, V_GFFN:V_GFFN + 8] = fm(inp["g_ffn"][0], 8)
    vec[:, V_GMEM:V_GMEM + 8] = fm(inp["g_mem"][0], 8)
    for tap in range(4):
        vec[:, V_CW + tap * 4:V_CW + tap * 4 + 4] = fm(inp["w_lru_conv"][0, tap], 4)
    vec[:, V_BCONV:V_BCONV + 4] = fm(inp["b_lru_conv"][0], 4)
    vec[:, V_BA:V_BA + 4] = fm(inp["b_lru_a"][0], 4)
    vec[:, V_BX:V_BX + 4] = fm(inp["b_lru_x"][0], 4)
    vec[:, V_LAM:V_LAM + 4] = fm(inp["lru_lambda"][0], 4)
    for tap in range(3):
        vec[:, V_FW + tap * 24:V_FW + tap * 24 + 24] = fm(inp["w_ffn_conv"][0, tap], 24)
    vec[:, V_FB:V_FB + 24] = fm(inp["b_ffn_conv"][0], 24)
    vec[:, V_SINK:V_SINK + 8] = np.asarray(inp["attn_sinks"][0], f32)[heads][None, :]
    vec[:, V_EPS] = 1e-6
    vec[:, V_ONE] = 1.0

    slopes = 2.0 ** (-np.arange(1, 9, dtype=np.float64))
    bown = np.full((128, 8, 256), NEG, f32)
    for s_i, h in enumerate(heads):
        for t in range(4):
            for j in range(t + 1):
                bown[t, s_i, 128 + j] = -8.0 * slopes[h] * (t - j)
    gfin = np.ascontiguousarray(np.broadcast_to(np.asarray(inp["g_final"], f32)[None, :], (128, D)))
    ident = np.eye(128, dtype=f32)
    bias = _bias_tables()
    common = dict(
        gfin=gfin, ident=ident, ident32=ident, bias_own=bown.reshape(128, -1),
        bias=bias.reshape(128, -1), w_in=w_in_p, w_out=w_out_p,
        w_q=np.ascontiguousarray(inp["w_mem_q"][0], f32), w_k=np.ascontiguousarray(inp["w_mem_k"][0], f32),
        w_v=np.ascontiguousarray(inp["w_mem_v"][0], f32), w_o=np.ascontiguousarray(inp["w_mem_o"][0], f32),
        wa_bd=bd(np.asarray(inp["w_lru_a"][0], f32)), wx_bd=bd(np.asarray(inp["w_lru_x"][0], f32)),
        w_g=np.ascontiguousarray(inp["w_ffn_gate"][0], f32), w_u=np.ascontiguousarray(inp["w_ffn_up"][0], f32),
        w_d=np.ascontiguousarray(inp["w_ffn_down"][0], f32),
    )
    in_maps = []
    for c in range(NCORES):
        b, q = c // 4, c % 4
        pre = np.zeros((PRE_T * 128, D), f32)
        if q > 0:
            pre[(3 - q) * 2048:] = xp[b, :q * 2048]
        main = xp[b, q * 2048:(q + 1) * 2048]
        smp = np.zeros((128, D), f32)
        smp[:64] = xsmp[c * 16:(c + 1) * 16].reshape(64, D)
        v = vec.copy()
        for k in range(3):
            v[:, V_FLAG + k] = 1.0 if k >= 3 - q else 0.0
        v[:, V_FLAG + 3] = 1.0 if q > 0 else 0.0
        b0 = bias[:, :, 0:128].copy()
        if q == 0:
            b0[:] = NEG
        m = dict(common)
        m.update(xs=np.ascontiguousarray(np.concatenate([pre, main, smp], axis=0)),
                 memx=np.ascontiguousarray(inp["mem_prompt"][b], f32), vec=v,
                 c_swa_k=np.ascontiguousarray(inp["cache_swa_k"][0, c * 16:(c + 1) * 16], f32).reshape(16, 128, 128),
                 c_swa_v=np.ascontiguousarray(inp["cache_swa_v"][0, c * 16:(c + 1) * 16], f32).reshape(16, 128, 128),
                 c_mem_k=np.ascontiguousarray(inp["cache_mem_k"][0, c * 16:(c + 1) * 16], f32).reshape(16, 256, 512),
                 c_mem_v=np.ascontiguousarray(inp["cache_mem_v"][0, c * 16:(c + 1) * 16], f32).reshape(16, 256, 512),
                 st_lconv=np.ascontiguousarray(inp["state_lru_conv"][0, c * 16:(c + 1) * 16], f32).reshape(48, 512),
                 st_lh=np.ascontiguousarray(inp["state_lru_h"][0, c * 16:(c + 1) * 16], f32),
                 st_fconv=np.ascontiguousarray(inp["state_ffn_conv"][0, c * 16:(c + 1) * 16], f32).reshape(32, 3072),
                 bias0=np.ascontiguousarray(b0.reshape(128, -1)))
        in_maps.append(m)

    if "nc" not in _NC_CACHE:
        _NC_CACHE["nc"] = build_nc()
    nc = _NC_CACHE["nc"]
    res = run_bass_kernel_spmd(nc, in_maps, core_ids=list(range(NCORES)))
    R = res.results

    y_prompt = np.zeros((2, 8192, D), f32)
    y_sample = np.zeros((128, 4, D), f32)
    for c in range(NCORES):
        b, q = c // 4, c % 4
        y_prompt[b, q * 2048:(q + 1) * 2048] = R[c]["y"][:2048]
        y_sample[c * 16:(c + 1) * 16] = R[c]["y"][2048:2048 + 64].reshape(16, 4, D)
    p_swa_k = np.stack([R[3]["o_kv"][:, :128], R[7]["o_kv"][:, :128]]).reshape(1, 2, 128, 2, 64)
    p_swa_v = np.stack([R[3]["o_kv"][:, 128:], R[7]["o_kv"][:, 128:]]).reshape(1, 2, 128, 2, 64)
    p_mem_k = np.stack([R[0]["o_mk"], R[4]["o_mk"]]).reshape(1, 2, 256, 4, 128)
    p_mem_v = np.stack([R[0]["o_mv"], R[4]["o_mv"]]).reshape(1, 2, 256, 4, 128)
    p_lru_conv = np.stack([R[3]["o_lconv"], R[7]["o_lconv"]]).reshape(1, 2, 3, 512)
    p_lru_h = np.stack([R[3]["o_lh"], R[7]["o_lh"]]).reshape(1, 2, 512)
    p_ffn_conv = np.stack([R[3]["o_fconv"], R[7]["o_fconv"]]).reshape(1, 2, 2, 3072)
    cat = lambda k: np.concatenate([R[c][k] for c in range(NCORES)], axis=0)
    s_swa_k = cat("o_sk").reshape(1, 128, 128, 2, 64)
    s_swa_v = cat("o_sv").reshape(1, 128, 128, 2, 64)
    s_lru_conv = cat("o_slc").reshape(1, 128, 3, 512)
    s_lru_h = cat("o_slh").reshape(1, 128, 512)
    s_ffn_conv = cat("o_sfc").reshape(1, 128, 2, 3072)
    return (y_prompt, y_sample, p_swa_k, p_swa_v, p_mem_k, p_mem_v, p_lru_conv, p_lru_h, p_ffn_conv,
            s_swa_k, s_swa_v, s_lru_conv, s_lru_h, s_ffn_conv)
```
